# Optimizing a Trainium2 kernel written in Bass

```python
import jax, jax.numpy as jnp
from jax import lax
import numpy as np

D_MODEL = 1024
BATCH = 32
SEQ = 2048
DEPTH = 2
DEC_BATCH = 8
DEC_SEQ = 2048
PAST_LEN = 128

N_MIXERS = 2
D_INNER = D_MODEL
HEAD_DIM = 64
N_HEADS = D_INNER // HEAD_DIM
DECAY_LORA = max(32, int(round(1.8 * D_MODEL ** 0.5 / 32)) * 32)
ICLR_LORA = max(32, int(round(1.8 * D_MODEL ** 0.5 / 32)) * 32)
N_SHIFT_MIX = 7
LNX_EPS = 64e-5
RMS_EPS = 1e-6
GRID_W = 64
WIN_ROWS = 8
WIN_COLS = 16
Q_COL_BLOCK = 16
KEY_COL_SPAN = Q_COL_BLOCK + WIN_COLS
NEG_INF = -1e30

kernel_name = 'hybrid_rwkv7_natten2d_gated_encoder'


def _rms_norm(x, g):
    x32 = x.astype(jnp.float32)
    y = x32 * lax.rsqrt(jnp.mean(x32 * x32, axis=-1, keepdims=True) + RMS_EPS) * g.astype(jnp.float32)
    return y.astype(x.dtype)


def _wkv7_scan(r, decay, k, v, a, b, reverse):
    bsz, _, h, n = r.shape

    def step(s, inp):
        r_t, w_t, k_t, v_t, a_t, b_t = inp
        sa = jnp.einsum('bhvk,bhk->bhv', s, a_t)
        s = s * w_t[:, :, None, :] + sa[..., None] * b_t[:, :, None, :] + v_t[..., None] * k_t[:, :, None, :]
        return s, jnp.einsum('bhvk,bhk->bhv', s, r_t)

    xs = tuple(jnp.swapaxes(u, 0, 1) for u in (r, decay, k, v, a, b))
    s0 = jnp.zeros((bsz, h, n, n), jnp.float32)
    _, y = lax.scan(step, s0, xs, reverse=reverse)
    return jnp.swapaxes(y, 0, 1)


def _rwkv7_branch(xn, mu, w_r, w_k, w_v, w_z, w0, w1, w2, a0, a1, a2, k_k, k_a, r_k, lnx_w, lnx_b, w_o):
    f32 = jnp.float32
    bsz, t, _ = xn.shape
    heads = lambda u: u.reshape(bsz, t, N_HEADS, HEAD_DIM)
    x_prev = jnp.pad(xn[:, :-1], ((0, 0), (1, 0), (0, 0)))
    x_next = jnp.pad(xn[:, 1:], ((0, 0), (0, 1), (0, 0)))
    xx = 0.5 * (x_prev + x_next) - xn
    lerp = lambda j: xn + xx * mu[j]
    r = heads(lerp(0) @ w_r).astype(f32)
    k = heads(lerp(1) @ w_k).astype(f32)
    v = heads(lerp(2) @ w_v).astype(f32)
    z = xn @ w_z
    kk = k * k_k.astype(f32).reshape(N_HEADS, HEAD_DIM)
    kk = kk / jnp.maximum(jnp.sqrt(jnp.sum(kk * kk, axis=-1, keepdims=True)), 1e-12)
    k_a_h = k_a.astype(f32).reshape(N_HEADS, HEAD_DIM)
    r_k_h = r_k.astype(f32)
    ys = []
    bonus = []
    for d in range(2):
        xw = lerp(3 + d)
        xa = lerp(5 + d)
        w_log = -jax.nn.softplus(-(w0[d] + jnp.tanh(xw @ w1[d]) @ w2[d]).astype(f32)) - 0.5
        decay = heads(jnp.exp(-jnp.exp(w_log)))
        a = heads(jax.nn.sigmoid((a0[d] + (xa @ a1[d]) @ a2[d]).astype(f32)))
        k_d = k * (1.0 + (a - 1.0) * k_a_h)
        ys.append(_wkv7_scan(r, decay, k_d, v, -kk, kk * a, reverse=(d == 1)))
        bonus.append(jnp.sum(r * k_d * r_k_h, axis=-1, keepdims=True) * v)
    y = ys[0] + ys[1]
    mean = jnp.mean(y, axis=-1, keepdims=True)
    var = jnp.mean(jnp.square(y - mean), axis=-1, keepdims=True)
    y = (y - mean) * lax.rsqrt(var + LNX_EPS) * lnx_w.astype(f32).reshape(N_HEADS, HEAD_DIM) \
        + lnx_b.astype(f32).reshape(N_HEADS, HEAD_DIM)
    y = y + bonus[0] + bonus[1]
    out = y.reshape(bsz, t, D_INNER).astype(xn.dtype) * jax.nn.silu(z)
    return out @ w_o


def _column_geometry():
    n_cb = GRID_W // Q_COL_BLOCK
    q_cols = np.arange(GRID_W).reshape(n_cb, Q_COL_BLOCK)
    win_start = np.clip(q_cols - WIN_COLS // 2, 0, GRID_W - WIN_COLS)
    blk_start = np.clip(q_cols[:, 0] - WIN_COLS // 2, 0, GRID_W - KEY_COL_SPAN)
    key_cols = blk_start[:, None] + np.arange(KEY_COL_SPAN)
    kc = key_cols[:, None, :]
    valid = (kc >= win_start[..., None]) & (kc < win_start[..., None] + WIN_COLS)
    rel_idx = np.clip(kc - q_cols[..., None] + WIN_COLS - 1, 0, 2 * WIN_COLS - 2)
    return key_cols, valid, rel_idx


def _neighbourhood_attention(q, k, v, rpb):
    f32 = jnp.float32
    bsz, t, h, dh = q.shape
    rows = t // GRID_W
    win_r = min(WIN_ROWS, rows)
    n_cb = GRID_W // Q_COL_BLOCK
    key_cols, valid, rel_idx = _column_geometry()
    qg = q.reshape(bsz, rows, n_cb, Q_COL_BLOCK, h, dh)
    kb = k.reshape(bsz, rows, GRID_W, h, dh)[:, :, key_cols]
    vb = v.reshape(bsz, rows, GRID_W, h, dh)[:, :, key_cols]
    rpb_c = rpb[:, :, rel_idx]
    mask = jnp.asarray(valid)[None, None, :, :, None, :]

    def one_row(r):
        rs = jnp.clip(r - win_r // 2, 0, rows - win_r)
        qr = lax.dynamic_index_in_dim(qg, r, axis=1, keepdims=False)
        kr = lax.dynamic_slice_in_dim(kb, rs, win_r, axis=1)
        vr = lax.dynamic_slice_in_dim(vb, rs, win_r, axis=1)
        ridx = rs + jnp.arange(win_r, dtype=jnp.int32) - r + WIN_ROWS - 1
        bias = jnp.transpose(jnp.take(rpb_c, ridx, axis=1), (0, 2, 3, 1, 4))
        s = jnp.einsum('bnqhd,bsnkhd->bhnqsk', qr, kr, preferred_element_type=f32)
        s = jnp.where(mask, s + bias.astype(f32), NEG_INF)
        p = jax.nn.softmax(s.reshape(s.shape[:4] + (-1,)), axis=-1).reshape(s.shape)
        return jnp.einsum('bhnqsk,bsnkhd->bnqhd', p.astype(v.dtype), vr)

    o = lax.map(one_row, jnp.arange(rows, dtype=jnp.int32))
    return jnp.moveaxis(o, 0, 1).reshape(bsz, t, h, dh)


def _natten_branch(xn, w_in, b_in, rpb, w_o, b_o):
    bsz, t, _ = xn.shape
    proj = xn @ w_in + b_in
    q, k, v, z = jnp.split(proj, 4, axis=-1)
    heads = lambda u: u.reshape(bsz, t, N_HEADS, HEAD_DIM)
    o = _neighbourhood_attention(heads(q) * (HEAD_DIM ** -0.5), heads(k), heads(v), rpb)
    return (o.reshape(bsz, t, D_INNER) * jax.nn.silu(z)) @ w_o + b_o


def _trunk(x, pre_norm_g, post_norm_g, rwkv_params, na_params):
    for i in range(DEPTH):
        j = i // N_MIXERS
        xn = _rms_norm(x, pre_norm_g[i])
        if i % N_MIXERS == 0:
            h = _rwkv7_branch(xn, *[p[j] for p in rwkv_params])
        else:
            h = _natten_branch(xn, *[p[j] for p in na_params])
        x = x + _rms_norm(h, post_norm_g[i])
    return x


def setup_inputs(seed: int = 0) -> dict:
    key = jax.random.key(seed)
    ks = jax.random.split(key, 32)
    f32 = jnp.float32
    nrm = lambda kk, shape, scale: jax.random.normal(kk, shape, f32) * scale
    na = (DEPTH + 1) // 2
    nb = DEPTH // 2
    E, D = D_INNER, D_MODEL
    return {
        'x_prompt': nrm(ks[0], (BATCH, SEQ, D), 1.0),
        'x_sample': nrm(ks[1], (DEC_BATCH, DEC_SEQ, D), 1.0),
        'pre_norm_g': 1.0 + nrm(ks[2], (DEPTH, D), 0.05),
        'post_norm_g': 1.0 + nrm(ks[3], (DEPTH, D), 0.05),
        'rk_mu': jax.random.uniform(ks[4], (na, N_SHIFT_MIX, D), f32),
        'rk_w_r': nrm(ks[5], (na, D, E), D ** -0.5),
        'rk_w_k': nrm(ks[6], (na, D, E), D ** -0.5),
        'rk_w_v': nrm(ks[7], (na, D, E), D ** -0.5),
        'rk_w_z': nrm(ks[8], (na, D, E), D ** -0.5),
        'rk_w0': jax.random.uniform(ks[9], (na, 2, E), f32, minval=-6.5, maxval=-1.5),
        'rk_w1': nrm(ks[10], (na, 2, D, DECAY_LORA), D ** -0.5),
        'rk_w2': nrm(ks[11], (na, 2, DECAY_LORA, E), 0.1 * DECAY_LORA ** -0.5),
        'rk_a0': nrm(ks[12], (na, 2, E), 0.1),
        'rk_a1': nrm(ks[13], (na, 2, D, ICLR_LORA), D ** -0.5),
        'rk_a2': nrm(ks[14], (na, 2, ICLR_LORA, E), 0.1 * ICLR_LORA ** -0.5),
        'rk_k_k': 0.85 + nrm(ks[15], (na, E), 0.05),
        'rk_k_a': 1.0 + nrm(ks[16], (na, E), 0.05),
        'rk_r_k': nrm(ks[17], (na, N_HEADS, HEAD_DIM), 0.1),
        'rk_lnx_w': 1.0 + nrm(ks[18], (na, E), 0.05),
        'rk_lnx_b': nrm(ks[19], (na, E), 0.02),
        'rk_w_o': nrm(ks[20], (na, E, D), E ** -0.5),
        'na_w_in': nrm(ks[21], (nb, D, 4 * E), D ** -0.5),
        'na_b_in': nrm(ks[22], (nb, 4 * E), 0.02),
        'na_rpb': nrm(ks[23], (nb, N_HEADS, 2 * WIN_ROWS - 1, 2 * WIN_COLS - 1), 0.1),
        'na_w_o': nrm(ks[24], (nb, E, D), E ** -0.5),
        'na_b_o': nrm(ks[25], (nb, D), 0.02),
    }


def reference(x_prompt, x_sample, pre_norm_g, post_norm_g, rk_mu, rk_w_r, rk_w_k, rk_w_v, rk_w_z,
              rk_w0, rk_w1, rk_w2, rk_a0, rk_a1, rk_a2, rk_k_k, rk_k_a, rk_r_k, rk_lnx_w, rk_lnx_b,
              rk_w_o, na_w_in, na_b_in, na_rpb, na_w_o, na_b_o):
    rwkv_params = (rk_mu, rk_w_r, rk_w_k, rk_w_v, rk_w_z, rk_w0, rk_w1, rk_w2, rk_a0, rk_a1, rk_a2,
                   rk_k_k, rk_k_a, rk_r_k, rk_lnx_w, rk_lnx_b, rk_w_o)
    na_params = (na_w_in, na_b_in, na_rpb, na_w_o, na_b_o)
    y_prompt = _trunk(x_prompt, pre_norm_g, post_norm_g, rwkv_params, na_params)
    y_sample = _trunk(x_sample, pre_norm_g, post_norm_g, rwkv_params, na_params)
    return (y_prompt, y_sample)
```

```python
import numpy as np
from contextlib import ExitStack
import concourse.bass as bass
import concourse.mybir as mybir
from concourse.bass_utils import run_bass_kernel_spmd
from concourse.alu_op_type import AluOpType as ALU

F32 = mybir.dt.float32
BF16 = mybir.dt.bfloat16
AF = mybir.ActivationFunctionType
AX = mybir.AxisListType

N_CORES = 8
D = 1024
KC = 8
P = 128
T_SEQ = 2048
LAM = float(np.exp(-0.5))
LNX_EPS = 64e-5
RMS_EPS = 1e-6

SAME_ENGINE_SYNC = True


class _Sem:
    def __init__(self, sem, name):
        self.sem = sem
        self.name = name
        self.n = 0


class Eng:
    def __init__(self, h, sem, name, is_pe=False):
        self.h = h
        self.s = _Sem(sem, name)
        self.name = name
        self.is_pe = is_pe
        self.waited = {}


class Buf:
    __slots__ = ("name", "w", "r", "excl")

    def __init__(self, name="", excl=False):
        self.name = name
        self.w = None
        self.r = {}
        self.excl = excl


class Trk:
    def __init__(self, nc, es, n_slots=16):
        self.nc = nc
        mk = lambda nm: es.enter_context(nc.semaphore(nm))
        self.pe = Eng(nc.tensor, mk("s_pe"), "pe", is_pe=True)
        self.act = Eng(nc.scalar, mk("s_act"), "act")
        self.dve = Eng(nc.vector, mk("s_dve"), "dve")
        self.pool = Eng(nc.gpsimd, mk("s_pool"), "pool")
        self.sp = Eng(nc.sync, mk("s_sp"), "sp")
        self.engs = [self.pe, self.act, self.dve, self.pool, self.sp]
        self.slots = [_Sem(mk("s_dma%d" % i), "dma%d" % i) for i in range(n_slots)]
        self.dma_i = 0
        self.n_inst = 0
        self.cnt = {}

    def _deps(self, reads, writes):
        deps = {}
        for b in reads:
            if b.w is not None:
                s, v = b.w
                if deps.get(s, 0) < v:
                    deps[s] = v
        for b in writes:
            if b.w is not None:
                s, v = b.w
                if deps.get(s, 0) < v:
                    deps[s] = v
            for s, v in b.r.items():
                if deps.get(s, 0) < v:
                    deps[s] = v
        return deps

    def _wait(self, eng, deps):
        for s, v in deps.items():
            if eng.waited.get(s, 0) >= v:
                continue
            if s is eng.s:
                if eng.is_pe or not SAME_ENGINE_SYNC:
                    continue
            eng.h.wait_ge(s.sem, v)
            eng.waited[s] = v

    def _mark(self, tok, reads, writes):
        s, v = tok
        for b in reads:
            if b.r.get(s, 0) < v:
                b.r[s] = v
        for b in writes:
            b.w = tok
            b.r = {}

    def op(self, eng, fn, reads=(), writes=(), sig=True):
        if any(b.excl for b in reads):
            writes = list(writes) + [b for b in reads if b.excl]
            reads = [b for b in reads if not b.excl]
        self._wait(eng, self._deps(reads, writes))
        inst = fn(eng.h)
        tok = (eng.s, eng.s.n + 1)
        if sig:
            inst.then_inc(eng.s.sem, 1)
            eng.s.n += 1
        self._mark(tok, reads, writes)
        self.n_inst += 1
        self.cnt[eng.name] = self.cnt.get(eng.name, 0) + 1
        return inst

    def dma(self, q, out, in_, reads=(), writes=(), **kw):
        slot = self.slots[self.dma_i % len(self.slots)]
        self.dma_i += 1
        deps = self._deps(reads, writes)
        if slot.n > 0 and deps.get(slot, 0) < slot.n:
            deps[slot] = slot.n
        self._wait(q, deps)
        inst = q.h.dma_start(out=out, in_=in_, **kw)
        inst.then_inc(slot.sem, 16)
        slot.n += 16
        self._mark((slot, slot.n), reads, writes)
        self.n_inst += 1
        self.cnt["dma"] = self.cnt.get("dma", 0) + 1
        return inst

    def barrier(self):
        allsems = [e.s for e in self.engs] + self.slots
        for e in self.engs:
            deps = {s: s.n for s in allsems if s.n > 0 and s is not e.s}
            self._wait(e, deps)

    def finish(self):
        deps = {s: s.n for s in self.slots if s.n > 0}
        self._wait(self.sp, deps)


class Task:
    def __init__(self, name):
        self.name = name
        self.deps = []
        self.done = False
        self.gen = None


def run_tasks(tasks, window=3):
    pending = list(tasks)
    active = []
    while pending or active:
        while pending and len(active) < window and all(d.done for d in pending[0].deps):
            active.append(pending.pop(0))
        if not active:
            raise RuntimeError("scheduler deadlock at %s" % pending[0].name)
        for t in list(active):
            try:
                next(t.gen)
            except StopIteration:
                t.done = True
                active.remove(t)


class RPool:
    def __init__(self, items):
        self.items = items
        self.i = 0
        self.users = [[] for _ in items]

    def acquire(self, task):
        k = self.i % len(self.items)
        self.i += 1
        task.deps += self.users[k]
        self.users[k] = [task]
        return k, self.items[k]

    def share(self, task, k):
        self.users[k].append(task)


class NS:
    pass


W_NAMES = ['pre_norm_g', 'post_norm_g', 'rk_mu', 'rk_w_r', 'rk_w_k', 'rk_w_v', 'rk_w_z', 'rk_w0', 'rk_w1', 'rk_w2',
           'rk_a0', 'rk_a1', 'rk_a2', 'rk_k_k', 'rk_k_a', 'rk_r_k', 'rk_lnx_w', 'rk_lnx_b', 'rk_w_o', 'na_w_in',
           'na_b_in', 'na_rpb', 'na_w_o', 'na_b_o']
W_SHAPES = {
    'pre_norm_g': [2, D], 'post_norm_g': [2, D], 'rk_mu': [1, 7, D], 'rk_w_r': [1, D, D], 'rk_w_k': [1, D, D],
    'rk_w_v': [1, D, D], 'rk_w_z': [1, D, D], 'rk_w0': [1, 2, D], 'rk_w1': [1, 2, D, 64], 'rk_w2': [1, 2, 64, D],
    'rk_a0': [1, 2, D], 'rk_a1': [1, 2, D, 64], 'rk_a2': [1, 2, 64, D], 'rk_k_k': [1, D], 'rk_k_a': [1, D],
    'rk_r_k': [1, 16, 64], 'rk_lnx_w': [1, D], 'rk_lnx_b': [1, D], 'rk_w_o': [1, D, D], 'na_w_in': [1, D, 4 * D],
    'na_b_in': [1, 4 * D], 'na_rpb': [1, 16, 15, 31], 'na_w_o': [1, D, D], 'na_b_o': [1, D],
}

PV = {}
_c = 0
for _nm, _n in [('mu', 7), ('w0', 2), ('a0', 2), ('k_k', 1), ('k_a', 1), ('r_k', 1), ('pre_g', 2), ('bq', 1),
                ('bk', 1), ('omk_a', 1), ('bq8', 1)]:
    PV[_nm] = _c
    _c += _n * 8
PV_COLS = _c
PV_ROWS = PV['omk_a']


def build(nseq, T=T_SEQ, stage="full"):
    NCH = T // P
    nc = bass.Bass("TRN2", target_bir_lowering=False)
    x_in = nc.dram_tensor("x", [nseq, T, D], F32, kind="ExternalInput").ap()
    y_out = nc.dram_tensor("y", [nseq, T, D], F32, kind="ExternalOutput").ap()
    wd = {nm: nc.dram_tensor(nm, W_SHAPES[nm], F32, kind="ExternalInput").ap() for nm in W_NAMES}
    wbJ = [nc.dram_tensor("wbJ%d" % l, [2, KC, P, KC, P], BF16, kind="Internal").ap() for l in range(2)]
    wbH = [nc.dram_tensor("wbH%d" % l, [3, 2, P, KC, 512], BF16, kind="Internal").ap() for l in range(2)]

    es = ExitStack()
    with es:
        T_ = Trk(nc, es)
        pe, act, dve, pool, sp = T_.pe, T_.act, T_.dve, T_.pool, T_.sp
        op = T_.op

        uid = [0]

        def sb(es_, nm, sh, dt):
            uid[0] += 1
            return es_.enter_context(nc.sbuf_tensor("%s_%d" % (nm, uid[0]), sh, dt))

        banks = [es.enter_context(nc.psum_tensor("pb%d" % i, [P, 512], F32)) for i in range(8)]
        bbuf = [Buf("pb%d" % i, excl=True) for i in range(8)]
        banks_bf = [b[:].bitcast(BF16) for b in banks]

        x1 = sb(es, "x1", [P, NCH, D], F32)
        bx1 = [Buf("x1_%d" % c) for c in range(NCH)]
        ringA = [sb(es, "ringA%d" % i, [P, KC, 512], BF16) for i in range(2)]
        b_ringA = [Buf("ringA%d" % i) for i in range(2)]
        ringB = [sb(es, "ringB%d" % i, [P, KC, P], BF16) for i in range(4)]
        b_ringB = [Buf("ringB%d" % i) for i in range(4)]
        rA_i = [0]
        rB_i = [0]
        identb = sb(es, "identb", [P, P], BF16)
        b_identb = Buf("identb")
        pv = sb(es, "pv", [P, PV_COLS], F32)
        b_pv = Buf("pv")
        onesrow = sb(es, "onesrow", [1, P], BF16)
        brow_hi = sb(es, "brow_hi", [1, 5 * D], BF16)
        brow_lo = sb(es, "brow_lo", [1, 5 * D], BF16)
        b_cst = Buf("consts")
        maskq = [sb(es, "maskq%d" % d, [P, 512], BF16) for d in range(2)]
        bdones = sb(es, "bdones", [P, P], BF16)
        hsel = sb(es, "hsel", [P, 2], BF16)
        ones_f = sb(es, "ones_f", [P, P], F32)
        Jrev = sb(es, "Jrev", [P, P], BF16)
        pes0 = ExitStack()
        io_r = sb(pes0, "io_r", [P, P], F32)
        io_c = sb(pes0, "io_c", [P, P], F32)

        def pvc(nm, idx):
            c0 = PV[nm] + idx
            return pv[:, c0:c0 + 1]

        def loadA(layer, m, half):
            k = rA_i[0] % 2
            rA_i[0] += 1
            T_.dma(sp, ringA[k][:], wbH[layer][m, half], writes=[b_ringA[k]])
            return ringA[k], b_ringA[k]

        def loadB(layer, m, j):
            k = rB_i[0] % 4
            rB_i[0] += 1
            T_.dma(sp, ringB[k][:], wbJ[layer][m, j], writes=[b_ringB[k]])
            return ringB[k], b_ringB[k]

        op(pool, lambda e: e.iota(io_r[:], pattern=[[0, P]], base=0, channel_multiplier=1,
                                  allow_small_or_imprecise_dtypes=True), writes=[b_cst])
        op(pool, lambda e: e.iota(io_c[:], pattern=[[1, P]], base=0, channel_multiplier=0,
                                  allow_small_or_imprecise_dtypes=True), writes=[b_cst])
        op(dve, lambda e: e.tensor_tensor(out=identb[:], in0=io_r[:], in1=io_c[:], op=ALU.is_equal),
           reads=[b_cst], writes=[b_identb])
        op(dve, lambda e: e.memset(onesrow[:], 1.0), writes=[b_cst])
        op(dve, lambda e: e.memset(ones_f[:], 1.0), writes=[b_cst])
        for d_, (o_s, o_i) in enumerate([(ALU.is_lt, ALU.is_le), (ALU.is_gt, ALU.is_ge)]):
            for q4 in range(4):
                o_ = o_s if q4 % 2 == 0 else o_i
                op(dve, lambda e, d_=d_, q4=q4, o_=o_: e.tensor_tensor(out=maskq[d_][:, q4 * P:(q4 + 1) * P],
                                                                      in0=io_r[:], in1=io_c[:], op=o_),
                   reads=[b_cst], writes=[b_cst])

        with ExitStack() as pes:
            identf = sb(pes, "identf", [P, P], F32)
            rb_ = sb(pes, "rb_", [P, P], F32)
            cb_ = sb(pes, "cb_", [P, P], F32)
            op(dve, lambda e: e.tensor_tensor(out=identf[:], in0=io_r[:], in1=io_c[:], op=ALU.is_equal),
               reads=[b_cst], writes=[b_cst])
            op(dve, lambda e: e.tensor_scalar(out=rb_[:], in0=io_r[:], scalar1=63.5, scalar2=None, op0=ALU.is_gt),
               reads=[b_cst], writes=[b_cst])
            op(dve, lambda e: e.tensor_scalar(out=cb_[:], in0=io_c[:], scalar1=63.5, scalar2=None, op0=ALU.is_gt),
               reads=[b_cst], writes=[b_cst])
            op(dve, lambda e: e.tensor_tensor(out=bdones[:], in0=rb_[:], in1=cb_[:], op=ALU.is_equal),
               reads=[b_cst], writes=[b_cst])
            op(dve, lambda e: e.tensor_copy(out=hsel[:, 1:2], in_=rb_[:, 0:1]), reads=[b_cst], writes=[b_cst])
            op(dve, lambda e: e.tensor_scalar(out=hsel[:, 0:1], in0=rb_[:, 0:1], scalar1=-1.0, scalar2=1.0,
                                              op0=ALU.mult, op1=ALU.add), reads=[b_cst], writes=[b_cst])

            rows = sb(pes, "pvrows", [P, 2, P], F32)
            b_rows = Buf("pvrows")
            op(dve, lambda e: e.memset(rows[:], 0.0), writes=[b_rows])

            def load_rows(r0, src):
                n = src.shape[0]
                g, o = divmod(r0, P)
                assert o + n <= P, (r0, n)
                T_.dma(sp, rows[o:o + n, g, :], src, writes=[b_rows])

            load_rows(PV['mu'], wd['rk_mu'][0].rearrange("m (j q) -> (m j) q", q=P))
            load_rows(PV['w0'], wd['rk_w0'][0].rearrange("m (j q) -> (m j) q", q=P))
            load_rows(PV['a0'], wd['rk_a0'][0].rearrange("m (j q) -> (m j) q", q=P))
            load_rows(PV['k_k'], wd['rk_k_k'][0].rearrange("(j q) -> j q", q=P))
            load_rows(PV['k_a'], wd['rk_k_a'][0].rearrange("(j q) -> j q", q=P))
            load_rows(PV['r_k'], wd['rk_r_k'][0].rearrange("(j h) c -> j (h c)", h=2))
            load_rows(PV['pre_g'], wd['pre_norm_g'].rearrange("m (j q) -> (m j) q", q=P))
            load_rows(PV['bq'], wd['na_b_in'][0, 0:D].rearrange("(j q) -> j q", q=P))
            load_rows(PV['bk'], wd['na_b_in'][0, D:2 * D].rearrange("(j q) -> j q", q=P))
            for g in range(2):
                op(pe, lambda e, g=g: e.transpose(out=banks[0][:, g * P:(g + 1) * P], in_=rows[:, g, :],
                                                  identity=identf[:]),
                   reads=[b_rows, b_cst], writes=[bbuf[0]])
            op(dve, lambda e: e.tensor_copy(out=pv[:, 0:PV_ROWS], in_=banks[0][:, 0:PV_ROWS]),
               reads=[bbuf[0]], writes=[b_pv])
            op(dve, lambda e: e.tensor_scalar(out=pv[:, PV['omk_a']:PV['omk_a'] + 8],
                                              in0=pv[:, PV['k_a']:PV['k_a'] + 8], scalar1=-1.0, scalar2=1.0,
                                              op0=ALU.mult, op1=ALU.add), reads=[b_pv], writes=[b_pv])
            op(dve, lambda e: e.tensor_scalar(out=pv[:, PV['bq8']:PV['bq8'] + 8], in0=pv[:, PV['bq']:PV['bq'] + 8],
                                              scalar1=0.125, scalar2=None, op0=ALU.mult), reads=[b_pv], writes=[b_pv])

            brow_f = sb(pes, "brow_f", [1, 5 * D], F32)
            brow_t = sb(pes, "brow_t", [1, 5 * D], F32)
            b_bf = Buf("brow_f")
            T_.dma(sp, brow_f[0:1, 0:4 * D], wd['na_b_in'][0:1, :], writes=[b_bf])
            T_.dma(sp, brow_f[0:1, 4 * D:5 * D], wd['na_b_o'][0:1, :], writes=[b_bf])
            op(act, lambda e: e.activation(out=brow_hi[:], in_=brow_f[:], func=AF.Copy), reads=[b_bf], writes=[b_cst])
            op(dve, lambda e: e.tensor_tensor(out=brow_t[:], in0=brow_f[:], in1=brow_hi[:], op=ALU.subtract),
               reads=[b_bf, b_cst], writes=[b_bf])
            op(act, lambda e: e.activation(out=brow_lo[:], in_=brow_t[:], func=AF.Copy), reads=[b_bf], writes=[b_cst])

            stg = [sb(pes, "stg%d" % i, [P, D], F32) for i in range(3)]
            stb = [sb(pes, "stb%d" % i, [P, D], BF16) for i in range(3)]
            b_stg = [Buf() for _ in range(3)]
            b_stb = [Buf() for _ in range(3)]
            srcs = []
            for kc in range(KC):
                rs_ = slice(kc * P, (kc + 1) * P)
                for m, nm in enumerate(['rk_w_r', 'rk_w_k']):
                    srcs.append((wd[nm][0, rs_, :], wbJ[0][m, :, :, kc, :].rearrange("j p n -> p j n"), 'J'))
                for m, nm in enumerate(['rk_w_v', 'rk_w_z', 'rk_w_o']):
                    srcs.append((wd[nm][0, rs_, :], wbH[0][m, :, :, kc, :].rearrange("h p n -> p h n"), 'H'))
                for m in range(2):
                    srcs.append((wd['na_w_in'][0, rs_, m * D:(m + 1) * D],
                                 wbJ[1][m, :, :, kc, :].rearrange("j p n -> p j n"), 'J'))
                for m in range(2):
                    srcs.append((wd['na_w_in'][0, rs_, (m + 2) * D:(m + 3) * D],
                                 wbH[1][m, :, :, kc, :].rearrange("h p n -> p h n"), 'H'))
                srcs.append((wd['na_w_o'][0, rs_, :], wbH[1][2, :, :, kc, :].rearrange("h p n -> p h n"), 'H'))
            cast_engs = [act, dve, pool]
            for i, (src, dst, kind) in enumerate(srcs):
                k = i % 3
                T_.dma(sp, stg[k][:], src, writes=[b_stg[k]])
                if cast_engs[k] is act:
                    op(act, lambda e, k=k: e.activation(out=stb[k][:], in_=stg[k][:], func=AF.Copy),
                       reads=[b_stg[k]], writes=[b_stb[k]])
                else:
                    op(cast_engs[k], lambda e, k=k: e.tensor_copy(out=stb[k][:], in_=stg[k][:]),
                       reads=[b_stg[k]], writes=[b_stb[k]])
                if kind == 'J':
                    srcv = stb[k][:].rearrange("p (j n) -> p j n", j=KC)
                else:
                    srcv = stb[k][:].rearrange("p (h n) -> p h n", h=2)
                T_.dma(sp, dst, srcv, reads=[b_stb[k]])
            T_.barrier()
            T_.finish()
        pes0.close()

        ctx = NS()
        ctx.__dict__.update(locals())
        if stage != "l0":
            l1_prologue(ctx)
        for si in range(nseq):
            layer0(ctx, si)
            T_.barrier()
            T_.finish()
            if stage == "l0":
                for c in range(NCH):
                    T_.dma(sp, y_out[si, c * P:(c + 1) * P, :], x1[:, c, :], reads=[bx1[c]])
            else:
                layer1(ctx, si)
            T_.barrier()
            T_.finish()
        print("instructions:", T_.n_inst, T_.cnt)
    return nc


def layer0(g, si):
    nc, T_, op = g.nc, g.T_, g.op
    pe, act, dve, pool, sp = g.pe, g.act, g.dve, g.pool, g.sp
    banks, bbuf, banks_bf = g.banks, g.bbuf, g.banks_bf
    NCH, x1, bx1, pv, b_pv, pvc = g.NCH, g.x1, g.bx1, g.pv, g.b_pv, g.pvc
    wd, sb = g.wd, g.sb
    b_cst = g.b_cst
    L = 0

    with ExitStack() as l0:
        w1b = sb(l0, "w1b", [P, 2, KC, 64], BF16)
        a1b = sb(l0, "a1b", [P, 2, KC, 64], BF16)
        w2b = sb(l0, "w2b", [64, 2, D], BF16)
        a2b = sb(l0, "a2b", [64, 2, D], BF16)
        lnxb = sb(l0, "lnxb", [P, 2, D], F32)
        postg = sb(l0, "postg", [P, D], F32)
        b_lc = Buf("l0consts")
        tmpA = sb(l0, "tmpA", [P, D], F32)
        tmpB = sb(l0, "tmpB", [P, D], F32)
        b_tmpA, b_tmpB = Buf("tmpA"), Buf("tmpB")
        T_.dma(sp, tmpA[:].rearrange("p (d k n) -> p d k n", d=2, k=KC),
               wd['rk_w1'][0].rearrange("d (k p) n -> p d k n", p=P), writes=[b_tmpA])
        op(dve, lambda e: e.tensor_copy(out=w1b[:].rearrange("p d k n -> p (d k n)"), in_=tmpA[:]),
           reads=[b_tmpA], writes=[b_lc])
        T_.dma(sp, tmpB[:].rearrange("p (d k n) -> p d k n", d=2, k=KC),
               wd['rk_a1'][0].rearrange("d (k p) n -> p d k n", p=P), writes=[b_tmpB])
        op(dve, lambda e: e.tensor_copy(out=a1b[:].rearrange("p d k n -> p (d k n)"), in_=tmpB[:]),
           reads=[b_tmpB], writes=[b_lc])
        for d_ in range(2):
            T_.dma(sp, tmpA[0:64, :], wd['rk_w2'][0, d_], writes=[b_tmpA])
            op(dve, lambda e, d_=d_: e.tensor_copy(out=w2b[:, d_, :], in_=tmpA[0:64, :]), reads=[b_tmpA], writes=[b_lc])
            T_.dma(sp, tmpB[0:64, :], wd['rk_a2'][0, d_], writes=[b_tmpB])
            op(dve, lambda e, d_=d_: e.tensor_copy(out=a2b[:, d_, :], in_=tmpB[0:64, :]), reads=[b_tmpB], writes=[b_lc])
        T_.dma(sp, lnxb[:, 0, :], wd['rk_lnx_w'][0].partition_broadcast(P), writes=[b_lc])
        T_.dma(sp, lnxb[:, 1, :], wd['rk_lnx_b'][0].partition_broadcast(P), writes=[b_lc])
        T_.dma(sp, postg[:], wd['post_norm_g'][L].partition_broadcast(P), writes=[b_lc])

        def mk(nm, sh, dt, n):
            return [(sb(l0, "%s%d" % (nm, i), sh, dt), Buf("%s%d" % (nm, i))) for i in range(n)]

        p_xin = RPool(mk("xin", [P, D], F32, 1))
        xs, b_xs = mk("xs", [P, D], BF16, 1)[0]
        stat = sb(l0, "stat", [P, 8], F32)
        b_stat = Buf("stat")
        p_slot = RPool(mk("xnT", [P, KC, P + 2], BF16, 3))
        xx, b_xx = mk("xx", [P, KC, P], BF16, 1)[0]
        ltmp, b_ltmp = mk("ltmp", [P, P], F32, 1)[0]
        p_lr = RPool(mk("lrp_r", [P, KC, P], BF16, 1))
        p_lk = RPool(mk("lrp_k", [P, KC, P], BF16, 1))
        p_lt = mk("lrp_t", [P, KC, P], BF16, 1)
        lt_i = [0]
        p_vt = RPool(mk("Vtm", [P, D], BF16, 1))
        p_zs = RPool(mk("zs", [P, D], BF16, 1))
        p_th = RPool(mk("th", [64, 2 * P], BF16, 2))
        Hf = sb(l0, "Hf", [P, KC, 64], F32)
        Hb = sb(l0, "Hb", [P, KC, 64], BF16)
        b_H = [Buf("H%d" % j) for j in range(KC)]
        bonF = sb(l0, "bonF", [P, NCH, 16], F32)
        b_bonF = [Buf("bonF%d" % c) for c in range(NCH)]
        p_bonB = RPool(mk("bonB", [P, 16], F32, 2))
        p_yb = RPool(mk("Yb", [P, D], F32, 1))
        yg, b_yg = mk("yg", [P, D], BF16, 1)[0]
        ygT, b_ygT = mk("ygT", [P, KC, P], BF16, 1)[0]
        gst = sb(l0, "gst", [P, 6, 16], F32)
        b_gst = Buf("gst")

        def mkset(i):
            u = NS()
            u.i = i
            def t(nm, sh, dt):
                tt = sb(l0, "u%d_%s" % (i, nm), sh, dt)
                setattr(u, nm, tt)
                setattr(u, "b_" + nm, Buf("u%d_%s" % (i, nm)))
            t("rk", [P, 2 * P], F32)
            for nm in ("sg", "al", "kk", "rs", "Ein", "Eex", "ein", "kd"):
                t(nm, [P, P], F32)
            t("sq", [P, P], BF16)
            t("arT", [P, 2 * P], BF16)
            t("btT", [P, P], BF16)
            t("ktT", [P, P], BF16)
            t("pr", [P, P], BF16)
            t("BK", [P, 2 * P], BF16)
            t("AT0", [P, 512], BF16)
            t("AT1", [P, 512], BF16)
            for h in range(2):
                for k in range(2):
                    t("C%d%d" % (h, k), [P, 3 * P], BF16)
            t("Xp", [P, P], BF16)
            t("Up", [P, P], BF16)
            t("Hs", [P, 64], F32)
            t("sc", [P, 4], F32)
            u.banks = (2 + 3 * i, 3 + 3 * i, 4 + 3 * i)
            return u

        p_uset = RPool([mkset(0), mkset(1)])

        def gen_N(cx, d, first, last, prev):
            c = cx.c
            xin, b_xin = cx.xin
            slot, b_slot = cx.slot
            T_.dma(sp, xin[:], g.x_in[si, c * P:(c + 1) * P, :], writes=[b_xin])
            op(act, lambda e: e.activation(out=xs[:], in_=xin[:], func=AF.Square, accum_out=stat[:, 0:1]),
               reads=[b_xin], writes=[b_xs, b_stat])
            op(dve, lambda e: e.tensor_scalar(out=stat[:, 1:2], in0=stat[:, 0:1], scalar1=1.0 / D, scalar2=RMS_EPS,
                                              op0=ALU.mult, op1=ALU.add), reads=[b_stat], writes=[b_stat])
            op(act, lambda e: e.activation(out=stat[:, 2:3], in_=stat[:, 1:2], func=AF.Sqrt), reads=[b_stat],
               writes=[b_stat])
            op(dve, lambda e: e.reciprocal(out=stat[:, 3:4], in_=stat[:, 2:3]), reads=[b_stat], writes=[b_stat])
            op(act, lambda e: e.activation(out=xs[:], in_=xin[:], func=AF.Copy, scale=stat[:, 3:4]),
               reads=[b_xin, b_stat], writes=[b_xs])
            yield
            for kc in range(KC):
                op(pe, lambda e, kc=kc: e.transpose(out=banks_bf[0][:, kc * P:(kc + 1) * P],
                                                    in_=xs[:, kc * P:(kc + 1) * P], identity=g.identb[:]),
                   reads=[b_xs, g.b_identb], writes=[bbuf[0]], sig=(kc == KC - 1))
            for kc in range(KC):
                sc = pvc('pre_g', L * 8 + kc)
                if kc % 2 == 0:
                    op(dve, lambda e, kc=kc, sc=sc: e.tensor_scalar(out=slot[:, kc, 1:P + 1],
                                                                  in0=banks_bf[0][:, kc * P:(kc + 1) * P],
                                                                  scalar1=sc, scalar2=None, op0=ALU.mult),
                       reads=[bbuf[0], b_pv], writes=[b_slot])
                else:
                    op(act, lambda e, kc=kc, sc=sc: e.activation(out=slot[:, kc, 1:P + 1],
                                                               in_=banks_bf[0][:, kc * P:(kc + 1) * P],
                                                               func=AF.Copy, scale=sc),
                       reads=[bbuf[0], b_pv], writes=[b_slot])
            near, far = (0, P + 1) if d == 0 else (P + 1, 0)
            if first:
                op(pool, lambda e: e.memset(slot[:, :, near:near + 1], 0.0), writes=[b_slot])
            else:
                pslot, b_pslot = prev.slot
                src_own = 1 if d == 0 else P
                src_prev = P if d == 0 else 1
                op(pool, lambda e: e.tensor_copy(out=slot[:, :, near:near + 1], in_=pslot[:, :, src_prev:src_prev + 1]),
                   reads=[b_pslot], writes=[b_slot])
                op(pool, lambda e: e.tensor_copy(out=pslot[:, :, far:far + 1], in_=slot[:, :, src_own:src_own + 1]),
                   reads=[b_slot], writes=[b_pslot])
            if last:
                op(pool, lambda e: e.memset(slot[:, :, far:far + 1], 0.0), writes=[b_slot])
            yield

        def gen_F(cx, d):
            slot, b_slot = cx.slot
            xn = slot[:, :, 1:P + 1]
            op(pool, lambda e: e.tensor_tensor(out=xx[:], in0=slot[:, :, 0:P], in1=slot[:, :, 2:P + 2], op=ALU.add),
               reads=[b_slot], writes=[b_xx])
            op(dve, lambda e: e.scalar_tensor_tensor(out=xx[:], in0=xx[:], scalar=0.5, in1=xn, op0=ALU.mult,
                                                     op1=ALU.subtract), reads=[b_xx, b_slot], writes=[b_xx])
            yield

            def lerp(m, dst, b_dst):
                for kc in range(KC):
                    sc = pvc('mu', m * 8 + kc)
                    if kc % 2 == 0:
                        op(dve, lambda e, kc=kc, sc=sc: e.scalar_tensor_tensor(
                            out=dst[:, kc, :], in0=xx[:, kc, :], scalar=sc, in1=slot[:, kc, 1:P + 1],
                            op0=ALU.mult, op1=ALU.add), reads=[b_xx, b_slot, b_pv], writes=[b_dst])
                    else:
                        op(pool, lambda e, kc=kc, sc=sc: e.tensor_scalar(out=ltmp[:], in0=xx[:, kc, :], scalar1=sc,
                                                                       scalar2=None, op0=ALU.mult),
                           reads=[b_xx, b_pv], writes=[b_ltmp])
                        op(pool, lambda e, kc=kc: e.tensor_tensor(out=dst[:, kc, :], in0=ltmp[:],
                                                                 in1=slot[:, kc, 1:P + 1], op=ALU.add),
                           reads=[b_ltmp, b_slot], writes=[b_dst])

            def next_lt():
                r = p_lt[0]
                lt_i[0] += 1
                return r

            lr, b_lr = cx.lr
            lk, b_lk = cx.lk
            vt, b_vt = cx.vt
            lerp(0, lr, b_lr)
            yield
            lerp(1, lk, b_lk)
            yield
            lv, b_lv = next_lt()
            lerp(2, lv, b_lv)
            yield
            for half in range(2):
                w, b_w = g.loadA(L, 0, half)
                for kc in range(KC):
                    op(pe, lambda e, kc=kc, w=w: e.matmul(banks[1][:, :], lhsT=lv[:, kc, :], rhs=w[:, kc, :],
                                                         start=(kc == 0), stop=(kc == KC - 1)),
                       reads=[b_lv, b_w], writes=[bbuf[1]], sig=(kc == KC - 1))
                op(act, lambda e, half=half: e.activation(out=vt[:, half * 512:(half + 1) * 512], in_=banks[1][:, :],
                                                          func=AF.Copy), reads=[bbuf[1]], writes=[b_vt])
                yield
            if d == 1:
                zs, b_zs = cx.zs
                for half in range(2):
                    w, b_w = g.loadA(L, 1, half)
                    for kc in range(KC):
                        op(pe, lambda e, kc=kc, w=w: e.matmul(banks[1][:, :], lhsT=slot[:, kc, 1:P + 1],
                                                             rhs=w[:, kc, :], start=(kc == 0), stop=(kc == KC - 1)),
                           reads=[b_slot, b_w], writes=[bbuf[1]], sig=(kc == KC - 1))
                    op(act, lambda e, half=half: e.activation(out=zs[:, half * 512:(half + 1) * 512],
                                                              in_=banks[1][:, :], func=AF.Silu),
                       reads=[bbuf[1]], writes=[b_zs])
                    yield
            th, b_th = cx.th
            lw, b_lw = next_lt()
            lerp(3 + d, lw, b_lw)
            yield
            la, b_la = next_lt()
            lerp(5 + d, la, b_la)
            yield
            for kc in range(KC):
                op(pe, lambda e, kc=kc: e.matmul(banks[1][0:64, 0:P], lhsT=w1b[:, d, kc, :], rhs=lw[:, kc, :],
                                                 start=(kc == 0), stop=(kc == KC - 1)),
                   reads=[b_lw, b_lc], writes=[bbuf[1]], sig=False)
            for kc in range(KC):
                op(pe, lambda e, kc=kc: e.matmul(banks[1][0:64, P:2 * P], lhsT=a1b[:, d, kc, :], rhs=la[:, kc, :],
                                                 start=(kc == 0), stop=(kc == KC - 1)),
                   reads=[b_la, b_lc], writes=[bbuf[1]], sig=(kc == KC - 1))
            op(act, lambda e: e.activation(out=th[:, 0:P], in_=banks[1][0:64, 0:P], func=AF.Tanh),
               reads=[bbuf[1]], writes=[b_th])
            op(act, lambda e: e.activation(out=th[:, P:2 * P], in_=banks[1][0:64, P:2 * P], func=AF.Copy),
               reads=[bbuf[1]], writes=[b_th])
            yield

        def gen_U(cx, d, j, u):
            c = cx.c
            Ba, Bb, Bc = u.banks
            lr, b_lr = cx.lr
            lk, b_lk = cx.lk
            vt, b_vt = cx.vt
            th, b_th = cx.th
            hq = [slice(0, 64), slice(64, 128)]
            mq = g.maskq[d]
            mA = g.maskq[1 - d][:, 0:P]
            wr, b_wr = g.loadB(L, 0, j)
            wk, b_wk = g.loadB(L, 1, j)
            for kc in range(KC):
                op(pe, lambda e, kc=kc: e.matmul(banks[Ba][:, 0:P], lhsT=wr[:, kc, :], rhs=lr[:, kc, :],
                                                 start=(kc == 0), stop=(kc == KC - 1)),
                   reads=[b_wr, b_lr], writes=[bbuf[Ba]], sig=False)
            for kc in range(KC):
                op(pe, lambda e, kc=kc: e.matmul(banks[Ba][:, P:2 * P], lhsT=wk[:, kc, :], rhs=lk[:, kc, :],
                                                 start=(kc == 0), stop=(kc == KC - 1)),
                   reads=[b_wk, b_lk], writes=[bbuf[Ba]], sig=False)
            op(pe, lambda e: e.matmul(banks[Ba][:, 2 * P:3 * P], lhsT=w2b[:, d, j * P:(j + 1) * P], rhs=th[:, 0:P],
                                      start=True, stop=True), reads=[b_lc, b_th], writes=[bbuf[Ba]], sig=False)
            op(pe, lambda e: e.matmul(banks[Ba][:, 3 * P:4 * P], lhsT=a2b[:, d, j * P:(j + 1) * P], rhs=th[:, P:2 * P],
                                      start=True, stop=True), reads=[b_lc, b_th], writes=[bbuf[Ba]])
            op(act, lambda e: e.activation(out=u.rk[:], in_=banks[Ba][:, 0:2 * P], func=AF.Copy),
               reads=[bbuf[Ba]], writes=[u.b_rk])
            op(act, lambda e: e.activation(out=u.sg[:], in_=banks[Ba][:, 2 * P:3 * P], func=AF.Sigmoid,
                                           bias=pvc('w0', d * 8 + j)), reads=[bbuf[Ba], b_pv], writes=[u.b_sg])
            op(act, lambda e: e.activation(out=u.al[:], in_=banks[Ba][:, 3 * P:4 * P], func=AF.Sigmoid,
                                           bias=pvc('a0', d * 8 + j)), reads=[bbuf[Ba], b_pv], writes=[u.b_al])
            yield
            rT = u.rk[:, 0:P]
            kT = u.rk[:, P:2 * P]
            op(dve, lambda e: e.tensor_scalar(out=u.kk[:], in0=kT, scalar1=pvc('k_k', j), scalar2=None, op0=ALU.mult),
               reads=[u.b_rk, b_pv], writes=[u.b_kk])
            op(pool, lambda e: e.tensor_tensor(out=u.sq[:], in0=u.kk[:], in1=u.kk[:], op=ALU.mult),
               reads=[u.b_kk], writes=[u.b_sq])
            op(pe, lambda e: e.matmul(banks[Ba][:, 0:P], lhsT=g.bdones[:], rhs=u.sq[:], start=True, stop=True),
               reads=[b_cst, u.b_sq], writes=[bbuf[Ba]])
            op(act, lambda e: e.activation(out=u.rs[:], in_=banks[Ba][:, 0:P], func=AF.Ln),
               reads=[bbuf[Ba]], writes=[u.b_rs])
            op(act, lambda e: e.activation(out=u.rs[:], in_=u.rs[:], func=AF.Exp, scale=-0.5),
               reads=[u.b_rs], writes=[u.b_rs])
            op(pool, lambda e: e.tensor_tensor(out=u.kk[:], in0=u.kk[:], in1=u.rs[:], op=ALU.mult),
               reads=[u.b_kk, u.b_rs], writes=[u.b_kk])
            op(dve, lambda e: e.tensor_tensor_scan(out=u.Ein[:], data0=g.ones_f[:], data1=u.sg[:], initial=0.0,
                                                   op0=ALU.mult, op1=ALU.add),
               reads=[b_cst, u.b_sg], writes=[u.b_Ein])
            tot = u.Ein[:, P - 1:P]
            op(act, lambda e: e.activation(out=u.sc[:, 0:1], in_=tot, func=AF.Exp, scale=-LAM),
               reads=[u.b_Ein], writes=[u.b_sc])
            if d == 0:
                op(pool, lambda e: e.tensor_tensor(out=u.Eex[:], in0=u.Ein[:], in1=u.sg[:], op=ALU.subtract),
                   reads=[u.b_Ein, u.b_sg], writes=[u.b_Eex])
            else:
                op(dve, lambda e: e.tensor_copy(out=u.sc[:, 1:2], in_=tot), reads=[u.b_Ein], writes=[u.b_sc])
                op(dve, lambda e: e.tensor_scalar(out=u.Eex[:], in0=u.Ein[:], scalar1=u.sc[:, 1:2], scalar2=-1.0,
                                                  op0=ALU.subtract, op1=ALU.mult),
                   reads=[u.b_Ein, u.b_sc], writes=[u.b_Eex])
                op(pool, lambda e: e.tensor_tensor(out=u.Ein[:], in0=u.Eex[:], in1=u.sg[:], op=ALU.add),
                   reads=[u.b_Eex, u.b_sg], writes=[u.b_Ein])
            yield
            op(act, lambda e: e.activation(out=u.ein[:], in_=u.Ein[:], func=AF.Exp, scale=-LAM),
               reads=[u.b_Ein], writes=[u.b_ein])
            op(act, lambda e: e.activation(out=u.Eex[:], in_=u.Eex[:], func=AF.Exp, scale=-LAM),
               reads=[u.b_Eex], writes=[u.b_Eex])
            op(act, lambda e: e.activation(out=u.Ein[:], in_=u.Ein[:], func=AF.Exp, scale=LAM),
               reads=[u.b_Ein], writes=[u.b_Ein])
            eng_ = u.Ein
            eex_ = u.Eex
            op(dve, lambda e: e.scalar_tensor_tensor(out=u.arT[:, 0:P], in0=u.kk[:], scalar=-1.0, in1=eex_[:],
                                                     op0=ALU.mult, op1=ALU.mult),
               reads=[u.b_kk, u.b_Eex], writes=[u.b_arT])
            op(pool, lambda e: e.tensor_tensor(out=u.arT[:, P:2 * P], in0=rT, in1=u.ein[:], op=ALU.mult),
               reads=[u.b_rk, u.b_ein], writes=[u.b_arT])
            op(pool, lambda e: e.tensor_scalar(out=u.kd[:], in0=u.al[:], scalar1=pvc('k_a', j),
                                               scalar2=pvc('omk_a', j), op0=ALU.mult, op1=ALU.add),
               reads=[u.b_al, b_pv], writes=[u.b_kd])
            op(pool, lambda e: e.tensor_tensor(out=u.kd[:], in0=u.kd[:], in1=kT, op=ALU.mult),
               reads=[u.b_kd, u.b_rk], writes=[u.b_kd])
            op(dve, lambda e: e.tensor_tensor(out=u.ktT[:], in0=u.kd[:], in1=eng_[:], op=ALU.mult),
               reads=[u.b_kd, u.b_Ein], writes=[u.b_ktT])
            op(pool, lambda e: e.tensor_tensor(out=u.al[:], in0=u.al[:], in1=u.kk[:], op=ALU.mult),
               reads=[u.b_al, u.b_kk], writes=[u.b_al])
            op(dve, lambda e: e.tensor_tensor(out=u.btT[:], in0=u.al[:], in1=eng_[:], op=ALU.mult),
               reads=[u.b_al, u.b_Ein], writes=[u.b_btT])
            op(dve, lambda e: e.scalar_tensor_tensor(out=u.pr[:], in0=rT, scalar=pvc('r_k', j), in1=u.kd[:],
                                                     op0=ALU.mult, op1=ALU.mult),
               reads=[u.b_rk, u.b_kd, b_pv], writes=[u.b_pr])
            yield
            op(pe, lambda e: e.transpose(out=banks_bf[Ba][:, 0:P], in_=u.btT[:], identity=g.identb[:]),
               reads=[u.b_btT, g.b_identb], writes=[bbuf[Ba]], sig=False)
            op(pe, lambda e: e.transpose(out=banks_bf[Ba][:, P:2 * P], in_=u.ktT[:], identity=g.identb[:]),
               reads=[u.b_ktT, g.b_identb], writes=[bbuf[Ba]], sig=False)
            op(pe, lambda e: e.matmul(banks[Ba][:, 2 * P:2 * P + 2], lhsT=u.pr[:], rhs=g.hsel[:], start=True, stop=True),
               reads=[u.b_pr, b_cst], writes=[bbuf[Ba]])
            op(act, lambda e: e.activation(out=u.BK[:], in_=banks_bf[Ba][:, 0:2 * P], func=AF.Copy),
               reads=[bbuf[Ba]], writes=[u.b_BK])
            if d == 0:
                bdst, b_bdst = bonF[:, c, 2 * j:2 * j + 2], b_bonF[c]
            else:
                bt_, b_bdst = cx.bonB
                bdst = bt_[:, 2 * j:2 * j + 2]
            op(act, lambda e: e.activation(out=bdst, in_=banks[Ba][:, 2 * P:2 * P + 2], func=AF.Copy),
               reads=[bbuf[Ba]], writes=[b_bdst])
            yield
            AT = [u.AT0, u.AT1]
            b_AT = [u.b_AT0, u.b_AT1]
            Cc = [[u.C00, u.C01], [u.C10, u.C11]]
            b_C = [[u.b_C00, u.b_C01], [u.b_C10, u.b_C11]]
            hb = [Bb, Bc]
            for h in range(2):
                B_ = hb[h]
                op(pe, lambda e, h=h, B_=B_: e.matmul(banks[B_][:, 0:2 * P], lhsT=u.btT[hq[h], :], rhs=u.arT[hq[h], :],
                                                      start=True, stop=True),
                   reads=[u.b_btT, u.b_arT], writes=[bbuf[B_]], sig=False)
                op(pe, lambda e, h=h, B_=B_: e.matmul(banks[B_][:, 2 * P:4 * P], lhsT=u.ktT[hq[h], :],
                                                      rhs=u.arT[hq[h], :], start=True, stop=True),
                   reads=[u.b_ktT, u.b_arT], writes=[bbuf[B_]])
                op(dve, lambda e, h=h, B_=B_: e.tensor_tensor(out=AT[h][:], in0=banks[B_][:, :], in1=mq[:],
                                                              op=ALU.mult),
                   reads=[bbuf[B_], b_cst], writes=[b_AT[h]])
                op(pe, lambda e, h=h, B_=B_: e.matmul(banks[B_][:, 0:P], lhsT=u.arT[hq[h], 0:P], rhs=u.btT[hq[h], :],
                                                      start=True, stop=True),
                   reads=[u.b_btT, u.b_arT], writes=[bbuf[B_]])
                op(dve, lambda e, h=h, B_=B_: e.tensor_tensor(out=Cc[h][0][:, 0:P], in0=banks[B_][:, 0:P], in1=mA,
                                                              op=ALU.mult),
                   reads=[bbuf[B_], b_cst], writes=[b_C[h][0]])
                op(pool, lambda e, h=h: e.tensor_tensor(out=Cc[h][1][:, 2 * P:3 * P], in0=AT[h][:, 0:P],
                                                        in1=g.identb[:], op=ALU.add),
                   reads=[b_AT[h], g.b_identb], writes=[b_C[h][1]])
            yield
            for h in range(2):
                B_ = hb[h]
                A0 = Cc[h][0][:, 0:P]
                B0 = AT[h][:, 0:P]
                op(pe, lambda e, B_=B_, A0=A0, B0=B0: e.matmul(banks[B_][:, 0:P], lhsT=B0, rhs=A0, start=True, stop=True),
                   reads=[b_AT[h], b_C[h][0]], writes=[bbuf[B_]], sig=False)
                op(pe, lambda e, B_=B_, A0=A0, B0=B0: e.matmul(banks[B_][:, P:2 * P], lhsT=A0, rhs=B0, start=True,
                                                               stop=True),
                   reads=[b_AT[h], b_C[h][0]], writes=[bbuf[B_]])
                ev = dve if h == 0 else act
                if ev is dve:
                    op(dve, lambda e, h=h, B_=B_: e.tensor_copy(out=Cc[h][1][:, 0:2 * P], in_=banks[B_][:, 0:2 * P]),
                       reads=[bbuf[B_]], writes=[b_C[h][1]])
                else:
                    op(act, lambda e, h=h, B_=B_: e.activation(out=Cc[h][1][:, 0:2 * P], in_=banks[B_][:, 0:2 * P],
                                                               func=AF.Copy),
                       reads=[bbuf[B_]], writes=[b_C[h][1]])
            yield
            for lev in range(1, 7):
                src_i = lev % 2
                dst_i = 1 - src_i
                for h in range(2):
                    B_ = hb[h]
                    S = Cc[h][src_i]
                    Dd = Cc[h][dst_i]
                    bS, bD = b_C[h][src_i], b_C[h][dst_i]
                    Ak, Bk, Mk = S[:, 0:P], S[:, P:2 * P], S[:, 2 * P:3 * P]
                    lo = 0
                    if lev <= 5:
                        op(pe, lambda e, B_=B_, Ak=Ak, Bk=Bk: e.matmul(banks[B_][:, 0:P], lhsT=Bk, rhs=Ak, start=True,
                                                                       stop=True),
                           reads=[bS], writes=[bbuf[B_]], sig=False)
                    else:
                        lo = 2 * P
                    if lev <= 4:
                        op(pe, lambda e, B_=B_, Ak=Ak, Bk=Bk: e.matmul(banks[B_][:, P:2 * P], lhsT=Ak, rhs=Bk,
                                                                       start=True, stop=True),
                           reads=[bS], writes=[bbuf[B_]], sig=False)
                    op(pe, lambda e, B_=B_, Ak=Ak, Mk=Mk: e.matmul(banks[B_][:, 2 * P:3 * P], lhsT=Ak, rhs=Mk,
                                                                   start=True, stop=False),
                       reads=[bS], writes=[bbuf[B_]], sig=False)
                    op(pe, lambda e, B_=B_, Mk=Mk: e.matmul(banks[B_][:, 2 * P:3 * P], lhsT=g.identb[:], rhs=Mk,
                                                            start=False, stop=True),
                       reads=[bS, g.b_identb], writes=[bbuf[B_]])
                    if (h + lev) % 2 == 0:
                        op(dve, lambda e, B_=B_, Dd=Dd, lo=lo: e.tensor_copy(out=Dd[:, lo:3 * P],
                                                                             in_=banks[B_][:, lo:3 * P]),
                           reads=[bbuf[B_]], writes=[bD])
                    else:
                        op(act, lambda e, B_=B_, Dd=Dd, lo=lo: e.activation(out=Dd[:, lo:3 * P],
                                                                            in_=banks[B_][:, lo:3 * P], func=AF.Copy),
                           reads=[bbuf[B_]], writes=[bD])
                yield
            Mfin = [Cc[h][1][:, 2 * P:3 * P] for h in range(2)]
            b_Mfin = [b_C[h][1] for h in range(2)]
            b_Hj = g_bH[j]
            for h in range(2):
                op(pe, lambda e, h=h: e.matmul(banks[Ba][:, h * 64:(h + 1) * 64], lhsT=u.arT[hq[h], 0:P],
                                               rhs=Hb[hq[h], j, :], start=True, stop=False),
                   reads=[u.b_arT, b_Hj], writes=[bbuf[Ba]], sig=False)
                op(pe, lambda e, h=h: e.matmul(banks[Ba][:, h * 64:(h + 1) * 64], lhsT=AT[h][:, 2 * P:3 * P],
                                               rhs=vt[:, (2 * j + h) * 64:(2 * j + h + 1) * 64], start=False, stop=True),
                   reads=[b_AT[h], b_vt], writes=[bbuf[Ba]], sig=(h == 1))
            op(act, lambda e: e.activation(out=u.Xp[:], in_=banks[Ba][:, 0:P], func=AF.Copy),
               reads=[bbuf[Ba]], writes=[u.b_Xp])
            for h in range(2):
                op(pe, lambda e, h=h: e.matmul(banks[Ba][:, P + h * 64:P + (h + 1) * 64], lhsT=Mfin[h],
                                               rhs=u.Xp[:, h * 64:(h + 1) * 64], start=True, stop=True),
                   reads=[b_Mfin[h], u.b_Xp], writes=[bbuf[Ba]], sig=(h == 1))
            op(dve, lambda e: e.tensor_copy(out=u.Up[:], in_=banks[Ba][:, P:2 * P]), reads=[bbuf[Ba]], writes=[u.b_Up])
            op(pool, lambda e: e.tensor_scalar(out=u.Hs[:], in0=Hf[:, j, :], scalar1=u.sc[:, 0:1], scalar2=None,
                                               op0=ALU.mult), reads=[b_Hj, u.b_sc], writes=[u.b_Hs])
            yield
            for h in range(2):
                yo = 2 * P + h * 64
                op(pe, lambda e, h=h, yo=yo: e.matmul(banks[Ba][:, yo:yo + 64], lhsT=u.arT[hq[h], P:2 * P],
                                                      rhs=Hb[hq[h], j, :], start=True, stop=False),
                   reads=[u.b_arT, b_Hj], writes=[bbuf[Ba]], sig=False)
                op(pe, lambda e, h=h, yo=yo: e.matmul(banks[Ba][:, yo:yo + 64], lhsT=AT[h][:, P:2 * P],
                                                      rhs=u.Up[:, h * 64:(h + 1) * 64], start=False, stop=False),
                   reads=[b_AT[h], u.b_Up], writes=[bbuf[Ba]], sig=False)
                op(pe, lambda e, h=h, yo=yo: e.matmul(banks[Ba][:, yo:yo + 64], lhsT=AT[h][:, 3 * P:4 * P],
                                                      rhs=vt[:, (2 * j + h) * 64:(2 * j + h + 1) * 64],
                                                      start=False, stop=True),
                   reads=[b_AT[h], b_vt], writes=[bbuf[Ba]], sig=False)
            op(pe, lambda e: e.matmul(banks[Ba][:, 3 * P:4 * P], lhsT=u.BK[:, 0:P], rhs=u.Up[:], start=True, stop=False),
               reads=[u.b_BK, u.b_Up], writes=[bbuf[Ba]], sig=False)
            op(pe, lambda e: e.matmul(banks[Ba][:, 3 * P:4 * P], lhsT=u.BK[:, P:2 * P], rhs=vt[:, j * P:(j + 1) * P],
                                      start=False, stop=True),
               reads=[u.b_BK, b_vt], writes=[bbuf[Ba]])
            if d == 0:
                ydst = x1[:, c, :].bitcast(BF16)[:, j * P:(j + 1) * P]
                b_yd = bx1[c]
            else:
                yb_, b_yd = cx.yb
                ydst = yb_[:, j * P:(j + 1) * P]
            op(act, lambda e: e.activation(out=ydst, in_=banks[Ba][:, 2 * P:3 * P], func=AF.Copy),
               reads=[bbuf[Ba]], writes=[b_yd])
            for h in range(2):
                op(dve, lambda e, h=h: e.scalar_tensor_tensor(out=Hf[hq[h], j, :],
                                                              in0=banks[Ba][hq[h], 3 * P + h * 64:3 * P + (h + 1) * 64],
                                                              scalar=u.sc[hq[h], 0:1], in1=u.Hs[hq[h], :],
                                                              op0=ALU.mult, op1=ALU.add),
                   reads=[bbuf[Ba], u.b_sc, u.b_Hs], writes=[b_Hj])
            op(pool, lambda e: e.tensor_copy(out=Hb[:, j, :], in_=Hf[:, j, :]), reads=[b_Hj], writes=[b_Hj])
            yield

        g_bH = b_H

        def gen_B(cx):
            c = cx.c
            yb_, b_yb = cx.yb
            zs, b_zs = cx.zs
            vt, b_vt = cx.vt
            bonB, b_bonB = cx.bonB
            xin, b_xin = cx.xres
            T_.dma(sp, xin[:], g.x_in[si, c * P:(c + 1) * P, :], writes=[b_xin])
            yf = x1[:, c, :].bitcast(BF16)[:, 0:D]
            op(dve, lambda e: e.tensor_tensor(out=yb_[:], in0=yb_[:], in1=yf, op=ALU.add),
               reads=[b_yb, bx1[c]], writes=[b_yb])
            y3 = yb_[:].rearrange("p (h n) -> p h n", h=16)
            op(dve, lambda e: e.tensor_reduce(out=gst[:, 0, :], in_=y3, axis=AX.X, op=ALU.add),
               reads=[b_yb], writes=[b_gst])
            op(act, lambda e: e.activation(out=tmpA[:], in_=yb_[:], func=AF.Square), reads=[b_yb], writes=[b_tmpA])
            op(dve, lambda e: e.tensor_reduce(out=gst[:, 1, :], in_=tmpA[:].rearrange("p (h n) -> p h n", h=16),
                                              axis=AX.X, op=ALU.add), reads=[b_tmpA], writes=[b_gst])
            yield
            op(dve, lambda e: e.tensor_scalar(out=gst[:, 2, :], in0=gst[:, 0, :], scalar1=1.0 / 64, scalar2=None,
                                              op0=ALU.mult), reads=[b_gst], writes=[b_gst])
            op(dve, lambda e: e.tensor_tensor(out=gst[:, 3, :], in0=gst[:, 2, :], in1=gst[:, 2, :], op=ALU.mult),
               reads=[b_gst], writes=[b_gst])
            op(dve, lambda e: e.scalar_tensor_tensor(out=gst[:, 3, :], in0=gst[:, 1, :], scalar=1.0 / 64,
                                                     in1=gst[:, 3, :], op0=ALU.mult, op1=ALU.subtract),
               reads=[b_gst], writes=[b_gst])
            op(dve, lambda e: e.tensor_scalar(out=gst[:, 3, :], in0=gst[:, 3, :], scalar1=LNX_EPS, scalar2=None,
                                              op0=ALU.add), reads=[b_gst], writes=[b_gst])
            op(act, lambda e: e.activation(out=gst[:, 3, :], in_=gst[:, 3, :], func=AF.Sqrt), reads=[b_gst],
               writes=[b_gst])
            op(dve, lambda e: e.reciprocal(out=gst[:, 4, :], in_=gst[:, 3, :]), reads=[b_gst], writes=[b_gst])
            op(dve, lambda e: e.tensor_tensor(out=gst[:, 5, :], in0=bonF[:, c, :], in1=bonB[:], op=ALU.add),
               reads=[b_bonF[c], b_bonB], writes=[b_gst])
            mean_b = gst[:, 2, :].unsqueeze(2).broadcast_to([P, 16, 64])
            rstd_b = gst[:, 4, :].unsqueeze(2).broadcast_to([P, 16, 64])
            bon_b = gst[:, 5, :].unsqueeze(2).broadcast_to([P, 16, 64])
            op(dve, lambda e: e.tensor_tensor(out=y3, in0=y3, in1=mean_b, op=ALU.subtract),
               reads=[b_yb, b_gst], writes=[b_yb])
            op(pool, lambda e: e.tensor_tensor(out=y3, in0=y3, in1=rstd_b, op=ALU.mult),
               reads=[b_yb, b_gst], writes=[b_yb])
            yield
            op(dve, lambda e: e.tensor_tensor(out=yb_[:], in0=yb_[:], in1=lnxb[:, 0, :], op=ALU.mult),
               reads=[b_yb, b_lc], writes=[b_yb])
            op(pool, lambda e: e.tensor_tensor(out=yb_[:], in0=yb_[:], in1=lnxb[:, 1, :], op=ALU.add),
               reads=[b_yb, b_lc], writes=[b_yb])
            t3 = tmpA[:].rearrange("p (h n) -> p h n", h=16)
            op(dve, lambda e: e.tensor_tensor(out=t3, in0=vt[:].rearrange("p (h n) -> p h n", h=16), in1=bon_b,
                                              op=ALU.mult), reads=[b_vt, b_gst], writes=[b_tmpA])
            op(pool, lambda e: e.tensor_tensor(out=yb_[:], in0=yb_[:], in1=tmpA[:], op=ALU.add),
               reads=[b_yb, b_tmpA], writes=[b_yb])
            op(dve, lambda e: e.tensor_tensor(out=yg[:], in0=yb_[:], in1=zs[:], op=ALU.mult),
               reads=[b_yb, b_zs], writes=[b_yg])
            yield
            for kc in range(KC):
                op(pe, lambda e, kc=kc: e.transpose(out=banks_bf[0][:, kc * P:(kc + 1) * P],
                                                    in_=yg[:, kc * P:(kc + 1) * P], identity=g.identb[:]),
                   reads=[b_yg, g.b_identb], writes=[bbuf[0]], sig=(kc == KC - 1))
            op(act, lambda e: e.activation(out=ygT[:].rearrange("p k n -> p (k n)"), in_=banks_bf[0][:, :],
                                           func=AF.Copy), reads=[bbuf[0]], writes=[b_ygT])
            yield
            for half in range(2):
                w, b_w = g.loadA(L, 2, half)
                for kc in range(KC):
                    op(pe, lambda e, kc=kc, w=w: e.matmul(banks[0][:, :], lhsT=ygT[:, kc, :], rhs=w[:, kc, :],
                                                         start=(kc == 0), stop=(kc == KC - 1)),
                       reads=[b_ygT, b_w], writes=[bbuf[0]], sig=(kc == KC - 1))
                op(act, lambda e, half=half: e.activation(out=tmpB[:, half * 512:(half + 1) * 512], in_=banks[0][:, :],
                                                          func=AF.Copy), reads=[bbuf[0]], writes=[b_tmpB])
                yield
            op(act, lambda e: e.activation(out=tmpA[:], in_=tmpB[:], func=AF.Square, accum_out=stat[:, 4:5]),
               reads=[b_tmpB], writes=[b_tmpA, b_stat])
            op(dve, lambda e: e.tensor_scalar(out=stat[:, 5:6], in0=stat[:, 4:5], scalar1=1.0 / D, scalar2=RMS_EPS,
                                              op0=ALU.mult, op1=ALU.add), reads=[b_stat], writes=[b_stat])
            op(act, lambda e: e.activation(out=stat[:, 6:7], in_=stat[:, 5:6], func=AF.Sqrt), reads=[b_stat],
               writes=[b_stat])
            op(dve, lambda e: e.reciprocal(out=stat[:, 7:8], in_=stat[:, 6:7]), reads=[b_stat], writes=[b_stat])
            op(dve, lambda e: e.scalar_tensor_tensor(out=tmpB[:], in0=tmpB[:], scalar=stat[:, 7:8], in1=postg[:],
                                                     op0=ALU.mult, op1=ALU.mult),
               reads=[b_tmpB, b_stat, b_lc], writes=[b_tmpB])
            op(pool, lambda e: e.tensor_tensor(out=x1[:, c, :], in0=tmpB[:], in1=xin[:], op=ALU.add),
               reads=[b_tmpB, b_xin], writes=[bx1[c]])
            yield

        def gen_Z():
            op(dve, lambda e: e.memset(Hf[:], 0.0), writes=b_H)
            op(pool, lambda e: e.memset(Hb[:], 0.0), writes=b_H)
            yield

        tasks = []
        lastB = None
        lastF = None
        for d in range(2):
            order = list(range(NCH)) if d == 0 else list(range(NCH - 1, -1, -1))
            tz = Task("Z%d" % d)
            tz.gen = gen_Z()
            tz.deps += [t for t in tasks if t.name.startswith("U")]
            tasks.append(tz)
            cxs = [None] * NCH

            def make_N(i):
                cx = NS()
                cx.c = order[i]
                tn = Task("N%d_%d" % (d, cx.c))
                _, cx.xin = p_xin.acquire(tn)
                cx.ks, cx.slot = p_slot.acquire(tn)
                prev = cxs[i - 1] if i > 0 else None
                if prev is not None:
                    p_slot.share(tn, prev.ks)
                    tn.deps.append(prev.tn)
                tn.gen = gen_N(cx, d, i == 0, i == NCH - 1, prev)
                cx.tn = tn
                cxs[i] = cx
                tasks.append(tn)

            make_N(0)
            if NCH > 1:
                make_N(1)
            for i in range(NCH):
                cx = cxs[i]
                c = cx.c
                tf = Task("F%d_%d" % (d, c))
                tf.deps.append(cx.tn)
                if i + 1 < NCH:
                    tf.deps.append(cxs[i + 1].tn)
                if lastF is not None:
                    tf.deps.append(lastF)
                lastF = tf
                p_slot.share(tf, cx.ks)
                klr, cx.lr = p_lr.acquire(tf)
                klk, cx.lk = p_lk.acquire(tf)
                kvt, cx.vt = p_vt.acquire(tf)
                kth, cx.th = p_th.acquire(tf)
                if d == 1:
                    kzs, cx.zs = p_zs.acquire(tf)
                tf.gen = gen_F(cx, d)
                tasks.append(tf)
                tb = None
                if d == 1:
                    tb = Task("B%d" % c)
                    _, cx.yb = p_yb.acquire(tb)
                    _, cx.bonB = p_bonB.acquire(tb)
                tus = []
                for j in range(KC):
                    tu = Task("U%d_%d_%d" % (d, c, j))
                    tu.deps += [tf, tz]
                    if d == 1:
                        tu.deps += list(tb.deps)
                    _, uset = p_uset.acquire(tu)
                    p_lr.share(tu, klr)
                    p_lk.share(tu, klk)
                    p_vt.share(tu, kvt)
                    p_th.share(tu, kth)
                    tu.gen = gen_U(cx, d, j, uset)
                    if i > 0:
                        tu.deps.append(cxs[i - 1].tus[j])
                    tus.append(tu)
                    tasks.append(tu)
                cx.tus = tus
                if d == 1:
                    tb.deps += tus + [tf]
                    if lastB is not None:
                        tb.deps.append(lastB)
                    _, cx.xres = p_xin.acquire(tb)
                    p_vt.share(tb, kvt)
                    p_zs.share(tb, kzs)
                    tb.gen = gen_B(cx)
                    tasks.append(tb)
                    lastB = tb
                if i + 2 < NCH:
                    make_N(i + 2)
        import os
        allow = os.environ.get("L0_TASKS", "ZNFUB")
        tasks = [t for t in tasks if t.name[0] in allow]
        for t in tasks:
            t.deps = [d_ for d_ in t.deps if d_.name[0] in allow]
        run_tasks(tasks, window=3)


GRID_W = 64
N_ROWS = T_SEQ // GRID_W
NEG = -30000.0


def _rs(r):
    return min(max(r - 4, 0), N_ROWS - 8)


def l1_geometry():
    geo = []
    pats = []
    for i in range(N_ROWS // 2):
        rows = [2 * i, 2 * i + 1]
        lo = min(_rs(r) for r in rows)
        hi = max(_rs(r) + 7 for r in rows)
        ent = []
        for kt in range(lo // 2, hi // 2 + 1):
            pat = tuple(tuple(1 if _rs(2 * i + rl) <= 2 * kt + krl < _rs(2 * i + rl) + 8 else 0 for krl in range(2))
                        for rl in range(2))
            if pat not in pats:
                pats.append(pat)
            ent.append((kt, (2 * (kt - i) + 6) // 2, pats.index(pat)))
        geo.append(ent)
    return geo, pats


def l1_prologue(g):
    nc, T_, op, sb = g.nc, g.T_, g.op, g.sb
    pe, act, dve, pool, sp = g.pe, g.act, g.dve, g.pool, g.sp
    geo, pats = l1_geometry()
    g.l1_geo, g.l1_pats = geo, pats
    NMK = len(pats)
    g.padD = nc.dram_tensor("padD", [240, P], F32, kind="Internal").ap()
    g.rbD = nc.dram_tensor("rbD", [P, 16 * 7 * P], BF16, kind="Internal").ap()
    g.mkD = nc.dram_tensor("mkD", [P, (NMK + 1) * P], BF16, kind="Internal").ap()
    x1 = g.x1
    with ExitStack() as p1:
        b_t = Buf("l1pro")
        padt = sb(p1, "padt", [120, 2, P], F32)
        op(dve, lambda e: e.memset(padt[:], 0.0), writes=[b_t])
        rp = g.wd['na_rpb'][0].rearrange("h r m -> (h r) m")
        for gi in range(2):
            T_.dma(sp, padt[:, gi, 48:79], rp[gi * 120:(gi + 1) * 120, :], writes=[b_t])
        for gi in range(2):
            T_.dma(sp, g.padD[gi * 120:(gi + 1) * 120, :], padt[:, gi, :], reads=[b_t])
        T_.barrier()
        T_.finish()
        Hs = x1[:].rearrange("p c d -> p (c d)")[:, 0:240 * 64].rearrange("p (b k) -> p b k", k=64)
        b_hs = Buf("Hs")
        for h in range(16):
            for r2 in range(2):
                src = bass.AP(tensor=g.padD.tensor, offset=h * 15 * P, ap=[[1, 64], [P, 15], [1, 64]])
                T_.dma(sp, Hs[64 * r2:64 * r2 + 64, h * 15:(h + 1) * 15, :], src, writes=[b_hs])
        RBs = sb(p1, "RBs", [P, 16, 7, P], BF16)
        b_rb = Buf("RBs")
        op(pool, lambda e: e.memset(RBs[:].rearrange("p h d k -> p (h d k)"), 0.0), writes=[b_rb])
        engs = [dve, pool, act]
        n = 0
        for h in range(16):
            for rl in range(2):
                for krl in range(2):
                    dis = [di for di in range(7) if 0 <= 2 * di + 1 + krl - rl <= 14]
                    d0, nd = dis[0], len(dis)
                    ri0 = 2 * d0 + 1 + krl - rl
                    srcv = Hs[64 * rl:64 * rl + 64, h * 15 + ri0:h * 15 + ri0 + 2 * (nd - 1) + 1:2, :]
                    dstv = RBs[64 * rl:64 * rl + 64, h, d0:d0 + nd, 64 * krl:64 * krl + 64]
                    e_ = engs[n % 3]
                    n += 1
                    if e_ is act:
                        op(act, lambda e, s=srcv, d=dstv: e.activation(out=d, in_=s, func=AF.Copy),
                           reads=[b_hs], writes=[b_rb])
                    else:
                        op(e_, lambda e, s=srcv, d=dstv: e.tensor_copy(out=d, in_=s), reads=[b_hs], writes=[b_rb])
        T_.dma(sp, g.rbD[:, :], RBs[:].rearrange("p h d k -> p (h d k)"), reads=[b_rb])
        ior = sb(p1, "ior1", [P, P], F32)
        ioc = sb(p1, "ioc1", [P, P], F32)
        t1 = sb(p1, "mt1", [P, P], F32)
        t2 = sb(p1, "mt2", [P, P], F32)
        cm = sb(p1, "cm", [P, P], BF16)
        neg = sb(p1, "negt", [P, P], BF16)
        MKs = sb(p1, "MKs", [P, NMK + 1, P], BF16)
        b_m = Buf("mk")
        op(pool, lambda e: e.iota(ior[:], pattern=[[0, P]], base=0, channel_multiplier=1,
                                  allow_small_or_imprecise_dtypes=True), writes=[b_m])
        op(pool, lambda e: e.iota(ioc[:], pattern=[[1, P]], base=0, channel_multiplier=0,
                                  allow_small_or_imprecise_dtypes=True), writes=[b_m])
        op(dve, lambda e: e.tensor_tensor(out=t1[:], in0=ior[:], in1=ioc[:], op=ALU.add), reads=[b_m], writes=[b_m])
        op(dve, lambda e: e.tensor_scalar(out=t2[:], in0=t1[:], scalar1=63.0, scalar2=None, op0=ALU.is_equal),
           reads=[b_m], writes=[b_m])
        op(dve, lambda e: e.tensor_scalar(out=t1[:], in0=t1[:], scalar1=191.0, scalar2=None, op0=ALU.is_equal),
           reads=[b_m], writes=[b_m])
        op(dve, lambda e: e.tensor_tensor(out=g.Jrev[:], in0=t1[:], in1=t2[:], op=ALU.add), reads=[b_m],
           writes=[g.b_cst])
        op(dve, lambda e: e.tensor_scalar(out=t1[:], in0=ior[:], scalar1=63.5, scalar2=-64.0, op0=ALU.is_gt,
                                          op1=ALU.mult), reads=[b_m], writes=[b_m])
        op(dve, lambda e: e.tensor_tensor(out=t1[:], in0=t1[:], in1=ior[:], op=ALU.add), reads=[b_m], writes=[b_m])
        op(dve, lambda e: e.tensor_scalar(out=t1[:], in0=t1[:], scalar1=-1.0, scalar2=55.0, op0=ALU.mult,
                                          op1=ALU.add), reads=[b_m], writes=[b_m])
        op(dve, lambda e: e.tensor_scalar(out=t1[:], in0=t1[:], scalar1=0.0, scalar2=48.0, op0=ALU.max, op1=ALU.min),
           reads=[b_m], writes=[b_m])
        op(dve, lambda e: e.tensor_scalar(out=t2[:], in0=ioc[:], scalar1=63.5, scalar2=-64.0, op0=ALU.is_gt,
                                          op1=ALU.mult), reads=[b_m], writes=[b_m])
        op(dve, lambda e: e.tensor_tensor(out=t2[:], in0=t2[:], in1=ioc[:], op=ALU.add), reads=[b_m], writes=[b_m])
        op(dve, lambda e: e.tensor_tensor(out=t2[:], in0=t2[:], in1=t1[:], op=ALU.subtract), reads=[b_m],
           writes=[b_m])
        op(dve, lambda e: e.tensor_scalar(out=t1[:], in0=t2[:], scalar1=-0.5, scalar2=None, op0=ALU.is_gt),
           reads=[b_m], writes=[b_m])
        op(dve, lambda e: e.tensor_scalar(out=t2[:], in0=t2[:], scalar1=15.5, scalar2=None, op0=ALU.is_lt),
           reads=[b_m], writes=[b_m])
        op(dve, lambda e: e.tensor_tensor(out=t1[:], in0=t1[:], in1=t2[:], op=ALU.mult), reads=[b_m], writes=[b_m])
        op(dve, lambda e: e.tensor_scalar(out=cm[:], in0=t1[:], scalar1=-1.0, scalar2=-NEG, op0=ALU.add, op1=ALU.mult),
           reads=[b_m], writes=[b_m])
        op(dve, lambda e: e.memset(neg[:], NEG), writes=[b_m])
        for pi, pat in enumerate(pats):
            for rl in range(2):
                for krl in range(2):
                    src_t = cm if pat[rl][krl] else neg
                    op(pool, lambda e, pi=pi, rl=rl, krl=krl, s=src_t: e.tensor_copy(
                        out=MKs[64 * rl:64 * rl + 64, pi, 64 * krl:64 * krl + 64],
                        in_=s[64 * rl:64 * rl + 64, 64 * krl:64 * krl + 64]), reads=[b_m], writes=[b_m])
        op(pool, lambda e: e.tensor_copy(out=MKs[:, NMK, :], in_=neg[:]), reads=[b_m], writes=[b_m])
        T_.dma(sp, g.mkD[:, :], MKs[:].rearrange("p n k -> p (n k)"), reads=[b_m])
        T_.barrier()
        T_.finish()


def layer1(g, si):
    nc, T_, op = g.nc, g.T_, g.op
    pe, act, dve, pool, sp = g.pe, g.act, g.dve, g.pool, g.sp
    banks, bbuf, banks_bf = g.banks, g.bbuf, g.banks_bf
    NCH, x1, bx1, pv, b_pv, pvc = g.NCH, g.x1, g.bx1, g.pv, g.b_pv, g.pvc
    wd, sb = g.wd, g.sb
    b_cst = g.b_cst
    L = 1
    geo, pats = g.l1_geo, g.l1_pats
    NMK = len(pats)
    hq = [slice(0, 64), slice(64, 128)]

    with ExitStack() as l1:
        postg = sb(l1, "postg1", [P, D], F32)
        RB = sb(l1, "RB", [P, 16, 7, P], BF16)
        MK = sb(l1, "MK", [P, NMK + 1, P], BF16)
        b_lc = Buf("l1consts")
        T_.dma(sp, postg[:], wd['post_norm_g'][L].partition_broadcast(P), writes=[b_lc])
        T_.dma(sp, RB[:].rearrange("p h d k -> p (h d k)"), g.rbD[:, :], writes=[b_lc])
        T_.dma(sp, MK[:].rearrange("p n k -> p (n k)"), g.mkD[:, :], writes=[b_lc])

        def mk(nm, sh, dt, n):
            return [(sb(l1, "%s%d" % (nm, i), sh, dt), Buf("%s%d" % (nm, i))) for i in range(n)]

        xs, b_xs = mk("xs1", [P, D], BF16, 1)[0]
        stat = sb(l1, "stat1", [P, 8], F32)
        b_stat = Buf("stat1")
        p_xn = RPool(mk("xnT1", [P, KC, P], BF16, 4))
        p_kT = RPool(mk("kT", [P, KC, P], BF16, 7))
        p_va = RPool(mk("Vaug", [P, 16, 65], BF16, 7))
        zs, b_zs = mk("zs1", [P, D], BF16, 1)[0]
        og, b_og = mk("og", [P, D], F32, 1)[0]
        yg, b_yg = mk("yg1", [P, D], BF16, 1)[0]
        ygT, b_ygT = mk("ygT1", [P, KC, P], BF16, 1)[0]
        tmpA, b_tmpA = mk("tmpA1", [P, D], F32, 1)[0]
        tmpB, b_tmpB = mk("tmpB1", [P, D], F32, 1)[0]

        def mkset(i):
            u = NS()
            def t(nm, sh, dt):
                setattr(u, nm, sb(l1, "a%d_%s" % (i, nm), sh, dt))
                setattr(u, "b_" + nm, Buf("a%d_%s" % (i, nm)))
            t("qT", [P, P], BF16)
            t("PT", [P, 5, P], BF16)
            t("rc", [P, 2], F32)
            u.banks = (2 + 3 * i, 3 + 3 * i, 4 + 3 * i)
            return u

        p_uset = RPool([mkset(0), mkset(1)])
        for (va, b_va) in p_va.items:
            op(pool, lambda e, va=va: e.memset(va[:, :, 64:65], 1.0), writes=[b_va])

        def gen_KV(cx):
            t = cx.t
            xn, b_xn = cx.xn
            kT, b_kT = cx.kT
            va, b_va = cx.va
            xsrc = x1[:, t, :]
            op(act, lambda e: e.activation(out=xs[:], in_=xsrc, func=AF.Square, accum_out=stat[:, 0:1]),
               reads=[bx1[t]], writes=[b_xs, b_stat])
            op(dve, lambda e: e.tensor_scalar(out=stat[:, 1:2], in0=stat[:, 0:1], scalar1=1.0 / D, scalar2=RMS_EPS,
                                              op0=ALU.mult, op1=ALU.add), reads=[b_stat], writes=[b_stat])
            op(act, lambda e: e.activation(out=stat[:, 2:3], in_=stat[:, 1:2], func=AF.Sqrt), reads=[b_stat],
               writes=[b_stat])
            op(dve, lambda e: e.reciprocal(out=stat[:, 3:4], in_=stat[:, 2:3]), reads=[b_stat], writes=[b_stat])
            op(act, lambda e: e.activation(out=xs[:], in_=xsrc, func=AF.Copy, scale=stat[:, 3:4]),
               reads=[bx1[t], b_stat], writes=[b_xs])
            yield
            for kc in range(KC):
                op(pe, lambda e, kc=kc: e.transpose(out=banks_bf[0][:, kc * P:(kc + 1) * P],
                                                    in_=xs[:, kc * P:(kc + 1) * P], identity=g.identb[:]),
                   reads=[b_xs, g.b_identb], writes=[bbuf[0]], sig=(kc == KC - 1))
            for kc in range(KC):
                sc = pvc('pre_g', L * 8 + kc)
                if kc % 2 == 0:
                    op(dve, lambda e, kc=kc, sc=sc: e.tensor_scalar(out=xn[:, kc, :],
                                                                  in0=banks_bf[0][:, kc * P:(kc + 1) * P],
                                                                  scalar1=sc, scalar2=None, op0=ALU.mult),
                       reads=[bbuf[0], b_pv], writes=[b_xn])
                else:
                    op(act, lambda e, kc=kc, sc=sc: e.activation(out=xn[:, kc, :],
                                                               in_=banks_bf[0][:, kc * P:(kc + 1) * P],
                                                               func=AF.Copy, scale=sc),
                       reads=[bbuf[0], b_pv], writes=[b_xn])
            yield
            for j0 in range(0, KC, 4):
                for j in range(j0, j0 + 4):
                    w, b_w = g.loadB(L, 1, j)
                    for kc in range(KC):
                        op(pe, lambda e, kc=kc, j=j, w=w: e.matmul(banks[1][:, (j - j0) * P:(j - j0 + 1) * P],
                                                                  lhsT=w[:, kc, :], rhs=xn[:, kc, :],
                                                                  start=(kc == 0), stop=(kc == KC - 1)),
                           reads=[b_w, b_xn], writes=[bbuf[1]], sig=(kc == KC - 1 and j == j0 + 3))
                for j in range(j0, j0 + 4):
                    op(act, lambda e, j=j: e.activation(out=kT[:, j, :], in_=banks[1][:, (j - j0) * P:(j - j0 + 1) * P],
                                                        func=AF.Identity, bias=pvc('bk', j)),
                       reads=[bbuf[1], b_pv], writes=[b_kT])
                yield
            for half in range(2):
                w, b_w = g.loadA(L, 0, half)
                for kc in range(KC):
                    op(pe, lambda e, kc=kc, w=w: e.matmul(banks[1][:, :], lhsT=xn[:, kc, :], rhs=w[:, kc, :],
                                                         start=(kc == 0), stop=False),
                       reads=[b_xn, b_w], writes=[bbuf[1]], sig=False)
                bo = 2 * D + half * 512
                op(pe, lambda e, bo=bo: e.matmul(banks[1][:, :], lhsT=g.onesrow[:], rhs=g.brow_hi[0:1, bo:bo + 512],
                                                 start=False, stop=False), reads=[b_cst], writes=[bbuf[1]], sig=False)
                op(pe, lambda e, bo=bo: e.matmul(banks[1][:, :], lhsT=g.onesrow[:], rhs=g.brow_lo[0:1, bo:bo + 512],
                                                 start=False, stop=True), reads=[b_cst], writes=[bbuf[1]])
                op(act, lambda e, half=half: e.activation(out=va[:, half * 8:(half + 1) * 8, 0:64],
                                                          in_=banks[1][:, :].rearrange("p (h n) -> p h n", h=8),
                                                          func=AF.Copy), reads=[bbuf[1]], writes=[b_va])
                yield

        def gen_Q(cx):
            xn, b_xn = cx.xn
            for half in range(2):
                w, b_w = g.loadA(L, 1, half)
                for kc in range(KC):
                    op(pe, lambda e, kc=kc, w=w: e.matmul(banks[1][:, :], lhsT=xn[:, kc, :], rhs=w[:, kc, :],
                                                         start=(kc == 0), stop=False),
                       reads=[b_xn, b_w], writes=[bbuf[1]], sig=False)
                bo = 3 * D + half * 512
                op(pe, lambda e, bo=bo: e.matmul(banks[1][:, :], lhsT=g.onesrow[:], rhs=g.brow_hi[0:1, bo:bo + 512],
                                                 start=False, stop=False), reads=[b_cst], writes=[bbuf[1]], sig=False)
                op(pe, lambda e, bo=bo: e.matmul(banks[1][:, :], lhsT=g.onesrow[:], rhs=g.brow_lo[0:1, bo:bo + 512],
                                                 start=False, stop=True), reads=[b_cst], writes=[bbuf[1]])
                op(act, lambda e, half=half: e.activation(out=zs[:, half * 512:(half + 1) * 512], in_=banks[1][:, :],
                                                          func=AF.Silu), reads=[bbuf[1]], writes=[b_zs])
                yield

        def gen_A(cx, j, u, kvs):
            i = cx.t
            xn, b_xn = cx.xn
            Ba, Bb, Bc = u.banks
            w, b_w = g.loadB(L, 0, j)
            for kc in range(KC):
                op(pe, lambda e, kc=kc: e.matmul(banks[Ba][:, 0:P], lhsT=w[:, kc, :], rhs=xn[:, kc, :],
                                                 start=(kc == 0), stop=(kc == KC - 1)),
                   reads=[b_w, b_xn], writes=[bbuf[Ba]], sig=(kc == KC - 1))
            op(act, lambda e: e.activation(out=u.qT[:], in_=banks[Ba][:, 0:P], func=AF.Identity, scale=0.125,
                                           bias=pvc('bq8', j)), reads=[bbuf[Ba], b_pv], writes=[u.b_qT])
            yield
            ent = geo[i]
            for h in range(2):
                hg = 2 * j + h
                for n_, (kt, di, pi) in enumerate(ent):
                    kT, b_kT = kvs[kt].kT
                    bk_ = Bb if n_ < 4 else Bc
                    co = (n_ % 4) * P
                    op(pe, lambda e, kT=kT, bk_=bk_, co=co, h=h: e.matmul(banks[bk_][:, co:co + P],
                                                                         lhsT=kT[hq[h], j, :], rhs=u.qT[hq[h], :],
                                                                         start=True, stop=False),
                       reads=[b_kT, u.b_qT], writes=[bbuf[bk_]], sig=False)
                    op(pe, lambda e, bk_=bk_, co=co, hg=hg, di=di: e.matmul(banks[bk_][:, co:co + P],
                                                                           lhsT=RB[:, hg, di, :], rhs=g.Jrev[:],
                                                                           start=False, stop=False),
                       reads=[b_lc, b_cst], writes=[bbuf[bk_]], sig=False)
                    last = (n_ == len(ent) - 1) or (n_ == 3)
                    op(pe, lambda e, bk_=bk_, co=co, pi=pi: e.matmul(banks[bk_][:, co:co + P], lhsT=MK[:, pi, :],
                                                                    rhs=g.Jrev[:], start=False, stop=True),
                       reads=[b_lc, b_cst], writes=[bbuf[bk_]], sig=last)
                n4 = min(4, len(ent))
                op(act, lambda e, n4=n4: e.activation(out=u.PT[:, 0:n4, :].rearrange("p n k -> p (n k)"),
                                                      in_=banks[Bb][:, 0:n4 * P], func=AF.Exp),
                   reads=[bbuf[Bb]], writes=[u.b_PT])
                if len(ent) > 4:
                    op(act, lambda e: e.activation(out=u.PT[:, 4, :], in_=banks[Bc][:, 0:P], func=AF.Exp),
                       reads=[bbuf[Bc]], writes=[u.b_PT])
                for n_, (kt, di, pi) in enumerate(ent):
                    va, b_va = kvs[kt].va
                    op(pe, lambda e, n_=n_, va=va, hg=hg, h=h: e.matmul(banks[Ba][:, 2 * P + h * 65:2 * P + h * 65 + 65],
                                                                       lhsT=u.PT[:, n_, :], rhs=va[:, hg, :],
                                                                       start=(n_ == 0), stop=(n_ == len(ent) - 1)),
                       reads=[u.b_PT, b_va], writes=[bbuf[Ba]], sig=(n_ == len(ent) - 1))
                yield
            for h in range(2):
                o0 = 2 * P + h * 65
                op(dve, lambda e, o0=o0, h=h: e.reciprocal(out=u.rc[:, h:h + 1], in_=banks[Ba][:, o0 + 64:o0 + 65]),
                   reads=[bbuf[Ba]], writes=[u.b_rc])
                op(dve, lambda e, o0=o0, h=h: e.tensor_scalar(out=og[:, (2 * j + h) * 64:(2 * j + h + 1) * 64],
                                                              in0=banks[Ba][:, o0:o0 + 64], scalar1=u.rc[:, h:h + 1],
                                                              scalar2=None, op0=ALU.mult),
                   reads=[bbuf[Ba], u.b_rc], writes=[b_og])
            yield

        def gen_O(cx):
            i = cx.t
            op(dve, lambda e: e.tensor_tensor(out=yg[:], in0=og[:], in1=zs[:], op=ALU.mult),
               reads=[b_og, b_zs], writes=[b_yg])
            for kc in range(KC):
                op(pe, lambda e, kc=kc: e.transpose(out=banks_bf[0][:, kc * P:(kc + 1) * P],
                                                    in_=yg[:, kc * P:(kc + 1) * P], identity=g.identb[:]),
                   reads=[b_yg, g.b_identb], writes=[bbuf[0]], sig=(kc == KC - 1))
            op(act, lambda e: e.activation(out=ygT[:].rearrange("p k n -> p (k n)"), in_=banks_bf[0][:, :],
                                           func=AF.Copy), reads=[bbuf[0]], writes=[b_ygT])
            yield
            for half in range(2):
                w, b_w = g.loadA(L, 2, half)
                for kc in range(KC):
                    op(pe, lambda e, kc=kc, w=w: e.matmul(banks[0][:, :], lhsT=ygT[:, kc, :], rhs=w[:, kc, :],
                                                         start=(kc == 0), stop=False),
                       reads=[b_ygT, b_w], writes=[bbuf[0]], sig=False)
                bo = 4 * D + half * 512
                op(pe, lambda e, bo=bo: e.matmul(banks[0][:, :], lhsT=g.onesrow[:], rhs=g.brow_hi[0:1, bo:bo + 512],
                                                 start=False, stop=False), reads=[b_cst], writes=[bbuf[0]], sig=False)
                op(pe, lambda e, bo=bo: e.matmul(banks[0][:, :], lhsT=g.onesrow[:], rhs=g.brow_lo[0:1, bo:bo + 512],
                                                 start=False, stop=True), reads=[b_cst], writes=[bbuf[0]])
                op(act, lambda e, half=half: e.activation(out=tmpB[:, half * 512:(half + 1) * 512], in_=banks[0][:, :],
                                                          func=AF.Copy), reads=[bbuf[0]], writes=[b_tmpB])
                yield
            op(act, lambda e: e.activation(out=tmpA[:], in_=tmpB[:], func=AF.Square, accum_out=stat[:, 4:5]),
               reads=[b_tmpB], writes=[b_tmpA, b_stat])
            op(dve, lambda e: e.tensor_scalar(out=stat[:, 5:6], in0=stat[:, 4:5], scalar1=1.0 / D, scalar2=RMS_EPS,
                                              op0=ALU.mult, op1=ALU.add), reads=[b_stat], writes=[b_stat])
            op(act, lambda e: e.activation(out=stat[:, 6:7], in_=stat[:, 5:6], func=AF.Sqrt), reads=[b_stat],
               writes=[b_stat])
            op(dve, lambda e: e.reciprocal(out=stat[:, 7:8], in_=stat[:, 6:7]), reads=[b_stat], writes=[b_stat])
            op(dve, lambda e: e.scalar_tensor_tensor(out=tmpB[:], in0=tmpB[:], scalar=stat[:, 7:8], in1=postg[:],
                                                     op0=ALU.mult, op1=ALU.mult),
               reads=[b_tmpB, b_stat, b_lc], writes=[b_tmpB])
            op(pool, lambda e: e.tensor_tensor(out=tmpA[:], in0=tmpB[:], in1=x1[:, i, :], op=ALU.add),
               reads=[b_tmpB, bx1[i]], writes=[b_tmpA])
            T_.dma(pool, g.y_out[si, i * P:(i + 1) * P, :], tmpA[:], reads=[b_tmpA])
            yield

        tasks = []
        kvs = [None] * NCH
        lastO = None
        lastKV = None

        def make_KV(t):
            cx = NS()
            cx.t = t
            tk = Task("K%d" % t)
            cx.kxn, cx.xn = p_xn.acquire(tk)
            cx.kkT, cx.kT = p_kT.acquire(tk)
            cx.kva, cx.va = p_va.acquire(tk)
            if lastKV[0] is not None:
                tk.deps.append(lastKV[0])
            tk.gen = gen_KV(cx)
            cx.tk = tk
            kvs[t] = cx
            tasks.append(tk)
            lastKV[0] = tk

        lastKV = [None]
        for s in range(NCH + 3):
            if s < NCH:
                make_KV(s)
            i = s - 3
            if i < 0:
                continue
            cx = kvs[i]
            need = [kvs[kt].tk for (kt, _, _) in geo[i]]
            tq = Task("Q%d" % i)
            tq.deps += [cx.tk]
            if lastO is not None:
                tq.deps.append(lastO)
            p_xn.share(tq, cx.kxn)
            tq.gen = gen_Q(cx)
            tasks.append(tq)
            tas = []
            for j in range(KC):
                ta = Task("A%d_%d" % (i, j))
                ta.deps += need + [cx.tk]
                if lastO is not None:
                    ta.deps.append(lastO)
                _, uset = p_uset.acquire(ta)
                p_xn.share(ta, cx.kxn)
                for (kt, _, _) in geo[i]:
                    p_kT.share(ta, kvs[kt].kkT)
                    p_va.share(ta, kvs[kt].kva)
                ta.gen = gen_A(cx, j, uset, kvs)
                tas.append(ta)
                tasks.append(ta)
            to = Task("O%d" % i)
            to.deps += tas + [tq]
            to.gen = gen_O(cx)
            tasks.append(to)
            lastO = to
        run_tasks(tasks, window=3)


def kernel(**inputs):
    xp = np.asarray(inputs['x_prompt'], dtype=np.float32)
    xs_ = np.asarray(inputs['x_sample'], dtype=np.float32)
    xall = np.concatenate([xp, xs_], axis=0)
    nseq = xall.shape[0] // N_CORES
    nc = build(nseq)
    in_maps = []
    for ci in range(N_CORES):
        m = {"x": np.ascontiguousarray(xall[ci * nseq:(ci + 1) * nseq])}
        for nm in W_NAMES:
            m[nm] = np.ascontiguousarray(np.asarray(inputs[nm], dtype=np.float32))
        in_maps.append(m)
    res = run_bass_kernel_spmd(nc, in_maps, core_ids=list(range(N_CORES)))
    yall = np.concatenate([r["y"] for r in res.results], axis=0)
    nb = xp.shape[0]
    return (np.ascontiguousarray(yall[:nb]), np.ascontiguousarray(yall[nb:]))
```

```python
import numpy as np
from contextlib import ExitStack
import concourse.bass as bass
import concourse.mybir as mybir
from concourse.bass_utils import run_bass_kernel_spmd
from concourse.alu_op_type import AluOpType as ALU

F32 = mybir.dt.float32
BF16 = mybir.dt.bfloat16
AF = mybir.ActivationFunctionType
AX = mybir.AxisListType

N_CORES = 8
D = 1024
KC = 8
P = 128
T_SEQ = 2048
LAM = float(np.exp(-0.5))
LNX_EPS = 64e-5
RMS_EPS = 1e-6

import os
SAME_ENGINE_SYNC = os.environ.get('K_SES', '1') == '1'
WINDOW = int(os.environ.get('K_WIN', '4'))
STAGGER = int(os.environ.get('K_STAG', '0'))
NQ2 = int(os.environ.get('K_NQ2', '1'))
W2 = 512 // NQ2
DBUF = int(os.environ.get('K_DBUF', '1'))


class _Sem:
    def __init__(self, sem, name):
        self.sem = sem
        self.name = name
        self.n = 0


class Eng:
    def __init__(self, h, sem, name, is_pe=False):
        self.h = h
        self.s = _Sem(sem, name)
        self.name = name
        self.is_pe = is_pe
        self.waited = {}


class Buf:
    __slots__ = ("name", "w", "r", "excl")

    def __init__(self, name="", excl=False):
        self.name = name
        self.w = None
        self.r = {}
        self.excl = excl


class Trk:
    def __init__(self, nc, es, n_slots=16):
        self.nc = nc
        mk = lambda nm: es.enter_context(nc.semaphore(nm))
        self.pe = Eng(nc.tensor, mk("s_pe"), "pe", is_pe=True)
        self.act = Eng(nc.scalar, mk("s_act"), "act")
        self.dve = Eng(nc.vector, mk("s_dve"), "dve")
        self.pool = Eng(nc.gpsimd, mk("s_pool"), "pool")
        self.sp = Eng(nc.sync, mk("s_sp"), "sp")
        self.engs = [self.pe, self.act, self.dve, self.pool, self.sp]
        self.slots = [_Sem(mk("s_dma%d" % i), "dma%d" % i) for i in range(n_slots)]
        self.dma_i = 0
        self.n_inst = 0
        self.cnt = {}

    def _deps(self, reads, writes):
        deps = {}
        for b in reads:
            if b.w is not None:
                s, v = b.w
                if deps.get(s, 0) < v:
                    deps[s] = v
        for b in writes:
            if b.w is not None:
                s, v = b.w
                if deps.get(s, 0) < v:
                    deps[s] = v
            for s, v in b.r.items():
                if deps.get(s, 0) < v:
                    deps[s] = v
        return deps

    def _wait(self, eng, deps):
        for s, v in deps.items():
            if eng.waited.get(s, 0) >= v:
                continue
            if s is eng.s:
                if eng.is_pe or not SAME_ENGINE_SYNC:
                    continue
            eng.h.wait_ge(s.sem, v)
            eng.waited[s] = v

    def _mark(self, tok, reads, writes):
        s, v = tok
        for b in reads:
            if b.r.get(s, 0) < v:
                b.r[s] = v
        for b in writes:
            b.w = tok
            b.r = {}

    def op(self, eng, fn, reads=(), writes=(), sig=True):
        if any(b.excl for b in reads):
            writes = list(writes) + [b for b in reads if b.excl]
            reads = [b for b in reads if not b.excl]
        self._wait(eng, self._deps(reads, writes))
        inst = fn(eng.h)
        tok = (eng.s, eng.s.n + 1)
        if sig:
            inst.then_inc(eng.s.sem, 1)
            eng.s.n += 1
        self._mark(tok, reads, writes)
        self.n_inst += 1
        self.cnt[eng.name] = self.cnt.get(eng.name, 0) + 1
        return inst

    def dma(self, q, out, in_, reads=(), writes=(), **kw):
        slot = self.slots[self.dma_i % len(self.slots)]
        self.dma_i += 1
        deps = self._deps(reads, writes)
        if slot.n > 0 and deps.get(slot, 0) < slot.n:
            deps[slot] = slot.n
        self._wait(q, deps)
        inst = q.h.dma_start(out=out, in_=in_, **kw)
        inst.then_inc(slot.sem, 16)
        slot.n += 16
        self._mark((slot, slot.n), reads, writes)
        self.n_inst += 1
        self.cnt["dma"] = self.cnt.get("dma", 0) + 1
        return inst

    def barrier(self):
        allsems = [e.s for e in self.engs] + self.slots
        for e in self.engs:
            deps = {s: s.n for s in allsems if s.n > 0 and s is not e.s}
            self._wait(e, deps)

    def finish(self):
        deps = {s: s.n for s in self.slots if s.n > 0}
        self._wait(self.sp, deps)


class Task:
    def __init__(self, name):
        self.name = name
        self.deps = []
        self.done = False
        self.gen = None


def run_tasks(tasks, window=3):
    pending = list(tasks)
    active = []
    while pending or active:
        while pending and len(active) < window and all(d.done for d in pending[0].deps):
            active.append(pending.pop(0))
        if not active:
            raise RuntimeError("scheduler deadlock at %s" % pending[0].name)
        for t in list(active):
            try:
                next(t.gen)
            except StopIteration:
                t.done = True
                active.remove(t)


class RPool:
    def __init__(self, items):
        self.items = items
        self.i = 0
        self.users = [[] for _ in items]

    def acquire(self, task):
        k = self.i % len(self.items)
        self.i += 1
        task.deps += self.users[k]
        self.users[k] = [task]
        return k, self.items[k]

    def share(self, task, k):
        self.users[k].append(task)


class NS:
    pass


W_NAMES = ['pre_norm_g', 'post_norm_g', 'rk_mu', 'rk_w_r', 'rk_w_k', 'rk_w_v', 'rk_w_z', 'rk_w0', 'rk_w1', 'rk_w2',
           'rk_a0', 'rk_a1', 'rk_a2', 'rk_k_k', 'rk_k_a', 'rk_r_k', 'rk_lnx_w', 'rk_lnx_b', 'rk_w_o', 'na_w_in',
           'na_b_in', 'na_rpb', 'na_w_o', 'na_b_o']
W_SHAPES = {
    'pre_norm_g': [2, D], 'post_norm_g': [2, D], 'rk_mu': [1, 7, D], 'rk_w_r': [1, D, D], 'rk_w_k': [1, D, D],
    'rk_w_v': [1, D, D], 'rk_w_z': [1, D, D], 'rk_w0': [1, 2, D], 'rk_w1': [1, 2, D, 64], 'rk_w2': [1, 2, 64, D],
    'rk_a0': [1, 2, D], 'rk_a1': [1, 2, D, 64], 'rk_a2': [1, 2, 64, D], 'rk_k_k': [1, D], 'rk_k_a': [1, D],
    'rk_r_k': [1, 16, 64], 'rk_lnx_w': [1, D], 'rk_lnx_b': [1, D], 'rk_w_o': [1, D, D], 'na_w_in': [1, D, 4 * D],
    'na_b_in': [1, 4 * D], 'na_rpb': [1, 16, 15, 31], 'na_w_o': [1, D, D], 'na_b_o': [1, D],
}

PV = {}
_c = 0
for _nm, _n in [('mu', 7), ('w0', 2), ('a0', 2), ('k_k', 1), ('k_a', 1), ('r_k', 1), ('pre_g', 2), ('bq', 1),
                ('bk', 1), ('omk_a', 1), ('bq8', 1)]:
    PV[_nm] = _c
    _c += _n * 8
PV_COLS = _c
PV_ROWS = PV['omk_a']


def build(nseq, T=T_SEQ, stage="full"):
    NCH = T // P
    nc = bass.Bass("TRN2", target_bir_lowering=False)
    x_in = nc.dram_tensor("x", [nseq, T, D], F32, kind="ExternalInput").ap()
    y_out = nc.dram_tensor("y", [nseq, T, D], F32, kind="ExternalOutput").ap()
    wd = {nm: nc.dram_tensor(nm, W_SHAPES[nm], F32, kind="ExternalInput").ap() for nm in W_NAMES}
    wbJ = [nc.dram_tensor("wbJ%d" % l, [2, KC, P, KC, P], BF16, kind="Internal").ap() for l in range(2)]
    wbH = [nc.dram_tensor("wbH%d" % l, [3, 2 * NQ2, P, KC, W2], BF16, kind="Internal").ap() for l in range(2)]

    es = ExitStack()
    with es:
        T_ = Trk(nc, es)
        pe, act, dve, pool, sp = T_.pe, T_.act, T_.dve, T_.pool, T_.sp
        op = T_.op

        uid = [0]

        def sb(es_, nm, sh, dt):
            uid[0] += 1
            return es_.enter_context(nc.sbuf_tensor("%s_%d" % (nm, uid[0]), sh, dt))

        banks = [es.enter_context(nc.psum_tensor("pb%d" % i, [P, 512], F32)) for i in range(8)]
        bbuf = [Buf("pb%d" % i, excl=True) for i in range(8)]
        banks_bf = [b[:].bitcast(BF16) for b in banks]

        x1 = sb(es, "x1", [P, NCH, D], F32)
        bx1 = [Buf("x1_%d" % c) for c in range(NCH)]
        ringA = [sb(es, "ringA%d" % i, [P, KC, W2], BF16) for i in range(2)]
        b_ringA = [Buf("ringA%d" % i) for i in range(2)]
        ringB = [sb(es, "ringB%d" % i, [P, KC, P], BF16) for i in range(4)]
        b_ringB = [Buf("ringB%d" % i) for i in range(4)]
        rA_i = [0]
        rB_i = [0]
        identb = sb(es, "identb", [P, P], BF16)
        b_identb = Buf("identb")
        pv = sb(es, "pv", [P, PV_COLS], F32)
        b_pv = Buf("pv")
        onesrow = sb(es, "onesrow", [1, P], BF16)
        brow_hi = sb(es, "brow_hi", [1, 5 * D], BF16)
        brow_lo = sb(es, "brow_lo", [1, 5 * D], BF16)
        b_cst = Buf("consts")
        maskq = [sb(es, "maskq%d" % d, [P, 512], BF16) for d in range(2)]
        bdones = sb(es, "bdones", [P, P], BF16)
        hsel = sb(es, "hsel", [P, 2], BF16)
        ones_f = sb(es, "ones_f", [P, P], F32)
        Jrev = sb(es, "Jrev", [P, P], BF16)
        pes0 = ExitStack()
        io_r = sb(pes0, "io_r", [P, P], F32)
        io_c = sb(pes0, "io_c", [P, P], F32)

        def pvc(nm, idx):
            c0 = PV[nm] + idx
            return pv[:, c0:c0 + 1]

        def loadA(layer, m, qt):
            k = rA_i[0] % 2
            rA_i[0] += 1
            T_.dma(sp, ringA[k][:], wbH[layer][m, qt], writes=[b_ringA[k]])
            return ringA[k], b_ringA[k]

        def loadB(layer, m, j):
            k = rB_i[0] % 4
            rB_i[0] += 1
            T_.dma(sp, ringB[k][:], wbJ[layer][m, j], writes=[b_ringB[k]])
            return ringB[k], b_ringB[k]

        op(pool, lambda e: e.iota(io_r[:], pattern=[[0, P]], base=0, channel_multiplier=1,
                                  allow_small_or_imprecise_dtypes=True), writes=[b_cst])
        op(pool, lambda e: e.iota(io_c[:], pattern=[[1, P]], base=0, channel_multiplier=0,
                                  allow_small_or_imprecise_dtypes=True), writes=[b_cst])
        op(dve, lambda e: e.tensor_tensor(out=identb[:], in0=io_r[:], in1=io_c[:], op=ALU.is_equal),
           reads=[b_cst], writes=[b_identb])
        op(dve, lambda e: e.memset(onesrow[:], 1.0), writes=[b_cst])
        op(dve, lambda e: e.memset(ones_f[:], 1.0), writes=[b_cst])
        for d_, (o_s, o_i) in enumerate([(ALU.is_lt, ALU.is_le), (ALU.is_gt, ALU.is_ge)]):
            for q4 in range(4):
                o_ = o_s if q4 % 2 == 0 else o_i
                op(dve, lambda e, d_=d_, q4=q4, o_=o_: e.tensor_tensor(out=maskq[d_][:, q4 * P:(q4 + 1) * P],
                                                                      in0=io_r[:], in1=io_c[:], op=o_),
                   reads=[b_cst], writes=[b_cst])

        with ExitStack() as pes:
            identf = sb(pes, "identf", [P, P], F32)
            rb_ = sb(pes, "rb_", [P, P], F32)
            cb_ = sb(pes, "cb_", [P, P], F32)
            op(dve, lambda e: e.tensor_tensor(out=identf[:], in0=io_r[:], in1=io_c[:], op=ALU.is_equal),
               reads=[b_cst], writes=[b_cst])
            op(dve, lambda e: e.tensor_scalar(out=rb_[:], in0=io_r[:], scalar1=63.5, scalar2=None, op0=ALU.is_gt),
               reads=[b_cst], writes=[b_cst])
            op(dve, lambda e: e.tensor_scalar(out=cb_[:], in0=io_c[:], scalar1=63.5, scalar2=None, op0=ALU.is_gt),
               reads=[b_cst], writes=[b_cst])
            op(dve, lambda e: e.tensor_tensor(out=bdones[:], in0=rb_[:], in1=cb_[:], op=ALU.is_equal),
               reads=[b_cst], writes=[b_cst])
            op(dve, lambda e: e.tensor_copy(out=hsel[:, 1:2], in_=rb_[:, 0:1]), reads=[b_cst], writes=[b_cst])
            op(dve, lambda e: e.tensor_scalar(out=hsel[:, 0:1], in0=rb_[:, 0:1], scalar1=-1.0, scalar2=1.0,
                                              op0=ALU.mult, op1=ALU.add), reads=[b_cst], writes=[b_cst])

            rows = sb(pes, "pvrows", [P, 2, P], F32)
            b_rows = Buf("pvrows")
            op(dve, lambda e: e.memset(rows[:], 0.0), writes=[b_rows])

            def load_rows(r0, src):
                n = src.shape[0]
                g, o = divmod(r0, P)
                assert o + n <= P, (r0, n)
                T_.dma(sp, rows[o:o + n, g, :], src, writes=[b_rows])

            load_rows(PV['mu'], wd['rk_mu'][0].rearrange("m (j q) -> (m j) q", q=P))
            load_rows(PV['w0'], wd['rk_w0'][0].rearrange("m (j q) -> (m j) q", q=P))
            load_rows(PV['a0'], wd['rk_a0'][0].rearrange("m (j q) -> (m j) q", q=P))
            load_rows(PV['k_k'], wd['rk_k_k'][0].rearrange("(j q) -> j q", q=P))
            load_rows(PV['k_a'], wd['rk_k_a'][0].rearrange("(j q) -> j q", q=P))
            load_rows(PV['r_k'], wd['rk_r_k'][0].rearrange("(j h) c -> j (h c)", h=2))
            load_rows(PV['pre_g'], wd['pre_norm_g'].rearrange("m (j q) -> (m j) q", q=P))
            load_rows(PV['bq'], wd['na_b_in'][0, 0:D].rearrange("(j q) -> j q", q=P))
            load_rows(PV['bk'], wd['na_b_in'][0, D:2 * D].rearrange("(j q) -> j q", q=P))
            for g in range(2):
                op(pe, lambda e, g=g: e.transpose(out=banks[0][:, g * P:(g + 1) * P], in_=rows[:, g, :],
                                                  identity=identf[:]),
                   reads=[b_rows, b_cst], writes=[bbuf[0]])
            op(dve, lambda e: e.tensor_copy(out=pv[:, 0:PV_ROWS], in_=banks[0][:, 0:PV_ROWS]),
               reads=[bbuf[0]], writes=[b_pv])
            op(dve, lambda e: e.tensor_scalar(out=pv[:, PV['omk_a']:PV['omk_a'] + 8],
                                              in0=pv[:, PV['k_a']:PV['k_a'] + 8], scalar1=-1.0, scalar2=1.0,
                                              op0=ALU.mult, op1=ALU.add), reads=[b_pv], writes=[b_pv])
            op(dve, lambda e: e.tensor_scalar(out=pv[:, PV['bq8']:PV['bq8'] + 8], in0=pv[:, PV['bq']:PV['bq'] + 8],
                                              scalar1=0.125, scalar2=None, op0=ALU.mult), reads=[b_pv], writes=[b_pv])

            brow_f = sb(pes, "brow_f", [1, 5 * D], F32)
            brow_t = sb(pes, "brow_t", [1, 5 * D], F32)
            b_bf = Buf("brow_f")
            T_.dma(sp, brow_f[0:1, 0:4 * D], wd['na_b_in'][0:1, :], writes=[b_bf])
            T_.dma(sp, brow_f[0:1, 4 * D:5 * D], wd['na_b_o'][0:1, :], writes=[b_bf])
            op(act, lambda e: e.activation(out=brow_hi[:], in_=brow_f[:], func=AF.Copy), reads=[b_bf], writes=[b_cst])
            op(dve, lambda e: e.tensor_tensor(out=brow_t[:], in0=brow_f[:], in1=brow_hi[:], op=ALU.subtract),
               reads=[b_bf, b_cst], writes=[b_bf])
            op(act, lambda e: e.activation(out=brow_lo[:], in_=brow_t[:], func=AF.Copy), reads=[b_bf], writes=[b_cst])

            stg = [sb(pes, "stg%d" % i, [P, D], F32) for i in range(3)]
            stb = [sb(pes, "stb%d" % i, [P, D], BF16) for i in range(3)]
            b_stg = [Buf() for _ in range(3)]
            b_stb = [Buf() for _ in range(3)]
            srcs = []
            for kc in range(KC):
                rs_ = slice(kc * P, (kc + 1) * P)
                for m, nm in enumerate(['rk_w_r', 'rk_w_k']):
                    srcs.append((wd[nm][0, rs_, :], wbJ[0][m, :, :, kc, :].rearrange("j p n -> p j n"), 'J'))
                for m, nm in enumerate(['rk_w_v', 'rk_w_z', 'rk_w_o']):
                    srcs.append((wd[nm][0, rs_, :], wbH[0][m, :, :, kc, :].rearrange("h p n -> p h n"), 'H'))
                for m in range(2):
                    srcs.append((wd['na_w_in'][0, rs_, m * D:(m + 1) * D],
                                 wbJ[1][m, :, :, kc, :].rearrange("j p n -> p j n"), 'J'))
                for m in range(2):
                    srcs.append((wd['na_w_in'][0, rs_, (m + 2) * D:(m + 3) * D],
                                 wbH[1][m, :, :, kc, :].rearrange("h p n -> p h n"), 'H'))
                srcs.append((wd['na_w_o'][0, rs_, :], wbH[1][2, :, :, kc, :].rearrange("h p n -> p h n"), 'H'))
            cast_engs = [act, dve, pool]
            for i, (src, dst, kind) in enumerate(srcs):
                k = i % 3
                T_.dma(sp, stg[k][:], src, writes=[b_stg[k]])
                if cast_engs[k] is act:
                    op(act, lambda e, k=k: e.activation(out=stb[k][:], in_=stg[k][:], func=AF.Copy),
                       reads=[b_stg[k]], writes=[b_stb[k]])
                else:
                    op(cast_engs[k], lambda e, k=k: e.tensor_copy(out=stb[k][:], in_=stg[k][:]),
                       reads=[b_stg[k]], writes=[b_stb[k]])
                if kind == 'J':
                    srcv = stb[k][:].rearrange("p (j n) -> p j n", j=KC)
                else:
                    srcv = stb[k][:].rearrange("p (h n) -> p h n", h=2 * NQ2)
                T_.dma(sp, dst, srcv, reads=[b_stb[k]])
            T_.barrier()
            T_.finish()
        pes0.close()

        ctx = NS()
        ctx.__dict__.update(locals())
        if stage != "l0":
            l1_prologue(ctx)
        for si in range(nseq):
            with nc.named_scope('L0_%d' % si):
                layer0(ctx, si)
            T_.barrier()
            T_.finish()
            if stage == "l0":
                for c in range(NCH):
                    T_.dma(sp, y_out[si, c * P:(c + 1) * P, :], x1[:, c, :], reads=[bx1[c]])
            else:
                with nc.named_scope('L1_%d' % si):
                    layer1(ctx, si)
            T_.barrier()
            T_.finish()
        print("instructions:", T_.n_inst, T_.cnt)
    return nc


def layer0(g, si):
    nc, T_, op = g.nc, g.T_, g.op
    pe, act, dve, pool, sp = g.pe, g.act, g.dve, g.pool, g.sp
    banks, bbuf, banks_bf = g.banks, g.bbuf, g.banks_bf
    NCH, x1, bx1, pv, b_pv, pvc = g.NCH, g.x1, g.bx1, g.pv, g.b_pv, g.pvc
    wd, sb = g.wd, g.sb
    b_cst = g.b_cst
    L = 0

    with ExitStack() as l0:
        w1b = sb(l0, "w1b", [P, 2, KC, 64], BF16)
        a1b = sb(l0, "a1b", [P, 2, KC, 64], BF16)
        w2b = sb(l0, "w2b", [64, 2, D], BF16)
        a2b = sb(l0, "a2b", [64, 2, D], BF16)
        lnxb = sb(l0, "lnxb", [P, 2, D], F32)
        postg = sb(l0, "postg", [P, D], F32)
        b_lc = Buf("l0consts")
        tmpA = sb(l0, "tmpA", [P, D], F32)
        tmpB = sb(l0, "tmpB", [P, D], F32)
        b_tmpA, b_tmpB = Buf("tmpA"), Buf("tmpB")
        T_.dma(sp, tmpA[:].rearrange("p (d k n) -> p d k n", d=2, k=KC),
               wd['rk_w1'][0].rearrange("d (k p) n -> p d k n", p=P), writes=[b_tmpA])
        op(dve, lambda e: e.tensor_copy(out=w1b[:].rearrange("p d k n -> p (d k n)"), in_=tmpA[:]),
           reads=[b_tmpA], writes=[b_lc])
        T_.dma(sp, tmpB[:].rearrange("p (d k n) -> p d k n", d=2, k=KC),
               wd['rk_a1'][0].rearrange("d (k p) n -> p d k n", p=P), writes=[b_tmpB])
        op(dve, lambda e: e.tensor_copy(out=a1b[:].rearrange("p d k n -> p (d k n)"), in_=tmpB[:]),
           reads=[b_tmpB], writes=[b_lc])
        for d_ in range(2):
            T_.dma(sp, tmpA[0:64, :], wd['rk_w2'][0, d_], writes=[b_tmpA])
            op(dve, lambda e, d_=d_: e.tensor_copy(out=w2b[:, d_, :], in_=tmpA[0:64, :]), reads=[b_tmpA], writes=[b_lc])
            T_.dma(sp, tmpB[0:64, :], wd['rk_a2'][0, d_], writes=[b_tmpB])
            op(dve, lambda e, d_=d_: e.tensor_copy(out=a2b[:, d_, :], in_=tmpB[0:64, :]), reads=[b_tmpB], writes=[b_lc])
        T_.dma(sp, lnxb[:, 0, :], wd['rk_lnx_w'][0].partition_broadcast(P), writes=[b_lc])
        T_.dma(sp, lnxb[:, 1, :], wd['rk_lnx_b'][0].partition_broadcast(P), writes=[b_lc])
        T_.dma(sp, postg[:], wd['post_norm_g'][L].partition_broadcast(P), writes=[b_lc])

        def mk(nm, sh, dt, n):
            return [(sb(l0, "%s%d" % (nm, i), sh, dt), Buf("%s%d" % (nm, i))) for i in range(n)]

        p_xin = RPool(mk("xin", [P, D], F32, 1))
        xs, b_xs = mk("xs", [P, D], BF16, 1)[0]
        stat = sb(l0, "stat", [P, 8], F32)
        b_stat = Buf("stat")
        p_slot = RPool(mk("xnT", [P, KC, P + 2], BF16, 3))
        xx, b_xx = mk("xx", [P, KC, P], BF16, 1)[0]
        ltmp, b_ltmp = mk("ltmp", [P, P], F32, 1)[0]
        p_lr = RPool(mk("lrp_r", [P, KC, P], BF16, DBUF))
        p_lk = RPool(mk("lrp_k", [P, KC, P], BF16, DBUF))
        p_lt = mk("lrp_t", [P, KC, P], BF16, 1)
        lt_i = [0]
        p_vt = RPool(mk("Vtm", [P, D], BF16, DBUF))
        p_zs = RPool(mk("zs", [P, D], BF16, 1))
        p_th = RPool(mk("th", [64, 2 * P], BF16, 2))
        Hf = sb(l0, "Hf", [P, KC, 64], F32)
        Hb = sb(l0, "Hb", [P, KC, 64], BF16)
        b_H = [Buf("H%d" % j) for j in range(KC)]
        bonF = sb(l0, "bonF", [P, NCH, 16], F32)
        b_bonF = [Buf("bonF%d" % c) for c in range(NCH)]
        p_bonB = RPool(mk("bonB", [P, 16], F32, 2))
        p_yb = RPool(mk("Yb", [P, D], F32, 1))
        yg, b_yg = mk("yg", [P, D], BF16, 1)[0]
        ygT, b_ygT = mk("ygT", [P, KC, P], BF16, 1)[0]
        gst = sb(l0, "gst", [P, 6, 16], F32)
        b_gst = Buf("gst")

        def mkset(i):
            u = NS()
            u.i = i
            def t(nm, sh, dt):
                tt = sb(l0, "u%d_%s" % (i, nm), sh, dt)
                setattr(u, nm, tt)
                setattr(u, "b_" + nm, Buf("u%d_%s" % (i, nm)))
            t("rk", [P, 2 * P], F32)
            for nm in ("sg", "al", "kk", "rs", "Ein", "Eex", "ein", "kd"):
                t(nm, [P, P], F32)
            t("sq", [P, P], BF16)
            t("arT", [P, 2 * P], BF16)
            t("btT", [P, P], BF16)
            t("ktT", [P, P], BF16)
            t("pr", [P, P], BF16)
            t("BK", [P, 2 * P], BF16)
            t("AT0", [P, 512], BF16)
            t("AT1", [P, 512], BF16)
            for h in range(2):
                for k in range(2):
                    t("C%d%d" % (h, k), [P, 3 * P], BF16)
            t("Xp", [P, P], BF16)
            t("Up", [P, P], BF16)
            t("Hs", [P, 64], F32)
            t("sc", [P, 4], F32)
            u.banks = (2 + 3 * i, 3 + 3 * i, 4 + 3 * i)
            return u

        p_uset = RPool([mkset(0), mkset(1)])

        def gen_N(cx, d, first, last, prev):
            c = cx.c
            xin, b_xin = cx.xin
            slot, b_slot = cx.slot
            T_.dma(sp, xin[:], g.x_in[si, c * P:(c + 1) * P, :], writes=[b_xin])
            op(act, lambda e: e.activation(out=xs[:], in_=xin[:], func=AF.Square, accum_out=stat[:, 0:1]),
               reads=[b_xin], writes=[b_xs, b_stat])
            op(dve, lambda e: e.tensor_scalar(out=stat[:, 1:2], in0=stat[:, 0:1], scalar1=1.0 / D, scalar2=RMS_EPS,
                                              op0=ALU.mult, op1=ALU.add), reads=[b_stat], writes=[b_stat])
            op(act, lambda e: e.activation(out=stat[:, 2:3], in_=stat[:, 1:2], func=AF.Sqrt), reads=[b_stat],
               writes=[b_stat])
            op(dve, lambda e: e.reciprocal(out=stat[:, 3:4], in_=stat[:, 2:3]), reads=[b_stat], writes=[b_stat])
            op(act, lambda e: e.activation(out=xs[:], in_=xin[:], func=AF.Copy, scale=stat[:, 3:4]),
               reads=[b_xin, b_stat], writes=[b_xs])
            yield
            for kc in range(KC):
                op(pe, lambda e, kc=kc: e.transpose(out=banks_bf[0][:, kc * P:(kc + 1) * P],
                                                    in_=xs[:, kc * P:(kc + 1) * P], identity=g.identb[:]),
                   reads=[b_xs, g.b_identb], writes=[bbuf[0]], sig=(kc == KC - 1))
            for kc in range(KC):
                sc = pvc('pre_g', L * 8 + kc)
                if kc % 2 == 0:
                    op(dve, lambda e, kc=kc, sc=sc: e.tensor_scalar(out=slot[:, kc, 1:P + 1],
                                                                  in0=banks_bf[0][:, kc * P:(kc + 1) * P],
                                                                  scalar1=sc, scalar2=None, op0=ALU.mult),
                       reads=[bbuf[0], b_pv], writes=[b_slot])
                else:
                    op(act, lambda e, kc=kc, sc=sc: e.activation(out=slot[:, kc, 1:P + 1],
                                                               in_=banks_bf[0][:, kc * P:(kc + 1) * P],
                                                               func=AF.Copy, scale=sc),
                       reads=[bbuf[0], b_pv], writes=[b_slot])
            near, far = (0, P + 1) if d == 0 else (P + 1, 0)
            if first:
                op(pool, lambda e: e.memset(slot[:, :, near:near + 1], 0.0), writes=[b_slot])
            else:
                pslot, b_pslot = prev.slot
                src_own = 1 if d == 0 else P
                src_prev = P if d == 0 else 1
                op(pool, lambda e: e.tensor_copy(out=slot[:, :, near:near + 1], in_=pslot[:, :, src_prev:src_prev + 1]),
                   reads=[b_pslot], writes=[b_slot])
                op(pool, lambda e: e.tensor_copy(out=pslot[:, :, far:far + 1], in_=slot[:, :, src_own:src_own + 1]),
                   reads=[b_slot], writes=[b_pslot])
            if last:
                op(pool, lambda e: e.memset(slot[:, :, far:far + 1], 0.0), writes=[b_slot])
            yield

        def gen_F(cx, d):
            slot, b_slot = cx.slot
            xn = slot[:, :, 1:P + 1]
            op(pool, lambda e: e.tensor_tensor(out=xx[:], in0=slot[:, :, 0:P], in1=slot[:, :, 2:P + 2], op=ALU.add),
               reads=[b_slot], writes=[b_xx])
            op(dve, lambda e: e.scalar_tensor_tensor(out=xx[:], in0=xx[:], scalar=0.5, in1=xn, op0=ALU.mult,
                                                     op1=ALU.subtract), reads=[b_xx, b_slot], writes=[b_xx])
            yield

            def lerp(m, dst, b_dst):
                for kc in range(KC):
                    sc = pvc('mu', m * 8 + kc)
                    op(dve, lambda e, kc=kc, sc=sc: e.scalar_tensor_tensor(
                        out=dst[:, kc, :], in0=xx[:, kc, :], scalar=sc, in1=slot[:, kc, 1:P + 1],
                        op0=ALU.mult, op1=ALU.add), reads=[b_xx, b_slot, b_pv], writes=[b_dst])

            def next_lt():
                r = p_lt[0]
                lt_i[0] += 1
                return r

            lr, b_lr = cx.lr
            lk, b_lk = cx.lk
            vt, b_vt = cx.vt
            lerp(0, lr, b_lr)
            yield
            lerp(1, lk, b_lk)
            yield
            lv, b_lv = next_lt()
            lerp(2, lv, b_lv)
            yield
            for half in range(2):
                for q2 in range(NQ2):
                    w, b_w = g.loadA(L, 0, half * NQ2 + q2)
                    for kc in range(KC):
                        op(pe, lambda e, kc=kc, w=w, q2=q2: e.matmul(banks[1][:, q2 * W2:(q2 + 1) * W2],
                                                                    lhsT=lv[:, kc, :], rhs=w[:, kc, :],
                                                                    start=(kc == 0), stop=(kc == KC - 1)),
                           reads=[b_lv, b_w], writes=[bbuf[1]], sig=(kc == KC - 1 and q2 == NQ2 - 1))
                op(act, lambda e, half=half: e.activation(out=vt[:, half * 512:(half + 1) * 512], in_=banks[1][:, :],
                                                          func=AF.Copy), reads=[bbuf[1]], writes=[b_vt])
                yield
            if d == 1:
                zs, b_zs = cx.zs
                for half in range(2):
                    for q2 in range(NQ2):
                        w, b_w = g.loadA(L, 1, half * NQ2 + q2)
                        for kc in range(KC):
                            op(pe, lambda e, kc=kc, w=w, q2=q2: e.matmul(banks[1][:, q2 * W2:(q2 + 1) * W2],
                                                                        lhsT=slot[:, kc, 1:P + 1], rhs=w[:, kc, :],
                                                                        start=(kc == 0), stop=(kc == KC - 1)),
                               reads=[b_slot, b_w], writes=[bbuf[1]], sig=(kc == KC - 1 and q2 == NQ2 - 1))
                    op(act, lambda e, half=half: e.activation(out=zs[:, half * 512:(half + 1) * 512],
                                                              in_=banks[1][:, :], func=AF.Silu),
                       reads=[bbuf[1]], writes=[b_zs])
                    yield
            th, b_th = cx.th
            lw, b_lw = next_lt()
            lerp(3 + d, lw, b_lw)
            yield
            la, b_la = next_lt()
            lerp(5 + d, la, b_la)
            yield
            for kc in range(KC):
                op(pe, lambda e, kc=kc: e.matmul(banks[1][0:64, 0:P], lhsT=w1b[:, d, kc, :], rhs=lw[:, kc, :],
                                                 start=(kc == 0), stop=(kc == KC - 1)),
                   reads=[b_lw, b_lc], writes=[bbuf[1]], sig=False)
            for kc in range(KC):
                op(pe, lambda e, kc=kc: e.matmul(banks[1][0:64, P:2 * P], lhsT=a1b[:, d, kc, :], rhs=la[:, kc, :],
                                                 start=(kc == 0), stop=(kc == KC - 1)),
                   reads=[b_la, b_lc], writes=[bbuf[1]], sig=(kc == KC - 1))
            op(act, lambda e: e.activation(out=th[:, 0:P], in_=banks[1][0:64, 0:P], func=AF.Tanh),
               reads=[bbuf[1]], writes=[b_th])
            op(act, lambda e: e.activation(out=th[:, P:2 * P], in_=banks[1][0:64, P:2 * P], func=AF.Copy),
               reads=[bbuf[1]], writes=[b_th])
            yield

        def gen_U(cx, d, j, u):
            c = cx.c
            Ba, Bb, Bc = u.banks
            lr, b_lr = cx.lr
            lk, b_lk = cx.lk
            vt, b_vt = cx.vt
            th, b_th = cx.th
            hq = [slice(0, 64), slice(64, 128)]
            mq = g.maskq[d]
            mA = g.maskq[1 - d][:, 0:P]
            for _ in range(STAGGER if (j % 2 == 1) else 0):
                yield
            wr, b_wr = g.loadB(L, 0, j)
            wk, b_wk = g.loadB(L, 1, j)
            for kc in range(KC):
                op(pe, lambda e, kc=kc: e.matmul(banks[Ba][:, 0:P], lhsT=wr[:, kc, :], rhs=lr[:, kc, :],
                                                 start=(kc == 0), stop=(kc == KC - 1)),
                   reads=[b_wr, b_lr], writes=[bbuf[Ba]], sig=False)
            for kc in range(KC):
                op(pe, lambda e, kc=kc: e.matmul(banks[Ba][:, P:2 * P], lhsT=wk[:, kc, :], rhs=lk[:, kc, :],
                                                 start=(kc == 0), stop=(kc == KC - 1)),
                   reads=[b_wk, b_lk], writes=[bbuf[Ba]], sig=False)
            op(pe, lambda e: e.matmul(banks[Ba][:, 2 * P:3 * P], lhsT=w2b[:, d, j * P:(j + 1) * P], rhs=th[:, 0:P],
                                      start=True, stop=True), reads=[b_lc, b_th], writes=[bbuf[Ba]], sig=False)
            op(pe, lambda e: e.matmul(banks[Ba][:, 3 * P:4 * P], lhsT=a2b[:, d, j * P:(j + 1) * P], rhs=th[:, P:2 * P],
                                      start=True, stop=True), reads=[b_lc, b_th], writes=[bbuf[Ba]])
            op(act, lambda e: e.activation(out=u.rk[:], in_=banks[Ba][:, 0:2 * P], func=AF.Copy),
               reads=[bbuf[Ba]], writes=[u.b_rk])
            op(act, lambda e: e.activation(out=u.sg[:], in_=banks[Ba][:, 2 * P:3 * P], func=AF.Sigmoid,
                                           bias=pvc('w0', d * 8 + j)), reads=[bbuf[Ba], b_pv], writes=[u.b_sg])
            op(act, lambda e: e.activation(out=u.sq[:], in_=banks[Ba][:, P:2 * P], func=AF.Square,
                                           scale=pvc('k_k', j)), reads=[bbuf[Ba], b_pv], writes=[u.b_sq])
            op(act, lambda e: e.activation(out=u.al[:], in_=banks[Ba][:, 3 * P:4 * P], func=AF.Sigmoid,
                                           bias=pvc('a0', d * 8 + j)), reads=[bbuf[Ba], b_pv], writes=[u.b_al])
            yield
            rT = u.rk[:, 0:P]
            kT = u.rk[:, P:2 * P]
            op(dve, lambda e: e.tensor_scalar(out=u.kk[:], in0=kT, scalar1=pvc('k_k', j), scalar2=None, op0=ALU.mult),
               reads=[u.b_rk, b_pv], writes=[u.b_kk])
            op(pe, lambda e: e.matmul(banks[Ba][:, 0:P], lhsT=g.bdones[:], rhs=u.sq[:], start=True, stop=True),
               reads=[b_cst, u.b_sq], writes=[bbuf[Ba]])
            op(act, lambda e: e.activation(out=u.rs[:], in_=banks[Ba][:, 0:P], func=AF.Ln),
               reads=[bbuf[Ba]], writes=[u.b_rs])
            op(act, lambda e: e.activation(out=u.rs[:], in_=u.rs[:], func=AF.Exp, scale=-0.5),
               reads=[u.b_rs], writes=[u.b_rs])
            op(pool, lambda e: e.tensor_tensor(out=u.kk[:], in0=u.kk[:], in1=u.rs[:], op=ALU.mult),
               reads=[u.b_kk, u.b_rs], writes=[u.b_kk])
            op(dve, lambda e: e.tensor_tensor_scan(out=u.Ein[:], data0=g.ones_f[:], data1=u.sg[:], initial=0.0,
                                                   op0=ALU.mult, op1=ALU.add),
               reads=[b_cst, u.b_sg], writes=[u.b_Ein])
            tot = u.Ein[:, P - 1:P]
            op(act, lambda e: e.activation(out=u.sc[:, 0:1], in_=tot, func=AF.Exp, scale=-LAM),
               reads=[u.b_Ein], writes=[u.b_sc])
            if d == 0:
                op(pool, lambda e: e.tensor_tensor(out=u.Eex[:], in0=u.Ein[:], in1=u.sg[:], op=ALU.subtract),
                   reads=[u.b_Ein, u.b_sg], writes=[u.b_Eex])
            else:
                op(dve, lambda e: e.tensor_copy(out=u.sc[:, 1:2], in_=tot), reads=[u.b_Ein], writes=[u.b_sc])
                op(dve, lambda e: e.tensor_scalar(out=u.Eex[:], in0=u.Ein[:], scalar1=u.sc[:, 1:2], scalar2=-1.0,
                                                  op0=ALU.subtract, op1=ALU.mult),
                   reads=[u.b_Ein, u.b_sc], writes=[u.b_Eex])
                op(pool, lambda e: e.tensor_tensor(out=u.Ein[:], in0=u.Eex[:], in1=u.sg[:], op=ALU.add),
                   reads=[u.b_Eex, u.b_sg], writes=[u.b_Ein])
            yield
            op(act, lambda e: e.activation(out=u.ein[:], in_=u.Ein[:], func=AF.Exp, scale=-LAM),
               reads=[u.b_Ein], writes=[u.b_ein])
            op(act, lambda e: e.activation(out=u.Eex[:], in_=u.Eex[:], func=AF.Exp, scale=-LAM),
               reads=[u.b_Eex], writes=[u.b_Eex])
            op(act, lambda e: e.activation(out=u.Ein[:], in_=u.Ein[:], func=AF.Exp, scale=LAM),
               reads=[u.b_Ein], writes=[u.b_Ein])
            eng_ = u.Ein
            eex_ = u.Eex
            op(dve, lambda e: e.scalar_tensor_tensor(out=u.arT[:, 0:P], in0=u.kk[:], scalar=-1.0, in1=eex_[:],
                                                     op0=ALU.mult, op1=ALU.mult),
               reads=[u.b_kk, u.b_Eex], writes=[u.b_arT])
            op(pool, lambda e: e.tensor_tensor(out=u.arT[:, P:2 * P], in0=rT, in1=u.ein[:], op=ALU.mult),
               reads=[u.b_rk, u.b_ein], writes=[u.b_arT])
            op(dve, lambda e: e.tensor_scalar(out=u.kd[:], in0=u.al[:], scalar1=pvc('k_a', j),
                                               scalar2=pvc('omk_a', j), op0=ALU.mult, op1=ALU.add),
               reads=[u.b_al, b_pv], writes=[u.b_kd])
            op(pool, lambda e: e.tensor_tensor(out=u.kd[:], in0=u.kd[:], in1=kT, op=ALU.mult),
               reads=[u.b_kd, u.b_rk], writes=[u.b_kd])
            op(dve, lambda e: e.tensor_tensor(out=u.ktT[:], in0=u.kd[:], in1=eng_[:], op=ALU.mult),
               reads=[u.b_kd, u.b_Ein], writes=[u.b_ktT])
            op(pool, lambda e: e.tensor_tensor(out=u.al[:], in0=u.al[:], in1=u.kk[:], op=ALU.mult),
               reads=[u.b_al, u.b_kk], writes=[u.b_al])
            op(dve, lambda e: e.tensor_tensor(out=u.btT[:], in0=u.al[:], in1=eng_[:], op=ALU.mult),
               reads=[u.b_al, u.b_Ein], writes=[u.b_btT])
            op(dve, lambda e: e.scalar_tensor_tensor(out=u.pr[:], in0=rT, scalar=pvc('r_k', j), in1=u.kd[:],
                                                     op0=ALU.mult, op1=ALU.mult),
               reads=[u.b_rk, u.b_kd, b_pv], writes=[u.b_pr])
            yield
            op(pe, lambda e: e.transpose(out=banks_bf[Ba][:, 0:P], in_=u.btT[:], identity=g.identb[:]),
               reads=[u.b_btT, g.b_identb], writes=[bbuf[Ba]], sig=False)
            op(pe, lambda e: e.transpose(out=banks_bf[Ba][:, P:2 * P], in_=u.ktT[:], identity=g.identb[:]),
               reads=[u.b_ktT, g.b_identb], writes=[bbuf[Ba]], sig=False)
            op(pe, lambda e: e.matmul(banks[Ba][:, 2 * P:2 * P + 2], lhsT=u.pr[:], rhs=g.hsel[:], start=True, stop=True),
               reads=[u.b_pr, b_cst], writes=[bbuf[Ba]])
            op(act, lambda e: e.activation(out=u.BK[:], in_=banks_bf[Ba][:, 0:2 * P], func=AF.Copy),
               reads=[bbuf[Ba]], writes=[u.b_BK])
            if d == 0:
                bdst, b_bdst = bonF[:, c, 2 * j:2 * j + 2], b_bonF[c]
            else:
                bt_, b_bdst = cx.bonB
                bdst = bt_[:, 2 * j:2 * j + 2]
            op(act, lambda e: e.activation(out=bdst, in_=banks[Ba][:, 2 * P:2 * P + 2], func=AF.Copy),
               reads=[bbuf[Ba]], writes=[b_bdst])
            yield
            AT = [u.AT0, u.AT1]
            b_AT = [u.b_AT0, u.b_AT1]
            Cc = [[u.C00, u.C01], [u.C10, u.C11]]
            b_C = [[u.b_C00, u.b_C01], [u.b_C10, u.b_C11]]
            hb = [Bb, Bc]
            for h in range(2):
                B_ = hb[h]
                op(pe, lambda e, h=h, B_=B_: e.matmul(banks[B_][:, 0:2 * P], lhsT=u.btT[hq[h], :], rhs=u.arT[hq[h], :],
                                                      start=True, stop=True),
                   reads=[u.b_btT, u.b_arT], writes=[bbuf[B_]], sig=False)
                op(pe, lambda e, h=h, B_=B_: e.matmul(banks[B_][:, 2 * P:4 * P], lhsT=u.ktT[hq[h], :],
                                                      rhs=u.arT[hq[h], :], start=True, stop=True),
                   reads=[u.b_ktT, u.b_arT], writes=[bbuf[B_]])
                op(dve, lambda e, h=h, B_=B_: e.tensor_tensor(out=AT[h][:], in0=banks[B_][:, :], in1=mq[:],
                                                              op=ALU.mult),
                   reads=[bbuf[B_], b_cst], writes=[b_AT[h]])
                op(pe, lambda e, h=h, B_=B_: e.matmul(banks[B_][:, 0:P], lhsT=u.arT[hq[h], 0:P], rhs=u.btT[hq[h], :],
                                                      start=True, stop=True),
                   reads=[u.b_btT, u.b_arT], writes=[bbuf[B_]])
                op(dve, lambda e, h=h, B_=B_: e.tensor_tensor(out=Cc[h][0][:, 0:P], in0=banks[B_][:, 0:P], in1=mA,
                                                              op=ALU.mult),
                   reads=[bbuf[B_], b_cst], writes=[b_C[h][0]])
                op(pool, lambda e, h=h: e.tensor_tensor(out=Cc[h][1][:, 2 * P:3 * P], in0=AT[h][:, 0:P],
                                                        in1=g.identb[:], op=ALU.add),
                   reads=[b_AT[h], g.b_identb], writes=[b_C[h][1]])
            yield
            for h in range(2):
                B_ = hb[h]
                A0 = Cc[h][0][:, 0:P]
                B0 = AT[h][:, 0:P]
                op(pe, lambda e, B_=B_, A0=A0, B0=B0: e.matmul(banks[B_][:, 0:P], lhsT=B0, rhs=A0, start=True, stop=True),
                   reads=[b_AT[h], b_C[h][0]], writes=[bbuf[B_]], sig=False)
                op(pe, lambda e, B_=B_, A0=A0, B0=B0: e.matmul(banks[B_][:, P:2 * P], lhsT=A0, rhs=B0, start=True,
                                                               stop=True),
                   reads=[b_AT[h], b_C[h][0]], writes=[bbuf[B_]])
                ev = dve if h == 0 else act
                if ev is dve:
                    op(dve, lambda e, h=h, B_=B_: e.tensor_copy(out=Cc[h][1][:, 0:2 * P], in_=banks[B_][:, 0:2 * P]),
                       reads=[bbuf[B_]], writes=[b_C[h][1]])
                else:
                    op(act, lambda e, h=h, B_=B_: e.activation(out=Cc[h][1][:, 0:2 * P], in_=banks[B_][:, 0:2 * P],
                                                               func=AF.Copy),
                       reads=[bbuf[B_]], writes=[b_C[h][1]])
            yield
            for lev in range(1, 7):
                src_i = lev % 2
                dst_i = 1 - src_i
                for h in range(2):
                    B_ = hb[h]
                    S = Cc[h][src_i]
                    Dd = Cc[h][dst_i]
                    bS, bD = b_C[h][src_i], b_C[h][dst_i]
                    Ak, Bk, Mk = S[:, 0:P], S[:, P:2 * P], S[:, 2 * P:3 * P]
                    lo = 0
                    if lev <= 5:
                        op(pe, lambda e, B_=B_, Ak=Ak, Bk=Bk: e.matmul(banks[B_][:, 0:P], lhsT=Bk, rhs=Ak, start=True,
                                                                       stop=True),
                           reads=[bS], writes=[bbuf[B_]], sig=False)
                    else:
                        lo = 2 * P
                    r0 = P if lev <= 4 else 2 * P
                    op(pe, lambda e, B_=B_, Ak=Ak, S=S, r0=r0: e.matmul(banks[B_][:, r0:3 * P], lhsT=Ak, rhs=S[:, r0:3 * P],
                                                                        start=True, stop=True),
                       reads=[bS], writes=[bbuf[B_]])
                    if lev <= 5:
                        hi_ = 2 * P if lev <= 4 else P
                        if h == 1:
                            op(act, lambda e, B_=B_, Dd=Dd, hi_=hi_: e.activation(out=Dd[:, 0:hi_],
                                                                                  in_=banks[B_][:, 0:hi_], func=AF.Copy),
                               reads=[bbuf[B_]], writes=[bD])
                        else:
                            op(dve, lambda e, B_=B_, Dd=Dd, hi_=hi_: e.tensor_copy(out=Dd[:, 0:hi_],
                                                                                   in_=banks[B_][:, 0:hi_]),
                               reads=[bbuf[B_]], writes=[bD])
                    op(dve, lambda e, B_=B_, Dd=Dd, Mk=Mk: e.tensor_tensor(out=Dd[:, 2 * P:3 * P],
                                                                           in0=banks[B_][:, 2 * P:3 * P], in1=Mk,
                                                                           op=ALU.add),
                       reads=[bbuf[B_], bS], writes=[bD])
                yield
            Mfin = [Cc[h][1][:, 2 * P:3 * P] for h in range(2)]
            b_Mfin = [b_C[h][1] for h in range(2)]
            b_Hj = g_bH[j]
            for h in range(2):
                op(pe, lambda e, h=h: e.matmul(banks[Ba][:, h * 64:(h + 1) * 64], lhsT=u.arT[hq[h], 0:P],
                                               rhs=Hb[hq[h], j, :], start=True, stop=False),
                   reads=[u.b_arT, b_Hj], writes=[bbuf[Ba]], sig=False)
                op(pe, lambda e, h=h: e.matmul(banks[Ba][:, h * 64:(h + 1) * 64], lhsT=AT[h][:, 2 * P:3 * P],
                                               rhs=vt[:, (2 * j + h) * 64:(2 * j + h + 1) * 64], start=False, stop=True),
                   reads=[b_AT[h], b_vt], writes=[bbuf[Ba]], sig=(h == 1))
            op(act, lambda e: e.activation(out=u.Xp[:], in_=banks[Ba][:, 0:P], func=AF.Copy),
               reads=[bbuf[Ba]], writes=[u.b_Xp])
            for h in range(2):
                op(pe, lambda e, h=h: e.matmul(banks[Ba][:, P + h * 64:P + (h + 1) * 64], lhsT=Mfin[h],
                                               rhs=u.Xp[:, h * 64:(h + 1) * 64], start=True, stop=True),
                   reads=[b_Mfin[h], u.b_Xp], writes=[bbuf[Ba]], sig=(h == 1))
            op(dve, lambda e: e.tensor_copy(out=u.Up[:], in_=banks[Ba][:, P:2 * P]), reads=[bbuf[Ba]], writes=[u.b_Up])
            op(dve, lambda e: e.tensor_scalar(out=u.Hs[:], in0=Hf[:, j, :], scalar1=u.sc[:, 0:1], scalar2=None,
                                               op0=ALU.mult), reads=[b_Hj, u.b_sc], writes=[u.b_Hs])
            yield
            for h in range(2):
                yo = 2 * P + h * 64
                op(pe, lambda e, h=h, yo=yo: e.matmul(banks[Ba][:, yo:yo + 64], lhsT=u.arT[hq[h], P:2 * P],
                                                      rhs=Hb[hq[h], j, :], start=True, stop=False),
                   reads=[u.b_arT, b_Hj], writes=[bbuf[Ba]], sig=False)
                op(pe, lambda e, h=h, yo=yo: e.matmul(banks[Ba][:, yo:yo + 64], lhsT=AT[h][:, P:2 * P],
                                                      rhs=u.Up[:, h * 64:(h + 1) * 64], start=False, stop=False),
                   reads=[b_AT[h], u.b_Up], writes=[bbuf[Ba]], sig=False)
                op(pe, lambda e, h=h, yo=yo: e.matmul(banks[Ba][:, yo:yo + 64], lhsT=AT[h][:, 3 * P:4 * P],
                                                      rhs=vt[:, (2 * j + h) * 64:(2 * j + h + 1) * 64],
                                                      start=False, stop=True),
                   reads=[b_AT[h], b_vt], writes=[bbuf[Ba]], sig=False)
            op(pe, lambda e: e.matmul(banks[Ba][:, 3 * P:4 * P], lhsT=u.BK[:, 0:P], rhs=u.Up[:], start=True, stop=False),
               reads=[u.b_BK, u.b_Up], writes=[bbuf[Ba]], sig=False)
            op(pe, lambda e: e.matmul(banks[Ba][:, 3 * P:4 * P], lhsT=u.BK[:, P:2 * P], rhs=vt[:, j * P:(j + 1) * P],
                                      start=False, stop=True),
               reads=[u.b_BK, b_vt], writes=[bbuf[Ba]])
            if d == 0:
                ydst = x1[:, c, :].bitcast(BF16)[:, j * P:(j + 1) * P]
                b_yd = bx1[c]
            else:
                yb_, b_yd = cx.yb
                ydst = yb_[:, j * P:(j + 1) * P]
            op(act, lambda e: e.activation(out=ydst, in_=banks[Ba][:, 2 * P:3 * P], func=AF.Copy),
               reads=[bbuf[Ba]], writes=[b_yd])
            for h in range(2):
                op(dve, lambda e, h=h: e.scalar_tensor_tensor(out=Hf[hq[h], j, :],
                                                              in0=banks[Ba][hq[h], 3 * P + h * 64:3 * P + (h + 1) * 64],
                                                              scalar=u.sc[hq[h], 0:1], in1=u.Hs[hq[h], :],
                                                              op0=ALU.mult, op1=ALU.add),
                   reads=[bbuf[Ba], u.b_sc, u.b_Hs], writes=[b_Hj])
            op(pool, lambda e: e.tensor_copy(out=Hb[:, j, :], in_=Hf[:, j, :]), reads=[b_Hj], writes=[b_Hj])
            yield

        g_bH = b_H

        def gen_B(cx):
            c = cx.c
            yb_, b_yb = cx.yb
            zs, b_zs = cx.zs
            vt, b_vt = cx.vt
            bonB, b_bonB = cx.bonB
            xin, b_xin = cx.xres
            T_.dma(sp, xin[:], g.x_in[si, c * P:(c + 1) * P, :], writes=[b_xin])
            yf = x1[:, c, :].bitcast(BF16)[:, 0:D]
            op(dve, lambda e: e.tensor_tensor(out=yb_[:], in0=yb_[:], in1=yf, op=ALU.add),
               reads=[b_yb, bx1[c]], writes=[b_yb])
            y3 = yb_[:].rearrange("p (h n) -> p h n", h=16)
            op(dve, lambda e: e.tensor_reduce(out=gst[:, 0, :], in_=y3, axis=AX.X, op=ALU.add),
               reads=[b_yb], writes=[b_gst])
            op(act, lambda e: e.activation(out=tmpA[:], in_=yb_[:], func=AF.Square), reads=[b_yb], writes=[b_tmpA])
            op(dve, lambda e: e.tensor_reduce(out=gst[:, 1, :], in_=tmpA[:].rearrange("p (h n) -> p h n", h=16),
                                              axis=AX.X, op=ALU.add), reads=[b_tmpA], writes=[b_gst])
            yield
            op(dve, lambda e: e.tensor_scalar(out=gst[:, 2, :], in0=gst[:, 0, :], scalar1=1.0 / 64, scalar2=None,
                                              op0=ALU.mult), reads=[b_gst], writes=[b_gst])
            op(dve, lambda e: e.tensor_tensor(out=gst[:, 3, :], in0=gst[:, 2, :], in1=gst[:, 2, :], op=ALU.mult),
               reads=[b_gst], writes=[b_gst])
            op(dve, lambda e: e.scalar_tensor_tensor(out=gst[:, 3, :], in0=gst[:, 1, :], scalar=1.0 / 64,
                                                     in1=gst[:, 3, :], op0=ALU.mult, op1=ALU.subtract),
               reads=[b_gst], writes=[b_gst])
            op(dve, lambda e: e.tensor_scalar(out=gst[:, 3, :], in0=gst[:, 3, :], scalar1=LNX_EPS, scalar2=None,
                                              op0=ALU.add), reads=[b_gst], writes=[b_gst])
            op(act, lambda e: e.activation(out=gst[:, 3, :], in_=gst[:, 3, :], func=AF.Sqrt), reads=[b_gst],
               writes=[b_gst])
            op(dve, lambda e: e.reciprocal(out=gst[:, 4, :], in_=gst[:, 3, :]), reads=[b_gst], writes=[b_gst])
            op(dve, lambda e: e.tensor_tensor(out=gst[:, 5, :], in0=bonF[:, c, :], in1=bonB[:], op=ALU.add),
               reads=[b_bonF[c], b_bonB], writes=[b_gst])
            mean_b = gst[:, 2, :].unsqueeze(2).broadcast_to([P, 16, 64])
            rstd_b = gst[:, 4, :].unsqueeze(2).broadcast_to([P, 16, 64])
            bon_b = gst[:, 5, :].unsqueeze(2).broadcast_to([P, 16, 64])
            op(dve, lambda e: e.tensor_tensor(out=y3, in0=y3, in1=mean_b, op=ALU.subtract),
               reads=[b_yb, b_gst], writes=[b_yb])
            op(pool, lambda e: e.tensor_tensor(out=y3, in0=y3, in1=rstd_b, op=ALU.mult),
               reads=[b_yb, b_gst], writes=[b_yb])
            yield
            op(dve, lambda e: e.tensor_tensor(out=yb_[:], in0=yb_[:], in1=lnxb[:, 0, :], op=ALU.mult),
               reads=[b_yb, b_lc], writes=[b_yb])
            op(pool, lambda e: e.tensor_tensor(out=yb_[:], in0=yb_[:], in1=lnxb[:, 1, :], op=ALU.add),
               reads=[b_yb, b_lc], writes=[b_yb])
            t3 = tmpA[:].rearrange("p (h n) -> p h n", h=16)
            op(dve, lambda e: e.tensor_tensor(out=t3, in0=vt[:].rearrange("p (h n) -> p h n", h=16), in1=bon_b,
                                              op=ALU.mult), reads=[b_vt, b_gst], writes=[b_tmpA])
            op(pool, lambda e: e.tensor_tensor(out=yb_[:], in0=yb_[:], in1=tmpA[:], op=ALU.add),
               reads=[b_yb, b_tmpA], writes=[b_yb])
            op(dve, lambda e: e.tensor_tensor(out=yg[:], in0=yb_[:], in1=zs[:], op=ALU.mult),
               reads=[b_yb, b_zs], writes=[b_yg])
            yield
            for kc in range(KC):
                op(pe, lambda e, kc=kc: e.transpose(out=banks_bf[0][:, kc * P:(kc + 1) * P],
                                                    in_=yg[:, kc * P:(kc + 1) * P], identity=g.identb[:]),
                   reads=[b_yg, g.b_identb], writes=[bbuf[0]], sig=(kc == KC - 1))
            op(act, lambda e: e.activation(out=ygT[:].rearrange("p k n -> p (k n)"), in_=banks_bf[0][:, :],
                                           func=AF.Copy), reads=[bbuf[0]], writes=[b_ygT])
            yield
            for half in range(2):
                for q2 in range(NQ2):
                    w, b_w = g.loadA(L, 2, half * NQ2 + q2)
                    for kc in range(KC):
                        op(pe, lambda e, kc=kc, w=w, q2=q2: e.matmul(banks[0][:, q2 * W2:(q2 + 1) * W2],
                                                                    lhsT=ygT[:, kc, :], rhs=w[:, kc, :],
                                                                    start=(kc == 0), stop=(kc == KC - 1)),
                           reads=[b_ygT, b_w], writes=[bbuf[0]], sig=(kc == KC - 1 and q2 == NQ2 - 1))
                op(act, lambda e, half=half: e.activation(out=tmpB[:, half * 512:(half + 1) * 512], in_=banks[0][:, :],
                                                          func=AF.Copy), reads=[bbuf[0]], writes=[b_tmpB])
                yield
            op(act, lambda e: e.activation(out=tmpA[:], in_=tmpB[:], func=AF.Square, accum_out=stat[:, 4:5]),
               reads=[b_tmpB], writes=[b_tmpA, b_stat])
            op(dve, lambda e: e.tensor_scalar(out=stat[:, 5:6], in0=stat[:, 4:5], scalar1=1.0 / D, scalar2=RMS_EPS,
                                              op0=ALU.mult, op1=ALU.add), reads=[b_stat], writes=[b_stat])
            op(act, lambda e: e.activation(out=stat[:, 6:7], in_=stat[:, 5:6], func=AF.Sqrt), reads=[b_stat],
               writes=[b_stat])
            op(dve, lambda e: e.reciprocal(out=stat[:, 7:8], in_=stat[:, 6:7]), reads=[b_stat], writes=[b_stat])
            op(dve, lambda e: e.scalar_tensor_tensor(out=tmpB[:], in0=tmpB[:], scalar=stat[:, 7:8], in1=postg[:],
                                                     op0=ALU.mult, op1=ALU.mult),
               reads=[b_tmpB, b_stat, b_lc], writes=[b_tmpB])
            op(pool, lambda e: e.tensor_tensor(out=x1[:, c, :], in0=tmpB[:], in1=xin[:], op=ALU.add),
               reads=[b_tmpB, b_xin], writes=[bx1[c]])
            yield

        def gen_Z():
            op(dve, lambda e: e.memset(Hf[:], 0.0), writes=b_H)
            op(pool, lambda e: e.memset(Hb[:], 0.0), writes=b_H)
            yield

        tasks = []
        lastB = None
        lastF = None
        for d in range(2):
            order = list(range(NCH)) if d == 0 else list(range(NCH - 1, -1, -1))
            tz = Task("Z%d" % d)
            tz.gen = gen_Z()
            tz.deps += [t for t in tasks if t.name.startswith("U")]
            tasks.append(tz)
            cxs = [None] * NCH

            def make_N(i):
                cx = NS()
                cx.c = order[i]
                tn = Task("N%d_%d" % (d, cx.c))
                _, cx.xin = p_xin.acquire(tn)
                cx.ks, cx.slot = p_slot.acquire(tn)
                prev = cxs[i - 1] if i > 0 else None
                if prev is not None:
                    p_slot.share(tn, prev.ks)
                    tn.deps.append(prev.tn)
                tn.gen = gen_N(cx, d, i == 0, i == NCH - 1, prev)
                cx.tn = tn
                cxs[i] = cx
                tasks.append(tn)

            make_N(0)
            if NCH > 1:
                make_N(1)
            for i in range(NCH):
                cx = cxs[i]
                c = cx.c
                tf = Task("F%d_%d" % (d, c))
                tf.deps.append(cx.tn)
                if i + 1 < NCH:
                    tf.deps.append(cxs[i + 1].tn)
                if lastF is not None:
                    tf.deps.append(lastF)
                lastF = tf
                p_slot.share(tf, cx.ks)
                klr, cx.lr = p_lr.acquire(tf)
                klk, cx.lk = p_lk.acquire(tf)
                kvt, cx.vt = p_vt.acquire(tf)
                kth, cx.th = p_th.acquire(tf)
                if d == 1:
                    kzs, cx.zs = p_zs.acquire(tf)
                tf.gen = gen_F(cx, d)
                tasks.append(tf)
                tb = None
                if d == 1:
                    tb = Task("B%d" % c)
                    _, cx.yb = p_yb.acquire(tb)
                    _, cx.bonB = p_bonB.acquire(tb)
                tus = []
                for j in range(KC):
                    tu = Task("U%d_%d_%d" % (d, c, j))
                    tu.deps += [tf, tz]
                    if d == 1:
                        tu.deps += list(tb.deps)
                    _, uset = p_uset.acquire(tu)
                    p_lr.share(tu, klr)
                    p_lk.share(tu, klk)
                    p_vt.share(tu, kvt)
                    p_th.share(tu, kth)
                    tu.gen = gen_U(cx, d, j, uset)
                    if i > 0:
                        tu.deps.append(cxs[i - 1].tus[j])
                    tus.append(tu)
                    tasks.append(tu)
                cx.tus = tus
                if d == 1:
                    tb.deps += tus + [tf]
                    if lastB is not None:
                        tb.deps.append(lastB)
                    _, cx.xres = p_xin.acquire(tb)
                    p_vt.share(tb, kvt)
                    p_zs.share(tb, kzs)
                    tb.gen = gen_B(cx)
                    tasks.append(tb)
                    lastB = tb
                if i + 2 < NCH:
                    make_N(i + 2)
        import os
        allow = os.environ.get("L0_TASKS", "ZNFUB")
        tasks = [t for t in tasks if t.name[0] in allow]
        for t in tasks:
            t.deps = [d_ for d_ in t.deps if d_.name[0] in allow]
        run_tasks(tasks, window=WINDOW)


GRID_W = 64
N_ROWS = T_SEQ // GRID_W
NEG = -30000.0


def _rs(r):
    return min(max(r - 4, 0), N_ROWS - 8)


def l1_geometry():
    geo = []
    pats = []
    for i in range(N_ROWS // 2):
        rows = [2 * i, 2 * i + 1]
        lo = min(_rs(r) for r in rows)
        hi = max(_rs(r) + 7 for r in rows)
        ent = []
        for kt in range(lo // 2, hi // 2 + 1):
            pat = tuple(tuple(1 if _rs(2 * i + rl) <= 2 * kt + krl < _rs(2 * i + rl) + 8 else 0 for krl in range(2))
                        for rl in range(2))
            if pat not in pats:
                pats.append(pat)
            ent.append((kt, (2 * (kt - i) + 6) // 2, pats.index(pat)))
        geo.append(ent)
    return geo, pats


def l1_prologue(g):
    nc, T_, op, sb = g.nc, g.T_, g.op, g.sb
    pe, act, dve, pool, sp = g.pe, g.act, g.dve, g.pool, g.sp
    geo, pats = l1_geometry()
    g.l1_geo, g.l1_pats = geo, pats
    NMK = len(pats)
    g.padD = nc.dram_tensor("padD", [240, P], F32, kind="Internal").ap()
    g.rbD = nc.dram_tensor("rbD", [P, 16 * 7 * P], BF16, kind="Internal").ap()
    g.mkD = nc.dram_tensor("mkD", [P, (NMK + 1) * P], BF16, kind="Internal").ap()
    x1 = g.x1
    with ExitStack() as p1:
        b_t = Buf("l1pro")
        padt = sb(p1, "padt", [120, 2, P], F32)
        op(dve, lambda e: e.memset(padt[:], 0.0), writes=[b_t])
        rp = g.wd['na_rpb'][0].rearrange("h r m -> (h r) m")
        for gi in range(2):
            T_.dma(sp, padt[:, gi, 48:79], rp[gi * 120:(gi + 1) * 120, :], writes=[b_t])
        for gi in range(2):
            T_.dma(sp, g.padD[gi * 120:(gi + 1) * 120, :], padt[:, gi, :], reads=[b_t])
        T_.barrier()
        T_.finish()
        Hs = x1[:].rearrange("p c d -> p (c d)")[:, 0:240 * 64].rearrange("p (b k) -> p b k", k=64)
        b_hs = Buf("Hs")
        for h in range(16):
            for r2 in range(2):
                src = bass.AP(tensor=g.padD.tensor, offset=h * 15 * P, ap=[[1, 64], [P, 15], [1, 64]])
                T_.dma(sp, Hs[64 * r2:64 * r2 + 64, h * 15:(h + 1) * 15, :], src, writes=[b_hs])
        RBs = sb(p1, "RBs", [P, 16, 7, P], BF16)
        b_rb = Buf("RBs")
        op(pool, lambda e: e.memset(RBs[:].rearrange("p h d k -> p (h d k)"), 0.0), writes=[b_rb])
        engs = [dve, pool, act]
        n = 0
        for h in range(16):
            for rl in range(2):
                for krl in range(2):
                    dis = [di for di in range(7) if 0 <= 2 * di + 1 + krl - rl <= 14]
                    d0, nd = dis[0], len(dis)
                    ri0 = 2 * d0 + 1 + krl - rl
                    srcv = Hs[64 * rl:64 * rl + 64, h * 15 + ri0:h * 15 + ri0 + 2 * (nd - 1) + 1:2, :]
                    dstv = RBs[64 * rl:64 * rl + 64, h, d0:d0 + nd, 64 * krl:64 * krl + 64]
                    e_ = engs[n % 3]
                    n += 1
                    if e_ is act:
                        op(act, lambda e, s=srcv, d=dstv: e.activation(out=d, in_=s, func=AF.Copy),
                           reads=[b_hs], writes=[b_rb])
                    else:
                        op(e_, lambda e, s=srcv, d=dstv: e.tensor_copy(out=d, in_=s), reads=[b_hs], writes=[b_rb])
        T_.dma(sp, g.rbD[:, :], RBs[:].rearrange("p h d k -> p (h d k)"), reads=[b_rb])
        ior = sb(p1, "ior1", [P, P], F32)
        ioc = sb(p1, "ioc1", [P, P], F32)
        t1 = sb(p1, "mt1", [P, P], F32)
        t2 = sb(p1, "mt2", [P, P], F32)
        cm = sb(p1, "cm", [P, P], BF16)
        neg = sb(p1, "negt", [P, P], BF16)
        MKs = sb(p1, "MKs", [P, NMK + 1, P], BF16)
        b_m = Buf("mk")
        op(pool, lambda e: e.iota(ior[:], pattern=[[0, P]], base=0, channel_multiplier=1,
                                  allow_small_or_imprecise_dtypes=True), writes=[b_m])
        op(pool, lambda e: e.iota(ioc[:], pattern=[[1, P]], base=0, channel_multiplier=0,
                                  allow_small_or_imprecise_dtypes=True), writes=[b_m])
        op(dve, lambda e: e.tensor_tensor(out=t1[:], in0=ior[:], in1=ioc[:], op=ALU.add), reads=[b_m], writes=[b_m])
        op(dve, lambda e: e.tensor_scalar(out=t2[:], in0=t1[:], scalar1=63.0, scalar2=None, op0=ALU.is_equal),
           reads=[b_m], writes=[b_m])
        op(dve, lambda e: e.tensor_scalar(out=t1[:], in0=t1[:], scalar1=191.0, scalar2=None, op0=ALU.is_equal),
           reads=[b_m], writes=[b_m])
        op(dve, lambda e: e.tensor_tensor(out=g.Jrev[:], in0=t1[:], in1=t2[:], op=ALU.add), reads=[b_m],
           writes=[g.b_cst])
        op(dve, lambda e: e.tensor_scalar(out=t1[:], in0=ior[:], scalar1=63.5, scalar2=-64.0, op0=ALU.is_gt,
                                          op1=ALU.mult), reads=[b_m], writes=[b_m])
        op(dve, lambda e: e.tensor_tensor(out=t1[:], in0=t1[:], in1=ior[:], op=ALU.add), reads=[b_m], writes=[b_m])
        op(dve, lambda e: e.tensor_scalar(out=t1[:], in0=t1[:], scalar1=-1.0, scalar2=55.0, op0=ALU.mult,
                                          op1=ALU.add), reads=[b_m], writes=[b_m])
        op(dve, lambda e: e.tensor_scalar(out=t1[:], in0=t1[:], scalar1=0.0, scalar2=48.0, op0=ALU.max, op1=ALU.min),
           reads=[b_m], writes=[b_m])
        op(dve, lambda e: e.tensor_scalar(out=t2[:], in0=ioc[:], scalar1=63.5, scalar2=-64.0, op0=ALU.is_gt,
                                          op1=ALU.mult), reads=[b_m], writes=[b_m])
        op(dve, lambda e: e.tensor_tensor(out=t2[:], in0=t2[:], in1=ioc[:], op=ALU.add), reads=[b_m], writes=[b_m])
        op(dve, lambda e: e.tensor_tensor(out=t2[:], in0=t2[:], in1=t1[:], op=ALU.subtract), reads=[b_m],
           writes=[b_m])
        op(dve, lambda e: e.tensor_scalar(out=t1[:], in0=t2[:], scalar1=-0.5, scalar2=None, op0=ALU.is_gt),
           reads=[b_m], writes=[b_m])
        op(dve, lambda e: e.tensor_scalar(out=t2[:], in0=t2[:], scalar1=15.5, scalar2=None, op0=ALU.is_lt),
           reads=[b_m], writes=[b_m])
        op(dve, lambda e: e.tensor_tensor(out=t1[:], in0=t1[:], in1=t2[:], op=ALU.mult), reads=[b_m], writes=[b_m])
        op(dve, lambda e: e.tensor_scalar(out=cm[:], in0=t1[:], scalar1=-1.0, scalar2=-NEG, op0=ALU.add, op1=ALU.mult),
           reads=[b_m], writes=[b_m])
        op(dve, lambda e: e.memset(neg[:], NEG), writes=[b_m])
        for pi, pat in enumerate(pats):
            for rl in range(2):
                for krl in range(2):
                    src_t = cm if pat[rl][krl] else neg
                    op(pool, lambda e, pi=pi, rl=rl, krl=krl, s=src_t: e.tensor_copy(
                        out=MKs[64 * rl:64 * rl + 64, pi, 64 * krl:64 * krl + 64],
                        in_=s[64 * rl:64 * rl + 64, 64 * krl:64 * krl + 64]), reads=[b_m], writes=[b_m])
        op(pool, lambda e: e.tensor_copy(out=MKs[:, NMK, :], in_=neg[:]), reads=[b_m], writes=[b_m])
        T_.dma(sp, g.mkD[:, :], MKs[:].rearrange("p n k -> p (n k)"), reads=[b_m])
        T_.barrier()
        T_.finish()


def layer1(g, si):
    nc, T_, op = g.nc, g.T_, g.op
    pe, act, dve, pool, sp = g.pe, g.act, g.dve, g.pool, g.sp
    banks, bbuf, banks_bf = g.banks, g.bbuf, g.banks_bf
    NCH, x1, bx1, pv, b_pv, pvc = g.NCH, g.x1, g.bx1, g.pv, g.b_pv, g.pvc
    wd, sb = g.wd, g.sb
    b_cst = g.b_cst
    L = 1
    geo, pats = g.l1_geo, g.l1_pats
    NMK = len(pats)
    hq = [slice(0, 64), slice(64, 128)]

    with ExitStack() as l1:
        postg = sb(l1, "postg1", [P, D], F32)
        RB = sb(l1, "RB", [P, 16, 7, P], BF16)
        MK = sb(l1, "MK", [P, NMK + 1, P], BF16)
        b_lc = Buf("l1consts")
        T_.dma(sp, postg[:], wd['post_norm_g'][L].partition_broadcast(P), writes=[b_lc])
        T_.dma(sp, RB[:].rearrange("p h d k -> p (h d k)"), g.rbD[:, :], writes=[b_lc])
        T_.dma(sp, MK[:].rearrange("p n k -> p (n k)"), g.mkD[:, :], writes=[b_lc])

        def mk(nm, sh, dt, n):
            return [(sb(l1, "%s%d" % (nm, i), sh, dt), Buf("%s%d" % (nm, i))) for i in range(n)]

        xs, b_xs = mk("xs1", [P, D], BF16, 1)[0]
        stat = sb(l1, "stat1", [P, 8], F32)
        b_stat = Buf("stat1")
        p_xn = RPool(mk("xnT1", [P, KC, P], BF16, 4))
        p_kT = RPool(mk("kT", [P, KC, P], BF16, 7))
        p_va = RPool(mk("Vaug", [P, 16, 65], BF16, 7))
        zs, b_zs = mk("zs1", [P, D], BF16, 1)[0]
        og, b_og = mk("og", [P, D], F32, 1)[0]
        yg, b_yg = mk("yg1", [P, D], BF16, 1)[0]
        ygT, b_ygT = mk("ygT1", [P, KC, P], BF16, 1)[0]
        tmpA, b_tmpA = mk("tmpA1", [P, D], F32, 1)[0]
        tmpB, b_tmpB = mk("tmpB1", [P, D], F32, 1)[0]

        def mkset(i):
            u = NS()
            def t(nm, sh, dt):
                setattr(u, nm, sb(l1, "a%d_%s" % (i, nm), sh, dt))
                setattr(u, "b_" + nm, Buf("a%d_%s" % (i, nm)))
            t("qT", [P, P], BF16)
            t("PT", [P, 5, P], BF16)
            t("rc", [P, 2], F32)
            u.banks = (2 + 3 * i, 3 + 3 * i, 4 + 3 * i)
            return u

        p_uset = RPool([mkset(0), mkset(1)])
        for (va, b_va) in p_va.items:
            op(pool, lambda e, va=va: e.memset(va[:, :, 64:65], 1.0), writes=[b_va])

        def gen_KV(cx):
            t = cx.t
            xn, b_xn = cx.xn
            kT, b_kT = cx.kT
            va, b_va = cx.va
            xsrc = x1[:, t, :]
            op(act, lambda e: e.activation(out=xs[:], in_=xsrc, func=AF.Square, accum_out=stat[:, 0:1]),
               reads=[bx1[t]], writes=[b_xs, b_stat])
            op(dve, lambda e: e.tensor_scalar(out=stat[:, 1:2], in0=stat[:, 0:1], scalar1=1.0 / D, scalar2=RMS_EPS,
                                              op0=ALU.mult, op1=ALU.add), reads=[b_stat], writes=[b_stat])
            op(act, lambda e: e.activation(out=stat[:, 2:3], in_=stat[:, 1:2], func=AF.Sqrt), reads=[b_stat],
               writes=[b_stat])
            op(dve, lambda e: e.reciprocal(out=stat[:, 3:4], in_=stat[:, 2:3]), reads=[b_stat], writes=[b_stat])
            op(act, lambda e: e.activation(out=xs[:], in_=xsrc, func=AF.Copy, scale=stat[:, 3:4]),
               reads=[bx1[t], b_stat], writes=[b_xs])
            yield
            for kc in range(KC):
                op(pe, lambda e, kc=kc: e.transpose(out=banks_bf[0][:, kc * P:(kc + 1) * P],
                                                    in_=xs[:, kc * P:(kc + 1) * P], identity=g.identb[:]),
                   reads=[b_xs, g.b_identb], writes=[bbuf[0]], sig=(kc == KC - 1))
            for kc in range(KC):
                sc = pvc('pre_g', L * 8 + kc)
                if kc % 2 == 0:
                    op(dve, lambda e, kc=kc, sc=sc: e.tensor_scalar(out=xn[:, kc, :],
                                                                  in0=banks_bf[0][:, kc * P:(kc + 1) * P],
                                                                  scalar1=sc, scalar2=None, op0=ALU.mult),
                       reads=[bbuf[0], b_pv], writes=[b_xn])
                else:
                    op(act, lambda e, kc=kc, sc=sc: e.activation(out=xn[:, kc, :],
                                                               in_=banks_bf[0][:, kc * P:(kc + 1) * P],
                                                               func=AF.Copy, scale=sc),
                       reads=[bbuf[0], b_pv], writes=[b_xn])
            yield
            for j0 in range(0, KC, 4):
                for j in range(j0, j0 + 4):
                    w, b_w = g.loadB(L, 1, j)
                    for kc in range(KC):
                        op(pe, lambda e, kc=kc, j=j, w=w: e.matmul(banks[1][:, (j - j0) * P:(j - j0 + 1) * P],
                                                                  lhsT=w[:, kc, :], rhs=xn[:, kc, :],
                                                                  start=(kc == 0), stop=(kc == KC - 1)),
                           reads=[b_w, b_xn], writes=[bbuf[1]], sig=(kc == KC - 1 and j == j0 + 3))
                for j in range(j0, j0 + 4):
                    op(act, lambda e, j=j: e.activation(out=kT[:, j, :], in_=banks[1][:, (j - j0) * P:(j - j0 + 1) * P],
                                                        func=AF.Identity, bias=pvc('bk', j)),
                       reads=[bbuf[1], b_pv], writes=[b_kT])
                yield
            for half in range(2):
                for q2 in range(NQ2):
                    w, b_w = g.loadA(L, 0, half * NQ2 + q2)
                    cs = slice(q2 * W2, (q2 + 1) * W2)
                    for kc in range(KC):
                        op(pe, lambda e, kc=kc, w=w, cs=cs: e.matmul(banks[1][:, cs], lhsT=xn[:, kc, :], rhs=w[:, kc, :],
                                                                    start=(kc == 0), stop=False),
                           reads=[b_xn, b_w], writes=[bbuf[1]], sig=False)
                    bo = 2 * D + half * 512 + q2 * W2
                    op(pe, lambda e, bo=bo, cs=cs: e.matmul(banks[1][:, cs], lhsT=g.onesrow[:],
                                                           rhs=g.brow_hi[0:1, bo:bo + W2], start=False, stop=False),
                       reads=[b_cst], writes=[bbuf[1]], sig=False)
                    op(pe, lambda e, bo=bo, cs=cs: e.matmul(banks[1][:, cs], lhsT=g.onesrow[:],
                                                           rhs=g.brow_lo[0:1, bo:bo + W2], start=False, stop=True),
                       reads=[b_cst], writes=[bbuf[1]], sig=(q2 == NQ2 - 1))
                op(act, lambda e, half=half: e.activation(out=va[:, half * 8:(half + 1) * 8, 0:64],
                                                          in_=banks[1][:, :].rearrange("p (h n) -> p h n", h=8),
                                                          func=AF.Copy), reads=[bbuf[1]], writes=[b_va])
                yield

        def gen_Q(cx):
            xn, b_xn = cx.xn
            for half in range(2):
                for q2 in range(NQ2):
                    w, b_w = g.loadA(L, 1, half * NQ2 + q2)
                    cs = slice(q2 * W2, (q2 + 1) * W2)
                    for kc in range(KC):
                        op(pe, lambda e, kc=kc, w=w, cs=cs: e.matmul(banks[1][:, cs], lhsT=xn[:, kc, :], rhs=w[:, kc, :],
                                                                    start=(kc == 0), stop=False),
                           reads=[b_xn, b_w], writes=[bbuf[1]], sig=False)
                    bo = 3 * D + half * 512 + q2 * W2
                    op(pe, lambda e, bo=bo, cs=cs: e.matmul(banks[1][:, cs], lhsT=g.onesrow[:],
                                                           rhs=g.brow_hi[0:1, bo:bo + W2], start=False, stop=False),
                       reads=[b_cst], writes=[bbuf[1]], sig=False)
                    op(pe, lambda e, bo=bo, cs=cs: e.matmul(banks[1][:, cs], lhsT=g.onesrow[:],
                                                           rhs=g.brow_lo[0:1, bo:bo + W2], start=False, stop=True),
                       reads=[b_cst], writes=[bbuf[1]], sig=(q2 == NQ2 - 1))
                op(act, lambda e, half=half: e.activation(out=zs[:, half * 512:(half + 1) * 512], in_=banks[1][:, :],
                                                          func=AF.Silu), reads=[bbuf[1]], writes=[b_zs])
                yield

        def gen_A(cx, j, u, kvs):
            i = cx.t
            xn, b_xn = cx.xn
            Ba, Bb, Bc = u.banks
            w, b_w = g.loadB(L, 0, j)
            for kc in range(KC):
                op(pe, lambda e, kc=kc: e.matmul(banks[Ba][:, 0:P], lhsT=w[:, kc, :], rhs=xn[:, kc, :],
                                                 start=(kc == 0), stop=(kc == KC - 1)),
                   reads=[b_w, b_xn], writes=[bbuf[Ba]], sig=(kc == KC - 1))
            op(act, lambda e: e.activation(out=u.qT[:], in_=banks[Ba][:, 0:P], func=AF.Identity, scale=0.125,
                                           bias=pvc('bq8', j)), reads=[bbuf[Ba], b_pv], writes=[u.b_qT])
            yield
            ent = geo[i]
            for h in range(2):
                hg = 2 * j + h
                for n_, (kt, di, pi) in enumerate(ent):
                    kT, b_kT = kvs[kt].kT
                    bk_ = Bb if n_ < 4 else Bc
                    co = (n_ % 4) * P
                    op(pe, lambda e, kT=kT, bk_=bk_, co=co, h=h: e.matmul(banks[bk_][:, co:co + P],
                                                                         lhsT=kT[hq[h], j, :], rhs=u.qT[hq[h], :],
                                                                         start=True, stop=False),
                       reads=[b_kT, u.b_qT], writes=[bbuf[bk_]], sig=False)
                    op(pe, lambda e, bk_=bk_, co=co, hg=hg, di=di: e.matmul(banks[bk_][:, co:co + P],
                                                                           lhsT=RB[:, hg, di, :], rhs=g.Jrev[:],
                                                                           start=False, stop=False),
                       reads=[b_lc, b_cst], writes=[bbuf[bk_]], sig=False)
                    last = (n_ == len(ent) - 1) or (n_ == 3)
                    op(pe, lambda e, bk_=bk_, co=co, pi=pi: e.matmul(banks[bk_][:, co:co + P], lhsT=MK[:, pi, :],
                                                                    rhs=g.Jrev[:], start=False, stop=True),
                       reads=[b_lc, b_cst], writes=[bbuf[bk_]], sig=last)
                n4 = min(4, len(ent))
                op(act, lambda e, n4=n4: e.activation(out=u.PT[:, 0:n4, :].rearrange("p n k -> p (n k)"),
                                                      in_=banks[Bb][:, 0:n4 * P], func=AF.Exp),
                   reads=[bbuf[Bb]], writes=[u.b_PT])
                if len(ent) > 4:
                    op(act, lambda e: e.activation(out=u.PT[:, 4, :], in_=banks[Bc][:, 0:P], func=AF.Exp),
                       reads=[bbuf[Bc]], writes=[u.b_PT])
                for n_, (kt, di, pi) in enumerate(ent):
                    va, b_va = kvs[kt].va
                    op(pe, lambda e, n_=n_, va=va, hg=hg, h=h: e.matmul(banks[Ba][:, 2 * P + h * 65:2 * P + h * 65 + 65],
                                                                       lhsT=u.PT[:, n_, :], rhs=va[:, hg, :],
                                                                       start=(n_ == 0), stop=(n_ == len(ent) - 1)),
                       reads=[u.b_PT, b_va], writes=[bbuf[Ba]], sig=(n_ == len(ent) - 1))
                yield
            for h in range(2):
                o0 = 2 * P + h * 65
                op(dve, lambda e, o0=o0, h=h: e.reciprocal(out=u.rc[:, h:h + 1], in_=banks[Ba][:, o0 + 64:o0 + 65]),
                   reads=[bbuf[Ba]], writes=[u.b_rc])
                op(dve, lambda e, o0=o0, h=h: e.tensor_scalar(out=og[:, (2 * j + h) * 64:(2 * j + h + 1) * 64],
                                                              in0=banks[Ba][:, o0:o0 + 64], scalar1=u.rc[:, h:h + 1],
                                                              scalar2=None, op0=ALU.mult),
                   reads=[bbuf[Ba], u.b_rc], writes=[b_og])
            yield

        def gen_O(cx):
            i = cx.t
            op(dve, lambda e: e.tensor_tensor(out=yg[:], in0=og[:], in1=zs[:], op=ALU.mult),
               reads=[b_og, b_zs], writes=[b_yg])
            for kc in range(KC):
                op(pe, lambda e, kc=kc: e.transpose(out=banks_bf[0][:, kc * P:(kc + 1) * P],
                                                    in_=yg[:, kc * P:(kc + 1) * P], identity=g.identb[:]),
                   reads=[b_yg, g.b_identb], writes=[bbuf[0]], sig=(kc == KC - 1))
            op(act, lambda e: e.activation(out=ygT[:].rearrange("p k n -> p (k n)"), in_=banks_bf[0][:, :],
                                           func=AF.Copy), reads=[bbuf[0]], writes=[b_ygT])
            yield
            for half in range(2):
                for q2 in range(NQ2):
                    w, b_w = g.loadA(L, 2, half * NQ2 + q2)
                    cs = slice(q2 * W2, (q2 + 1) * W2)
                    for kc in range(KC):
                        op(pe, lambda e, kc=kc, w=w, cs=cs: e.matmul(banks[0][:, cs], lhsT=ygT[:, kc, :], rhs=w[:, kc, :],
                                                                    start=(kc == 0), stop=False),
                           reads=[b_ygT, b_w], writes=[bbuf[0]], sig=False)
                    bo = 4 * D + half * 512 + q2 * W2
                    op(pe, lambda e, bo=bo, cs=cs: e.matmul(banks[0][:, cs], lhsT=g.onesrow[:],
                                                           rhs=g.brow_hi[0:1, bo:bo + W2], start=False, stop=False),
                       reads=[b_cst], writes=[bbuf[0]], sig=False)
                    op(pe, lambda e, bo=bo, cs=cs: e.matmul(banks[0][:, cs], lhsT=g.onesrow[:],
                                                           rhs=g.brow_lo[0:1, bo:bo + W2], start=False, stop=True),
                       reads=[b_cst], writes=[bbuf[0]], sig=(q2 == NQ2 - 1))
                op(act, lambda e, half=half: e.activation(out=tmpB[:, half * 512:(half + 1) * 512], in_=banks[0][:, :],
                                                          func=AF.Copy), reads=[bbuf[0]], writes=[b_tmpB])
                yield
            op(act, lambda e: e.activation(out=tmpA[:], in_=tmpB[:], func=AF.Square, accum_out=stat[:, 4:5]),
               reads=[b_tmpB], writes=[b_tmpA, b_stat])
            op(dve, lambda e: e.tensor_scalar(out=stat[:, 5:6], in0=stat[:, 4:5], scalar1=1.0 / D, scalar2=RMS_EPS,
                                              op0=ALU.mult, op1=ALU.add), reads=[b_stat], writes=[b_stat])
            op(act, lambda e: e.activation(out=stat[:, 6:7], in_=stat[:, 5:6], func=AF.Sqrt), reads=[b_stat],
               writes=[b_stat])
            op(dve, lambda e: e.reciprocal(out=stat[:, 7:8], in_=stat[:, 6:7]), reads=[b_stat], writes=[b_stat])
            op(dve, lambda e: e.scalar_tensor_tensor(out=tmpB[:], in0=tmpB[:], scalar=stat[:, 7:8], in1=postg[:],
                                                     op0=ALU.mult, op1=ALU.mult),
               reads=[b_tmpB, b_stat, b_lc], writes=[b_tmpB])
            op(pool, lambda e: e.tensor_tensor(out=tmpA[:], in0=tmpB[:], in1=x1[:, i, :], op=ALU.add),
               reads=[b_tmpB, bx1[i]], writes=[b_tmpA])
            T_.dma(pool, g.y_out[si, i * P:(i + 1) * P, :], tmpA[:], reads=[b_tmpA])
            yield

        tasks = []
        kvs = [None] * NCH
        lastO = None
        lastKV = None

        def make_KV(t):
            cx = NS()
            cx.t = t
            tk = Task("K%d" % t)
            cx.kxn, cx.xn = p_xn.acquire(tk)
            cx.kkT, cx.kT = p_kT.acquire(tk)
            cx.kva, cx.va = p_va.acquire(tk)
            if lastKV[0] is not None:
                tk.deps.append(lastKV[0])
            tk.gen = gen_KV(cx)
            cx.tk = tk
            kvs[t] = cx
            tasks.append(tk)
            lastKV[0] = tk

        lastKV = [None]
        for s in range(NCH + 3):
            if s < NCH:
                make_KV(s)
            i = s - 3
            if i < 0:
                continue
            cx = kvs[i]
            need = [kvs[kt].tk for (kt, _, _) in geo[i]]
            tq = Task("Q%d" % i)
            tq.deps += [cx.tk]
            if lastO is not None:
                tq.deps.append(lastO)
            p_xn.share(tq, cx.kxn)
            tq.gen = gen_Q(cx)
            tasks.append(tq)
            tas = []
            for j in range(KC):
                ta = Task("A%d_%d" % (i, j))
                ta.deps += need + [cx.tk]
                if lastO is not None:
                    ta.deps.append(lastO)
                _, uset = p_uset.acquire(ta)
                p_xn.share(ta, cx.kxn)
                for (kt, _, _) in geo[i]:
                    p_kT.share(ta, kvs[kt].kkT)
                    p_va.share(ta, kvs[kt].kva)
                ta.gen = gen_A(cx, j, uset, kvs)
                tas.append(ta)
                tasks.append(ta)
            to = Task("O%d" % i)
            to.deps += tas + [tq]
            to.gen = gen_O(cx)
            tasks.append(to)
            lastO = to
        run_tasks(tasks, window=WINDOW)


def kernel(**inputs):
    xp = np.asarray(inputs['x_prompt'], dtype=np.float32)
    xs_ = np.asarray(inputs['x_sample'], dtype=np.float32)
    xall = np.concatenate([xp, xs_], axis=0)
    nseq = xall.shape[0] // N_CORES
    nc = build(nseq)
    in_maps = []
    for ci in range(N_CORES):
        m = {"x": np.ascontiguousarray(xall[ci * nseq:(ci + 1) * nseq])}
        for nm in W_NAMES:
            m[nm] = np.ascontiguousarray(np.asarray(inputs[nm], dtype=np.float32))
        in_maps.append(m)
    res = run_bass_kernel_spmd(nc, in_maps, core_ids=list(range(N_CORES)))
    yall = np.concatenate([r["y"] for r in res.results], axis=0)
    nb = xp.shape[0]
    return (np.ascontiguousarray(yall[:nb]), np.ascontiguousarray(yall[nb:]))
```

```python
import numpy as np
from contextlib import ExitStack
import concourse.bass as bass
import concourse.mybir as mybir
from concourse.bass_utils import run_bass_kernel_spmd
from concourse.alu_op_type import AluOpType as ALU

F32 = mybir.dt.float32
BF16 = mybir.dt.bfloat16
AF = mybir.ActivationFunctionType
AX = mybir.AxisListType

N_CORES = 8
D = 1024
KC = 8
P = 128
T_SEQ = 2048
LAM = float(np.exp(-0.5))
LNX_EPS = 64e-5
RMS_EPS = 1e-6

import os
SAME_ENGINE_SYNC = os.environ.get('K_SES', '1') == '1'
WINDOW = int(os.environ.get('K_WIN', '4'))
STAGGER = int(os.environ.get('K_STAG', '0'))
NQ2 = int(os.environ.get('K_NQ2', '1'))
W2 = 512 // NQ2
DBUF = int(os.environ.get('K_DBUF', '1'))


class _Sem:
    def __init__(self, sem, name):
        self.sem = sem
        self.name = name
        self.n = 0


class Eng:
    def __init__(self, h, sem, name, is_pe=False):
        self.h = h
        self.s = _Sem(sem, name)
        self.name = name
        self.is_pe = is_pe
        self.waited = {}


class Buf:
    __slots__ = ("name", "w", "r", "excl")

    def __init__(self, name="", excl=False):
        self.name = name
        self.w = None
        self.r = {}
        self.excl = excl


class Trk:
    def __init__(self, nc, es, n_slots=16):
        self.nc = nc
        mk = lambda nm: es.enter_context(nc.semaphore(nm))
        self.pe = Eng(nc.tensor, mk("s_pe"), "pe", is_pe=True)
        self.act = Eng(nc.scalar, mk("s_act"), "act")
        self.dve = Eng(nc.vector, mk("s_dve"), "dve")
        self.pool = Eng(nc.gpsimd, mk("s_pool"), "pool")
        self.sp = Eng(nc.sync, mk("s_sp"), "sp")
        self.engs = [self.pe, self.act, self.dve, self.pool, self.sp]
        self.slots = [_Sem(mk("s_dma%d" % i), "dma%d" % i) for i in range(n_slots)]
        self.dma_i = 0
        self.slots_sw = [_Sem(mk("s_swdma%d" % i), "swdma%d" % i) for i in range(4)]
        self.dma_sw_i = 0
        self.n_inst = 0
        self.cnt = {}

    def _deps(self, reads, writes):
        deps = {}
        for b in reads:
            if b.w is not None:
                s, v = b.w
                if deps.get(s, 0) < v:
                    deps[s] = v
        for b in writes:
            if b.w is not None:
                s, v = b.w
                if deps.get(s, 0) < v:
                    deps[s] = v
            for s, v in b.r.items():
                if deps.get(s, 0) < v:
                    deps[s] = v
        return deps

    def _wait(self, eng, deps):
        for s, v in deps.items():
            if eng.waited.get(s, 0) >= v:
                continue
            if s is eng.s:
                if eng.is_pe or not SAME_ENGINE_SYNC:
                    continue
            eng.h.wait_ge(s.sem, v)
            eng.waited[s] = v

    def _mark(self, tok, reads, writes):
        s, v = tok
        for b in reads:
            if b.r.get(s, 0) < v:
                b.r[s] = v
        for b in writes:
            b.w = tok
            b.r = {}

    def op(self, eng, fn, reads=(), writes=(), sig=True):
        if any(b.excl for b in reads):
            writes = list(writes) + [b for b in reads if b.excl]
            reads = [b for b in reads if not b.excl]
        self._wait(eng, self._deps(reads, writes))
        inst = fn(eng.h)
        tok = (eng.s, eng.s.n + 1)
        if sig:
            inst.then_inc(eng.s.sem, 1)
            eng.s.n += 1
        self._mark(tok, reads, writes)
        self.n_inst += 1
        self.cnt[eng.name] = self.cnt.get(eng.name, 0) + 1
        return inst

    def dma(self, q, out, in_, reads=(), writes=(), **kw):
        if q is self.pool:
            slot = self.slots_sw[self.dma_sw_i % len(self.slots_sw)]
            self.dma_sw_i += 1
        else:
            slot = self.slots[self.dma_i % len(self.slots)]
            self.dma_i += 1
        deps = self._deps(reads, writes)
        if slot.n > 0 and deps.get(slot, 0) < slot.n:
            deps[slot] = slot.n
        self._wait(q, deps)
        inst = q.h.dma_start(out=out, in_=in_, **kw)
        inst.then_inc(slot.sem, 16)
        slot.n += 16
        self._mark((slot, slot.n), reads, writes)
        self.n_inst += 1
        self.cnt["dma"] = self.cnt.get("dma", 0) + 1
        return inst

    def barrier(self):
        allsems = [e.s for e in self.engs] + self.slots + self.slots_sw
        for e in self.engs:
            deps = {s: s.n for s in allsems if s.n > 0 and s is not e.s}
            self._wait(e, deps)

    def finish(self):
        deps = {s: s.n for s in self.slots + self.slots_sw if s.n > 0}
        self._wait(self.sp, deps)


class Task:
    def __init__(self, name):
        self.name = name
        self.deps = []
        self.done = False
        self.gen = None


def run_tasks(tasks, window=3):
    pending = list(tasks)
    active = []
    while pending or active:
        while pending and len(active) < window and all(d.done for d in pending[0].deps):
            active.append(pending.pop(0))
        if not active:
            raise RuntimeError("scheduler deadlock at %s" % pending[0].name)
        for t in list(active):
            try:
                next(t.gen)
            except StopIteration:
                t.done = True
                active.remove(t)


class RPool:
    def __init__(self, items):
        self.items = items
        self.i = 0
        self.users = [[] for _ in items]

    def acquire(self, task):
        k = self.i % len(self.items)
        self.i += 1
        task.deps += self.users[k]
        self.users[k] = [task]
        return k, self.items[k]

    def share(self, task, k):
        self.users[k].append(task)


class NS:
    pass


W_NAMES = ['pre_norm_g', 'post_norm_g', 'rk_mu', 'rk_w_r', 'rk_w_k', 'rk_w_v', 'rk_w_z', 'rk_w0', 'rk_w1', 'rk_w2',
           'rk_a0', 'rk_a1', 'rk_a2', 'rk_k_k', 'rk_k_a', 'rk_r_k', 'rk_lnx_w', 'rk_lnx_b', 'rk_w_o', 'na_w_in',
           'na_b_in', 'na_rpb', 'na_w_o', 'na_b_o']
W_SHAPES = {
    'pre_norm_g': [2, D], 'post_norm_g': [2, D], 'rk_mu': [1, 7, D], 'rk_w_r': [1, D, D], 'rk_w_k': [1, D, D],
    'rk_w_v': [1, D, D], 'rk_w_z': [1, D, D], 'rk_w0': [1, 2, D], 'rk_w1': [1, 2, D, 64], 'rk_w2': [1, 2, 64, D],
    'rk_a0': [1, 2, D], 'rk_a1': [1, 2, D, 64], 'rk_a2': [1, 2, 64, D], 'rk_k_k': [1, D], 'rk_k_a': [1, D],
    'rk_r_k': [1, 16, 64], 'rk_lnx_w': [1, D], 'rk_lnx_b': [1, D], 'rk_w_o': [1, D, D], 'na_w_in': [1, D, 4 * D],
    'na_b_in': [1, 4 * D], 'na_rpb': [1, 16, 15, 31], 'na_w_o': [1, D, D], 'na_b_o': [1, D],
}

PV = {}
_c = 0
for _nm, _n in [('mu', 7), ('w0', 2), ('a0', 2), ('k_k', 1), ('k_a', 1), ('r_k', 1), ('pre_g', 2), ('bq', 1),
                ('bk', 1), ('omk_a', 1), ('bq8', 1)]:
    PV[_nm] = _c
    _c += _n * 8
PV_COLS = _c
PV_ROWS = PV['omk_a']


def build(nseq, T=T_SEQ, stage="full"):
    NCH = T // P
    nc = bass.Bass("TRN2", target_bir_lowering=False)
    x_in = nc.dram_tensor("x", [nseq, T, D], F32, kind="ExternalInput").ap()
    y_out = nc.dram_tensor("y", [nseq, T, D], F32, kind="ExternalOutput").ap()
    wd = {nm: nc.dram_tensor(nm, W_SHAPES[nm], F32, kind="ExternalInput").ap() for nm in W_NAMES}
    wbJ = [nc.dram_tensor("wbJ%d" % l, [2, KC, P, KC, P], BF16, kind="Internal").ap() for l in range(2)]
    wbH = [nc.dram_tensor("wbH%d" % l, [3, 2 * NQ2, P, KC, W2], BF16, kind="Internal").ap() for l in range(2)]

    es = ExitStack()
    with es:
        T_ = Trk(nc, es)
        pe, act, dve, pool, sp = T_.pe, T_.act, T_.dve, T_.pool, T_.sp
        op = T_.op

        uid = [0]

        def sb(es_, nm, sh, dt):
            uid[0] += 1
            return es_.enter_context(nc.sbuf_tensor("%s_%d" % (nm, uid[0]), sh, dt))

        banks = [es.enter_context(nc.psum_tensor("pb%d" % i, [P, 512], F32)) for i in range(8)]
        bbuf = [Buf("pb%d" % i, excl=True) for i in range(8)]
        banks_bf = [b[:].bitcast(BF16) for b in banks]

        x1 = sb(es, "x1", [P, NCH, D], F32)
        bx1 = [Buf("x1_%d" % c) for c in range(NCH)]
        ringA = [sb(es, "ringA%d" % i, [P, KC, W2], BF16) for i in range(2)]
        b_ringA = [Buf("ringA%d" % i) for i in range(2)]
        ringB = [sb(es, "ringB%d" % i, [P, KC, P], BF16) for i in range(4)]
        b_ringB = [Buf("ringB%d" % i) for i in range(4)]
        rA_i = [0]
        rB_i = [0]
        identb = sb(es, "identb", [P, P], BF16)
        b_identb = Buf("identb")
        pv = sb(es, "pv", [P, PV_COLS], F32)
        b_pv = Buf("pv")
        onesrow = sb(es, "onesrow", [1, P], BF16)
        brow_hi = sb(es, "brow_hi", [1, 5 * D], BF16)
        brow_lo = sb(es, "brow_lo", [1, 5 * D], BF16)
        b_cst = Buf("consts")
        maskq = [sb(es, "maskq%d" % d, [P, 512], BF16) for d in range(2)]
        bdones = sb(es, "bdones", [P, P], BF16)
        hsel = sb(es, "hsel", [P, 2], BF16)
        ones_f = sb(es, "ones_f", [P, P], F32)
        Jrev = sb(es, "Jrev", [P, P], BF16)
        pes0 = ExitStack()
        io_r = sb(pes0, "io_r", [P, P], F32)
        io_c = sb(pes0, "io_c", [P, P], F32)

        def pvc(nm, idx):
            c0 = PV[nm] + idx
            return pv[:, c0:c0 + 1]

        def loadA(layer, m, qt):
            k = rA_i[0] % 2
            rA_i[0] += 1
            T_.dma(sp, ringA[k][:], wbH[layer][m, qt], writes=[b_ringA[k]])
            return ringA[k], b_ringA[k]

        def loadB(layer, m, j):
            k = rB_i[0] % 4
            rB_i[0] += 1
            T_.dma(sp, ringB[k][:], wbJ[layer][m, j], writes=[b_ringB[k]])
            return ringB[k], b_ringB[k]

        op(pool, lambda e: e.iota(io_r[:], pattern=[[0, P]], base=0, channel_multiplier=1,
                                  allow_small_or_imprecise_dtypes=True), writes=[b_cst])
        op(pool, lambda e: e.iota(io_c[:], pattern=[[1, P]], base=0, channel_multiplier=0,
                                  allow_small_or_imprecise_dtypes=True), writes=[b_cst])
        op(dve, lambda e: e.tensor_tensor(out=identb[:], in0=io_r[:], in1=io_c[:], op=ALU.is_equal),
           reads=[b_cst], writes=[b_identb])
        op(dve, lambda e: e.memset(onesrow[:], 1.0), writes=[b_cst])
        op(dve, lambda e: e.memset(ones_f[:], 1.0), writes=[b_cst])
        for d_, (o_s, o_i) in enumerate([(ALU.is_lt, ALU.is_le), (ALU.is_gt, ALU.is_ge)]):
            for q4 in range(4):
                o_ = o_s if q4 % 2 == 0 else o_i
                op(dve, lambda e, d_=d_, q4=q4, o_=o_: e.tensor_tensor(out=maskq[d_][:, q4 * P:(q4 + 1) * P],
                                                                      in0=io_r[:], in1=io_c[:], op=o_),
                   reads=[b_cst], writes=[b_cst])

        with ExitStack() as pes:
            identf = sb(pes, "identf", [P, P], F32)
            rb_ = sb(pes, "rb_", [P, P], F32)
            cb_ = sb(pes, "cb_", [P, P], F32)
            op(dve, lambda e: e.tensor_tensor(out=identf[:], in0=io_r[:], in1=io_c[:], op=ALU.is_equal),
               reads=[b_cst], writes=[b_cst])
            op(dve, lambda e: e.tensor_scalar(out=rb_[:], in0=io_r[:], scalar1=63.5, scalar2=None, op0=ALU.is_gt),
               reads=[b_cst], writes=[b_cst])
            op(dve, lambda e: e.tensor_scalar(out=cb_[:], in0=io_c[:], scalar1=63.5, scalar2=None, op0=ALU.is_gt),
               reads=[b_cst], writes=[b_cst])
            op(dve, lambda e: e.tensor_tensor(out=bdones[:], in0=rb_[:], in1=cb_[:], op=ALU.is_equal),
               reads=[b_cst], writes=[b_cst])
            op(dve, lambda e: e.tensor_copy(out=hsel[:, 1:2], in_=rb_[:, 0:1]), reads=[b_cst], writes=[b_cst])
            op(dve, lambda e: e.tensor_scalar(out=hsel[:, 0:1], in0=rb_[:, 0:1], scalar1=-1.0, scalar2=1.0,
                                              op0=ALU.mult, op1=ALU.add), reads=[b_cst], writes=[b_cst])

            rows = sb(pes, "pvrows", [P, 2, P], F32)
            b_rows = Buf("pvrows")
            op(dve, lambda e: e.memset(rows[:], 0.0), writes=[b_rows])

            def load_rows(r0, src):
                n = src.shape[0]
                g, o = divmod(r0, P)
                assert o + n <= P, (r0, n)
                T_.dma(sp, rows[o:o + n, g, :], src, writes=[b_rows])

            load_rows(PV['mu'], wd['rk_mu'][0].rearrange("m (j q) -> (m j) q", q=P))
            load_rows(PV['w0'], wd['rk_w0'][0].rearrange("m (j q) -> (m j) q", q=P))
            load_rows(PV['a0'], wd['rk_a0'][0].rearrange("m (j q) -> (m j) q", q=P))
            load_rows(PV['k_k'], wd['rk_k_k'][0].rearrange("(j q) -> j q", q=P))
            load_rows(PV['k_a'], wd['rk_k_a'][0].rearrange("(j q) -> j q", q=P))
            load_rows(PV['r_k'], wd['rk_r_k'][0].rearrange("(j h) c -> j (h c)", h=2))
            load_rows(PV['pre_g'], wd['pre_norm_g'].rearrange("m (j q) -> (m j) q", q=P))
            load_rows(PV['bq'], wd['na_b_in'][0, 0:D].rearrange("(j q) -> j q", q=P))
            load_rows(PV['bk'], wd['na_b_in'][0, D:2 * D].rearrange("(j q) -> j q", q=P))
            for g in range(2):
                op(pe, lambda e, g=g: e.transpose(out=banks[0][:, g * P:(g + 1) * P], in_=rows[:, g, :],
                                                  identity=identf[:]),
                   reads=[b_rows, b_cst], writes=[bbuf[0]])
            op(dve, lambda e: e.tensor_copy(out=pv[:, 0:PV_ROWS], in_=banks[0][:, 0:PV_ROWS]),
               reads=[bbuf[0]], writes=[b_pv])
            op(dve, lambda e: e.tensor_scalar(out=pv[:, PV['omk_a']:PV['omk_a'] + 8],
                                              in0=pv[:, PV['k_a']:PV['k_a'] + 8], scalar1=-1.0, scalar2=1.0,
                                              op0=ALU.mult, op1=ALU.add), reads=[b_pv], writes=[b_pv])
            op(dve, lambda e: e.tensor_scalar(out=pv[:, PV['bq8']:PV['bq8'] + 8], in0=pv[:, PV['bq']:PV['bq'] + 8],
                                              scalar1=0.125, scalar2=None, op0=ALU.mult), reads=[b_pv], writes=[b_pv])

            brow_f = sb(pes, "brow_f", [1, 5 * D], F32)
            brow_t = sb(pes, "brow_t", [1, 5 * D], F32)
            b_bf = Buf("brow_f")
            T_.dma(sp, brow_f[0:1, 0:4 * D], wd['na_b_in'][0:1, :], writes=[b_bf])
            T_.dma(sp, brow_f[0:1, 4 * D:5 * D], wd['na_b_o'][0:1, :], writes=[b_bf])
            op(act, lambda e: e.activation(out=brow_hi[:], in_=brow_f[:], func=AF.Copy), reads=[b_bf], writes=[b_cst])
            op(dve, lambda e: e.tensor_tensor(out=brow_t[:], in0=brow_f[:], in1=brow_hi[:], op=ALU.subtract),
               reads=[b_bf, b_cst], writes=[b_bf])
            op(act, lambda e: e.activation(out=brow_lo[:], in_=brow_t[:], func=AF.Copy), reads=[b_bf], writes=[b_cst])

            stg = [sb(pes, "stg%d" % i, [P, D], F32) for i in range(3)]
            stb = [sb(pes, "stb%d" % i, [P, D], BF16) for i in range(3)]
            b_stg = [Buf() for _ in range(3)]
            b_stb = [Buf() for _ in range(3)]
            srcs = []
            for kc in range(KC):
                rs_ = slice(kc * P, (kc + 1) * P)
                for m, nm in enumerate(['rk_w_r', 'rk_w_k']):
                    srcs.append((wd[nm][0, rs_, :], wbJ[0][m, :, :, kc, :].rearrange("j p n -> p j n"), 'J'))
                for m, nm in enumerate(['rk_w_v', 'rk_w_z', 'rk_w_o']):
                    srcs.append((wd[nm][0, rs_, :], wbH[0][m, :, :, kc, :].rearrange("h p n -> p h n"), 'H'))
                for m in range(2):
                    srcs.append((wd['na_w_in'][0, rs_, m * D:(m + 1) * D],
                                 wbJ[1][m, :, :, kc, :].rearrange("j p n -> p j n"), 'J'))
                for m in range(2):
                    srcs.append((wd['na_w_in'][0, rs_, (m + 2) * D:(m + 3) * D],
                                 wbH[1][m, :, :, kc, :].rearrange("h p n -> p h n"), 'H'))
                srcs.append((wd['na_w_o'][0, rs_, :], wbH[1][2, :, :, kc, :].rearrange("h p n -> p h n"), 'H'))
            cast_engs = [act, dve, pool]
            for i, (src, dst, kind) in enumerate(srcs):
                k = i % 3
                T_.dma(sp, stg[k][:], src, writes=[b_stg[k]])
                if cast_engs[k] is act:
                    op(act, lambda e, k=k: e.activation(out=stb[k][:], in_=stg[k][:], func=AF.Copy),
                       reads=[b_stg[k]], writes=[b_stb[k]])
                else:
                    op(cast_engs[k], lambda e, k=k: e.tensor_copy(out=stb[k][:], in_=stg[k][:]),
                       reads=[b_stg[k]], writes=[b_stb[k]])
                if kind == 'J':
                    srcv = stb[k][:].rearrange("p (j n) -> p j n", j=KC)
                else:
                    srcv = stb[k][:].rearrange("p (h n) -> p h n", h=2 * NQ2)
                T_.dma(sp, dst, srcv, reads=[b_stb[k]])
            T_.barrier()
            T_.finish()
        pes0.close()

        ctx = NS()
        ctx.__dict__.update(locals())
        if stage != "l0":
            l1_prologue(ctx)
        for si in range(nseq):
            with nc.named_scope('L0_%d' % si):
                layer0(ctx, si)
            T_.barrier()
            T_.finish()
            if stage == "l0":
                for c in range(NCH):
                    T_.dma(sp, y_out[si, c * P:(c + 1) * P, :], x1[:, c, :], reads=[bx1[c]])
            else:
                with nc.named_scope('L1_%d' % si):
                    layer1(ctx, si)
            T_.barrier()
            T_.finish()
        print("instructions:", T_.n_inst, T_.cnt)
    return nc


def layer0(g, si):
    nc, T_, op = g.nc, g.T_, g.op
    pe, act, dve, pool, sp = g.pe, g.act, g.dve, g.pool, g.sp
    banks, bbuf, banks_bf = g.banks, g.bbuf, g.banks_bf
    NCH, x1, bx1, pv, b_pv, pvc = g.NCH, g.x1, g.bx1, g.pv, g.b_pv, g.pvc
    wd, sb = g.wd, g.sb
    b_cst = g.b_cst
    L = 0

    with ExitStack() as l0:
        w1b = sb(l0, "w1b", [P, 2, KC, 64], BF16)
        a1b = sb(l0, "a1b", [P, 2, KC, 64], BF16)
        w2b = sb(l0, "w2b", [64, 2, D], BF16)
        a2b = sb(l0, "a2b", [64, 2, D], BF16)
        lnxb = sb(l0, "lnxb", [P, 2, D], F32)
        postg = sb(l0, "postg", [P, D], F32)
        b_lc = Buf("l0consts")
        tmpA = sb(l0, "tmpA", [P, D], F32)
        tmpB = sb(l0, "tmpB", [P, D], F32)
        b_tmpA, b_tmpB = Buf("tmpA"), Buf("tmpB")
        T_.dma(sp, tmpA[:].rearrange("p (d k n) -> p d k n", d=2, k=KC),
               wd['rk_w1'][0].rearrange("d (k p) n -> p d k n", p=P), writes=[b_tmpA])
        op(dve, lambda e: e.tensor_copy(out=w1b[:].rearrange("p d k n -> p (d k n)"), in_=tmpA[:]),
           reads=[b_tmpA], writes=[b_lc])
        T_.dma(sp, tmpB[:].rearrange("p (d k n) -> p d k n", d=2, k=KC),
               wd['rk_a1'][0].rearrange("d (k p) n -> p d k n", p=P), writes=[b_tmpB])
        op(dve, lambda e: e.tensor_copy(out=a1b[:].rearrange("p d k n -> p (d k n)"), in_=tmpB[:]),
           reads=[b_tmpB], writes=[b_lc])
        for d_ in range(2):
            T_.dma(sp, tmpA[0:64, :], wd['rk_w2'][0, d_], writes=[b_tmpA])
            op(dve, lambda e, d_=d_: e.tensor_copy(out=w2b[:, d_, :], in_=tmpA[0:64, :]), reads=[b_tmpA], writes=[b_lc])
            T_.dma(sp, tmpB[0:64, :], wd['rk_a2'][0, d_], writes=[b_tmpB])
            op(dve, lambda e, d_=d_: e.tensor_copy(out=a2b[:, d_, :], in_=tmpB[0:64, :]), reads=[b_tmpB], writes=[b_lc])
        T_.dma(sp, lnxb[:, 0, :], wd['rk_lnx_w'][0].partition_broadcast(P), writes=[b_lc])
        T_.dma(sp, lnxb[:, 1, :], wd['rk_lnx_b'][0].partition_broadcast(P), writes=[b_lc])
        T_.dma(sp, postg[:], wd['post_norm_g'][L].partition_broadcast(P), writes=[b_lc])

        def mk(nm, sh, dt, n):
            return [(sb(l0, "%s%d" % (nm, i), sh, dt), Buf("%s%d" % (nm, i))) for i in range(n)]

        p_xin = RPool(mk("xin", [P, D], F32, 1))
        xs, b_xs = mk("xs", [P, D], BF16, 1)[0]
        stat = sb(l0, "stat", [P, 8], F32)
        b_stat = Buf("stat")
        p_slot = RPool(mk("xnT", [P, KC, P + 2], BF16, 3))
        xx, b_xx = mk("xx", [P, KC, P], BF16, 1)[0]
        ltmp, b_ltmp = mk("ltmp", [P, P], F32, 1)[0]
        p_lr = RPool(mk("lrp_r", [P, KC, P], BF16, DBUF))
        p_lk = RPool(mk("lrp_k", [P, KC, P], BF16, DBUF))
        p_lt = mk("lrp_t", [P, KC, P], BF16, 1)
        lt_i = [0]
        p_vt = RPool(mk("Vtm", [P, D], BF16, DBUF))
        p_zs = RPool(mk("zs", [P, D], BF16, 1))
        p_th = RPool(mk("th", [64, 2 * P], BF16, 2))
        Hf = sb(l0, "Hf", [P, KC, 64], F32)
        Hb = sb(l0, "Hb", [P, KC, 64], BF16)
        b_H = [Buf("H%d" % j) for j in range(KC)]
        bonF = sb(l0, "bonF", [P, NCH, 16], F32)
        b_bonF = [Buf("bonF%d" % c) for c in range(NCH)]
        p_bonB = RPool(mk("bonB", [P, 16], F32, 2))
        p_yb = RPool(mk("Yb", [P, D], F32, 1))
        yg, b_yg = mk("yg", [P, D], BF16, 1)[0]
        ygT, b_ygT = mk("ygT", [P, KC, P], BF16, 1)[0]
        gst = sb(l0, "gst", [P, 6, 16], F32)
        b_gst = Buf("gst")

        def mkset(i):
            u = NS()
            u.i = i
            def t(nm, sh, dt):
                tt = sb(l0, "u%d_%s" % (i, nm), sh, dt)
                setattr(u, nm, tt)
                setattr(u, "b_" + nm, Buf("u%d_%s" % (i, nm)))
            t("rk", [P, 2 * P], F32)
            for nm in ("sg", "al", "kk", "rs", "Ein", "Eex", "ein", "kd"):
                t(nm, [P, P], F32)
            t("sq", [P, P], BF16)
            t("arT", [P, 2 * P], BF16)
            t("btT", [P, P], BF16)
            t("ktT", [P, P], BF16)
            t("pr", [P, P], BF16)
            t("BK", [P, 2 * P], BF16)
            t("AT0", [P, 512], BF16)
            t("AT1", [P, 512], BF16)
            for h in range(2):
                for k in range(2):
                    t("C%d%d" % (h, k), [P, 3 * P], BF16)
            t("Xp", [P, P], BF16)
            t("Up", [P, P], BF16)
            t("Hs", [P, 64], F32)
            t("sc", [P, 4], F32)
            u.banks = (2 + 3 * i, 3 + 3 * i, 4 + 3 * i)
            return u

        p_uset = RPool([mkset(0), mkset(1)])

        def gen_Na(cx):
            c = cx.c
            xin, b_xin = cx.xin
            T_.dma(sp, xin[:], g.x_in[si, c * P:(c + 1) * P, :], writes=[b_xin])
            op(act, lambda e: e.activation(out=xs[:], in_=xin[:], func=AF.Square, accum_out=stat[:, 0:1]),
               reads=[b_xin], writes=[b_xs, b_stat])
            op(dve, lambda e: e.tensor_scalar(out=stat[:, 1:2], in0=stat[:, 0:1], scalar1=1.0 / D, scalar2=RMS_EPS,
                                              op0=ALU.mult, op1=ALU.add), reads=[b_stat], writes=[b_stat])
            op(act, lambda e: e.activation(out=stat[:, 2:3], in_=stat[:, 1:2], func=AF.Sqrt), reads=[b_stat],
               writes=[b_stat])
            op(dve, lambda e: e.reciprocal(out=stat[:, 3:4], in_=stat[:, 2:3]), reads=[b_stat], writes=[b_stat])
            op(act, lambda e: e.activation(out=xs[:], in_=xin[:], func=AF.Copy, scale=stat[:, 3:4]),
               reads=[b_xin, b_stat], writes=[b_xs])
            yield

        def gen_Nb(cx, d, first, last, prev):
            c = cx.c
            slot, b_slot = cx.slot
            for kc in range(KC):
                op(pe, lambda e, kc=kc: e.transpose(out=banks_bf[0][:, kc * P:(kc + 1) * P],
                                                    in_=xs[:, kc * P:(kc + 1) * P], identity=g.identb[:]),
                   reads=[b_xs, g.b_identb], writes=[bbuf[0]], sig=(kc == KC - 1))
            for kc in range(KC):
                sc = pvc('pre_g', L * 8 + kc)
                if kc % 2 == 0:
                    op(dve, lambda e, kc=kc, sc=sc: e.tensor_scalar(out=slot[:, kc, 1:P + 1],
                                                                  in0=banks_bf[0][:, kc * P:(kc + 1) * P],
                                                                  scalar1=sc, scalar2=None, op0=ALU.mult),
                       reads=[bbuf[0], b_pv], writes=[b_slot])
                else:
                    op(act, lambda e, kc=kc, sc=sc: e.activation(out=slot[:, kc, 1:P + 1],
                                                               in_=banks_bf[0][:, kc * P:(kc + 1) * P],
                                                               func=AF.Copy, scale=sc),
                       reads=[bbuf[0], b_pv], writes=[b_slot])
            near, far = (0, P + 1) if d == 0 else (P + 1, 0)
            if first:
                op(pool, lambda e: e.memset(slot[:, :, near:near + 1], 0.0), writes=[b_slot])
            else:
                pslot, b_pslot = prev.slot
                src_own = 1 if d == 0 else P
                src_prev = P if d == 0 else 1
                op(pool, lambda e: e.tensor_copy(out=slot[:, :, near:near + 1], in_=pslot[:, :, src_prev:src_prev + 1]),
                   reads=[b_pslot], writes=[b_slot])
                op(pool, lambda e: e.tensor_copy(out=pslot[:, :, far:far + 1], in_=slot[:, :, src_own:src_own + 1]),
                   reads=[b_slot], writes=[b_pslot])
            if last:
                op(pool, lambda e: e.memset(slot[:, :, far:far + 1], 0.0), writes=[b_slot])
            yield

        def gen_F(cx, d):
            slot, b_slot = cx.slot
            xn = slot[:, :, 1:P + 1]
            op(pool, lambda e: e.tensor_tensor(out=xx[:], in0=slot[:, :, 0:P], in1=slot[:, :, 2:P + 2], op=ALU.add),
               reads=[b_slot], writes=[b_xx])
            op(dve, lambda e: e.scalar_tensor_tensor(out=xx[:], in0=xx[:], scalar=0.5, in1=xn, op0=ALU.mult,
                                                     op1=ALU.subtract), reads=[b_xx, b_slot], writes=[b_xx])
            yield

            def lerp(m, dst, b_dst):
                for kc in range(KC):
                    sc = pvc('mu', m * 8 + kc)
                    op(dve, lambda e, kc=kc, sc=sc: e.scalar_tensor_tensor(
                        out=dst[:, kc, :], in0=xx[:, kc, :], scalar=sc, in1=slot[:, kc, 1:P + 1],
                        op0=ALU.mult, op1=ALU.add), reads=[b_xx, b_slot, b_pv], writes=[b_dst])

            def next_lt():
                r = p_lt[0]
                lt_i[0] += 1
                return r

            lr, b_lr = cx.lr
            lk, b_lk = cx.lk
            vt, b_vt = cx.vt
            lerp(0, lr, b_lr)
            yield
            lerp(1, lk, b_lk)
            yield
            lv, b_lv = next_lt()
            lerp(2, lv, b_lv)
            yield
            for half in range(2):
                for q2 in range(NQ2):
                    w, b_w = g.loadA(L, 0, half * NQ2 + q2)
                    for kc in range(KC):
                        op(pe, lambda e, kc=kc, w=w, q2=q2: e.matmul(banks[1][:, q2 * W2:(q2 + 1) * W2],
                                                                    lhsT=lv[:, kc, :], rhs=w[:, kc, :],
                                                                    start=(kc == 0), stop=(kc == KC - 1)),
                           reads=[b_lv, b_w], writes=[bbuf[1]], sig=(kc == KC - 1 and q2 == NQ2 - 1))
                op(act, lambda e, half=half: e.activation(out=vt[:, half * 512:(half + 1) * 512], in_=banks[1][:, :],
                                                          func=AF.Copy), reads=[bbuf[1]], writes=[b_vt])
                yield
            if d == 1:
                zs, b_zs = cx.zs
                for half in range(2):
                    for q2 in range(NQ2):
                        w, b_w = g.loadA(L, 1, half * NQ2 + q2)
                        for kc in range(KC):
                            op(pe, lambda e, kc=kc, w=w, q2=q2: e.matmul(banks[1][:, q2 * W2:(q2 + 1) * W2],
                                                                        lhsT=slot[:, kc, 1:P + 1], rhs=w[:, kc, :],
                                                                        start=(kc == 0), stop=(kc == KC - 1)),
                               reads=[b_slot, b_w], writes=[bbuf[1]], sig=(kc == KC - 1 and q2 == NQ2 - 1))
                    op(act, lambda e, half=half: e.activation(out=zs[:, half * 512:(half + 1) * 512],
                                                              in_=banks[1][:, :], func=AF.Silu),
                       reads=[bbuf[1]], writes=[b_zs])
                    yield
            th, b_th = cx.th
            lw, b_lw = next_lt()
            lerp(3 + d, lw, b_lw)
            yield
            la, b_la = next_lt()
            lerp(5 + d, la, b_la)
            yield
            for kc in range(KC):
                op(pe, lambda e, kc=kc: e.matmul(banks[1][0:64, 0:P], lhsT=w1b[:, d, kc, :], rhs=lw[:, kc, :],
                                                 start=(kc == 0), stop=(kc == KC - 1)),
                   reads=[b_lw, b_lc], writes=[bbuf[1]], sig=False)
            for kc in range(KC):
                op(pe, lambda e, kc=kc: e.matmul(banks[1][0:64, P:2 * P], lhsT=a1b[:, d, kc, :], rhs=la[:, kc, :],
                                                 start=(kc == 0), stop=(kc == KC - 1)),
                   reads=[b_la, b_lc], writes=[bbuf[1]], sig=(kc == KC - 1))
            op(act, lambda e: e.activation(out=th[:, 0:P], in_=banks[1][0:64, 0:P], func=AF.Tanh),
               reads=[bbuf[1]], writes=[b_th])
            op(act, lambda e: e.activation(out=th[:, P:2 * P], in_=banks[1][0:64, P:2 * P], func=AF.Copy),
               reads=[bbuf[1]], writes=[b_th])
            yield

        def gen_U(cx, d, j, u):
            c = cx.c
            Ba, Bb, Bc = u.banks
            lr, b_lr = cx.lr
            lk, b_lk = cx.lk
            vt, b_vt = cx.vt
            th, b_th = cx.th
            hq = [slice(0, 64), slice(64, 128)]
            mq = g.maskq[d]
            mA = g.maskq[1 - d][:, 0:P]
            for _ in range(STAGGER if (j % 2 == 1) else 0):
                yield
            wr, b_wr = g.loadB(L, 0, j)
            wk, b_wk = g.loadB(L, 1, j)
            for kc in range(KC):
                op(pe, lambda e, kc=kc: e.matmul(banks[Ba][:, 0:P], lhsT=wr[:, kc, :], rhs=lr[:, kc, :],
                                                 start=(kc == 0), stop=(kc == KC - 1)),
                   reads=[b_wr, b_lr], writes=[bbuf[Ba]], sig=False)
            for kc in range(KC):
                op(pe, lambda e, kc=kc: e.matmul(banks[Ba][:, P:2 * P], lhsT=wk[:, kc, :], rhs=lk[:, kc, :],
                                                 start=(kc == 0), stop=(kc == KC - 1)),
                   reads=[b_wk, b_lk], writes=[bbuf[Ba]], sig=False)
            op(pe, lambda e: e.matmul(banks[Ba][:, 2 * P:3 * P], lhsT=w2b[:, d, j * P:(j + 1) * P], rhs=th[:, 0:P],
                                      start=True, stop=True), reads=[b_lc, b_th], writes=[bbuf[Ba]], sig=False)
            op(pe, lambda e: e.matmul(banks[Ba][:, 3 * P:4 * P], lhsT=a2b[:, d, j * P:(j + 1) * P], rhs=th[:, P:2 * P],
                                      start=True, stop=True), reads=[b_lc, b_th], writes=[bbuf[Ba]])
            op(act, lambda e: e.activation(out=u.rk[:], in_=banks[Ba][:, 0:2 * P], func=AF.Copy),
               reads=[bbuf[Ba]], writes=[u.b_rk])
            op(act, lambda e: e.activation(out=u.sg[:], in_=banks[Ba][:, 2 * P:3 * P], func=AF.Sigmoid,
                                           bias=pvc('w0', d * 8 + j)), reads=[bbuf[Ba], b_pv], writes=[u.b_sg])
            op(act, lambda e: e.activation(out=u.sq[:], in_=banks[Ba][:, P:2 * P], func=AF.Square,
                                           scale=pvc('k_k', j)), reads=[bbuf[Ba], b_pv], writes=[u.b_sq])
            op(act, lambda e: e.activation(out=u.al[:], in_=banks[Ba][:, 3 * P:4 * P], func=AF.Sigmoid,
                                           bias=pvc('a0', d * 8 + j)), reads=[bbuf[Ba], b_pv], writes=[u.b_al])
            yield
            rT = u.rk[:, 0:P]
            kT = u.rk[:, P:2 * P]
            op(dve, lambda e: e.tensor_scalar(out=u.kk[:], in0=kT, scalar1=pvc('k_k', j), scalar2=None, op0=ALU.mult),
               reads=[u.b_rk, b_pv], writes=[u.b_kk])
            op(pe, lambda e: e.matmul(banks[Ba][:, 0:P], lhsT=g.bdones[:], rhs=u.sq[:], start=True, stop=True),
               reads=[b_cst, u.b_sq], writes=[bbuf[Ba]])
            op(act, lambda e: e.activation(out=u.rs[:], in_=banks[Ba][:, 0:P], func=AF.Ln),
               reads=[bbuf[Ba]], writes=[u.b_rs])
            op(act, lambda e: e.activation(out=u.rs[:], in_=u.rs[:], func=AF.Exp, scale=-0.5),
               reads=[u.b_rs], writes=[u.b_rs])
            op(pool, lambda e: e.tensor_tensor(out=u.kk[:], in0=u.kk[:], in1=u.rs[:], op=ALU.mult),
               reads=[u.b_kk, u.b_rs], writes=[u.b_kk])
            op(dve, lambda e: e.tensor_tensor_scan(out=u.Ein[:], data0=g.ones_f[:], data1=u.sg[:], initial=0.0,
                                                   op0=ALU.mult, op1=ALU.add),
               reads=[b_cst, u.b_sg], writes=[u.b_Ein])
            tot = u.Ein[:, P - 1:P]
            op(act, lambda e: e.activation(out=u.sc[:, 0:1], in_=tot, func=AF.Exp, scale=-LAM),
               reads=[u.b_Ein], writes=[u.b_sc])
            if d == 0:
                op(pool, lambda e: e.tensor_tensor(out=u.Eex[:], in0=u.Ein[:], in1=u.sg[:], op=ALU.subtract),
                   reads=[u.b_Ein, u.b_sg], writes=[u.b_Eex])
            else:
                op(dve, lambda e: e.tensor_copy(out=u.sc[:, 1:2], in_=tot), reads=[u.b_Ein], writes=[u.b_sc])
                op(dve, lambda e: e.tensor_scalar(out=u.Eex[:], in0=u.Ein[:], scalar1=u.sc[:, 1:2], scalar2=-1.0,
                                                  op0=ALU.subtract, op1=ALU.mult),
                   reads=[u.b_Ein, u.b_sc], writes=[u.b_Eex])
                op(pool, lambda e: e.tensor_tensor(out=u.Ein[:], in0=u.Eex[:], in1=u.sg[:], op=ALU.add),
                   reads=[u.b_Eex, u.b_sg], writes=[u.b_Ein])
            yield
            op(act, lambda e: e.activation(out=u.ein[:], in_=u.Ein[:], func=AF.Exp, scale=-LAM),
               reads=[u.b_Ein], writes=[u.b_ein])
            op(act, lambda e: e.activation(out=u.Eex[:], in_=u.Eex[:], func=AF.Exp, scale=-LAM),
               reads=[u.b_Eex], writes=[u.b_Eex])
            op(act, lambda e: e.activation(out=u.Ein[:], in_=u.Ein[:], func=AF.Exp, scale=LAM),
               reads=[u.b_Ein], writes=[u.b_Ein])
            eng_ = u.Ein
            eex_ = u.Eex
            op(dve, lambda e: e.scalar_tensor_tensor(out=u.arT[:, 0:P], in0=u.kk[:], scalar=-1.0, in1=eex_[:],
                                                     op0=ALU.mult, op1=ALU.mult),
               reads=[u.b_kk, u.b_Eex], writes=[u.b_arT])
            op(pool, lambda e: e.tensor_tensor(out=u.arT[:, P:2 * P], in0=rT, in1=u.ein[:], op=ALU.mult),
               reads=[u.b_rk, u.b_ein], writes=[u.b_arT])
            op(dve, lambda e: e.tensor_scalar(out=u.kd[:], in0=u.al[:], scalar1=pvc('k_a', j),
                                               scalar2=pvc('omk_a', j), op0=ALU.mult, op1=ALU.add),
               reads=[u.b_al, b_pv], writes=[u.b_kd])
            op(pool, lambda e: e.tensor_tensor(out=u.kd[:], in0=u.kd[:], in1=kT, op=ALU.mult),
               reads=[u.b_kd, u.b_rk], writes=[u.b_kd])
            op(dve, lambda e: e.tensor_tensor(out=u.ktT[:], in0=u.kd[:], in1=eng_[:], op=ALU.mult),
               reads=[u.b_kd, u.b_Ein], writes=[u.b_ktT])
            op(pool, lambda e: e.tensor_tensor(out=u.al[:], in0=u.al[:], in1=u.kk[:], op=ALU.mult),
               reads=[u.b_al, u.b_kk], writes=[u.b_al])
            op(dve, lambda e: e.tensor_tensor(out=u.btT[:], in0=u.al[:], in1=eng_[:], op=ALU.mult),
               reads=[u.b_al, u.b_Ein], writes=[u.b_btT])
            op(dve, lambda e: e.scalar_tensor_tensor(out=u.pr[:], in0=rT, scalar=pvc('r_k', j), in1=u.kd[:],
                                                     op0=ALU.mult, op1=ALU.mult),
               reads=[u.b_rk, u.b_kd, b_pv], writes=[u.b_pr])
            yield
            op(pe, lambda e: e.transpose(out=banks_bf[Ba][:, 0:P], in_=u.btT[:], identity=g.identb[:]),
               reads=[u.b_btT, g.b_identb], writes=[bbuf[Ba]], sig=False)
            op(pe, lambda e: e.transpose(out=banks_bf[Ba][:, P:2 * P], in_=u.ktT[:], identity=g.identb[:]),
               reads=[u.b_ktT, g.b_identb], writes=[bbuf[Ba]], sig=False)
            op(pe, lambda e: e.matmul(banks[Ba][:, 2 * P:2 * P + 2], lhsT=u.pr[:], rhs=g.hsel[:], start=True, stop=True),
               reads=[u.b_pr, b_cst], writes=[bbuf[Ba]])
            op(act, lambda e: e.activation(out=u.BK[:], in_=banks_bf[Ba][:, 0:2 * P], func=AF.Copy),
               reads=[bbuf[Ba]], writes=[u.b_BK])
            if d == 0:
                bdst, b_bdst = bonF[:, c, 2 * j:2 * j + 2], b_bonF[c]
            else:
                bt_, b_bdst = cx.bonB
                bdst = bt_[:, 2 * j:2 * j + 2]
            op(act, lambda e: e.activation(out=bdst, in_=banks[Ba][:, 2 * P:2 * P + 2], func=AF.Copy),
               reads=[bbuf[Ba]], writes=[b_bdst])
            yield
            AT = [u.AT0, u.AT1]
            b_AT = [u.b_AT0, u.b_AT1]
            Cc = [[u.C00, u.C01], [u.C10, u.C11]]
            b_C = [[u.b_C00, u.b_C01], [u.b_C10, u.b_C11]]
            hb = [Bb, Bc]
            for h in range(2):
                B_ = hb[h]
                op(pe, lambda e, h=h, B_=B_: e.matmul(banks[B_][:, 0:2 * P], lhsT=u.btT[hq[h], :], rhs=u.arT[hq[h], :],
                                                      start=True, stop=True),
                   reads=[u.b_btT, u.b_arT], writes=[bbuf[B_]], sig=False)
                op(pe, lambda e, h=h, B_=B_: e.matmul(banks[B_][:, 2 * P:4 * P], lhsT=u.ktT[hq[h], :],
                                                      rhs=u.arT[hq[h], :], start=True, stop=True),
                   reads=[u.b_ktT, u.b_arT], writes=[bbuf[B_]])
                op(dve, lambda e, h=h, B_=B_: e.tensor_tensor(out=AT[h][:], in0=banks[B_][:, :], in1=mq[:],
                                                              op=ALU.mult),
                   reads=[bbuf[B_], b_cst], writes=[b_AT[h]])
                op(pe, lambda e, h=h, B_=B_: e.matmul(banks[B_][:, 0:P], lhsT=u.arT[hq[h], 0:P], rhs=u.btT[hq[h], :],
                                                      start=True, stop=True),
                   reads=[u.b_btT, u.b_arT], writes=[bbuf[B_]])
                op(dve, lambda e, h=h, B_=B_: e.tensor_tensor(out=Cc[h][0][:, 0:P], in0=banks[B_][:, 0:P], in1=mA,
                                                              op=ALU.mult),
                   reads=[bbuf[B_], b_cst], writes=[b_C[h][0]])
                op(pool, lambda e, h=h: e.tensor_tensor(out=Cc[h][1][:, 2 * P:3 * P], in0=AT[h][:, 0:P],
                                                        in1=g.identb[:], op=ALU.add),
                   reads=[b_AT[h], g.b_identb], writes=[b_C[h][1]])
            yield
            for h in range(2):
                B_ = hb[h]
                A0 = Cc[h][0][:, 0:P]
                B0 = AT[h][:, 0:P]
                op(pe, lambda e, B_=B_, A0=A0, B0=B0: e.matmul(banks[B_][:, 0:P], lhsT=B0, rhs=A0, start=True, stop=True),
                   reads=[b_AT[h], b_C[h][0]], writes=[bbuf[B_]], sig=False)
                op(pe, lambda e, B_=B_, A0=A0, B0=B0: e.matmul(banks[B_][:, P:2 * P], lhsT=A0, rhs=B0, start=True,
                                                               stop=True),
                   reads=[b_AT[h], b_C[h][0]], writes=[bbuf[B_]])
                ev = dve if h == 0 else act
                if ev is dve:
                    op(dve, lambda e, h=h, B_=B_: e.tensor_copy(out=Cc[h][1][:, 0:2 * P], in_=banks[B_][:, 0:2 * P]),
                       reads=[bbuf[B_]], writes=[b_C[h][1]])
                else:
                    op(act, lambda e, h=h, B_=B_: e.activation(out=Cc[h][1][:, 0:2 * P], in_=banks[B_][:, 0:2 * P],
                                                               func=AF.Copy),
                       reads=[bbuf[B_]], writes=[b_C[h][1]])
            yield
            for lev in range(1, 7):
                src_i = lev % 2
                dst_i = 1 - src_i
                for h in range(2):
                    B_ = hb[h]
                    S = Cc[h][src_i]
                    Dd = Cc[h][dst_i]
                    bS, bD = b_C[h][src_i], b_C[h][dst_i]
                    Ak, Bk, Mk = S[:, 0:P], S[:, P:2 * P], S[:, 2 * P:3 * P]
                    lo = 0
                    if lev <= 5:
                        op(pe, lambda e, B_=B_, Ak=Ak, Bk=Bk: e.matmul(banks[B_][:, 0:P], lhsT=Bk, rhs=Ak, start=True,
                                                                       stop=True),
                           reads=[bS], writes=[bbuf[B_]], sig=False)
                    else:
                        lo = 2 * P
                    r0 = P if lev <= 4 else 2 * P
                    op(pe, lambda e, B_=B_, Ak=Ak, S=S, r0=r0: e.matmul(banks[B_][:, r0:3 * P], lhsT=Ak, rhs=S[:, r0:3 * P],
                                                                        start=True, stop=True),
                       reads=[bS], writes=[bbuf[B_]])
                    if lev <= 5:
                        hi_ = 2 * P if lev <= 4 else P
                        if h == 1:
                            op(act, lambda e, B_=B_, Dd=Dd, hi_=hi_: e.activation(out=Dd[:, 0:hi_],
                                                                                  in_=banks[B_][:, 0:hi_], func=AF.Copy),
                               reads=[bbuf[B_]], writes=[bD])
                        else:
                            op(dve, lambda e, B_=B_, Dd=Dd, hi_=hi_: e.tensor_copy(out=Dd[:, 0:hi_],
                                                                                   in_=banks[B_][:, 0:hi_]),
                               reads=[bbuf[B_]], writes=[bD])
                    op(dve, lambda e, B_=B_, Dd=Dd, Mk=Mk: e.tensor_tensor(out=Dd[:, 2 * P:3 * P],
                                                                           in0=banks[B_][:, 2 * P:3 * P], in1=Mk,
                                                                           op=ALU.add),
                       reads=[bbuf[B_], bS], writes=[bD])
                yield
            Mfin = [Cc[h][1][:, 2 * P:3 * P] for h in range(2)]
            b_Mfin = [b_C[h][1] for h in range(2)]
            b_Hj = g_bH[j]
            for h in range(2):
                op(pe, lambda e, h=h: e.matmul(banks[Ba][:, h * 64:(h + 1) * 64], lhsT=u.arT[hq[h], 0:P],
                                               rhs=Hb[hq[h], j, :], start=True, stop=False),
                   reads=[u.b_arT, b_Hj], writes=[bbuf[Ba]], sig=False)
                op(pe, lambda e, h=h: e.matmul(banks[Ba][:, h * 64:(h + 1) * 64], lhsT=AT[h][:, 2 * P:3 * P],
                                               rhs=vt[:, (2 * j + h) * 64:(2 * j + h + 1) * 64], start=False, stop=True),
                   reads=[b_AT[h], b_vt], writes=[bbuf[Ba]], sig=(h == 1))
            op(act, lambda e: e.activation(out=u.Xp[:], in_=banks[Ba][:, 0:P], func=AF.Copy),
               reads=[bbuf[Ba]], writes=[u.b_Xp])
            for h in range(2):
                op(pe, lambda e, h=h: e.matmul(banks[Ba][:, P + h * 64:P + (h + 1) * 64], lhsT=Mfin[h],
                                               rhs=u.Xp[:, h * 64:(h + 1) * 64], start=True, stop=True),
                   reads=[b_Mfin[h], u.b_Xp], writes=[bbuf[Ba]], sig=(h == 1))
            op(dve, lambda e: e.tensor_copy(out=u.Up[:], in_=banks[Ba][:, P:2 * P]), reads=[bbuf[Ba]], writes=[u.b_Up])
            op(dve, lambda e: e.tensor_scalar(out=u.Hs[:], in0=Hf[:, j, :], scalar1=u.sc[:, 0:1], scalar2=None,
                                               op0=ALU.mult), reads=[b_Hj, u.b_sc], writes=[u.b_Hs])
            yield
            for h in range(2):
                yo = 2 * P + h * 64
                op(pe, lambda e, h=h, yo=yo: e.matmul(banks[Ba][:, yo:yo + 64], lhsT=u.arT[hq[h], P:2 * P],
                                                      rhs=Hb[hq[h], j, :], start=True, stop=False),
                   reads=[u.b_arT, b_Hj], writes=[bbuf[Ba]], sig=False)
                op(pe, lambda e, h=h, yo=yo: e.matmul(banks[Ba][:, yo:yo + 64], lhsT=AT[h][:, P:2 * P],
                                                      rhs=u.Up[:, h * 64:(h + 1) * 64], start=False, stop=False),
                   reads=[b_AT[h], u.b_Up], writes=[bbuf[Ba]], sig=False)
                op(pe, lambda e, h=h, yo=yo: e.matmul(banks[Ba][:, yo:yo + 64], lhsT=AT[h][:, 3 * P:4 * P],
                                                      rhs=vt[:, (2 * j + h) * 64:(2 * j + h + 1) * 64],
                                                      start=False, stop=True),
                   reads=[b_AT[h], b_vt], writes=[bbuf[Ba]], sig=False)
            op(pe, lambda e: e.matmul(banks[Ba][:, 3 * P:4 * P], lhsT=u.BK[:, 0:P], rhs=u.Up[:], start=True, stop=False),
               reads=[u.b_BK, u.b_Up], writes=[bbuf[Ba]], sig=False)
            op(pe, lambda e: e.matmul(banks[Ba][:, 3 * P:4 * P], lhsT=u.BK[:, P:2 * P], rhs=vt[:, j * P:(j + 1) * P],
                                      start=False, stop=True),
               reads=[u.b_BK, b_vt], writes=[bbuf[Ba]])
            if d == 0:
                ydst = x1[:, c, :].bitcast(BF16)[:, j * P:(j + 1) * P]
                b_yd = bx1[c]
            else:
                yb_, b_yd = cx.yb
                ydst = yb_[:, j * P:(j + 1) * P]
            op(act, lambda e: e.activation(out=ydst, in_=banks[Ba][:, 2 * P:3 * P], func=AF.Copy),
               reads=[bbuf[Ba]], writes=[b_yd])
            for h in range(2):
                op(dve, lambda e, h=h: e.scalar_tensor_tensor(out=Hf[hq[h], j, :],
                                                              in0=banks[Ba][hq[h], 3 * P + h * 64:3 * P + (h + 1) * 64],
                                                              scalar=u.sc[hq[h], 0:1], in1=u.Hs[hq[h], :],
                                                              op0=ALU.mult, op1=ALU.add),
                   reads=[bbuf[Ba], u.b_sc, u.b_Hs], writes=[b_Hj])
            op(pool, lambda e: e.tensor_copy(out=Hb[:, j, :], in_=Hf[:, j, :]), reads=[b_Hj], writes=[b_Hj])
            yield

        g_bH = b_H

        def gen_Ba(cx):
            c = cx.c
            yb_, b_yb = cx.yb
            zs, b_zs = cx.zs
            vt, b_vt = cx.vt
            bonB, b_bonB = cx.bonB
            yf = x1[:, c, :].bitcast(BF16)[:, 0:D]
            op(dve, lambda e: e.tensor_tensor(out=yb_[:], in0=yb_[:], in1=yf, op=ALU.add),
               reads=[b_yb, bx1[c]], writes=[b_yb])
            y3 = yb_[:].rearrange("p (h n) -> p h n", h=16)
            op(dve, lambda e: e.tensor_reduce(out=gst[:, 0, :], in_=y3, axis=AX.X, op=ALU.add),
               reads=[b_yb], writes=[b_gst])
            op(act, lambda e: e.activation(out=tmpA[:], in_=yb_[:], func=AF.Square), reads=[b_yb], writes=[b_tmpA])
            op(dve, lambda e: e.tensor_reduce(out=gst[:, 1, :], in_=tmpA[:].rearrange("p (h n) -> p h n", h=16),
                                              axis=AX.X, op=ALU.add), reads=[b_tmpA], writes=[b_gst])
            yield
            op(dve, lambda e: e.tensor_scalar(out=gst[:, 2, :], in0=gst[:, 0, :], scalar1=1.0 / 64, scalar2=None,
                                              op0=ALU.mult), reads=[b_gst], writes=[b_gst])
            op(dve, lambda e: e.tensor_tensor(out=gst[:, 3, :], in0=gst[:, 2, :], in1=gst[:, 2, :], op=ALU.mult),
               reads=[b_gst], writes=[b_gst])
            op(dve, lambda e: e.scalar_tensor_tensor(out=gst[:, 3, :], in0=gst[:, 1, :], scalar=1.0 / 64,
                                                     in1=gst[:, 3, :], op0=ALU.mult, op1=ALU.subtract),
               reads=[b_gst], writes=[b_gst])
            op(dve, lambda e: e.tensor_scalar(out=gst[:, 3, :], in0=gst[:, 3, :], scalar1=LNX_EPS, scalar2=None,
                                              op0=ALU.add), reads=[b_gst], writes=[b_gst])
            op(act, lambda e: e.activation(out=gst[:, 3, :], in_=gst[:, 3, :], func=AF.Sqrt), reads=[b_gst],
               writes=[b_gst])
            op(dve, lambda e: e.reciprocal(out=gst[:, 4, :], in_=gst[:, 3, :]), reads=[b_gst], writes=[b_gst])
            op(dve, lambda e: e.tensor_tensor(out=gst[:, 5, :], in0=bonF[:, c, :], in1=bonB[:], op=ALU.add),
               reads=[b_bonF[c], b_bonB], writes=[b_gst])
            mean_b = gst[:, 2, :].unsqueeze(2).broadcast_to([P, 16, 64])
            rstd_b = gst[:, 4, :].unsqueeze(2).broadcast_to([P, 16, 64])
            bon_b = gst[:, 5, :].unsqueeze(2).broadcast_to([P, 16, 64])
            op(dve, lambda e: e.tensor_tensor(out=y3, in0=y3, in1=mean_b, op=ALU.subtract),
               reads=[b_yb, b_gst], writes=[b_yb])
            op(pool, lambda e: e.tensor_tensor(out=y3, in0=y3, in1=rstd_b, op=ALU.mult),
               reads=[b_yb, b_gst], writes=[b_yb])
            yield
            op(dve, lambda e: e.tensor_tensor(out=yb_[:], in0=yb_[:], in1=lnxb[:, 0, :], op=ALU.mult),
               reads=[b_yb, b_lc], writes=[b_yb])
            op(pool, lambda e: e.tensor_tensor(out=yb_[:], in0=yb_[:], in1=lnxb[:, 1, :], op=ALU.add),
               reads=[b_yb, b_lc], writes=[b_yb])
            t3 = tmpA[:].rearrange("p (h n) -> p h n", h=16)
            op(dve, lambda e: e.tensor_tensor(out=t3, in0=vt[:].rearrange("p (h n) -> p h n", h=16), in1=bon_b,
                                              op=ALU.mult), reads=[b_vt, b_gst], writes=[b_tmpA])
            op(pool, lambda e: e.tensor_tensor(out=yb_[:], in0=yb_[:], in1=tmpA[:], op=ALU.add),
               reads=[b_yb, b_tmpA], writes=[b_yb])
            op(dve, lambda e: e.tensor_tensor(out=yg[:], in0=yb_[:], in1=zs[:], op=ALU.mult),
               reads=[b_yb, b_zs], writes=[b_yg])
            yield

        def gen_Bb(cx):
            c = cx.c
            xin, b_xin = tmpA, b_tmpA
            T_.dma(sp, xin[:], g.x_in[si, c * P:(c + 1) * P, :], writes=[b_xin])
            for kc in range(KC):
                op(pe, lambda e, kc=kc: e.transpose(out=banks_bf[0][:, kc * P:(kc + 1) * P],
                                                    in_=yg[:, kc * P:(kc + 1) * P], identity=g.identb[:]),
                   reads=[b_yg, g.b_identb], writes=[bbuf[0]], sig=(kc == KC - 1))
            op(act, lambda e: e.activation(out=ygT[:].rearrange("p k n -> p (k n)"), in_=banks_bf[0][:, :],
                                           func=AF.Copy), reads=[bbuf[0]], writes=[b_ygT])
            yield
            for half in range(2):
                for q2 in range(NQ2):
                    w, b_w = g.loadA(L, 2, half * NQ2 + q2)
                    for kc in range(KC):
                        op(pe, lambda e, kc=kc, w=w, q2=q2: e.matmul(banks[0][:, q2 * W2:(q2 + 1) * W2],
                                                                    lhsT=ygT[:, kc, :], rhs=w[:, kc, :],
                                                                    start=(kc == 0), stop=(kc == KC - 1)),
                           reads=[b_ygT, b_w], writes=[bbuf[0]], sig=(kc == KC - 1 and q2 == NQ2 - 1))
                op(act, lambda e, half=half: e.activation(out=tmpB[:, half * 512:(half + 1) * 512], in_=banks[0][:, :],
                                                          func=AF.Copy), reads=[bbuf[0]], writes=[b_tmpB])
                yield
            op(act, lambda e: e.activation(out=yg[:], in_=tmpB[:], func=AF.Square, accum_out=stat[:, 4:5]),
               reads=[b_tmpB], writes=[b_yg, b_stat])
            op(dve, lambda e: e.tensor_scalar(out=stat[:, 5:6], in0=stat[:, 4:5], scalar1=1.0 / D, scalar2=RMS_EPS,
                                              op0=ALU.mult, op1=ALU.add), reads=[b_stat], writes=[b_stat])
            op(act, lambda e: e.activation(out=stat[:, 6:7], in_=stat[:, 5:6], func=AF.Sqrt), reads=[b_stat],
               writes=[b_stat])
            op(dve, lambda e: e.reciprocal(out=stat[:, 7:8], in_=stat[:, 6:7]), reads=[b_stat], writes=[b_stat])
            op(dve, lambda e: e.scalar_tensor_tensor(out=tmpB[:], in0=tmpB[:], scalar=stat[:, 7:8], in1=postg[:],
                                                     op0=ALU.mult, op1=ALU.mult),
               reads=[b_tmpB, b_stat, b_lc], writes=[b_tmpB])
            op(pool, lambda e: e.tensor_tensor(out=x1[:, c, :], in0=tmpB[:], in1=xin[:], op=ALU.add),
               reads=[b_tmpB, b_xin], writes=[bx1[c]])
            yield

        def gen_Z():
            op(dve, lambda e: e.memset(Hf[:], 0.0), writes=b_H)
            op(pool, lambda e: e.memset(Hb[:], 0.0), writes=b_H)
            yield

        tasks = []
        lastB = None
        lastF = None
        for d in range(2):
            order = list(range(NCH)) if d == 0 else list(range(NCH - 1, -1, -1))
            tz = Task("Z%d" % d)
            tz.gen = gen_Z()
            tz.deps += [t for t in tasks if t.name.startswith("U")]
            tasks.append(tz)
            cxs = [None] * NCH

            def make_Na(i):
                cx = NS()
                cx.c = order[i]
                tn = Task("N%d_%d" % (d, cx.c))
                _, cx.xin = p_xin.acquire(tn)
                if i > 0:
                    tn.deps.append(cxs[i - 1].tnb)
                tn.gen = gen_Na(cx)
                cx.tna = tn
                cxs[i] = cx
                tasks.append(tn)

            def make_Nb(i):
                cx = cxs[i]
                tn = Task("N%d_%db" % (d, cx.c))
                cx.ks, cx.slot = p_slot.acquire(tn)
                tn.deps.append(cx.tna)
                prev = cxs[i - 1] if i > 0 else None
                if prev is not None:
                    p_slot.share(tn, prev.ks)
                tn.gen = gen_Nb(cx, d, i == 0, i == NCH - 1, prev)
                cx.tnb = tn
                cx.tn = tn
                tasks.append(tn)

            make_Na(0)
            make_Nb(0)
            if NCH > 1:
                make_Na(1)
                make_Nb(1)
            pendBb = None
            for i in range(NCH):
                cx = cxs[i]
                c = cx.c
                tf = Task("F%d_%d" % (d, c))
                tf.deps.append(cx.tn)
                if i + 1 < NCH:
                    tf.deps.append(cxs[i + 1].tn)
                if lastF is not None:
                    tf.deps.append(lastF)
                lastF = tf
                p_slot.share(tf, cx.ks)
                klr, cx.lr = p_lr.acquire(tf)
                klk, cx.lk = p_lk.acquire(tf)
                kvt, cx.vt = p_vt.acquire(tf)
                kth, cx.th = p_th.acquire(tf)
                if d == 1:
                    kzs, cx.zs = p_zs.acquire(tf)
                tf.gen = gen_F(cx, d)
                tasks.append(tf)
                tba = None
                if d == 1:
                    tba = Task("B%d" % c)
                    _, cx.yb = p_yb.acquire(tba)
                    _, cx.bonB = p_bonB.acquire(tba)
                tus = []
                for j in range(KC):
                    tu = Task("U%d_%d_%d" % (d, c, j))
                    tu.deps += [tf, tz]
                    if d == 1:
                        tu.deps += list(tba.deps)
                    _, uset = p_uset.acquire(tu)
                    p_lr.share(tu, klr)
                    p_lk.share(tu, klk)
                    p_vt.share(tu, kvt)
                    p_th.share(tu, kth)
                    tu.gen = gen_U(cx, d, j, uset)
                    if i > 0:
                        tu.deps.append(cxs[i - 1].tus[j])
                    tus.append(tu)
                    tasks.append(tu)
                    if j == 1 and i + 2 < NCH:
                        make_Na(i + 2)
                    if j == 3 and pendBb is not None:
                        tasks.append(pendBb)
                        pendBb = None
                cx.tus = tus
                if d == 1:
                    tba.deps += tus + [tf]
                    if lastB is not None:
                        tba.deps.append(lastB)
                    p_vt.share(tba, kvt)
                    p_zs.share(tba, kzs)
                    tba.gen = gen_Ba(cx)
                    tasks.append(tba)
                    tbb = Task("B%db" % c)
                    tbb.deps.append(tba)
                    tbb.gen = gen_Bb(cx)
                    lastB = tbb
                    if i == NCH - 1:
                        tasks.append(tbb)
                    else:
                        pendBb = tbb
                if i + 2 < NCH:
                    make_Nb(i + 2)
        import os
        allow = os.environ.get("L0_TASKS", "ZNFUB")
        tasks = [t for t in tasks if t.name[0] in allow]
        for t in tasks:
            t.deps = [d_ for d_ in t.deps if d_.name[0] in allow]
        run_tasks(tasks, window=WINDOW)


GRID_W = 64
N_ROWS = T_SEQ // GRID_W
NEG = -30000.0


def _rs(r):
    return min(max(r - 4, 0), N_ROWS - 8)


def l1_geometry():
    geo = []
    pats = []
    for i in range(N_ROWS // 2):
        rows = [2 * i, 2 * i + 1]
        lo = min(_rs(r) for r in rows)
        hi = max(_rs(r) + 7 for r in rows)
        ent = []
        for kt in range(lo // 2, hi // 2 + 1):
            pat = tuple(tuple(1 if _rs(2 * i + rl) <= 2 * kt + krl < _rs(2 * i + rl) + 8 else 0 for krl in range(2))
                        for rl in range(2))
            if pat not in pats:
                pats.append(pat)
            ent.append((kt, (2 * (kt - i) + 6) // 2, pats.index(pat)))
        geo.append(ent)
    return geo, pats


def l1_prologue(g):
    nc, T_, op, sb = g.nc, g.T_, g.op, g.sb
    pe, act, dve, pool, sp = g.pe, g.act, g.dve, g.pool, g.sp
    geo, pats = l1_geometry()
    g.l1_geo, g.l1_pats = geo, pats
    NMK = len(pats)
    g.padD = nc.dram_tensor("padD", [240, P], F32, kind="Internal").ap()
    g.rbD = nc.dram_tensor("rbD", [P, 16 * 7 * P], BF16, kind="Internal").ap()
    g.mkD = nc.dram_tensor("mkD", [P, (NMK + 1) * P], BF16, kind="Internal").ap()
    x1 = g.x1
    with ExitStack() as p1:
        b_t = Buf("l1pro")
        padt = sb(p1, "padt", [120, 2, P], F32)
        op(dve, lambda e: e.memset(padt[:], 0.0), writes=[b_t])
        rp = g.wd['na_rpb'][0].rearrange("h r m -> (h r) m")
        for gi in range(2):
            T_.dma(sp, padt[:, gi, 48:79], rp[gi * 120:(gi + 1) * 120, :], writes=[b_t])
        for gi in range(2):
            T_.dma(sp, g.padD[gi * 120:(gi + 1) * 120, :], padt[:, gi, :], reads=[b_t])
        T_.barrier()
        T_.finish()
        Hs = x1[:].rearrange("p c d -> p (c d)")[:, 0:240 * 64].rearrange("p (b k) -> p b k", k=64)
        b_hs = Buf("Hs")
        for h in range(16):
            for r2 in range(2):
                src = bass.AP(tensor=g.padD.tensor, offset=h * 15 * P, ap=[[1, 64], [P, 15], [1, 64]])
                T_.dma(sp, Hs[64 * r2:64 * r2 + 64, h * 15:(h + 1) * 15, :], src, writes=[b_hs])
        RBs = sb(p1, "RBs", [P, 16, 7, P], BF16)
        b_rb = Buf("RBs")
        op(pool, lambda e: e.memset(RBs[:].rearrange("p h d k -> p (h d k)"), 0.0), writes=[b_rb])
        engs = [dve, pool, act]
        n = 0
        for h in range(16):
            for rl in range(2):
                for krl in range(2):
                    dis = [di for di in range(7) if 0 <= 2 * di + 1 + krl - rl <= 14]
                    d0, nd = dis[0], len(dis)
                    ri0 = 2 * d0 + 1 + krl - rl
                    srcv = Hs[64 * rl:64 * rl + 64, h * 15 + ri0:h * 15 + ri0 + 2 * (nd - 1) + 1:2, :]
                    dstv = RBs[64 * rl:64 * rl + 64, h, d0:d0 + nd, 64 * krl:64 * krl + 64]
                    e_ = engs[n % 3]
                    n += 1
                    if e_ is act:
                        op(act, lambda e, s=srcv, d=dstv: e.activation(out=d, in_=s, func=AF.Copy),
                           reads=[b_hs], writes=[b_rb])
                    else:
                        op(e_, lambda e, s=srcv, d=dstv: e.tensor_copy(out=d, in_=s), reads=[b_hs], writes=[b_rb])
        T_.dma(sp, g.rbD[:, :], RBs[:].rearrange("p h d k -> p (h d k)"), reads=[b_rb])
        ior = sb(p1, "ior1", [P, P], F32)
        ioc = sb(p1, "ioc1", [P, P], F32)
        t1 = sb(p1, "mt1", [P, P], F32)
        t2 = sb(p1, "mt2", [P, P], F32)
        cm = sb(p1, "cm", [P, P], BF16)
        neg = sb(p1, "negt", [P, P], BF16)
        MKs = sb(p1, "MKs", [P, NMK + 1, P], BF16)
        b_m = Buf("mk")
        op(pool, lambda e: e.iota(ior[:], pattern=[[0, P]], base=0, channel_multiplier=1,
                                  allow_small_or_imprecise_dtypes=True), writes=[b_m])
        op(pool, lambda e: e.iota(ioc[:], pattern=[[1, P]], base=0, channel_multiplier=0,
                                  allow_small_or_imprecise_dtypes=True), writes=[b_m])
        op(dve, lambda e: e.tensor_tensor(out=t1[:], in0=ior[:], in1=ioc[:], op=ALU.add), reads=[b_m], writes=[b_m])
        op(dve, lambda e: e.tensor_scalar(out=t2[:], in0=t1[:], scalar1=63.0, scalar2=None, op0=ALU.is_equal),
           reads=[b_m], writes=[b_m])
        op(dve, lambda e: e.tensor_scalar(out=t1[:], in0=t1[:], scalar1=191.0, scalar2=None, op0=ALU.is_equal),
           reads=[b_m], writes=[b_m])
        op(dve, lambda e: e.tensor_tensor(out=g.Jrev[:], in0=t1[:], in1=t2[:], op=ALU.add), reads=[b_m],
           writes=[g.b_cst])
        op(dve, lambda e: e.tensor_scalar(out=t1[:], in0=ior[:], scalar1=63.5, scalar2=-64.0, op0=ALU.is_gt,
                                          op1=ALU.mult), reads=[b_m], writes=[b_m])
        op(dve, lambda e: e.tensor_tensor(out=t1[:], in0=t1[:], in1=ior[:], op=ALU.add), reads=[b_m], writes=[b_m])
        op(dve, lambda e: e.tensor_scalar(out=t1[:], in0=t1[:], scalar1=-1.0, scalar2=55.0, op0=ALU.mult,
                                          op1=ALU.add), reads=[b_m], writes=[b_m])
        op(dve, lambda e: e.tensor_scalar(out=t1[:], in0=t1[:], scalar1=0.0, scalar2=48.0, op0=ALU.max, op1=ALU.min),
           reads=[b_m], writes=[b_m])
        op(dve, lambda e: e.tensor_scalar(out=t2[:], in0=ioc[:], scalar1=63.5, scalar2=-64.0, op0=ALU.is_gt,
                                          op1=ALU.mult), reads=[b_m], writes=[b_m])
        op(dve, lambda e: e.tensor_tensor(out=t2[:], in0=t2[:], in1=ioc[:], op=ALU.add), reads=[b_m], writes=[b_m])
        op(dve, lambda e: e.tensor_tensor(out=t2[:], in0=t2[:], in1=t1[:], op=ALU.subtract), reads=[b_m],
           writes=[b_m])
        op(dve, lambda e: e.tensor_scalar(out=t1[:], in0=t2[:], scalar1=-0.5, scalar2=None, op0=ALU.is_gt),
           reads=[b_m], writes=[b_m])
        op(dve, lambda e: e.tensor_scalar(out=t2[:], in0=t2[:], scalar1=15.5, scalar2=None, op0=ALU.is_lt),
           reads=[b_m], writes=[b_m])
        op(dve, lambda e: e.tensor_tensor(out=t1[:], in0=t1[:], in1=t2[:], op=ALU.mult), reads=[b_m], writes=[b_m])
        op(dve, lambda e: e.tensor_scalar(out=cm[:], in0=t1[:], scalar1=-1.0, scalar2=-NEG, op0=ALU.add, op1=ALU.mult),
           reads=[b_m], writes=[b_m])
        op(dve, lambda e: e.memset(neg[:], NEG), writes=[b_m])
        for pi, pat in enumerate(pats):
            for rl in range(2):
                for krl in range(2):
                    src_t = cm if pat[rl][krl] else neg
                    op(pool, lambda e, pi=pi, rl=rl, krl=krl, s=src_t: e.tensor_copy(
                        out=MKs[64 * rl:64 * rl + 64, pi, 64 * krl:64 * krl + 64],
                        in_=s[64 * rl:64 * rl + 64, 64 * krl:64 * krl + 64]), reads=[b_m], writes=[b_m])
        op(pool, lambda e: e.tensor_copy(out=MKs[:, NMK, :], in_=neg[:]), reads=[b_m], writes=[b_m])
        T_.dma(sp, g.mkD[:, :], MKs[:].rearrange("p n k -> p (n k)"), reads=[b_m])
        T_.barrier()
        T_.finish()


def layer1(g, si):
    nc, T_, op = g.nc, g.T_, g.op
    pe, act, dve, pool, sp = g.pe, g.act, g.dve, g.pool, g.sp
    banks, bbuf, banks_bf = g.banks, g.bbuf, g.banks_bf
    NCH, x1, bx1, pv, b_pv, pvc = g.NCH, g.x1, g.bx1, g.pv, g.b_pv, g.pvc
    wd, sb = g.wd, g.sb
    b_cst = g.b_cst
    L = 1
    geo, pats = g.l1_geo, g.l1_pats
    NMK = len(pats)
    hq = [slice(0, 64), slice(64, 128)]

    with ExitStack() as l1:
        postg = sb(l1, "postg1", [P, D], F32)
        RB = sb(l1, "RB", [P, 16, 7, P], BF16)
        MK = sb(l1, "MK", [P, NMK + 1, P], BF16)
        b_lc = Buf("l1consts")
        T_.dma(sp, postg[:], wd['post_norm_g'][L].partition_broadcast(P), writes=[b_lc])
        T_.dma(sp, RB[:].rearrange("p h d k -> p (h d k)"), g.rbD[:, :], writes=[b_lc])
        T_.dma(sp, MK[:].rearrange("p n k -> p (n k)"), g.mkD[:, :], writes=[b_lc])

        def mk(nm, sh, dt, n):
            return [(sb(l1, "%s%d" % (nm, i), sh, dt), Buf("%s%d" % (nm, i))) for i in range(n)]

        xs, b_xs = mk("xs1", [P, D], BF16, 1)[0]
        stat = sb(l1, "stat1", [P, 8], F32)
        b_stat = Buf("stat1")
        p_xn = RPool(mk("xnT1", [P, KC, P], BF16, 4))
        p_kT = RPool(mk("kT", [P, KC, P], BF16, 7))
        p_va = RPool(mk("Vaug", [P, 16, 65], BF16, 7))
        zs, b_zs = mk("zs1", [P, D], BF16, 1)[0]
        og, b_og = mk("og", [P, D], F32, 1)[0]
        yg, b_yg = mk("yg1", [P, D], BF16, 1)[0]
        ygT, b_ygT = mk("ygT1", [P, KC, P], BF16, 1)[0]
        tmpA, b_tmpA = mk("tmpA1", [P, D], F32, 1)[0]
        tmpB, b_tmpB = mk("tmpB1", [P, D], F32, 1)[0]

        def mkset(i):
            u = NS()
            def t(nm, sh, dt):
                setattr(u, nm, sb(l1, "a%d_%s" % (i, nm), sh, dt))
                setattr(u, "b_" + nm, Buf("a%d_%s" % (i, nm)))
            t("qT", [P, P], BF16)
            t("PT", [P, 5, P], BF16)
            t("rc", [P, 2], F32)
            u.banks = (2 + 3 * i, 3 + 3 * i, 4 + 3 * i)
            return u

        p_uset = RPool([mkset(0), mkset(1)])
        for (va, b_va) in p_va.items:
            op(pool, lambda e, va=va: e.memset(va[:, :, 64:65], 1.0), writes=[b_va])

        def gen_KV(cx):
            t = cx.t
            xn, b_xn = cx.xn
            kT, b_kT = cx.kT
            va, b_va = cx.va
            xsrc = x1[:, t, :]
            op(act, lambda e: e.activation(out=xs[:], in_=xsrc, func=AF.Square, accum_out=stat[:, 0:1]),
               reads=[bx1[t]], writes=[b_xs, b_stat])
            op(dve, lambda e: e.tensor_scalar(out=stat[:, 1:2], in0=stat[:, 0:1], scalar1=1.0 / D, scalar2=RMS_EPS,
                                              op0=ALU.mult, op1=ALU.add), reads=[b_stat], writes=[b_stat])
            op(act, lambda e: e.activation(out=stat[:, 2:3], in_=stat[:, 1:2], func=AF.Sqrt), reads=[b_stat],
               writes=[b_stat])
            op(dve, lambda e: e.reciprocal(out=stat[:, 3:4], in_=stat[:, 2:3]), reads=[b_stat], writes=[b_stat])
            op(act, lambda e: e.activation(out=xs[:], in_=xsrc, func=AF.Copy, scale=stat[:, 3:4]),
               reads=[bx1[t], b_stat], writes=[b_xs])
            yield
            for kc in range(KC):
                op(pe, lambda e, kc=kc: e.transpose(out=banks_bf[0][:, kc * P:(kc + 1) * P],
                                                    in_=xs[:, kc * P:(kc + 1) * P], identity=g.identb[:]),
                   reads=[b_xs, g.b_identb], writes=[bbuf[0]], sig=(kc == KC - 1))
            for kc in range(KC):
                sc = pvc('pre_g', L * 8 + kc)
                if kc % 2 == 0:
                    op(dve, lambda e, kc=kc, sc=sc: e.tensor_scalar(out=xn[:, kc, :],
                                                                  in0=banks_bf[0][:, kc * P:(kc + 1) * P],
                                                                  scalar1=sc, scalar2=None, op0=ALU.mult),
                       reads=[bbuf[0], b_pv], writes=[b_xn])
                else:
                    op(act, lambda e, kc=kc, sc=sc: e.activation(out=xn[:, kc, :],
                                                               in_=banks_bf[0][:, kc * P:(kc + 1) * P],
                                                               func=AF.Copy, scale=sc),
                       reads=[bbuf[0], b_pv], writes=[b_xn])
            yield
            for j0 in range(0, KC, 4):
                for j in range(j0, j0 + 4):
                    w, b_w = g.loadB(L, 1, j)
                    for kc in range(KC):
                        op(pe, lambda e, kc=kc, j=j, w=w: e.matmul(banks[1][:, (j - j0) * P:(j - j0 + 1) * P],
                                                                  lhsT=w[:, kc, :], rhs=xn[:, kc, :],
                                                                  start=(kc == 0), stop=(kc == KC - 1)),
                           reads=[b_w, b_xn], writes=[bbuf[1]], sig=(kc == KC - 1 and j == j0 + 3))
                for j in range(j0, j0 + 4):
                    op(act, lambda e, j=j: e.activation(out=kT[:, j, :], in_=banks[1][:, (j - j0) * P:(j - j0 + 1) * P],
                                                        func=AF.Identity, bias=pvc('bk', j)),
                       reads=[bbuf[1], b_pv], writes=[b_kT])
                yield
            for half in range(2):
                for q2 in range(NQ2):
                    w, b_w = g.loadA(L, 0, half * NQ2 + q2)
                    cs = slice(q2 * W2, (q2 + 1) * W2)
                    for kc in range(KC):
                        op(pe, lambda e, kc=kc, w=w, cs=cs: e.matmul(banks[1][:, cs], lhsT=xn[:, kc, :], rhs=w[:, kc, :],
                                                                    start=(kc == 0), stop=False),
                           reads=[b_xn, b_w], writes=[bbuf[1]], sig=False)
                    bo = 2 * D + half * 512 + q2 * W2
                    op(pe, lambda e, bo=bo, cs=cs: e.matmul(banks[1][:, cs], lhsT=g.onesrow[:],
                                                           rhs=g.brow_hi[0:1, bo:bo + W2], start=False, stop=False),
                       reads=[b_cst], writes=[bbuf[1]], sig=False)
                    op(pe, lambda e, bo=bo, cs=cs: e.matmul(banks[1][:, cs], lhsT=g.onesrow[:],
                                                           rhs=g.brow_lo[0:1, bo:bo + W2], start=False, stop=True),
                       reads=[b_cst], writes=[bbuf[1]], sig=(q2 == NQ2 - 1))
                op(act, lambda e, half=half: e.activation(out=va[:, half * 8:(half + 1) * 8, 0:64],
                                                          in_=banks[1][:, :].rearrange("p (h n) -> p h n", h=8),
                                                          func=AF.Copy), reads=[bbuf[1]], writes=[b_va])
                yield

        def gen_Q(cx):
            xn, b_xn = cx.xn
            for half in range(2):
                for q2 in range(NQ2):
                    w, b_w = g.loadA(L, 1, half * NQ2 + q2)
                    cs = slice(q2 * W2, (q2 + 1) * W2)
                    for kc in range(KC):
                        op(pe, lambda e, kc=kc, w=w, cs=cs: e.matmul(banks[1][:, cs], lhsT=xn[:, kc, :], rhs=w[:, kc, :],
                                                                    start=(kc == 0), stop=False),
                           reads=[b_xn, b_w], writes=[bbuf[1]], sig=False)
                    bo = 3 * D + half * 512 + q2 * W2
                    op(pe, lambda e, bo=bo, cs=cs: e.matmul(banks[1][:, cs], lhsT=g.onesrow[:],
                                                           rhs=g.brow_hi[0:1, bo:bo + W2], start=False, stop=False),
                       reads=[b_cst], writes=[bbuf[1]], sig=False)
                    op(pe, lambda e, bo=bo, cs=cs: e.matmul(banks[1][:, cs], lhsT=g.onesrow[:],
                                                           rhs=g.brow_lo[0:1, bo:bo + W2], start=False, stop=True),
                       reads=[b_cst], writes=[bbuf[1]], sig=(q2 == NQ2 - 1))
                op(act, lambda e, half=half: e.activation(out=zs[:, half * 512:(half + 1) * 512], in_=banks[1][:, :],
                                                          func=AF.Silu), reads=[bbuf[1]], writes=[b_zs])
                yield

        def gen_A(cx, j, u, kvs):
            i = cx.t
            xn, b_xn = cx.xn
            Ba, Bb, Bc = u.banks
            w, b_w = g.loadB(L, 0, j)
            for kc in range(KC):
                op(pe, lambda e, kc=kc: e.matmul(banks[Ba][:, 0:P], lhsT=w[:, kc, :], rhs=xn[:, kc, :],
                                                 start=(kc == 0), stop=(kc == KC - 1)),
                   reads=[b_w, b_xn], writes=[bbuf[Ba]], sig=(kc == KC - 1))
            op(act, lambda e: e.activation(out=u.qT[:], in_=banks[Ba][:, 0:P], func=AF.Identity, scale=0.125,
                                           bias=pvc('bq8', j)), reads=[bbuf[Ba], b_pv], writes=[u.b_qT])
            yield
            ent = geo[i]
            for h in range(2):
                hg = 2 * j + h
                for n_, (kt, di, pi) in enumerate(ent):
                    kT, b_kT = kvs[kt].kT
                    bk_ = Bb if n_ < 4 else Bc
                    co = (n_ % 4) * P
                    op(pe, lambda e, kT=kT, bk_=bk_, co=co, h=h: e.matmul(banks[bk_][:, co:co + P],
                                                                         lhsT=kT[hq[h], j, :], rhs=u.qT[hq[h], :],
                                                                         start=True, stop=False),
                       reads=[b_kT, u.b_qT], writes=[bbuf[bk_]], sig=False)
                    op(pe, lambda e, bk_=bk_, co=co, hg=hg, di=di: e.matmul(banks[bk_][:, co:co + P],
                                                                           lhsT=RB[:, hg, di, :], rhs=g.Jrev[:],
                                                                           start=False, stop=False),
                       reads=[b_lc, b_cst], writes=[bbuf[bk_]], sig=False)
                    last = (n_ == len(ent) - 1) or (n_ == 3)
                    op(pe, lambda e, bk_=bk_, co=co, pi=pi: e.matmul(banks[bk_][:, co:co + P], lhsT=MK[:, pi, :],
                                                                    rhs=g.Jrev[:], start=False, stop=True),
                       reads=[b_lc, b_cst], writes=[bbuf[bk_]], sig=last)
                n4 = min(4, len(ent))
                op(act, lambda e, n4=n4: e.activation(out=u.PT[:, 0:n4, :].rearrange("p n k -> p (n k)"),
                                                      in_=banks[Bb][:, 0:n4 * P], func=AF.Exp),
                   reads=[bbuf[Bb]], writes=[u.b_PT])
                if len(ent) > 4:
                    op(act, lambda e: e.activation(out=u.PT[:, 4, :], in_=banks[Bc][:, 0:P], func=AF.Exp),
                       reads=[bbuf[Bc]], writes=[u.b_PT])
                for n_, (kt, di, pi) in enumerate(ent):
                    va, b_va = kvs[kt].va
                    op(pe, lambda e, n_=n_, va=va, hg=hg, h=h: e.matmul(banks[Ba][:, 2 * P + h * 65:2 * P + h * 65 + 65],
                                                                       lhsT=u.PT[:, n_, :], rhs=va[:, hg, :],
                                                                       start=(n_ == 0), stop=(n_ == len(ent) - 1)),
                       reads=[u.b_PT, b_va], writes=[bbuf[Ba]], sig=(n_ == len(ent) - 1))
                yield
            for h in range(2):
                o0 = 2 * P + h * 65
                op(dve, lambda e, o0=o0, h=h: e.reciprocal(out=u.rc[:, h:h + 1], in_=banks[Ba][:, o0 + 64:o0 + 65]),
                   reads=[bbuf[Ba]], writes=[u.b_rc])
                op(dve, lambda e, o0=o0, h=h: e.tensor_scalar(out=og[:, (2 * j + h) * 64:(2 * j + h + 1) * 64],
                                                              in0=banks[Ba][:, o0:o0 + 64], scalar1=u.rc[:, h:h + 1],
                                                              scalar2=None, op0=ALU.mult),
                   reads=[bbuf[Ba], u.b_rc], writes=[b_og])
            yield

        def gen_O(cx):
            i = cx.t
            op(dve, lambda e: e.tensor_tensor(out=yg[:], in0=og[:], in1=zs[:], op=ALU.mult),
               reads=[b_og, b_zs], writes=[b_yg])
            for kc in range(KC):
                op(pe, lambda e, kc=kc: e.transpose(out=banks_bf[0][:, kc * P:(kc + 1) * P],
                                                    in_=yg[:, kc * P:(kc + 1) * P], identity=g.identb[:]),
                   reads=[b_yg, g.b_identb], writes=[bbuf[0]], sig=(kc == KC - 1))
            op(act, lambda e: e.activation(out=ygT[:].rearrange("p k n -> p (k n)"), in_=banks_bf[0][:, :],
                                           func=AF.Copy), reads=[bbuf[0]], writes=[b_ygT])
            yield
            for half in range(2):
                for q2 in range(NQ2):
                    w, b_w = g.loadA(L, 2, half * NQ2 + q2)
                    cs = slice(q2 * W2, (q2 + 1) * W2)
                    for kc in range(KC):
                        op(pe, lambda e, kc=kc, w=w, cs=cs: e.matmul(banks[0][:, cs], lhsT=ygT[:, kc, :], rhs=w[:, kc, :],
                                                                    start=(kc == 0), stop=False),
                           reads=[b_ygT, b_w], writes=[bbuf[0]], sig=False)
                    bo = 4 * D + half * 512 + q2 * W2
                    op(pe, lambda e, bo=bo, cs=cs: e.matmul(banks[0][:, cs], lhsT=g.onesrow[:],
                                                           rhs=g.brow_hi[0:1, bo:bo + W2], start=False, stop=False),
                       reads=[b_cst], writes=[bbuf[0]], sig=False)
                    op(pe, lambda e, bo=bo, cs=cs: e.matmul(banks[0][:, cs], lhsT=g.onesrow[:],
                                                           rhs=g.brow_lo[0:1, bo:bo + W2], start=False, stop=True),
                       reads=[b_cst], writes=[bbuf[0]], sig=(q2 == NQ2 - 1))
                op(act, lambda e, half=half: e.activation(out=tmpB[:, half * 512:(half + 1) * 512], in_=banks[0][:, :],
                                                          func=AF.Copy), reads=[bbuf[0]], writes=[b_tmpB])
                yield
            op(act, lambda e: e.activation(out=tmpA[:], in_=tmpB[:], func=AF.Square, accum_out=stat[:, 4:5]),
               reads=[b_tmpB], writes=[b_tmpA, b_stat])
            op(dve, lambda e: e.tensor_scalar(out=stat[:, 5:6], in0=stat[:, 4:5], scalar1=1.0 / D, scalar2=RMS_EPS,
                                              op0=ALU.mult, op1=ALU.add), reads=[b_stat], writes=[b_stat])
            op(act, lambda e: e.activation(out=stat[:, 6:7], in_=stat[:, 5:6], func=AF.Sqrt), reads=[b_stat],
               writes=[b_stat])
            op(dve, lambda e: e.reciprocal(out=stat[:, 7:8], in_=stat[:, 6:7]), reads=[b_stat], writes=[b_stat])
            op(dve, lambda e: e.scalar_tensor_tensor(out=tmpB[:], in0=tmpB[:], scalar=stat[:, 7:8], in1=postg[:],
                                                     op0=ALU.mult, op1=ALU.mult),
               reads=[b_tmpB, b_stat, b_lc], writes=[b_tmpB])
            op(pool, lambda e: e.tensor_tensor(out=tmpA[:], in0=tmpB[:], in1=x1[:, i, :], op=ALU.add),
               reads=[b_tmpB, bx1[i]], writes=[b_tmpA])
            T_.dma(pool, g.y_out[si, i * P:(i + 1) * P, :], tmpA[:], reads=[b_tmpA])
            yield

        tasks = []
        kvs = [None] * NCH
        lastO = None
        lastKV = None

        def make_KV(t):
            cx = NS()
            cx.t = t
            tk = Task("K%d" % t)
            cx.kxn, cx.xn = p_xn.acquire(tk)
            cx.kkT, cx.kT = p_kT.acquire(tk)
            cx.kva, cx.va = p_va.acquire(tk)
            if lastKV[0] is not None:
                tk.deps.append(lastKV[0])
            tk.gen = gen_KV(cx)
            cx.tk = tk
            kvs[t] = cx
            tasks.append(tk)
            lastKV[0] = tk

        lastKV = [None]
        for s in range(NCH + 3):
            if s < NCH:
                make_KV(s)
            i = s - 3
            if i < 0:
                continue
            cx = kvs[i]
            need = [kvs[kt].tk for (kt, _, _) in geo[i]]
            tq = Task("Q%d" % i)
            tq.deps += [cx.tk]
            if lastO is not None:
                tq.deps.append(lastO)
            p_xn.share(tq, cx.kxn)
            tq.gen = gen_Q(cx)
            tasks.append(tq)
            tas = []
            for j in range(KC):
                ta = Task("A%d_%d" % (i, j))
                ta.deps += need + [cx.tk]
                if lastO is not None:
                    ta.deps.append(lastO)
                _, uset = p_uset.acquire(ta)
                p_xn.share(ta, cx.kxn)
                for (kt, _, _) in geo[i]:
                    p_kT.share(ta, kvs[kt].kkT)
                    p_va.share(ta, kvs[kt].kva)
                ta.gen = gen_A(cx, j, uset, kvs)
                tas.append(ta)
                tasks.append(ta)
            to = Task("O%d" % i)
            to.deps += tas + [tq]
            to.gen = gen_O(cx)
            tasks.append(to)
            lastO = to
        run_tasks(tasks, window=WINDOW)


def kernel(**inputs):
    xp = np.asarray(inputs['x_prompt'], dtype=np.float32)
    xs_ = np.asarray(inputs['x_sample'], dtype=np.float32)
    xall = np.concatenate([xp, xs_], axis=0)
    nseq = xall.shape[0] // N_CORES
    nc = build(nseq)
    in_maps = []
    for ci in range(N_CORES):
        m = {"x": np.ascontiguousarray(xall[ci * nseq:(ci + 1) * nseq])}
        for nm in W_NAMES:
            m[nm] = np.ascontiguousarray(np.asarray(inputs[nm], dtype=np.float32))
        in_maps.append(m)
    res = run_bass_kernel_spmd(nc, in_maps, core_ids=list(range(N_CORES)))
    yall = np.concatenate([r["y"] for r in res.results], axis=0)
    nb = xp.shape[0]
    return (np.ascontiguousarray(yall[:nb]), np.ascontiguousarray(yall[nb:]))
```

```python
import numpy as np
from contextlib import ExitStack
import concourse.bass as bass
import concourse.mybir as mybir
from concourse.bass_utils import run_bass_kernel_spmd
from concourse.alu_op_type import AluOpType as ALU

F32 = mybir.dt.float32
BF16 = mybir.dt.bfloat16
AF = mybir.ActivationFunctionType
AX = mybir.AxisListType

N_CORES = 8
D = 1024
KC = 8
P = 128
T_SEQ = 2048
LAM = float(np.exp(-0.5))
LNX_EPS = 64e-5
RMS_EPS = 1e-6

import os
SAME_ENGINE_SYNC = os.environ.get('K_SES', '1') == '1'
WINDOW = int(os.environ.get('K_WIN', '4'))
STAGGER = int(os.environ.get('K_STAG', '0'))
NQ2 = int(os.environ.get('K_NQ2', '1'))
W2 = 512 // NQ2
DBUF = int(os.environ.get('K_DBUF', '1'))


class _Sem:
    def __init__(self, sem, name):
        self.sem = sem
        self.name = name
        self.n = 0


class Eng:
    def __init__(self, h, sem, name, is_pe=False):
        self.h = h
        self.s = _Sem(sem, name)
        self.name = name
        self.is_pe = is_pe
        self.waited = {}


class Buf:
    __slots__ = ("name", "w", "r", "excl")

    def __init__(self, name="", excl=False):
        self.name = name
        self.w = None
        self.r = {}
        self.excl = excl


class Trk:
    def __init__(self, nc, es, n_slots=16):
        self.nc = nc
        mk = lambda nm: es.enter_context(nc.semaphore(nm))
        self.pe = Eng(nc.tensor, mk("s_pe"), "pe", is_pe=True)
        self.act = Eng(nc.scalar, mk("s_act"), "act")
        self.dve = Eng(nc.vector, mk("s_dve"), "dve")
        self.pool = Eng(nc.gpsimd, mk("s_pool"), "pool")
        self.sp = Eng(nc.sync, mk("s_sp"), "sp")
        self.engs = [self.pe, self.act, self.dve, self.pool, self.sp]
        self.slots = [_Sem(mk("s_dma%d" % i), "dma%d" % i) for i in range(n_slots)]
        self.dma_i = 0
        self.slots_sw = [_Sem(mk("s_swdma%d" % i), "swdma%d" % i) for i in range(4)]
        self.dma_sw_i = 0
        self.n_inst = 0
        self.cnt = {}

    def _deps(self, reads, writes):
        deps = {}
        for b in reads:
            if b.w is not None:
                s, v = b.w
                if deps.get(s, 0) < v:
                    deps[s] = v
        for b in writes:
            if b.w is not None:
                s, v = b.w
                if deps.get(s, 0) < v:
                    deps[s] = v
            for s, v in b.r.items():
                if deps.get(s, 0) < v:
                    deps[s] = v
        return deps

    def _wait(self, eng, deps):
        for s, v in deps.items():
            if eng.waited.get(s, 0) >= v:
                continue
            if s is eng.s:
                if eng.is_pe or not SAME_ENGINE_SYNC:
                    continue
            eng.h.wait_ge(s.sem, v)
            eng.waited[s] = v

    def _mark(self, tok, reads, writes):
        s, v = tok
        for b in reads:
            if b.r.get(s, 0) < v:
                b.r[s] = v
        for b in writes:
            b.w = tok
            b.r = {}

    def op(self, eng, fn, reads=(), writes=(), sig=True):
        if any(b.excl for b in reads):
            writes = list(writes) + [b for b in reads if b.excl]
            reads = [b for b in reads if not b.excl]
        self._wait(eng, self._deps(reads, writes))
        inst = fn(eng.h)
        tok = (eng.s, eng.s.n + 1)
        if sig:
            inst.then_inc(eng.s.sem, 1)
            eng.s.n += 1
        self._mark(tok, reads, writes)
        self.n_inst += 1
        self.cnt[eng.name] = self.cnt.get(eng.name, 0) + 1
        return inst

    def dma(self, q, out, in_, reads=(), writes=(), **kw):
        if q is self.pool:
            slot = self.slots_sw[self.dma_sw_i % len(self.slots_sw)]
            self.dma_sw_i += 1
        else:
            slot = self.slots[self.dma_i % len(self.slots)]
            self.dma_i += 1
        deps = self._deps(reads, writes)
        if slot.n > 0 and deps.get(slot, 0) < slot.n:
            deps[slot] = slot.n
        self._wait(q, deps)
        inst = q.h.dma_start(out=out, in_=in_, **kw)
        inst.then_inc(slot.sem, 16)
        slot.n += 16
        self._mark((slot, slot.n), reads, writes)
        self.n_inst += 1
        self.cnt["dma"] = self.cnt.get("dma", 0) + 1
        return inst

    def barrier(self):
        allsems = [e.s for e in self.engs] + self.slots + self.slots_sw
        for e in self.engs:
            deps = {s: s.n for s in allsems if s.n > 0 and s is not e.s}
            self._wait(e, deps)

    def finish(self):
        deps = {s: s.n for s in self.slots + self.slots_sw if s.n > 0}
        self._wait(self.sp, deps)


class Task:
    def __init__(self, name):
        self.name = name
        self.deps = []
        self.done = False
        self.gen = None


def run_tasks(tasks, window=3):
    pending = list(tasks)
    active = []
    while pending or active:
        while pending and len(active) < window and all(d.done for d in pending[0].deps):
            active.append(pending.pop(0))
        if not active:
            raise RuntimeError("scheduler deadlock at %s" % pending[0].name)
        for t in list(active):
            try:
                next(t.gen)
            except StopIteration:
                t.done = True
                active.remove(t)


class RPool:
    def __init__(self, items):
        self.items = items
        self.i = 0
        self.users = [[] for _ in items]

    def acquire(self, task):
        k = self.i % len(self.items)
        self.i += 1
        task.deps += self.users[k]
        self.users[k] = [task]
        return k, self.items[k]

    def share(self, task, k):
        self.users[k].append(task)


class NS:
    pass


W_NAMES = ['pre_norm_g', 'post_norm_g', 'rk_mu', 'rk_w_r', 'rk_w_k', 'rk_w_v', 'rk_w_z', 'rk_w0', 'rk_w1', 'rk_w2',
           'rk_a0', 'rk_a1', 'rk_a2', 'rk_k_k', 'rk_k_a', 'rk_r_k', 'rk_lnx_w', 'rk_lnx_b', 'rk_w_o', 'na_w_in',
           'na_b_in', 'na_rpb', 'na_w_o', 'na_b_o']
W_SHAPES = {
    'pre_norm_g': [2, D], 'post_norm_g': [2, D], 'rk_mu': [1, 7, D], 'rk_w_r': [1, D, D], 'rk_w_k': [1, D, D],
    'rk_w_v': [1, D, D], 'rk_w_z': [1, D, D], 'rk_w0': [1, 2, D], 'rk_w1': [1, 2, D, 64], 'rk_w2': [1, 2, 64, D],
    'rk_a0': [1, 2, D], 'rk_a1': [1, 2, D, 64], 'rk_a2': [1, 2, 64, D], 'rk_k_k': [1, D], 'rk_k_a': [1, D],
    'rk_r_k': [1, 16, 64], 'rk_lnx_w': [1, D], 'rk_lnx_b': [1, D], 'rk_w_o': [1, D, D], 'na_w_in': [1, D, 4 * D],
    'na_b_in': [1, 4 * D], 'na_rpb': [1, 16, 15, 31], 'na_w_o': [1, D, D], 'na_b_o': [1, D],
}

PV = {}
_c = 0
for _nm, _n in [('mu', 7), ('w0', 2), ('a0', 2), ('k_k', 1), ('k_a', 1), ('r_k', 1), ('pre_g', 2), ('bq', 1),
                ('bk', 1), ('omk_a', 1), ('bq8', 1)]:
    PV[_nm] = _c
    _c += _n * 8
PV_COLS = _c
PV_ROWS = PV['omk_a']


def build(nseq, T=T_SEQ, stage="full"):
    NCH = T // P
    nc = bass.Bass("TRN2", target_bir_lowering=False)
    x_in = nc.dram_tensor("x", [nseq, T, D], F32, kind="ExternalInput").ap()
    y_out = nc.dram_tensor("y", [nseq, T, D], F32, kind="ExternalOutput").ap()
    wd = {nm: nc.dram_tensor(nm, W_SHAPES[nm], F32, kind="ExternalInput").ap() for nm in W_NAMES}
    wbJ = [nc.dram_tensor("wbJ%d" % l, [2, KC, P, KC, P], BF16, kind="Internal").ap() for l in range(2)]
    wbH = [nc.dram_tensor("wbH%d" % l, [3, 2 * NQ2, P, KC, W2], BF16, kind="Internal").ap() for l in range(2)]

    es = ExitStack()
    with es:
        T_ = Trk(nc, es)
        pe, act, dve, pool, sp = T_.pe, T_.act, T_.dve, T_.pool, T_.sp
        op = T_.op

        uid = [0]

        def sb(es_, nm, sh, dt):
            uid[0] += 1
            return es_.enter_context(nc.sbuf_tensor("%s_%d" % (nm, uid[0]), sh, dt))

        banks = [es.enter_context(nc.psum_tensor("pb%d" % i, [P, 512], F32)) for i in range(8)]
        bbuf = [Buf("pb%d" % i, excl=True) for i in range(8)]
        banks_bf = [b[:].bitcast(BF16) for b in banks]

        x1 = sb(es, "x1", [P, NCH, D], F32)
        bx1 = [Buf("x1_%d" % c) for c in range(NCH)]
        ringA = [sb(es, "ringA%d" % i, [P, KC, W2], BF16) for i in range(2)]
        b_ringA = [Buf("ringA%d" % i) for i in range(2)]
        ringB = [sb(es, "ringB%d" % i, [P, KC, P], BF16) for i in range(4)]
        b_ringB = [Buf("ringB%d" % i) for i in range(4)]
        rA_i = [0]
        rB_i = [0]
        identb = sb(es, "identb", [P, P], BF16)
        b_identb = Buf("identb")
        pv = sb(es, "pv", [P, PV_COLS], F32)
        b_pv = Buf("pv")
        onesrow = sb(es, "onesrow", [1, P], BF16)
        brow_hi = sb(es, "brow_hi", [1, 5 * D], BF16)
        brow_lo = sb(es, "brow_lo", [1, 5 * D], BF16)
        b_cst = Buf("consts")
        maskq = [sb(es, "maskq%d" % d, [P, 512], BF16) for d in range(2)]
        bdones = sb(es, "bdones", [P, P], BF16)
        hsel = sb(es, "hsel", [P, 2], BF16)
        ones_f = sb(es, "ones_f", [P, P], F32)
        Jrev = sb(es, "Jrev", [P, P], BF16)
        pes0 = ExitStack()
        io_r = sb(pes0, "io_r", [P, P], F32)
        io_c = sb(pes0, "io_c", [P, P], F32)

        def pvc(nm, idx):
            c0 = PV[nm] + idx
            return pv[:, c0:c0 + 1]

        def loadA(layer, m, qt):
            k = rA_i[0] % 2
            rA_i[0] += 1
            T_.dma(sp, ringA[k][:], wbH[layer][m, qt], writes=[b_ringA[k]])
            return ringA[k], b_ringA[k]

        def loadB(layer, m, j):
            k = rB_i[0] % 4
            rB_i[0] += 1
            T_.dma(sp, ringB[k][:], wbJ[layer][m, j], writes=[b_ringB[k]])
            return ringB[k], b_ringB[k]

        op(pool, lambda e: e.iota(io_r[:], pattern=[[0, P]], base=0, channel_multiplier=1,
                                  allow_small_or_imprecise_dtypes=True), writes=[b_cst])
        op(pool, lambda e: e.iota(io_c[:], pattern=[[1, P]], base=0, channel_multiplier=0,
                                  allow_small_or_imprecise_dtypes=True), writes=[b_cst])
        op(dve, lambda e: e.tensor_tensor(out=identb[:], in0=io_r[:], in1=io_c[:], op=ALU.is_equal),
           reads=[b_cst], writes=[b_identb])
        op(dve, lambda e: e.memset(onesrow[:], 1.0), writes=[b_cst])
        op(dve, lambda e: e.memset(ones_f[:], 1.0), writes=[b_cst])
        for d_, (o_s, o_i) in enumerate([(ALU.is_lt, ALU.is_le), (ALU.is_gt, ALU.is_ge)]):
            for q4 in range(4):
                o_ = o_s if q4 % 2 == 0 else o_i
                op(dve, lambda e, d_=d_, q4=q4, o_=o_: e.tensor_tensor(out=maskq[d_][:, q4 * P:(q4 + 1) * P],
                                                                      in0=io_r[:], in1=io_c[:], op=o_),
                   reads=[b_cst], writes=[b_cst])

        with ExitStack() as pes:
            identf = sb(pes, "identf", [P, P], F32)
            rb_ = sb(pes, "rb_", [P, P], F32)
            cb_ = sb(pes, "cb_", [P, P], F32)
            op(dve, lambda e: e.tensor_tensor(out=identf[:], in0=io_r[:], in1=io_c[:], op=ALU.is_equal),
               reads=[b_cst], writes=[b_cst])
            op(dve, lambda e: e.tensor_scalar(out=rb_[:], in0=io_r[:], scalar1=63.5, scalar2=None, op0=ALU.is_gt),
               reads=[b_cst], writes=[b_cst])
            op(dve, lambda e: e.tensor_scalar(out=cb_[:], in0=io_c[:], scalar1=63.5, scalar2=None, op0=ALU.is_gt),
               reads=[b_cst], writes=[b_cst])
            op(dve, lambda e: e.tensor_tensor(out=bdones[:], in0=rb_[:], in1=cb_[:], op=ALU.is_equal),
               reads=[b_cst], writes=[b_cst])
            op(dve, lambda e: e.tensor_copy(out=hsel[:, 1:2], in_=rb_[:, 0:1]), reads=[b_cst], writes=[b_cst])
            op(dve, lambda e: e.tensor_scalar(out=hsel[:, 0:1], in0=rb_[:, 0:1], scalar1=-1.0, scalar2=1.0,
                                              op0=ALU.mult, op1=ALU.add), reads=[b_cst], writes=[b_cst])

            rows = sb(pes, "pvrows", [P, 2, P], F32)
            b_rows = Buf("pvrows")
            op(dve, lambda e: e.memset(rows[:], 0.0), writes=[b_rows])

            def load_rows(r0, src):
                n = src.shape[0]
                g, o = divmod(r0, P)
                assert o + n <= P, (r0, n)
                T_.dma(sp, rows[o:o + n, g, :], src, writes=[b_rows])

            load_rows(PV['mu'], wd['rk_mu'][0].rearrange("m (j q) -> (m j) q", q=P))
            load_rows(PV['w0'], wd['rk_w0'][0].rearrange("m (j q) -> (m j) q", q=P))
            load_rows(PV['a0'], wd['rk_a0'][0].rearrange("m (j q) -> (m j) q", q=P))
            load_rows(PV['k_k'], wd['rk_k_k'][0].rearrange("(j q) -> j q", q=P))
            load_rows(PV['k_a'], wd['rk_k_a'][0].rearrange("(j q) -> j q", q=P))
            load_rows(PV['r_k'], wd['rk_r_k'][0].rearrange("(j h) c -> j (h c)", h=2))
            load_rows(PV['pre_g'], wd['pre_norm_g'].rearrange("m (j q) -> (m j) q", q=P))
            load_rows(PV['bq'], wd['na_b_in'][0, 0:D].rearrange("(j q) -> j q", q=P))
            load_rows(PV['bk'], wd['na_b_in'][0, D:2 * D].rearrange("(j q) -> j q", q=P))
            for g in range(2):
                op(pe, lambda e, g=g: e.transpose(out=banks[0][:, g * P:(g + 1) * P], in_=rows[:, g, :],
                                                  identity=identf[:]),
                   reads=[b_rows, b_cst], writes=[bbuf[0]])
            op(dve, lambda e: e.tensor_copy(out=pv[:, 0:PV_ROWS], in_=banks[0][:, 0:PV_ROWS]),
               reads=[bbuf[0]], writes=[b_pv])
            op(dve, lambda e: e.tensor_scalar(out=pv[:, PV['omk_a']:PV['omk_a'] + 8],
                                              in0=pv[:, PV['k_a']:PV['k_a'] + 8], scalar1=-1.0, scalar2=1.0,
                                              op0=ALU.mult, op1=ALU.add), reads=[b_pv], writes=[b_pv])
            op(dve, lambda e: e.tensor_scalar(out=pv[:, PV['bq8']:PV['bq8'] + 8], in0=pv[:, PV['bq']:PV['bq'] + 8],
                                              scalar1=0.125, scalar2=None, op0=ALU.mult), reads=[b_pv], writes=[b_pv])

            brow_f = sb(pes, "brow_f", [1, 5 * D], F32)
            brow_t = sb(pes, "brow_t", [1, 5 * D], F32)
            b_bf = Buf("brow_f")
            T_.dma(sp, brow_f[0:1, 0:4 * D], wd['na_b_in'][0:1, :], writes=[b_bf])
            T_.dma(sp, brow_f[0:1, 4 * D:5 * D], wd['na_b_o'][0:1, :], writes=[b_bf])
            op(act, lambda e: e.activation(out=brow_hi[:], in_=brow_f[:], func=AF.Copy), reads=[b_bf], writes=[b_cst])
            op(dve, lambda e: e.tensor_tensor(out=brow_t[:], in0=brow_f[:], in1=brow_hi[:], op=ALU.subtract),
               reads=[b_bf, b_cst], writes=[b_bf])
            op(act, lambda e: e.activation(out=brow_lo[:], in_=brow_t[:], func=AF.Copy), reads=[b_bf], writes=[b_cst])

            stg = [sb(pes, "stg%d" % i, [P, D], F32) for i in range(3)]
            stb = [sb(pes, "stb%d" % i, [P, D], BF16) for i in range(3)]
            b_stg = [Buf() for _ in range(3)]
            b_stb = [Buf() for _ in range(3)]
            srcs = []
            for kc in range(KC):
                rs_ = slice(kc * P, (kc + 1) * P)
                for m, nm in enumerate(['rk_w_r', 'rk_w_k']):
                    srcs.append((wd[nm][0, rs_, :], wbJ[0][m, :, :, kc, :].rearrange("j p n -> p j n"), 'J'))
                for m, nm in enumerate(['rk_w_v', 'rk_w_z', 'rk_w_o']):
                    srcs.append((wd[nm][0, rs_, :], wbH[0][m, :, :, kc, :].rearrange("h p n -> p h n"), 'H'))
                for m in range(2):
                    srcs.append((wd['na_w_in'][0, rs_, m * D:(m + 1) * D],
                                 wbJ[1][m, :, :, kc, :].rearrange("j p n -> p j n"), 'J'))
                for m in range(2):
                    srcs.append((wd['na_w_in'][0, rs_, (m + 2) * D:(m + 3) * D],
                                 wbH[1][m, :, :, kc, :].rearrange("h p n -> p h n"), 'H'))
                srcs.append((wd['na_w_o'][0, rs_, :], wbH[1][2, :, :, kc, :].rearrange("h p n -> p h n"), 'H'))
            cast_engs = [act, dve, pool]
            for i, (src, dst, kind) in enumerate(srcs):
                k = i % 3
                T_.dma(sp, stg[k][:], src, writes=[b_stg[k]])
                if cast_engs[k] is act:
                    op(act, lambda e, k=k: e.activation(out=stb[k][:], in_=stg[k][:], func=AF.Copy),
                       reads=[b_stg[k]], writes=[b_stb[k]])
                else:
                    op(cast_engs[k], lambda e, k=k: e.tensor_copy(out=stb[k][:], in_=stg[k][:]),
                       reads=[b_stg[k]], writes=[b_stb[k]])
                if kind == 'J':
                    srcv = stb[k][:].rearrange("p (j n) -> p j n", j=KC)
                else:
                    srcv = stb[k][:].rearrange("p (h n) -> p h n", h=2 * NQ2)
                T_.dma(sp, dst, srcv, reads=[b_stb[k]])
            T_.barrier()
            T_.finish()
        pes0.close()

        ctx = NS()
        ctx.__dict__.update(locals())
        if stage != "l0":
            l1_prologue(ctx)
        for si in range(nseq):
            with nc.named_scope('L0_%d' % si):
                layer0(ctx, si)
            T_.barrier()
            T_.finish()
            if stage == "l0":
                for c in range(NCH):
                    T_.dma(sp, y_out[si, c * P:(c + 1) * P, :], x1[:, c, :], reads=[bx1[c]])
            else:
                with nc.named_scope('L1_%d' % si):
                    layer1(ctx, si)
            T_.barrier()
            T_.finish()
        print("instructions:", T_.n_inst, T_.cnt)
    return nc


def layer0(g, si):
    nc, T_, op = g.nc, g.T_, g.op
    pe, act, dve, pool, sp = g.pe, g.act, g.dve, g.pool, g.sp
    banks, bbuf, banks_bf = g.banks, g.bbuf, g.banks_bf
    NCH, x1, bx1, pv, b_pv, pvc = g.NCH, g.x1, g.bx1, g.pv, g.b_pv, g.pvc
    wd, sb = g.wd, g.sb
    b_cst = g.b_cst
    L = 0

    with ExitStack() as l0:
        w1b = sb(l0, "w1b", [P, 2, KC, 64], BF16)
        a1b = sb(l0, "a1b", [P, 2, KC, 64], BF16)
        w2b = sb(l0, "w2b", [64, 2, D], BF16)
        a2b = sb(l0, "a2b", [64, 2, D], BF16)
        lnxb = sb(l0, "lnxb", [P, 2, D], F32)
        postg = sb(l0, "postg", [P, D], F32)
        b_lc = Buf("l0consts")
        tmpA = sb(l0, "tmpA", [P, D], F32)
        tmpB = sb(l0, "tmpB", [P, D], F32)
        b_tmpA, b_tmpB = Buf("tmpA"), Buf("tmpB")
        T_.dma(sp, tmpA[:].rearrange("p (d k n) -> p d k n", d=2, k=KC),
               wd['rk_w1'][0].rearrange("d (k p) n -> p d k n", p=P), writes=[b_tmpA])
        op(dve, lambda e: e.tensor_copy(out=w1b[:].rearrange("p d k n -> p (d k n)"), in_=tmpA[:]),
           reads=[b_tmpA], writes=[b_lc])
        T_.dma(sp, tmpB[:].rearrange("p (d k n) -> p d k n", d=2, k=KC),
               wd['rk_a1'][0].rearrange("d (k p) n -> p d k n", p=P), writes=[b_tmpB])
        op(dve, lambda e: e.tensor_copy(out=a1b[:].rearrange("p d k n -> p (d k n)"), in_=tmpB[:]),
           reads=[b_tmpB], writes=[b_lc])
        for d_ in range(2):
            T_.dma(sp, tmpA[0:64, :], wd['rk_w2'][0, d_], writes=[b_tmpA])
            op(dve, lambda e, d_=d_: e.tensor_copy(out=w2b[:, d_, :], in_=tmpA[0:64, :]), reads=[b_tmpA], writes=[b_lc])
            T_.dma(sp, tmpB[0:64, :], wd['rk_a2'][0, d_], writes=[b_tmpB])
            op(dve, lambda e, d_=d_: e.tensor_copy(out=a2b[:, d_, :], in_=tmpB[0:64, :]), reads=[b_tmpB], writes=[b_lc])
        T_.dma(sp, lnxb[:, 0, :], wd['rk_lnx_w'][0].partition_broadcast(P), writes=[b_lc])
        T_.dma(sp, lnxb[:, 1, :], wd['rk_lnx_b'][0].partition_broadcast(P), writes=[b_lc])
        T_.dma(sp, postg[:], wd['post_norm_g'][L].partition_broadcast(P), writes=[b_lc])

        def mk(nm, sh, dt, n):
            return [(sb(l0, "%s%d" % (nm, i), sh, dt), Buf("%s%d" % (nm, i))) for i in range(n)]

        p_xin = RPool(mk("xin", [P, D], F32, 1))
        xs, b_xs = mk("xs", [P, D], BF16, 1)[0]
        stat = sb(l0, "stat", [P, 8], F32)
        b_stat = Buf("stat")
        p_slot = RPool(mk("xnT", [P, KC, P + 2], BF16, 3))
        xx, b_xx = mk("xx", [P, KC, P], BF16, 1)[0]
        ltmp, b_ltmp = mk("ltmp", [P, P], F32, 1)[0]
        p_lr = RPool(mk("lrp_r", [P, KC, P], BF16, DBUF))
        p_lk = RPool(mk("lrp_k", [P, KC, P], BF16, DBUF))
        p_lt = mk("lrp_t", [P, KC, P], BF16, 1)
        lt_i = [0]
        p_vt = RPool(mk("Vtm", [P, D], BF16, DBUF))
        p_zs = RPool(mk("zs", [P, D], BF16, 1))
        p_th = RPool(mk("th", [64, 2 * P], BF16, 2))
        Hf = sb(l0, "Hf", [P, KC, 64], F32)
        Hb = sb(l0, "Hb", [P, KC, 64], BF16)
        b_H = [Buf("H%d" % j) for j in range(KC)]
        bonF = sb(l0, "bonF", [P, NCH, 16], F32)
        b_bonF = [Buf("bonF%d" % c) for c in range(NCH)]
        p_bonB = RPool(mk("bonB", [P, 16], F32, 2))
        p_yb = RPool(mk("Yb", [P, D], F32, 1))
        yg, b_yg = mk("yg", [P, D], BF16, 1)[0]
        ygT, b_ygT = mk("ygT", [P, KC, P], BF16, 1)[0]
        gst = sb(l0, "gst", [P, 6, 16], F32)
        b_gst = Buf("gst")

        def mkset(i):
            u = NS()
            u.i = i
            def t(nm, sh, dt):
                tt = sb(l0, "u%d_%s" % (i, nm), sh, dt)
                setattr(u, nm, tt)
                setattr(u, "b_" + nm, Buf("u%d_%s" % (i, nm)))
            t("rk", [P, 2 * P], F32)
            for nm in ("sg", "al", "kk", "rs", "Ein", "Eex", "ein", "kd"):
                t(nm, [P, P], F32)
            t("sq", [P, P], BF16)
            t("arT", [P, 2 * P], BF16)
            t("btT", [P, P], BF16)
            t("ktT", [P, P], BF16)
            t("pr", [P, P], BF16)
            t("BK", [P, 2 * P], BF16)
            t("AT0", [P, 512], BF16)
            t("AT1", [P, 512], BF16)
            for h in range(2):
                for k in range(2):
                    t("C%d%d" % (h, k), [P, 3 * P], BF16)
            t("Xp", [P, P], BF16)
            t("Up", [P, P], BF16)
            t("Hs", [P, 64], F32)
            t("sc", [P, 4], F32)
            u.banks = (2 + 3 * i, 3 + 3 * i, 4 + 3 * i)
            return u

        p_uset = RPool([mkset(0), mkset(1)])

        def gen_Na(cx):
            c = cx.c
            xin, b_xin = cx.xin
            T_.dma(sp, xin[:], g.x_in[si, c * P:(c + 1) * P, :], writes=[b_xin])
            op(act, lambda e: e.activation(out=xs[:], in_=xin[:], func=AF.Square, accum_out=stat[:, 0:1]),
               reads=[b_xin], writes=[b_xs, b_stat])
            op(dve, lambda e: e.tensor_scalar(out=stat[:, 1:2], in0=stat[:, 0:1], scalar1=1.0 / D, scalar2=RMS_EPS,
                                              op0=ALU.mult, op1=ALU.add), reads=[b_stat], writes=[b_stat])
            op(act, lambda e: e.activation(out=stat[:, 2:3], in_=stat[:, 1:2], func=AF.Sqrt), reads=[b_stat],
               writes=[b_stat])
            op(dve, lambda e: e.reciprocal(out=stat[:, 3:4], in_=stat[:, 2:3]), reads=[b_stat], writes=[b_stat])
            op(act, lambda e: e.activation(out=xs[:], in_=xin[:], func=AF.Copy, scale=stat[:, 3:4]),
               reads=[b_xin, b_stat], writes=[b_xs])
            yield

        def gen_Nb(cx, d, first, last, prev):
            c = cx.c
            slot, b_slot = cx.slot
            for kc in range(KC):
                op(pe, lambda e, kc=kc: e.transpose(out=banks_bf[0][:, kc * P:(kc + 1) * P],
                                                    in_=xs[:, kc * P:(kc + 1) * P], identity=g.identb[:]),
                   reads=[b_xs, g.b_identb], writes=[bbuf[0]], sig=(kc == KC - 1))
            for kc in range(KC):
                sc = pvc('pre_g', L * 8 + kc)
                if kc % 2 == 0:
                    op(dve, lambda e, kc=kc, sc=sc: e.tensor_scalar(out=slot[:, kc, 1:P + 1],
                                                                  in0=banks_bf[0][:, kc * P:(kc + 1) * P],
                                                                  scalar1=sc, scalar2=None, op0=ALU.mult),
                       reads=[bbuf[0], b_pv], writes=[b_slot])
                else:
                    op(act, lambda e, kc=kc, sc=sc: e.activation(out=slot[:, kc, 1:P + 1],
                                                               in_=banks_bf[0][:, kc * P:(kc + 1) * P],
                                                               func=AF.Copy, scale=sc),
                       reads=[bbuf[0], b_pv], writes=[b_slot])
            near, far = (0, P + 1) if d == 0 else (P + 1, 0)
            if first:
                op(pool, lambda e: e.memset(slot[:, :, near:near + 1], 0.0), writes=[b_slot])
            else:
                pslot, b_pslot = prev.slot
                src_own = 1 if d == 0 else P
                src_prev = P if d == 0 else 1
                op(pool, lambda e: e.tensor_copy(out=slot[:, :, near:near + 1], in_=pslot[:, :, src_prev:src_prev + 1]),
                   reads=[b_pslot], writes=[b_slot])
                op(pool, lambda e: e.tensor_copy(out=pslot[:, :, far:far + 1], in_=slot[:, :, src_own:src_own + 1]),
                   reads=[b_slot], writes=[b_pslot])
            if last:
                op(pool, lambda e: e.memset(slot[:, :, far:far + 1], 0.0), writes=[b_slot])
            yield

        def gen_F(cx, d):
            slot, b_slot = cx.slot
            xn = slot[:, :, 1:P + 1]
            op(pool, lambda e: e.tensor_tensor(out=xx[:], in0=slot[:, :, 0:P], in1=slot[:, :, 2:P + 2], op=ALU.add),
               reads=[b_slot], writes=[b_xx])
            op(dve, lambda e: e.scalar_tensor_tensor(out=xx[:], in0=xx[:], scalar=0.5, in1=xn, op0=ALU.mult,
                                                     op1=ALU.subtract), reads=[b_xx, b_slot], writes=[b_xx])
            yield

            def lerp(m, dst, b_dst):
                for kc in range(KC):
                    sc = pvc('mu', m * 8 + kc)
                    op(dve, lambda e, kc=kc, sc=sc: e.scalar_tensor_tensor(
                        out=dst[:, kc, :], in0=xx[:, kc, :], scalar=sc, in1=slot[:, kc, 1:P + 1],
                        op0=ALU.mult, op1=ALU.add), reads=[b_xx, b_slot, b_pv], writes=[b_dst])

            def next_lt():
                r = p_lt[0]
                lt_i[0] += 1
                return r

            lr, b_lr = cx.lr
            lk, b_lk = cx.lk
            vt, b_vt = cx.vt
            lerp(0, lr, b_lr)
            yield
            lerp(1, lk, b_lk)
            yield
            lv, b_lv = next_lt()
            lerp(2, lv, b_lv)
            yield
            for half in range(2):
                for q2 in range(NQ2):
                    w, b_w = g.loadA(L, 0, half * NQ2 + q2)
                    for kc in range(KC):
                        op(pe, lambda e, kc=kc, w=w, q2=q2: e.matmul(banks[1][:, q2 * W2:(q2 + 1) * W2],
                                                                    lhsT=lv[:, kc, :], rhs=w[:, kc, :],
                                                                    start=(kc == 0), stop=(kc == KC - 1)),
                           reads=[b_lv, b_w], writes=[bbuf[1]], sig=(kc == KC - 1 and q2 == NQ2 - 1))
                op(act, lambda e, half=half: e.activation(out=vt[:, half * 512:(half + 1) * 512], in_=banks[1][:, :],
                                                          func=AF.Copy), reads=[bbuf[1]], writes=[b_vt])
                yield
            if d == 1:
                zs, b_zs = cx.zs
                for half in range(2):
                    for q2 in range(NQ2):
                        w, b_w = g.loadA(L, 1, half * NQ2 + q2)
                        for kc in range(KC):
                            op(pe, lambda e, kc=kc, w=w, q2=q2: e.matmul(banks[1][:, q2 * W2:(q2 + 1) * W2],
                                                                        lhsT=slot[:, kc, 1:P + 1], rhs=w[:, kc, :],
                                                                        start=(kc == 0), stop=(kc == KC - 1)),
                               reads=[b_slot, b_w], writes=[bbuf[1]], sig=(kc == KC - 1 and q2 == NQ2 - 1))
                    op(act, lambda e, half=half: e.activation(out=zs[:, half * 512:(half + 1) * 512],
                                                              in_=banks[1][:, :], func=AF.Silu),
                       reads=[bbuf[1]], writes=[b_zs])
                    yield
            th, b_th = cx.th
            lw, b_lw = next_lt()
            lerp(3 + d, lw, b_lw)
            yield
            la, b_la = next_lt()
            lerp(5 + d, la, b_la)
            yield
            for kc in range(KC):
                op(pe, lambda e, kc=kc: e.matmul(banks[1][0:64, 0:P], lhsT=w1b[:, d, kc, :], rhs=lw[:, kc, :],
                                                 start=(kc == 0), stop=(kc == KC - 1)),
                   reads=[b_lw, b_lc], writes=[bbuf[1]], sig=False)
            for kc in range(KC):
                op(pe, lambda e, kc=kc: e.matmul(banks[1][0:64, P:2 * P], lhsT=a1b[:, d, kc, :], rhs=la[:, kc, :],
                                                 start=(kc == 0), stop=(kc == KC - 1)),
                   reads=[b_la, b_lc], writes=[bbuf[1]], sig=(kc == KC - 1))
            op(act, lambda e: e.activation(out=th[:, 0:P], in_=banks[1][0:64, 0:P], func=AF.Tanh),
               reads=[bbuf[1]], writes=[b_th])
            op(act, lambda e: e.activation(out=th[:, P:2 * P], in_=banks[1][0:64, P:2 * P], func=AF.Copy),
               reads=[bbuf[1]], writes=[b_th])
            yield

        def gen_U(cx, d, j, u):
            c = cx.c
            Ba, Bb, Bc = u.banks
            lr, b_lr = cx.lr
            lk, b_lk = cx.lk
            vt, b_vt = cx.vt
            th, b_th = cx.th
            hq = [slice(0, 64), slice(64, 128)]
            mq = g.maskq[d]
            mA = g.maskq[1 - d][:, 0:P]
            for _ in range(STAGGER if (j % 2 == 1) else 0):
                yield
            wr, b_wr = g.loadB(L, 0, j)
            wk, b_wk = g.loadB(L, 1, j)
            for kc in range(KC):
                op(pe, lambda e, kc=kc: e.matmul(banks[Ba][:, 0:P], lhsT=wr[:, kc, :], rhs=lr[:, kc, :],
                                                 start=(kc == 0), stop=(kc == KC - 1)),
                   reads=[b_wr, b_lr], writes=[bbuf[Ba]], sig=False)
            for kc in range(KC):
                op(pe, lambda e, kc=kc: e.matmul(banks[Ba][:, P:2 * P], lhsT=wk[:, kc, :], rhs=lk[:, kc, :],
                                                 start=(kc == 0), stop=(kc == KC - 1)),
                   reads=[b_wk, b_lk], writes=[bbuf[Ba]], sig=False)
            op(pe, lambda e: e.matmul(banks[Ba][:, 2 * P:3 * P], lhsT=w2b[:, d, j * P:(j + 1) * P], rhs=th[:, 0:P],
                                      start=True, stop=True), reads=[b_lc, b_th], writes=[bbuf[Ba]], sig=False)
            op(pe, lambda e: e.matmul(banks[Ba][:, 3 * P:4 * P], lhsT=a2b[:, d, j * P:(j + 1) * P], rhs=th[:, P:2 * P],
                                      start=True, stop=True), reads=[b_lc, b_th], writes=[bbuf[Ba]])
            op(act, lambda e: e.activation(out=u.rk[:], in_=banks[Ba][:, 0:2 * P], func=AF.Copy),
               reads=[bbuf[Ba]], writes=[u.b_rk])
            op(act, lambda e: e.activation(out=u.sg[:], in_=banks[Ba][:, 2 * P:3 * P], func=AF.Sigmoid,
                                           bias=pvc('w0', d * 8 + j)), reads=[bbuf[Ba], b_pv], writes=[u.b_sg])
            op(act, lambda e: e.activation(out=u.sq[:], in_=banks[Ba][:, P:2 * P], func=AF.Square,
                                           scale=pvc('k_k', j)), reads=[bbuf[Ba], b_pv], writes=[u.b_sq])
            op(act, lambda e: e.activation(out=u.al[:], in_=banks[Ba][:, 3 * P:4 * P], func=AF.Sigmoid,
                                           bias=pvc('a0', d * 8 + j)), reads=[bbuf[Ba], b_pv], writes=[u.b_al])
            yield
            rT = u.rk[:, 0:P]
            kT = u.rk[:, P:2 * P]
            op(dve, lambda e: e.tensor_scalar(out=u.kk[:], in0=kT, scalar1=pvc('k_k', j), scalar2=None, op0=ALU.mult),
               reads=[u.b_rk, b_pv], writes=[u.b_kk])
            op(pe, lambda e: e.matmul(banks[Ba][:, 0:P], lhsT=g.bdones[:], rhs=u.sq[:], start=True, stop=True),
               reads=[b_cst, u.b_sq], writes=[bbuf[Ba]])
            op(act, lambda e: e.activation(out=u.rs[:], in_=banks[Ba][:, 0:P], func=AF.Ln),
               reads=[bbuf[Ba]], writes=[u.b_rs])
            op(act, lambda e: e.activation(out=u.rs[:], in_=u.rs[:], func=AF.Exp, scale=-0.5),
               reads=[u.b_rs], writes=[u.b_rs])
            op(pool, lambda e: e.tensor_tensor(out=u.kk[:], in0=u.kk[:], in1=u.rs[:], op=ALU.mult),
               reads=[u.b_kk, u.b_rs], writes=[u.b_kk])
            op(dve, lambda e: e.tensor_tensor_scan(out=u.Ein[:], data0=g.ones_f[:], data1=u.sg[:], initial=0.0,
                                                   op0=ALU.mult, op1=ALU.add),
               reads=[b_cst, u.b_sg], writes=[u.b_Ein])
            tot = u.Ein[:, P - 1:P]
            op(act, lambda e: e.activation(out=u.sc[:, 0:1], in_=tot, func=AF.Exp, scale=-LAM),
               reads=[u.b_Ein], writes=[u.b_sc])
            if d == 0:
                op(pool, lambda e: e.tensor_tensor(out=u.Eex[:], in0=u.Ein[:], in1=u.sg[:], op=ALU.subtract),
                   reads=[u.b_Ein, u.b_sg], writes=[u.b_Eex])
            else:
                op(dve, lambda e: e.tensor_copy(out=u.sc[:, 1:2], in_=tot), reads=[u.b_Ein], writes=[u.b_sc])
                op(dve, lambda e: e.tensor_scalar(out=u.Eex[:], in0=u.Ein[:], scalar1=u.sc[:, 1:2], scalar2=-1.0,
                                                  op0=ALU.subtract, op1=ALU.mult),
                   reads=[u.b_Ein, u.b_sc], writes=[u.b_Eex])
                op(pool, lambda e: e.tensor_tensor(out=u.Ein[:], in0=u.Eex[:], in1=u.sg[:], op=ALU.add),
                   reads=[u.b_Eex, u.b_sg], writes=[u.b_Ein])
            yield
            op(act, lambda e: e.activation(out=u.ein[:], in_=u.Ein[:], func=AF.Exp, scale=-LAM),
               reads=[u.b_Ein], writes=[u.b_ein])
            op(act, lambda e: e.activation(out=u.Eex[:], in_=u.Eex[:], func=AF.Exp, scale=-LAM),
               reads=[u.b_Eex], writes=[u.b_Eex])
            op(act, lambda e: e.activation(out=u.Ein[:], in_=u.Ein[:], func=AF.Exp, scale=LAM),
               reads=[u.b_Ein], writes=[u.b_Ein])
            eng_ = u.Ein
            eex_ = u.Eex
            op(dve, lambda e: e.scalar_tensor_tensor(out=u.arT[:, 0:P], in0=u.kk[:], scalar=-1.0, in1=eex_[:],
                                                     op0=ALU.mult, op1=ALU.mult),
               reads=[u.b_kk, u.b_Eex], writes=[u.b_arT])
            op(pool, lambda e: e.tensor_tensor(out=u.arT[:, P:2 * P], in0=rT, in1=u.ein[:], op=ALU.mult),
               reads=[u.b_rk, u.b_ein], writes=[u.b_arT])
            op(dve, lambda e: e.tensor_scalar(out=u.kd[:], in0=u.al[:], scalar1=pvc('k_a', j),
                                               scalar2=pvc('omk_a', j), op0=ALU.mult, op1=ALU.add),
               reads=[u.b_al, b_pv], writes=[u.b_kd])
            op(pool, lambda e: e.tensor_tensor(out=u.kd[:], in0=u.kd[:], in1=kT, op=ALU.mult),
               reads=[u.b_kd, u.b_rk], writes=[u.b_kd])
            op(dve, lambda e: e.tensor_tensor(out=u.ktT[:], in0=u.kd[:], in1=eng_[:], op=ALU.mult),
               reads=[u.b_kd, u.b_Ein], writes=[u.b_ktT])
            op(pool, lambda e: e.tensor_tensor(out=u.al[:], in0=u.al[:], in1=u.kk[:], op=ALU.mult),
               reads=[u.b_al, u.b_kk], writes=[u.b_al])
            op(dve, lambda e: e.tensor_tensor(out=u.btT[:], in0=u.al[:], in1=eng_[:], op=ALU.mult),
               reads=[u.b_al, u.b_Ein], writes=[u.b_btT])
            op(dve, lambda e: e.scalar_tensor_tensor(out=u.pr[:], in0=rT, scalar=pvc('r_k', j), in1=u.kd[:],
                                                     op0=ALU.mult, op1=ALU.mult),
               reads=[u.b_rk, u.b_kd, b_pv], writes=[u.b_pr])
            yield
            op(pe, lambda e: e.transpose(out=banks_bf[Ba][:, 0:P], in_=u.btT[:], identity=g.identb[:]),
               reads=[u.b_btT, g.b_identb], writes=[bbuf[Ba]], sig=False)
            op(pe, lambda e: e.transpose(out=banks_bf[Ba][:, P:2 * P], in_=u.ktT[:], identity=g.identb[:]),
               reads=[u.b_ktT, g.b_identb], writes=[bbuf[Ba]], sig=False)
            op(pe, lambda e: e.matmul(banks[Ba][:, 2 * P:2 * P + 2], lhsT=u.pr[:], rhs=g.hsel[:], start=True, stop=True),
               reads=[u.b_pr, b_cst], writes=[bbuf[Ba]])
            op(act, lambda e: e.activation(out=u.BK[:], in_=banks_bf[Ba][:, 0:2 * P], func=AF.Copy),
               reads=[bbuf[Ba]], writes=[u.b_BK])
            if d == 0:
                bdst, b_bdst = bonF[:, c, 2 * j:2 * j + 2], b_bonF[c]
            else:
                bt_, b_bdst = cx.bonB
                bdst = bt_[:, 2 * j:2 * j + 2]
            op(act, lambda e: e.activation(out=bdst, in_=banks[Ba][:, 2 * P:2 * P + 2], func=AF.Copy),
               reads=[bbuf[Ba]], writes=[b_bdst])
            yield
            AT = [u.AT0, u.AT1]
            b_AT = [u.b_AT0, u.b_AT1]
            Cc = [[u.C00, u.C01], [u.C10, u.C11]]
            b_C = [[u.b_C00, u.b_C01], [u.b_C10, u.b_C11]]
            hb = [Bb, Bc]
            for h in range(2):
                B_ = hb[h]
                op(pe, lambda e, h=h, B_=B_: e.matmul(banks[B_][:, 0:2 * P], lhsT=u.btT[hq[h], :], rhs=u.arT[hq[h], :],
                                                      start=True, stop=True),
                   reads=[u.b_btT, u.b_arT], writes=[bbuf[B_]], sig=False)
                op(pe, lambda e, h=h, B_=B_: e.matmul(banks[B_][:, 2 * P:4 * P], lhsT=u.ktT[hq[h], :],
                                                      rhs=u.arT[hq[h], :], start=True, stop=True),
                   reads=[u.b_ktT, u.b_arT], writes=[bbuf[B_]])
                op(dve, lambda e, h=h, B_=B_: e.tensor_tensor(out=AT[h][:], in0=banks[B_][:, :], in1=mq[:],
                                                              op=ALU.mult),
                   reads=[bbuf[B_], b_cst], writes=[b_AT[h]])
                op(pe, lambda e, h=h, B_=B_: e.matmul(banks[B_][:, 0:P], lhsT=u.arT[hq[h], 0:P], rhs=u.btT[hq[h], :],
                                                      start=True, stop=True),
                   reads=[u.b_btT, u.b_arT], writes=[bbuf[B_]])
                op(dve, lambda e, h=h, B_=B_: e.tensor_tensor(out=Cc[h][0][:, 0:P], in0=banks[B_][:, 0:P], in1=mA,
                                                              op=ALU.mult),
                   reads=[bbuf[B_], b_cst], writes=[b_C[h][0]])
                op(pool, lambda e, h=h: e.tensor_tensor(out=Cc[h][1][:, 2 * P:3 * P], in0=AT[h][:, 0:P],
                                                        in1=g.identb[:], op=ALU.add),
                   reads=[b_AT[h], g.b_identb], writes=[b_C[h][1]])
            yield
            for h in range(2):
                B_ = hb[h]
                A0 = Cc[h][0][:, 0:P]
                B0 = AT[h][:, 0:P]
                op(pe, lambda e, B_=B_, A0=A0, B0=B0: e.matmul(banks[B_][:, 0:P], lhsT=B0, rhs=A0, start=True, stop=True),
                   reads=[b_AT[h], b_C[h][0]], writes=[bbuf[B_]], sig=False)
                op(pe, lambda e, B_=B_, A0=A0, B0=B0: e.matmul(banks[B_][:, P:2 * P], lhsT=A0, rhs=B0, start=True,
                                                               stop=True),
                   reads=[b_AT[h], b_C[h][0]], writes=[bbuf[B_]])
                ev = dve if h == 0 else act
                if ev is dve:
                    op(dve, lambda e, h=h, B_=B_: e.tensor_copy(out=Cc[h][1][:, 0:2 * P], in_=banks[B_][:, 0:2 * P]),
                       reads=[bbuf[B_]], writes=[b_C[h][1]])
                else:
                    op(act, lambda e, h=h, B_=B_: e.activation(out=Cc[h][1][:, 0:2 * P], in_=banks[B_][:, 0:2 * P],
                                                               func=AF.Copy),
                       reads=[bbuf[B_]], writes=[b_C[h][1]])
            yield
            for lev in range(1, 7):
                src_i = lev % 2
                dst_i = 1 - src_i
                for h in range(2):
                    B_ = hb[h]
                    S = Cc[h][src_i]
                    Dd = Cc[h][dst_i]
                    bS, bD = b_C[h][src_i], b_C[h][dst_i]
                    Ak, Bk, Mk = S[:, 0:P], S[:, P:2 * P], S[:, 2 * P:3 * P]
                    lo = 0
                    if lev <= 5:
                        op(pe, lambda e, B_=B_, Ak=Ak, Bk=Bk: e.matmul(banks[B_][:, 0:P], lhsT=Bk, rhs=Ak, start=True,
                                                                       stop=True),
                           reads=[bS], writes=[bbuf[B_]], sig=False)
                    else:
                        lo = 2 * P
                    r0 = P if lev <= 4 else 2 * P
                    op(pe, lambda e, B_=B_, Ak=Ak, S=S, r0=r0: e.matmul(banks[B_][:, r0:3 * P], lhsT=Ak, rhs=S[:, r0:3 * P],
                                                                        start=True, stop=True),
                       reads=[bS], writes=[bbuf[B_]], sig=(h == 0))
                    if h == 1:
                        op(pe, lambda e, B_=B_, Mk=Mk: e.matmul(banks[B_][:, 2 * P:3 * P], lhsT=g.identb[:], rhs=Mk,
                                                                start=False, stop=True),
                           reads=[bS, g.b_identb], writes=[bbuf[B_]])
                        op(act, lambda e, B_=B_, Dd=Dd, lo=lo: e.activation(out=Dd[:, lo:3 * P],
                                                                            in_=banks[B_][:, lo:3 * P], func=AF.Copy),
                           reads=[bbuf[B_]], writes=[bD])
                    else:
                        if lev <= 5:
                            hi_ = 2 * P if lev <= 4 else P
                            op(dve, lambda e, B_=B_, Dd=Dd, hi_=hi_: e.tensor_copy(out=Dd[:, 0:hi_],
                                                                                   in_=banks[B_][:, 0:hi_]),
                               reads=[bbuf[B_]], writes=[bD])
                        op(dve, lambda e, B_=B_, Dd=Dd, Mk=Mk: e.tensor_tensor(out=Dd[:, 2 * P:3 * P],
                                                                               in0=banks[B_][:, 2 * P:3 * P], in1=Mk,
                                                                               op=ALU.add),
                           reads=[bbuf[B_], bS], writes=[bD])
                yield
            Mfin = [Cc[h][1][:, 2 * P:3 * P] for h in range(2)]
            b_Mfin = [b_C[h][1] for h in range(2)]
            b_Hj = g_bH[j]
            for h in range(2):
                op(pe, lambda e, h=h: e.matmul(banks[Ba][:, h * 64:(h + 1) * 64], lhsT=u.arT[hq[h], 0:P],
                                               rhs=Hb[hq[h], j, :], start=True, stop=False),
                   reads=[u.b_arT, b_Hj], writes=[bbuf[Ba]], sig=False)
                op(pe, lambda e, h=h: e.matmul(banks[Ba][:, h * 64:(h + 1) * 64], lhsT=AT[h][:, 2 * P:3 * P],
                                               rhs=vt[:, (2 * j + h) * 64:(2 * j + h + 1) * 64], start=False, stop=True),
                   reads=[b_AT[h], b_vt], writes=[bbuf[Ba]], sig=(h == 1))
            op(act, lambda e: e.activation(out=u.Xp[:], in_=banks[Ba][:, 0:P], func=AF.Copy),
               reads=[bbuf[Ba]], writes=[u.b_Xp])
            for h in range(2):
                op(pe, lambda e, h=h: e.matmul(banks[Ba][:, P + h * 64:P + (h + 1) * 64], lhsT=Mfin[h],
                                               rhs=u.Xp[:, h * 64:(h + 1) * 64], start=True, stop=True),
                   reads=[b_Mfin[h], u.b_Xp], writes=[bbuf[Ba]], sig=(h == 1))
            op(dve, lambda e: e.tensor_copy(out=u.Up[:], in_=banks[Ba][:, P:2 * P]), reads=[bbuf[Ba]], writes=[u.b_Up])
            op(dve, lambda e: e.tensor_scalar(out=u.Hs[:], in0=Hf[:, j, :], scalar1=u.sc[:, 0:1], scalar2=None,
                                               op0=ALU.mult), reads=[b_Hj, u.b_sc], writes=[u.b_Hs])
            yield
            for h in range(2):
                yo = 2 * P + h * 64
                op(pe, lambda e, h=h, yo=yo: e.matmul(banks[Ba][:, yo:yo + 64], lhsT=u.arT[hq[h], P:2 * P],
                                                      rhs=Hb[hq[h], j, :], start=True, stop=False),
                   reads=[u.b_arT, b_Hj], writes=[bbuf[Ba]], sig=False)
                op(pe, lambda e, h=h, yo=yo: e.matmul(banks[Ba][:, yo:yo + 64], lhsT=AT[h][:, P:2 * P],
                                                      rhs=u.Up[:, h * 64:(h + 1) * 64], start=False, stop=False),
                   reads=[b_AT[h], u.b_Up], writes=[bbuf[Ba]], sig=False)
                op(pe, lambda e, h=h, yo=yo: e.matmul(banks[Ba][:, yo:yo + 64], lhsT=AT[h][:, 3 * P:4 * P],
                                                      rhs=vt[:, (2 * j + h) * 64:(2 * j + h + 1) * 64],
                                                      start=False, stop=True),
                   reads=[b_AT[h], b_vt], writes=[bbuf[Ba]], sig=False)
            op(pe, lambda e: e.matmul(banks[Ba][:, 3 * P:4 * P], lhsT=u.BK[:, 0:P], rhs=u.Up[:], start=True, stop=False),
               reads=[u.b_BK, u.b_Up], writes=[bbuf[Ba]], sig=False)
            op(pe, lambda e: e.matmul(banks[Ba][:, 3 * P:4 * P], lhsT=u.BK[:, P:2 * P], rhs=vt[:, j * P:(j + 1) * P],
                                      start=False, stop=True),
               reads=[u.b_BK, b_vt], writes=[bbuf[Ba]])
            if d == 0:
                ydst = x1[:, c, :].bitcast(BF16)[:, j * P:(j + 1) * P]
                b_yd = bx1[c]
            else:
                yb_, b_yd = cx.yb
                ydst = yb_[:, j * P:(j + 1) * P]
            op(act, lambda e: e.activation(out=ydst, in_=banks[Ba][:, 2 * P:3 * P], func=AF.Copy),
               reads=[bbuf[Ba]], writes=[b_yd])
            for h in range(2):
                op(dve, lambda e, h=h: e.scalar_tensor_tensor(out=Hf[hq[h], j, :],
                                                              in0=banks[Ba][hq[h], 3 * P + h * 64:3 * P + (h + 1) * 64],
                                                              scalar=u.sc[hq[h], 0:1], in1=u.Hs[hq[h], :],
                                                              op0=ALU.mult, op1=ALU.add),
                   reads=[bbuf[Ba], u.b_sc, u.b_Hs], writes=[b_Hj])
            op(pool, lambda e: e.tensor_copy(out=Hb[:, j, :], in_=Hf[:, j, :]), reads=[b_Hj], writes=[b_Hj])
            yield

        g_bH = b_H

        def gen_Ba(cx):
            c = cx.c
            yb_, b_yb = cx.yb
            zs, b_zs = cx.zs
            vt, b_vt = cx.vt
            bonB, b_bonB = cx.bonB
            yf = x1[:, c, :].bitcast(BF16)[:, 0:D]
            op(dve, lambda e: e.tensor_tensor(out=yb_[:], in0=yb_[:], in1=yf, op=ALU.add),
               reads=[b_yb, bx1[c]], writes=[b_yb])
            y3 = yb_[:].rearrange("p (h n) -> p h n", h=16)
            op(dve, lambda e: e.tensor_reduce(out=gst[:, 0, :], in_=y3, axis=AX.X, op=ALU.add),
               reads=[b_yb], writes=[b_gst])
            op(act, lambda e: e.activation(out=tmpA[:], in_=yb_[:], func=AF.Square), reads=[b_yb], writes=[b_tmpA])
            op(dve, lambda e: e.tensor_reduce(out=gst[:, 1, :], in_=tmpA[:].rearrange("p (h n) -> p h n", h=16),
                                              axis=AX.X, op=ALU.add), reads=[b_tmpA], writes=[b_gst])
            yield
            op(dve, lambda e: e.tensor_scalar(out=gst[:, 2, :], in0=gst[:, 0, :], scalar1=1.0 / 64, scalar2=None,
                                              op0=ALU.mult), reads=[b_gst], writes=[b_gst])
            op(dve, lambda e: e.tensor_tensor(out=gst[:, 3, :], in0=gst[:, 2, :], in1=gst[:, 2, :], op=ALU.mult),
               reads=[b_gst], writes=[b_gst])
            op(dve, lambda e: e.scalar_tensor_tensor(out=gst[:, 3, :], in0=gst[:, 1, :], scalar=1.0 / 64,
                                                     in1=gst[:, 3, :], op0=ALU.mult, op1=ALU.subtract),
               reads=[b_gst], writes=[b_gst])
            op(dve, lambda e: e.tensor_scalar(out=gst[:, 3, :], in0=gst[:, 3, :], scalar1=LNX_EPS, scalar2=None,
                                              op0=ALU.add), reads=[b_gst], writes=[b_gst])
            op(act, lambda e: e.activation(out=gst[:, 3, :], in_=gst[:, 3, :], func=AF.Sqrt), reads=[b_gst],
               writes=[b_gst])
            op(dve, lambda e: e.reciprocal(out=gst[:, 4, :], in_=gst[:, 3, :]), reads=[b_gst], writes=[b_gst])
            op(dve, lambda e: e.tensor_tensor(out=gst[:, 5, :], in0=bonF[:, c, :], in1=bonB[:], op=ALU.add),
               reads=[b_bonF[c], b_bonB], writes=[b_gst])
            mean_b = gst[:, 2, :].unsqueeze(2).broadcast_to([P, 16, 64])
            rstd_b = gst[:, 4, :].unsqueeze(2).broadcast_to([P, 16, 64])
            bon_b = gst[:, 5, :].unsqueeze(2).broadcast_to([P, 16, 64])
            op(dve, lambda e: e.tensor_tensor(out=y3, in0=y3, in1=mean_b, op=ALU.subtract),
               reads=[b_yb, b_gst], writes=[b_yb])
            op(pool, lambda e: e.tensor_tensor(out=y3, in0=y3, in1=rstd_b, op=ALU.mult),
               reads=[b_yb, b_gst], writes=[b_yb])
            yield
            op(dve, lambda e: e.tensor_tensor(out=yb_[:], in0=yb_[:], in1=lnxb[:, 0, :], op=ALU.mult),
               reads=[b_yb, b_lc], writes=[b_yb])
            op(pool, lambda e: e.tensor_tensor(out=yb_[:], in0=yb_[:], in1=lnxb[:, 1, :], op=ALU.add),
               reads=[b_yb, b_lc], writes=[b_yb])
            t3 = tmpA[:].rearrange("p (h n) -> p h n", h=16)
            op(dve, lambda e: e.tensor_tensor(out=t3, in0=vt[:].rearrange("p (h n) -> p h n", h=16), in1=bon_b,
                                              op=ALU.mult), reads=[b_vt, b_gst], writes=[b_tmpA])
            op(pool, lambda e: e.tensor_tensor(out=yb_[:], in0=yb_[:], in1=tmpA[:], op=ALU.add),
               reads=[b_yb, b_tmpA], writes=[b_yb])
            op(dve, lambda e: e.tensor_tensor(out=yg[:], in0=yb_[:], in1=zs[:], op=ALU.mult),
               reads=[b_yb, b_zs], writes=[b_yg])
            yield

        def gen_Bb(cx):
            c = cx.c
            xin, b_xin = tmpA, b_tmpA
            T_.dma(sp, xin[:], g.x_in[si, c * P:(c + 1) * P, :], writes=[b_xin])
            for kc in range(KC):
                op(pe, lambda e, kc=kc: e.transpose(out=banks_bf[0][:, kc * P:(kc + 1) * P],
                                                    in_=yg[:, kc * P:(kc + 1) * P], identity=g.identb[:]),
                   reads=[b_yg, g.b_identb], writes=[bbuf[0]], sig=(kc == KC - 1))
            op(act, lambda e: e.activation(out=ygT[:].rearrange("p k n -> p (k n)"), in_=banks_bf[0][:, :],
                                           func=AF.Copy), reads=[bbuf[0]], writes=[b_ygT])
            yield
            for half in range(2):
                for q2 in range(NQ2):
                    w, b_w = g.loadA(L, 2, half * NQ2 + q2)
                    for kc in range(KC):
                        op(pe, lambda e, kc=kc, w=w, q2=q2: e.matmul(banks[0][:, q2 * W2:(q2 + 1) * W2],
                                                                    lhsT=ygT[:, kc, :], rhs=w[:, kc, :],
                                                                    start=(kc == 0), stop=(kc == KC - 1)),
                           reads=[b_ygT, b_w], writes=[bbuf[0]], sig=(kc == KC - 1 and q2 == NQ2 - 1))
                op(act, lambda e, half=half: e.activation(out=tmpB[:, half * 512:(half + 1) * 512], in_=banks[0][:, :],
                                                          func=AF.Copy), reads=[bbuf[0]], writes=[b_tmpB])
                yield
            op(act, lambda e: e.activation(out=yg[:], in_=tmpB[:], func=AF.Square, accum_out=stat[:, 4:5]),
               reads=[b_tmpB], writes=[b_yg, b_stat])
            op(dve, lambda e: e.tensor_scalar(out=stat[:, 5:6], in0=stat[:, 4:5], scalar1=1.0 / D, scalar2=RMS_EPS,
                                              op0=ALU.mult, op1=ALU.add), reads=[b_stat], writes=[b_stat])
            op(act, lambda e: e.activation(out=stat[:, 6:7], in_=stat[:, 5:6], func=AF.Sqrt), reads=[b_stat],
               writes=[b_stat])
            op(dve, lambda e: e.reciprocal(out=stat[:, 7:8], in_=stat[:, 6:7]), reads=[b_stat], writes=[b_stat])
            op(dve, lambda e: e.scalar_tensor_tensor(out=tmpB[:], in0=tmpB[:], scalar=stat[:, 7:8], in1=postg[:],
                                                     op0=ALU.mult, op1=ALU.mult),
               reads=[b_tmpB, b_stat, b_lc], writes=[b_tmpB])
            op(pool, lambda e: e.tensor_tensor(out=x1[:, c, :], in0=tmpB[:], in1=xin[:], op=ALU.add),
               reads=[b_tmpB, b_xin], writes=[bx1[c]])
            yield

        def gen_Z():
            op(dve, lambda e: e.memset(Hf[:], 0.0), writes=b_H)
            op(pool, lambda e: e.memset(Hb[:], 0.0), writes=b_H)
            yield

        tasks = []
        lastB = None
        lastF = None
        for d in range(2):
            order = list(range(NCH)) if d == 0 else list(range(NCH - 1, -1, -1))
            tz = Task("Z%d" % d)
            tz.gen = gen_Z()
            tz.deps += [t for t in tasks if t.name.startswith("U")]
            tasks.append(tz)
            cxs = [None] * NCH

            def make_Na(i):
                cx = NS()
                cx.c = order[i]
                tn = Task("N%d_%d" % (d, cx.c))
                _, cx.xin = p_xin.acquire(tn)
                if i > 0:
                    tn.deps.append(cxs[i - 1].tnb)
                tn.gen = gen_Na(cx)
                cx.tna = tn
                cxs[i] = cx
                tasks.append(tn)

            def make_Nb(i):
                cx = cxs[i]
                tn = Task("N%d_%db" % (d, cx.c))
                cx.ks, cx.slot = p_slot.acquire(tn)
                tn.deps.append(cx.tna)
                prev = cxs[i - 1] if i > 0 else None
                if prev is not None:
                    p_slot.share(tn, prev.ks)
                tn.gen = gen_Nb(cx, d, i == 0, i == NCH - 1, prev)
                cx.tnb = tn
                cx.tn = tn
                tasks.append(tn)

            make_Na(0)
            make_Nb(0)
            if NCH > 1:
                make_Na(1)
                make_Nb(1)
            pendBb = None
            for i in range(NCH):
                cx = cxs[i]
                c = cx.c
                tf = Task("F%d_%d" % (d, c))
                tf.deps.append(cx.tn)
                if i + 1 < NCH:
                    tf.deps.append(cxs[i + 1].tn)
                if lastF is not None:
                    tf.deps.append(lastF)
                lastF = tf
                p_slot.share(tf, cx.ks)
                klr, cx.lr = p_lr.acquire(tf)
                klk, cx.lk = p_lk.acquire(tf)
                kvt, cx.vt = p_vt.acquire(tf)
                kth, cx.th = p_th.acquire(tf)
                if d == 1:
                    kzs, cx.zs = p_zs.acquire(tf)
                tf.gen = gen_F(cx, d)
                tasks.append(tf)
                tba = None
                if d == 1:
                    tba = Task("B%d" % c)
                    _, cx.yb = p_yb.acquire(tba)
                    _, cx.bonB = p_bonB.acquire(tba)
                tus = []
                for j in range(KC):
                    tu = Task("U%d_%d_%d" % (d, c, j))
                    tu.deps += [tf, tz]
                    if d == 1:
                        tu.deps += list(tba.deps)
                    _, uset = p_uset.acquire(tu)
                    p_lr.share(tu, klr)
                    p_lk.share(tu, klk)
                    p_vt.share(tu, kvt)
                    p_th.share(tu, kth)
                    tu.gen = gen_U(cx, d, j, uset)
                    if i > 0:
                        tu.deps.append(cxs[i - 1].tus[j])
                    tus.append(tu)
                    tasks.append(tu)
                    if j == 1 and i + 2 < NCH:
                        make_Na(i + 2)
                    if j == 3 and pendBb is not None:
                        tasks.append(pendBb)
                        pendBb = None
                cx.tus = tus
                if d == 1:
                    tba.deps += tus + [tf]
                    if lastB is not None:
                        tba.deps.append(lastB)
                    p_vt.share(tba, kvt)
                    p_zs.share(tba, kzs)
                    tba.gen = gen_Ba(cx)
                    tasks.append(tba)
                    tbb = Task("B%db" % c)
                    tbb.deps.append(tba)
                    tbb.gen = gen_Bb(cx)
                    lastB = tbb
                    if i == NCH - 1:
                        tasks.append(tbb)
                    else:
                        pendBb = tbb
                if i + 2 < NCH:
                    make_Nb(i + 2)
        import os
        allow = os.environ.get("L0_TASKS", "ZNFUB")
        tasks = [t for t in tasks if t.name[0] in allow]
        for t in tasks:
            t.deps = [d_ for d_ in t.deps if d_.name[0] in allow]
        run_tasks(tasks, window=WINDOW)


GRID_W = 64
N_ROWS = T_SEQ // GRID_W
NEG = -30000.0


def _rs(r):
    return min(max(r - 4, 0), N_ROWS - 8)


def l1_geometry():
    geo = []
    pats = []
    for i in range(N_ROWS // 2):
        rows = [2 * i, 2 * i + 1]
        lo = min(_rs(r) for r in rows)
        hi = max(_rs(r) + 7 for r in rows)
        ent = []
        for kt in range(lo // 2, hi // 2 + 1):
            pat = tuple(tuple(1 if _rs(2 * i + rl) <= 2 * kt + krl < _rs(2 * i + rl) + 8 else 0 for krl in range(2))
                        for rl in range(2))
            if pat not in pats:
                pats.append(pat)
            ent.append((kt, (2 * (kt - i) + 6) // 2, pats.index(pat)))
        geo.append(ent)
    return geo, pats


def l1_prologue(g):
    nc, T_, op, sb = g.nc, g.T_, g.op, g.sb
    pe, act, dve, pool, sp = g.pe, g.act, g.dve, g.pool, g.sp
    geo, pats = l1_geometry()
    g.l1_geo, g.l1_pats = geo, pats
    NMK = len(pats)
    g.padD = nc.dram_tensor("padD", [240, P], F32, kind="Internal").ap()
    g.rbD = nc.dram_tensor("rbD", [P, 16 * 7 * P], BF16, kind="Internal").ap()
    g.mkD = nc.dram_tensor("mkD", [P, (NMK + 1) * P], BF16, kind="Internal").ap()
    x1 = g.x1
    with ExitStack() as p1:
        b_t = Buf("l1pro")
        padt = sb(p1, "padt", [120, 2, P], F32)
        op(dve, lambda e: e.memset(padt[:], 0.0), writes=[b_t])
        rp = g.wd['na_rpb'][0].rearrange("h r m -> (h r) m")
        for gi in range(2):
            T_.dma(sp, padt[:, gi, 48:79], rp[gi * 120:(gi + 1) * 120, :], writes=[b_t])
        for gi in range(2):
            T_.dma(sp, g.padD[gi * 120:(gi + 1) * 120, :], padt[:, gi, :], reads=[b_t])
        T_.barrier()
        T_.finish()
        Hs = x1[:].rearrange("p c d -> p (c d)")[:, 0:240 * 64].rearrange("p (b k) -> p b k", k=64)
        b_hs = Buf("Hs")
        for h in range(16):
            for r2 in range(2):
                src = bass.AP(tensor=g.padD.tensor, offset=h * 15 * P, ap=[[1, 64], [P, 15], [1, 64]])
                T_.dma(sp, Hs[64 * r2:64 * r2 + 64, h * 15:(h + 1) * 15, :], src, writes=[b_hs])
        RBs = sb(p1, "RBs", [P, 16, 7, P], BF16)
        b_rb = Buf("RBs")
        op(pool, lambda e: e.memset(RBs[:].rearrange("p h d k -> p (h d k)"), 0.0), writes=[b_rb])
        engs = [dve, pool, act]
        n = 0
        for h in range(16):
            for rl in range(2):
                for krl in range(2):
                    dis = [di for di in range(7) if 0 <= 2 * di + 1 + krl - rl <= 14]
                    d0, nd = dis[0], len(dis)
                    ri0 = 2 * d0 + 1 + krl - rl
                    srcv = Hs[64 * rl:64 * rl + 64, h * 15 + ri0:h * 15 + ri0 + 2 * (nd - 1) + 1:2, :]
                    dstv = RBs[64 * rl:64 * rl + 64, h, d0:d0 + nd, 64 * krl:64 * krl + 64]
                    e_ = engs[n % 3]
                    n += 1
                    if e_ is act:
                        op(act, lambda e, s=srcv, d=dstv: e.activation(out=d, in_=s, func=AF.Copy),
                           reads=[b_hs], writes=[b_rb])
                    else:
                        op(e_, lambda e, s=srcv, d=dstv: e.tensor_copy(out=d, in_=s), reads=[b_hs], writes=[b_rb])
        T_.dma(sp, g.rbD[:, :], RBs[:].rearrange("p h d k -> p (h d k)"), reads=[b_rb])
        ior = sb(p1, "ior1", [P, P], F32)
        ioc = sb(p1, "ioc1", [P, P], F32)
        t1 = sb(p1, "mt1", [P, P], F32)
        t2 = sb(p1, "mt2", [P, P], F32)
        cm = sb(p1, "cm", [P, P], BF16)
        neg = sb(p1, "negt", [P, P], BF16)
        MKs = sb(p1, "MKs", [P, NMK + 1, P], BF16)
        b_m = Buf("mk")
        op(pool, lambda e: e.iota(ior[:], pattern=[[0, P]], base=0, channel_multiplier=1,
                                  allow_small_or_imprecise_dtypes=True), writes=[b_m])
        op(pool, lambda e: e.iota(ioc[:], pattern=[[1, P]], base=0, channel_multiplier=0,
                                  allow_small_or_imprecise_dtypes=True), writes=[b_m])
        op(dve, lambda e: e.tensor_tensor(out=t1[:], in0=ior[:], in1=ioc[:], op=ALU.add), reads=[b_m], writes=[b_m])
        op(dve, lambda e: e.tensor_scalar(out=t2[:], in0=t1[:], scalar1=63.0, scalar2=None, op0=ALU.is_equal),
           reads=[b_m], writes=[b_m])
        op(dve, lambda e: e.tensor_scalar(out=t1[:], in0=t1[:], scalar1=191.0, scalar2=None, op0=ALU.is_equal),
           reads=[b_m], writes=[b_m])
        op(dve, lambda e: e.tensor_tensor(out=g.Jrev[:], in0=t1[:], in1=t2[:], op=ALU.add), reads=[b_m],
           writes=[g.b_cst])
        op(dve, lambda e: e.tensor_scalar(out=t1[:], in0=ior[:], scalar1=63.5, scalar2=-64.0, op0=ALU.is_gt,
                                          op1=ALU.mult), reads=[b_m], writes=[b_m])
        op(dve, lambda e: e.tensor_tensor(out=t1[:], in0=t1[:], in1=ior[:], op=ALU.add), reads=[b_m], writes=[b_m])
        op(dve, lambda e: e.tensor_scalar(out=t1[:], in0=t1[:], scalar1=-1.0, scalar2=55.0, op0=ALU.mult,
                                          op1=ALU.add), reads=[b_m], writes=[b_m])
        op(dve, lambda e: e.tensor_scalar(out=t1[:], in0=t1[:], scalar1=0.0, scalar2=48.0, op0=ALU.max, op1=ALU.min),
           reads=[b_m], writes=[b_m])
        op(dve, lambda e: e.tensor_scalar(out=t2[:], in0=ioc[:], scalar1=63.5, scalar2=-64.0, op0=ALU.is_gt,
                                          op1=ALU.mult), reads=[b_m], writes=[b_m])
        op(dve, lambda e: e.tensor_tensor(out=t2[:], in0=t2[:], in1=ioc[:], op=ALU.add), reads=[b_m], writes=[b_m])
        op(dve, lambda e: e.tensor_tensor(out=t2[:], in0=t2[:], in1=t1[:], op=ALU.subtract), reads=[b_m],
           writes=[b_m])
        op(dve, lambda e: e.tensor_scalar(out=t1[:], in0=t2[:], scalar1=-0.5, scalar2=None, op0=ALU.is_gt),
           reads=[b_m], writes=[b_m])
        op(dve, lambda e: e.tensor_scalar(out=t2[:], in0=t2[:], scalar1=15.5, scalar2=None, op0=ALU.is_lt),
           reads=[b_m], writes=[b_m])
        op(dve, lambda e: e.tensor_tensor(out=t1[:], in0=t1[:], in1=t2[:], op=ALU.mult), reads=[b_m], writes=[b_m])
        op(dve, lambda e: e.tensor_scalar(out=cm[:], in0=t1[:], scalar1=-1.0, scalar2=-NEG, op0=ALU.add, op1=ALU.mult),
           reads=[b_m], writes=[b_m])
        op(dve, lambda e: e.memset(neg[:], NEG), writes=[b_m])
        for pi, pat in enumerate(pats):
            for rl in range(2):
                for krl in range(2):
                    src_t = cm if pat[rl][krl] else neg
                    op(pool, lambda e, pi=pi, rl=rl, krl=krl, s=src_t: e.tensor_copy(
                        out=MKs[64 * rl:64 * rl + 64, pi, 64 * krl:64 * krl + 64],
                        in_=s[64 * rl:64 * rl + 64, 64 * krl:64 * krl + 64]), reads=[b_m], writes=[b_m])
        op(pool, lambda e: e.tensor_copy(out=MKs[:, NMK, :], in_=neg[:]), reads=[b_m], writes=[b_m])
        T_.dma(sp, g.mkD[:, :], MKs[:].rearrange("p n k -> p (n k)"), reads=[b_m])
        T_.barrier()
        T_.finish()


def layer1(g, si):
    nc, T_, op = g.nc, g.T_, g.op
    pe, act, dve, pool, sp = g.pe, g.act, g.dve, g.pool, g.sp
    banks, bbuf, banks_bf = g.banks, g.bbuf, g.banks_bf
    NCH, x1, bx1, pv, b_pv, pvc = g.NCH, g.x1, g.bx1, g.pv, g.b_pv, g.pvc
    wd, sb = g.wd, g.sb
    b_cst = g.b_cst
    L = 1
    geo, pats = g.l1_geo, g.l1_pats
    NMK = len(pats)
    hq = [slice(0, 64), slice(64, 128)]

    with ExitStack() as l1:
        postg = sb(l1, "postg1", [P, D], F32)
        RB = sb(l1, "RB", [P, 16, 7, P], BF16)
        MK = sb(l1, "MK", [P, NMK + 1, P], BF16)
        b_lc = Buf("l1consts")
        T_.dma(sp, postg[:], wd['post_norm_g'][L].partition_broadcast(P), writes=[b_lc])
        T_.dma(sp, RB[:].rearrange("p h d k -> p (h d k)"), g.rbD[:, :], writes=[b_lc])
        T_.dma(sp, MK[:].rearrange("p n k -> p (n k)"), g.mkD[:, :], writes=[b_lc])

        def mk(nm, sh, dt, n):
            return [(sb(l1, "%s%d" % (nm, i), sh, dt), Buf("%s%d" % (nm, i))) for i in range(n)]

        xs, b_xs = mk("xs1", [P, D], BF16, 1)[0]
        stat = sb(l1, "stat1", [P, 8], F32)
        b_stat = Buf("stat1")
        p_xn = RPool(mk("xnT1", [P, KC, P], BF16, 4))
        p_kT = RPool(mk("kT", [P, KC, P], BF16, 7))
        p_va = RPool(mk("Vaug", [P, 16, 65], BF16, 7))
        zs, b_zs = mk("zs1", [P, D], BF16, 1)[0]
        og, b_og = mk("og", [P, D], F32, 1)[0]
        yg, b_yg = mk("yg1", [P, D], BF16, 1)[0]
        ygT, b_ygT = mk("ygT1", [P, KC, P], BF16, 1)[0]
        tmpA, b_tmpA = mk("tmpA1", [P, D], F32, 1)[0]
        tmpB, b_tmpB = mk("tmpB1", [P, D], F32, 1)[0]

        def mkset(i):
            u = NS()
            def t(nm, sh, dt):
                setattr(u, nm, sb(l1, "a%d_%s" % (i, nm), sh, dt))
                setattr(u, "b_" + nm, Buf("a%d_%s" % (i, nm)))
            t("qT", [P, P], BF16)
            t("PT", [P, 5, P], BF16)
            t("rc", [P, 2], F32)
            u.banks = (2 + 3 * i, 3 + 3 * i, 4 + 3 * i)
            return u

        p_uset = RPool([mkset(0), mkset(1)])
        for (va, b_va) in p_va.items:
            op(pool, lambda e, va=va: e.memset(va[:, :, 64:65], 1.0), writes=[b_va])

        def gen_KV(cx):
            t = cx.t
            xn, b_xn = cx.xn
            kT, b_kT = cx.kT
            va, b_va = cx.va
            xsrc = x1[:, t, :]
            op(act, lambda e: e.activation(out=xs[:], in_=xsrc, func=AF.Square, accum_out=stat[:, 0:1]),
               reads=[bx1[t]], writes=[b_xs, b_stat])
            op(dve, lambda e: e.tensor_scalar(out=stat[:, 1:2], in0=stat[:, 0:1], scalar1=1.0 / D, scalar2=RMS_EPS,
                                              op0=ALU.mult, op1=ALU.add), reads=[b_stat], writes=[b_stat])
            op(act, lambda e: e.activation(out=stat[:, 2:3], in_=stat[:, 1:2], func=AF.Sqrt), reads=[b_stat],
               writes=[b_stat])
            op(dve, lambda e: e.reciprocal(out=stat[:, 3:4], in_=stat[:, 2:3]), reads=[b_stat], writes=[b_stat])
            op(act, lambda e: e.activation(out=xs[:], in_=xsrc, func=AF.Copy, scale=stat[:, 3:4]),
               reads=[bx1[t], b_stat], writes=[b_xs])
            yield
            for kc in range(KC):
                op(pe, lambda e, kc=kc: e.transpose(out=banks_bf[0][:, kc * P:(kc + 1) * P],
                                                    in_=xs[:, kc * P:(kc + 1) * P], identity=g.identb[:]),
                   reads=[b_xs, g.b_identb], writes=[bbuf[0]], sig=(kc == KC - 1))
            for kc in range(KC):
                sc = pvc('pre_g', L * 8 + kc)
                if kc % 2 == 0:
                    op(dve, lambda e, kc=kc, sc=sc: e.tensor_scalar(out=xn[:, kc, :],
                                                                  in0=banks_bf[0][:, kc * P:(kc + 1) * P],
                                                                  scalar1=sc, scalar2=None, op0=ALU.mult),
                       reads=[bbuf[0], b_pv], writes=[b_xn])
                else:
                    op(act, lambda e, kc=kc, sc=sc: e.activation(out=xn[:, kc, :],
                                                               in_=banks_bf[0][:, kc * P:(kc + 1) * P],
                                                               func=AF.Copy, scale=sc),
                       reads=[bbuf[0], b_pv], writes=[b_xn])
            yield
            for j0 in range(0, KC, 4):
                for j in range(j0, j0 + 4):
                    w, b_w = g.loadB(L, 1, j)
                    for kc in range(KC):
                        op(pe, lambda e, kc=kc, j=j, w=w: e.matmul(banks[1][:, (j - j0) * P:(j - j0 + 1) * P],
                                                                  lhsT=w[:, kc, :], rhs=xn[:, kc, :],
                                                                  start=(kc == 0), stop=(kc == KC - 1)),
                           reads=[b_w, b_xn], writes=[bbuf[1]], sig=(kc == KC - 1 and j == j0 + 3))
                for j in range(j0, j0 + 4):
                    op(act, lambda e, j=j: e.activation(out=kT[:, j, :], in_=banks[1][:, (j - j0) * P:(j - j0 + 1) * P],
                                                        func=AF.Identity, bias=pvc('bk', j)),
                       reads=[bbuf[1], b_pv], writes=[b_kT])
                yield
            for half in range(2):
                for q2 in range(NQ2):
                    w, b_w = g.loadA(L, 0, half * NQ2 + q2)
                    cs = slice(q2 * W2, (q2 + 1) * W2)
                    for kc in range(KC):
                        op(pe, lambda e, kc=kc, w=w, cs=cs: e.matmul(banks[1][:, cs], lhsT=xn[:, kc, :], rhs=w[:, kc, :],
                                                                    start=(kc == 0), stop=False),
                           reads=[b_xn, b_w], writes=[bbuf[1]], sig=False)
                    bo = 2 * D + half * 512 + q2 * W2
                    op(pe, lambda e, bo=bo, cs=cs: e.matmul(banks[1][:, cs], lhsT=g.onesrow[:],
                                                           rhs=g.brow_hi[0:1, bo:bo + W2], start=False, stop=False),
                       reads=[b_cst], writes=[bbuf[1]], sig=False)
                    op(pe, lambda e, bo=bo, cs=cs: e.matmul(banks[1][:, cs], lhsT=g.onesrow[:],
                                                           rhs=g.brow_lo[0:1, bo:bo + W2], start=False, stop=True),
                       reads=[b_cst], writes=[bbuf[1]], sig=(q2 == NQ2 - 1))
                op(act, lambda e, half=half: e.activation(out=va[:, half * 8:(half + 1) * 8, 0:64],
                                                          in_=banks[1][:, :].rearrange("p (h n) -> p h n", h=8),
                                                          func=AF.Copy), reads=[bbuf[1]], writes=[b_va])
                yield

        def gen_Q(cx):
            xn, b_xn = cx.xn
            for half in range(2):
                for q2 in range(NQ2):
                    w, b_w = g.loadA(L, 1, half * NQ2 + q2)
                    cs = slice(q2 * W2, (q2 + 1) * W2)
                    for kc in range(KC):
                        op(pe, lambda e, kc=kc, w=w, cs=cs: e.matmul(banks[1][:, cs], lhsT=xn[:, kc, :], rhs=w[:, kc, :],
                                                                    start=(kc == 0), stop=False),
                           reads=[b_xn, b_w], writes=[bbuf[1]], sig=False)
                    bo = 3 * D + half * 512 + q2 * W2
                    op(pe, lambda e, bo=bo, cs=cs: e.matmul(banks[1][:, cs], lhsT=g.onesrow[:],
                                                           rhs=g.brow_hi[0:1, bo:bo + W2], start=False, stop=False),
                       reads=[b_cst], writes=[bbuf[1]], sig=False)
                    op(pe, lambda e, bo=bo, cs=cs: e.matmul(banks[1][:, cs], lhsT=g.onesrow[:],
                                                           rhs=g.brow_lo[0:1, bo:bo + W2], start=False, stop=True),
                       reads=[b_cst], writes=[bbuf[1]], sig=(q2 == NQ2 - 1))
                op(act, lambda e, half=half: e.activation(out=zs[:, half * 512:(half + 1) * 512], in_=banks[1][:, :],
                                                          func=AF.Silu), reads=[bbuf[1]], writes=[b_zs])
                yield

        def gen_A(cx, j, u, kvs):
            i = cx.t
            xn, b_xn = cx.xn
            Ba, Bb, Bc = u.banks
            w, b_w = g.loadB(L, 0, j)
            for kc in range(KC):
                op(pe, lambda e, kc=kc: e.matmul(banks[Ba][:, 0:P], lhsT=w[:, kc, :], rhs=xn[:, kc, :],
                                                 start=(kc == 0), stop=(kc == KC - 1)),
                   reads=[b_w, b_xn], writes=[bbuf[Ba]], sig=(kc == KC - 1))
            op(act, lambda e: e.activation(out=u.qT[:], in_=banks[Ba][:, 0:P], func=AF.Identity, scale=0.125,
                                           bias=pvc('bq8', j)), reads=[bbuf[Ba], b_pv], writes=[u.b_qT])
            yield
            ent = geo[i]
            for h in range(2):
                hg = 2 * j + h
                for n_, (kt, di, pi) in enumerate(ent):
                    kT, b_kT = kvs[kt].kT
                    bk_ = Bb if n_ < 4 else Bc
                    co = (n_ % 4) * P
                    op(pe, lambda e, kT=kT, bk_=bk_, co=co, h=h: e.matmul(banks[bk_][:, co:co + P],
                                                                         lhsT=kT[hq[h], j, :], rhs=u.qT[hq[h], :],
                                                                         start=True, stop=False),
                       reads=[b_kT, u.b_qT], writes=[bbuf[bk_]], sig=False)
                    op(pe, lambda e, bk_=bk_, co=co, hg=hg, di=di: e.matmul(banks[bk_][:, co:co + P],
                                                                           lhsT=RB[:, hg, di, :], rhs=g.Jrev[:],
                                                                           start=False, stop=False),
                       reads=[b_lc, b_cst], writes=[bbuf[bk_]], sig=False)
                    last = (n_ == len(ent) - 1) or (n_ == 3)
                    op(pe, lambda e, bk_=bk_, co=co, pi=pi: e.matmul(banks[bk_][:, co:co + P], lhsT=MK[:, pi, :],
                                                                    rhs=g.Jrev[:], start=False, stop=True),
                       reads=[b_lc, b_cst], writes=[bbuf[bk_]], sig=last)
                n4 = min(4, len(ent))
                op(act, lambda e, n4=n4: e.activation(out=u.PT[:, 0:n4, :].rearrange("p n k -> p (n k)"),
                                                      in_=banks[Bb][:, 0:n4 * P], func=AF.Exp),
                   reads=[bbuf[Bb]], writes=[u.b_PT])
                if len(ent) > 4:
                    op(act, lambda e: e.activation(out=u.PT[:, 4, :], in_=banks[Bc][:, 0:P], func=AF.Exp),
                       reads=[bbuf[Bc]], writes=[u.b_PT])
                for n_, (kt, di, pi) in enumerate(ent):
                    va, b_va = kvs[kt].va
                    op(pe, lambda e, n_=n_, va=va, hg=hg, h=h: e.matmul(banks[Ba][:, 2 * P + h * 65:2 * P + h * 65 + 65],
                                                                       lhsT=u.PT[:, n_, :], rhs=va[:, hg, :],
                                                                       start=(n_ == 0), stop=(n_ == len(ent) - 1)),
                       reads=[u.b_PT, b_va], writes=[bbuf[Ba]], sig=(n_ == len(ent) - 1))
                yield
            for h in range(2):
                o0 = 2 * P + h * 65
                op(dve, lambda e, o0=o0, h=h: e.reciprocal(out=u.rc[:, h:h + 1], in_=banks[Ba][:, o0 + 64:o0 + 65]),
                   reads=[bbuf[Ba]], writes=[u.b_rc])
                op(dve, lambda e, o0=o0, h=h: e.tensor_scalar(out=og[:, (2 * j + h) * 64:(2 * j + h + 1) * 64],
                                                              in0=banks[Ba][:, o0:o0 + 64], scalar1=u.rc[:, h:h + 1],
                                                              scalar2=None, op0=ALU.mult),
                   reads=[bbuf[Ba], u.b_rc], writes=[b_og])
            yield

        def gen_O(cx):
            i = cx.t
            op(dve, lambda e: e.tensor_tensor(out=yg[:], in0=og[:], in1=zs[:], op=ALU.mult),
               reads=[b_og, b_zs], writes=[b_yg])
            for kc in range(KC):
                op(pe, lambda e, kc=kc: e.transpose(out=banks_bf[0][:, kc * P:(kc + 1) * P],
                                                    in_=yg[:, kc * P:(kc + 1) * P], identity=g.identb[:]),
                   reads=[b_yg, g.b_identb], writes=[bbuf[0]], sig=(kc == KC - 1))
            op(act, lambda e: e.activation(out=ygT[:].rearrange("p k n -> p (k n)"), in_=banks_bf[0][:, :],
                                           func=AF.Copy), reads=[bbuf[0]], writes=[b_ygT])
            yield
            for half in range(2):
                for q2 in range(NQ2):
                    w, b_w = g.loadA(L, 2, half * NQ2 + q2)
                    cs = slice(q2 * W2, (q2 + 1) * W2)
                    for kc in range(KC):
                        op(pe, lambda e, kc=kc, w=w, cs=cs: e.matmul(banks[0][:, cs], lhsT=ygT[:, kc, :], rhs=w[:, kc, :],
                                                                    start=(kc == 0), stop=False),
                           reads=[b_ygT, b_w], writes=[bbuf[0]], sig=False)
                    bo = 4 * D + half * 512 + q2 * W2
                    op(pe, lambda e, bo=bo, cs=cs: e.matmul(banks[0][:, cs], lhsT=g.onesrow[:],
                                                           rhs=g.brow_hi[0:1, bo:bo + W2], start=False, stop=False),
                       reads=[b_cst], writes=[bbuf[0]], sig=False)
                    op(pe, lambda e, bo=bo, cs=cs: e.matmul(banks[0][:, cs], lhsT=g.onesrow[:],
                                                           rhs=g.brow_lo[0:1, bo:bo + W2], start=False, stop=True),
                       reads=[b_cst], writes=[bbuf[0]], sig=(q2 == NQ2 - 1))
                op(act, lambda e, half=half: e.activation(out=tmpB[:, half * 512:(half + 1) * 512], in_=banks[0][:, :],
                                                          func=AF.Copy), reads=[bbuf[0]], writes=[b_tmpB])
                yield
            op(act, lambda e: e.activation(out=tmpA[:], in_=tmpB[:], func=AF.Square, accum_out=stat[:, 4:5]),
               reads=[b_tmpB], writes=[b_tmpA, b_stat])
            op(dve, lambda e: e.tensor_scalar(out=stat[:, 5:6], in0=stat[:, 4:5], scalar1=1.0 / D, scalar2=RMS_EPS,
                                              op0=ALU.mult, op1=ALU.add), reads=[b_stat], writes=[b_stat])
            op(act, lambda e: e.activation(out=stat[:, 6:7], in_=stat[:, 5:6], func=AF.Sqrt), reads=[b_stat],
               writes=[b_stat])
            op(dve, lambda e: e.reciprocal(out=stat[:, 7:8], in_=stat[:, 6:7]), reads=[b_stat], writes=[b_stat])
            op(dve, lambda e: e.scalar_tensor_tensor(out=tmpB[:], in0=tmpB[:], scalar=stat[:, 7:8], in1=postg[:],
                                                     op0=ALU.mult, op1=ALU.mult),
               reads=[b_tmpB, b_stat, b_lc], writes=[b_tmpB])
            op(pool, lambda e: e.tensor_tensor(out=tmpA[:], in0=tmpB[:], in1=x1[:, i, :], op=ALU.add),
               reads=[b_tmpB, bx1[i]], writes=[b_tmpA])
            T_.dma(pool, g.y_out[si, i * P:(i + 1) * P, :], tmpA[:], reads=[b_tmpA])
            yield

        tasks = []
        kvs = [None] * NCH
        lastO = None
        lastKV = None

        def make_KV(t):
            cx = NS()
            cx.t = t
            tk = Task("K%d" % t)
            cx.kxn, cx.xn = p_xn.acquire(tk)
            cx.kkT, cx.kT = p_kT.acquire(tk)
            cx.kva, cx.va = p_va.acquire(tk)
            if lastKV[0] is not None:
                tk.deps.append(lastKV[0])
            tk.gen = gen_KV(cx)
            cx.tk = tk
            kvs[t] = cx
            tasks.append(tk)
            lastKV[0] = tk

        lastKV = [None]
        for s in range(NCH + 3):
            if s < NCH:
                make_KV(s)
            i = s - 3
            if i < 0:
                continue
            cx = kvs[i]
            need = [kvs[kt].tk for (kt, _, _) in geo[i]]
            tq = Task("Q%d" % i)
            tq.deps += [cx.tk]
            if lastO is not None:
                tq.deps.append(lastO)
            p_xn.share(tq, cx.kxn)
            tq.gen = gen_Q(cx)
            tasks.append(tq)
            tas = []
            for j in range(KC):
                ta = Task("A%d_%d" % (i, j))
                ta.deps += need + [cx.tk]
                if lastO is not None:
                    ta.deps.append(lastO)
                _, uset = p_uset.acquire(ta)
                p_xn.share(ta, cx.kxn)
                for (kt, _, _) in geo[i]:
                    p_kT.share(ta, kvs[kt].kkT)
                    p_va.share(ta, kvs[kt].kva)
                ta.gen = gen_A(cx, j, uset, kvs)
                tas.append(ta)
                tasks.append(ta)
            to = Task("O%d" % i)
            to.deps += tas + [tq]
            to.gen = gen_O(cx)
            tasks.append(to)
            lastO = to
        run_tasks(tasks, window=WINDOW)


def kernel(**inputs):
    xp = np.asarray(inputs['x_prompt'], dtype=np.float32)
    xs_ = np.asarray(inputs['x_sample'], dtype=np.float32)
    xall = np.concatenate([xp, xs_], axis=0)
    nseq = xall.shape[0] // N_CORES
    nc = build(nseq)
    in_maps = []
    for ci in range(N_CORES):
        m = {"x": np.ascontiguousarray(xall[ci * nseq:(ci + 1) * nseq])}
        for nm in W_NAMES:
            m[nm] = np.ascontiguousarray(np.asarray(inputs[nm], dtype=np.float32))
        in_maps.append(m)
    res = run_bass_kernel_spmd(nc, in_maps, core_ids=list(range(N_CORES)))
    yall = np.concatenate([r["y"] for r in res.results], axis=0)
    nb = xp.shape[0]
    return (np.ascontiguousarray(yall[:nb]), np.ascontiguousarray(yall[nb:]))
```

```python
import numpy as np
from contextlib import ExitStack
import concourse.bass as bass
import concourse.mybir as mybir
from concourse.bass_utils import run_bass_kernel_spmd
from concourse.alu_op_type import AluOpType as ALU

F32 = mybir.dt.float32
BF16 = mybir.dt.bfloat16
AF = mybir.ActivationFunctionType
AX = mybir.AxisListType

N_CORES = 8
D = 1024
KC = 8
P = 128
T_SEQ = 2048
LAM = float(np.exp(-0.5))
LNX_EPS = 64e-5
RMS_EPS = 1e-6

import os
SAME_ENGINE_SYNC = os.environ.get('K_SES', '1') == '1'
WINDOW = int(os.environ.get('K_WIN', '4'))
FY = int(os.environ.get('K_FY', '0'))
UY = int(os.environ.get('K_UY', '1'))
STAGGER = int(os.environ.get('K_STAG', '0'))
NQ2 = int(os.environ.get('K_NQ2', '1'))
W2 = 512 // NQ2
DBUF = int(os.environ.get('K_DBUF', '1'))


class _Sem:
    def __init__(self, sem, name):
        self.sem = sem
        self.name = name
        self.n = 0


class Eng:
    def __init__(self, h, sem, name, is_pe=False):
        self.h = h
        self.s = _Sem(sem, name)
        self.name = name
        self.is_pe = is_pe
        self.waited = {}


class Buf:
    __slots__ = ("name", "w", "r", "excl")

    def __init__(self, name="", excl=False):
        self.name = name
        self.w = None
        self.r = {}
        self.excl = excl


class Trk:
    def __init__(self, nc, es, n_slots=16):
        self.nc = nc
        mk = lambda nm: es.enter_context(nc.semaphore(nm))
        self.pe = Eng(nc.tensor, mk("s_pe"), "pe", is_pe=True)
        self.act = Eng(nc.scalar, mk("s_act"), "act")
        self.dve = Eng(nc.vector, mk("s_dve"), "dve")
        self.pool = Eng(nc.gpsimd, mk("s_pool"), "pool")
        self.sp = Eng(nc.sync, mk("s_sp"), "sp")
        self.engs = [self.pe, self.act, self.dve, self.pool, self.sp]
        self.slots = [_Sem(mk("s_dma%d" % i), "dma%d" % i) for i in range(n_slots)]
        self.dma_i = 0
        self.slots_sw = [_Sem(mk("s_swdma%d" % i), "swdma%d" % i) for i in range(4)]
        self.dma_sw_i = 0
        self.n_inst = 0
        self.cnt = {}

    def _deps(self, reads, writes):
        deps = {}
        for b in reads:
            if b.w is not None:
                s, v = b.w
                if deps.get(s, 0) < v:
                    deps[s] = v
        for b in writes:
            if b.w is not None:
                s, v = b.w
                if deps.get(s, 0) < v:
                    deps[s] = v
            for s, v in b.r.items():
                if deps.get(s, 0) < v:
                    deps[s] = v
        return deps

    def _wait(self, eng, deps):
        for s, v in deps.items():
            if eng.waited.get(s, 0) >= v:
                continue
            if s is eng.s:
                if eng.is_pe or not SAME_ENGINE_SYNC:
                    continue
            eng.h.wait_ge(s.sem, v)
            eng.waited[s] = v

    def _mark(self, tok, reads, writes):
        s, v = tok
        for b in reads:
            if b.r.get(s, 0) < v:
                b.r[s] = v
        for b in writes:
            b.w = tok
            b.r = {}

    def op(self, eng, fn, reads=(), writes=(), sig=True):
        if any(b.excl for b in reads):
            writes = list(writes) + [b for b in reads if b.excl]
            reads = [b for b in reads if not b.excl]
        self._wait(eng, self._deps(reads, writes))
        inst = fn(eng.h)
        tok = (eng.s, eng.s.n + 1)
        if sig:
            inst.then_inc(eng.s.sem, 1)
            eng.s.n += 1
        self._mark(tok, reads, writes)
        self.n_inst += 1
        self.cnt[eng.name] = self.cnt.get(eng.name, 0) + 1
        return inst

    def dma(self, q, out, in_, reads=(), writes=(), **kw):
        if q is self.pool:
            slot = self.slots_sw[self.dma_sw_i % len(self.slots_sw)]
            self.dma_sw_i += 1
        else:
            slot = self.slots[self.dma_i % len(self.slots)]
            self.dma_i += 1
        deps = self._deps(reads, writes)
        if slot.n > 0 and deps.get(slot, 0) < slot.n:
            deps[slot] = slot.n
        self._wait(q, deps)
        inst = q.h.dma_start(out=out, in_=in_, **kw)
        inst.then_inc(slot.sem, 16)
        slot.n += 16
        self._mark((slot, slot.n), reads, writes)
        self.n_inst += 1
        self.cnt["dma"] = self.cnt.get("dma", 0) + 1
        return inst

    def barrier(self):
        allsems = [e.s for e in self.engs] + self.slots + self.slots_sw
        for e in self.engs:
            deps = {s: s.n for s in allsems if s.n > 0 and s is not e.s}
            self._wait(e, deps)

    def finish(self):
        deps = {s: s.n for s in self.slots + self.slots_sw if s.n > 0}
        self._wait(self.sp, deps)


class Task:
    def __init__(self, name):
        self.name = name
        self.deps = []
        self.done = False
        self.gen = None


def run_tasks(tasks, window=3):
    pending = list(tasks)
    active = []
    while pending or active:
        while pending and len(active) < window and all(d.done for d in pending[0].deps):
            active.append(pending.pop(0))
        if not active:
            raise RuntimeError("scheduler deadlock at %s" % pending[0].name)
        for t in list(active):
            try:
                next(t.gen)
            except StopIteration:
                t.done = True
                active.remove(t)


class RPool:
    def __init__(self, items):
        self.items = items
        self.i = 0
        self.users = [[] for _ in items]

    def acquire(self, task):
        k = self.i % len(self.items)
        self.i += 1
        task.deps += self.users[k]
        self.users[k] = [task]
        return k, self.items[k]

    def share(self, task, k):
        self.users[k].append(task)


class NS:
    pass


W_NAMES = ['pre_norm_g', 'post_norm_g', 'rk_mu', 'rk_w_r', 'rk_w_k', 'rk_w_v', 'rk_w_z', 'rk_w0', 'rk_w1', 'rk_w2',
           'rk_a0', 'rk_a1', 'rk_a2', 'rk_k_k', 'rk_k_a', 'rk_r_k', 'rk_lnx_w', 'rk_lnx_b', 'rk_w_o', 'na_w_in',
           'na_b_in', 'na_rpb', 'na_w_o', 'na_b_o']
W_SHAPES = {
    'pre_norm_g': [2, D], 'post_norm_g': [2, D], 'rk_mu': [1, 7, D], 'rk_w_r': [1, D, D], 'rk_w_k': [1, D, D],
    'rk_w_v': [1, D, D], 'rk_w_z': [1, D, D], 'rk_w0': [1, 2, D], 'rk_w1': [1, 2, D, 64], 'rk_w2': [1, 2, 64, D],
    'rk_a0': [1, 2, D], 'rk_a1': [1, 2, D, 64], 'rk_a2': [1, 2, 64, D], 'rk_k_k': [1, D], 'rk_k_a': [1, D],
    'rk_r_k': [1, 16, 64], 'rk_lnx_w': [1, D], 'rk_lnx_b': [1, D], 'rk_w_o': [1, D, D], 'na_w_in': [1, D, 4 * D],
    'na_b_in': [1, 4 * D], 'na_rpb': [1, 16, 15, 31], 'na_w_o': [1, D, D], 'na_b_o': [1, D],
}

PV = {}
_c = 0
for _nm, _n in [('mu', 7), ('w0', 2), ('a0', 2), ('k_k', 1), ('k_a', 1), ('r_k', 1), ('pre_g', 2), ('bq', 1),
                ('bk', 1), ('omk_a', 1), ('bq8', 1)]:
    PV[_nm] = _c
    _c += _n * 8
PV_COLS = _c
PV_ROWS = PV['omk_a']


def build(nseq, T=T_SEQ, stage="full"):
    NCH = T // P
    nc = bass.Bass("TRN2", target_bir_lowering=False)
    x_in = nc.dram_tensor("x", [nseq, T, D], F32, kind="ExternalInput").ap()
    y_out = nc.dram_tensor("y", [nseq, T, D], F32, kind="ExternalOutput").ap()
    wd = {nm: nc.dram_tensor(nm, W_SHAPES[nm], F32, kind="ExternalInput").ap() for nm in W_NAMES}
    wbJ = [nc.dram_tensor("wbJ%d" % l, [2, KC, P, KC, P], BF16, kind="Internal").ap() for l in range(2)]
    wbH = [nc.dram_tensor("wbH%d" % l, [3, 2 * NQ2, P, KC, W2], BF16, kind="Internal").ap() for l in range(2)]

    es = ExitStack()
    with es:
        T_ = Trk(nc, es)
        pe, act, dve, pool, sp = T_.pe, T_.act, T_.dve, T_.pool, T_.sp
        op = T_.op

        uid = [0]

        def sb(es_, nm, sh, dt):
            uid[0] += 1
            return es_.enter_context(nc.sbuf_tensor("%s_%d" % (nm, uid[0]), sh, dt))

        banks = [es.enter_context(nc.psum_tensor("pb%d" % i, [P, 512], F32)) for i in range(8)]
        bbuf = [Buf("pb%d" % i, excl=True) for i in range(8)]
        banks_bf = [b[:].bitcast(BF16) for b in banks]

        x1 = sb(es, "x1", [P, NCH, D], F32)
        bx1 = [Buf("x1_%d" % c) for c in range(NCH)]
        ringA = [sb(es, "ringA%d" % i, [P, KC, W2], BF16) for i in range(2)]
        b_ringA = [Buf("ringA%d" % i) for i in range(2)]
        ringB = [sb(es, "ringB%d" % i, [P, KC, P], BF16) for i in range(4)]
        b_ringB = [Buf("ringB%d" % i) for i in range(4)]
        rA_i = [0]
        rB_i = [0]
        identb = sb(es, "identb", [P, P], BF16)
        b_identb = Buf("identb")
        pv = sb(es, "pv", [P, PV_COLS], F32)
        b_pv = Buf("pv")
        onesrow = sb(es, "onesrow", [1, P], BF16)
        brow_hi = sb(es, "brow_hi", [1, 5 * D], BF16)
        brow_lo = sb(es, "brow_lo", [1, 5 * D], BF16)
        b_cst = Buf("consts")
        maskq = [sb(es, "maskq%d" % d, [P, 512], BF16) for d in range(2)]
        bdones = sb(es, "bdones", [P, P], BF16)
        hsel = sb(es, "hsel", [P, 2], BF16)
        ones_f = sb(es, "ones_f", [P, P], F32)
        Jrev = sb(es, "Jrev", [P, P], BF16)
        pes0 = ExitStack()
        io_r = sb(pes0, "io_r", [P, P], F32)
        io_c = sb(pes0, "io_c", [P, P], F32)

        def pvc(nm, idx):
            c0 = PV[nm] + idx
            return pv[:, c0:c0 + 1]

        def loadA(layer, m, qt):
            k = rA_i[0] % 2
            rA_i[0] += 1
            T_.dma(sp, ringA[k][:], wbH[layer][m, qt], writes=[b_ringA[k]])
            return ringA[k], b_ringA[k]

        def loadB(layer, m, j):
            k = rB_i[0] % 4
            rB_i[0] += 1
            T_.dma(sp, ringB[k][:], wbJ[layer][m, j], writes=[b_ringB[k]])
            return ringB[k], b_ringB[k]

        op(pool, lambda e: e.iota(io_r[:], pattern=[[0, P]], base=0, channel_multiplier=1,
                                  allow_small_or_imprecise_dtypes=True), writes=[b_cst])
        op(pool, lambda e: e.iota(io_c[:], pattern=[[1, P]], base=0, channel_multiplier=0,
                                  allow_small_or_imprecise_dtypes=True), writes=[b_cst])
        op(dve, lambda e: e.tensor_tensor(out=identb[:], in0=io_r[:], in1=io_c[:], op=ALU.is_equal),
           reads=[b_cst], writes=[b_identb])
        op(dve, lambda e: e.memset(onesrow[:], 1.0), writes=[b_cst])
        op(dve, lambda e: e.memset(ones_f[:], 1.0), writes=[b_cst])
        for d_, (o_s, o_i) in enumerate([(ALU.is_lt, ALU.is_le), (ALU.is_gt, ALU.is_ge)]):
            for q4 in range(4):
                o_ = o_s if q4 % 2 == 0 else o_i
                op(dve, lambda e, d_=d_, q4=q4, o_=o_: e.tensor_tensor(out=maskq[d_][:, q4 * P:(q4 + 1) * P],
                                                                      in0=io_r[:], in1=io_c[:], op=o_),
                   reads=[b_cst], writes=[b_cst])

        with ExitStack() as pes:
            identf = sb(pes, "identf", [P, P], F32)
            rb_ = sb(pes, "rb_", [P, P], F32)
            cb_ = sb(pes, "cb_", [P, P], F32)
            op(dve, lambda e: e.tensor_tensor(out=identf[:], in0=io_r[:], in1=io_c[:], op=ALU.is_equal),
               reads=[b_cst], writes=[b_cst])
            op(dve, lambda e: e.tensor_scalar(out=rb_[:], in0=io_r[:], scalar1=63.5, scalar2=None, op0=ALU.is_gt),
               reads=[b_cst], writes=[b_cst])
            op(dve, lambda e: e.tensor_scalar(out=cb_[:], in0=io_c[:], scalar1=63.5, scalar2=None, op0=ALU.is_gt),
               reads=[b_cst], writes=[b_cst])
            op(dve, lambda e: e.tensor_tensor(out=bdones[:], in0=rb_[:], in1=cb_[:], op=ALU.is_equal),
               reads=[b_cst], writes=[b_cst])
            op(dve, lambda e: e.tensor_copy(out=hsel[:, 1:2], in_=rb_[:, 0:1]), reads=[b_cst], writes=[b_cst])
            op(dve, lambda e: e.tensor_scalar(out=hsel[:, 0:1], in0=rb_[:, 0:1], scalar1=-1.0, scalar2=1.0,
                                              op0=ALU.mult, op1=ALU.add), reads=[b_cst], writes=[b_cst])

            rows = sb(pes, "pvrows", [P, 2, P], F32)
            b_rows = Buf("pvrows")
            op(dve, lambda e: e.memset(rows[:], 0.0), writes=[b_rows])

            def load_rows(r0, src):
                n = src.shape[0]
                g, o = divmod(r0, P)
                assert o + n <= P, (r0, n)
                T_.dma(sp, rows[o:o + n, g, :], src, writes=[b_rows])

            load_rows(PV['mu'], wd['rk_mu'][0].rearrange("m (j q) -> (m j) q", q=P))
            load_rows(PV['w0'], wd['rk_w0'][0].rearrange("m (j q) -> (m j) q", q=P))
            load_rows(PV['a0'], wd['rk_a0'][0].rearrange("m (j q) -> (m j) q", q=P))
            load_rows(PV['k_k'], wd['rk_k_k'][0].rearrange("(j q) -> j q", q=P))
            load_rows(PV['k_a'], wd['rk_k_a'][0].rearrange("(j q) -> j q", q=P))
            load_rows(PV['r_k'], wd['rk_r_k'][0].rearrange("(j h) c -> j (h c)", h=2))
            load_rows(PV['pre_g'], wd['pre_norm_g'].rearrange("m (j q) -> (m j) q", q=P))
            load_rows(PV['bq'], wd['na_b_in'][0, 0:D].rearrange("(j q) -> j q", q=P))
            load_rows(PV['bk'], wd['na_b_in'][0, D:2 * D].rearrange("(j q) -> j q", q=P))
            for g in range(2):
                op(pe, lambda e, g=g: e.transpose(out=banks[0][:, g * P:(g + 1) * P], in_=rows[:, g, :],
                                                  identity=identf[:]),
                   reads=[b_rows, b_cst], writes=[bbuf[0]])
            op(dve, lambda e: e.tensor_copy(out=pv[:, 0:PV_ROWS], in_=banks[0][:, 0:PV_ROWS]),
               reads=[bbuf[0]], writes=[b_pv])
            op(dve, lambda e: e.tensor_scalar(out=pv[:, PV['omk_a']:PV['omk_a'] + 8],
                                              in0=pv[:, PV['k_a']:PV['k_a'] + 8], scalar1=-1.0, scalar2=1.0,
                                              op0=ALU.mult, op1=ALU.add), reads=[b_pv], writes=[b_pv])
            op(dve, lambda e: e.tensor_scalar(out=pv[:, PV['bq8']:PV['bq8'] + 8], in0=pv[:, PV['bq']:PV['bq'] + 8],
                                              scalar1=0.125, scalar2=None, op0=ALU.mult), reads=[b_pv], writes=[b_pv])

            brow_f = sb(pes, "brow_f", [1, 5 * D], F32)
            brow_t = sb(pes, "brow_t", [1, 5 * D], F32)
            b_bf = Buf("brow_f")
            T_.dma(sp, brow_f[0:1, 0:4 * D], wd['na_b_in'][0:1, :], writes=[b_bf])
            T_.dma(sp, brow_f[0:1, 4 * D:5 * D], wd['na_b_o'][0:1, :], writes=[b_bf])
            op(act, lambda e: e.activation(out=brow_hi[:], in_=brow_f[:], func=AF.Copy), reads=[b_bf], writes=[b_cst])
            op(dve, lambda e: e.tensor_tensor(out=brow_t[:], in0=brow_f[:], in1=brow_hi[:], op=ALU.subtract),
               reads=[b_bf, b_cst], writes=[b_bf])
            op(act, lambda e: e.activation(out=brow_lo[:], in_=brow_t[:], func=AF.Copy), reads=[b_bf], writes=[b_cst])

            stg = [sb(pes, "stg%d" % i, [P, D], F32) for i in range(3)]
            stb = [sb(pes, "stb%d" % i, [P, D], BF16) for i in range(3)]
            b_stg = [Buf() for _ in range(3)]
            b_stb = [Buf() for _ in range(3)]
            srcs = []
            for kc in range(KC):
                rs_ = slice(kc * P, (kc + 1) * P)
                for m, nm in enumerate(['rk_w_r', 'rk_w_k']):
                    srcs.append((wd[nm][0, rs_, :], wbJ[0][m, :, :, kc, :].rearrange("j p n -> p j n"), 'J'))
                for m, nm in enumerate(['rk_w_v', 'rk_w_z', 'rk_w_o']):
                    srcs.append((wd[nm][0, rs_, :], wbH[0][m, :, :, kc, :].rearrange("h p n -> p h n"), 'H'))
                for m in range(2):
                    srcs.append((wd['na_w_in'][0, rs_, m * D:(m + 1) * D],
                                 wbJ[1][m, :, :, kc, :].rearrange("j p n -> p j n"), 'J'))
                for m in range(2):
                    srcs.append((wd['na_w_in'][0, rs_, (m + 2) * D:(m + 3) * D],
                                 wbH[1][m, :, :, kc, :].rearrange("h p n -> p h n"), 'H'))
                srcs.append((wd['na_w_o'][0, rs_, :], wbH[1][2, :, :, kc, :].rearrange("h p n -> p h n"), 'H'))
            cast_engs = [act, dve, pool]
            for i, (src, dst, kind) in enumerate(srcs):
                k = i % 3
                T_.dma(sp, stg[k][:], src, writes=[b_stg[k]])
                if cast_engs[k] is act:
                    op(act, lambda e, k=k: e.activation(out=stb[k][:], in_=stg[k][:], func=AF.Copy),
                       reads=[b_stg[k]], writes=[b_stb[k]])
                else:
                    op(cast_engs[k], lambda e, k=k: e.tensor_copy(out=stb[k][:], in_=stg[k][:]),
                       reads=[b_stg[k]], writes=[b_stb[k]])
                if kind == 'J':
                    srcv = stb[k][:].rearrange("p (j n) -> p j n", j=KC)
                else:
                    srcv = stb[k][:].rearrange("p (h n) -> p h n", h=2 * NQ2)
                T_.dma(sp, dst, srcv, reads=[b_stb[k]])
            T_.barrier()
            T_.finish()
        pes0.close()

        ctx = NS()
        ctx.__dict__.update(locals())
        if stage != "l0":
            l1_prologue(ctx)
        for si in range(nseq):
            with nc.named_scope('L0_%d' % si):
                layer0(ctx, si)
            T_.barrier()
            T_.finish()
            if stage == "l0":
                for c in range(NCH):
                    T_.dma(sp, y_out[si, c * P:(c + 1) * P, :], x1[:, c, :], reads=[bx1[c]])
            else:
                with nc.named_scope('L1_%d' % si):
                    layer1(ctx, si)
            T_.barrier()
            T_.finish()
        print("instructions:", T_.n_inst, T_.cnt)
    return nc


def layer0(g, si):
    nc, T_, op = g.nc, g.T_, g.op
    pe, act, dve, pool, sp = g.pe, g.act, g.dve, g.pool, g.sp
    banks, bbuf, banks_bf = g.banks, g.bbuf, g.banks_bf
    NCH, x1, bx1, pv, b_pv, pvc = g.NCH, g.x1, g.bx1, g.pv, g.b_pv, g.pvc
    wd, sb = g.wd, g.sb
    b_cst = g.b_cst
    L = 0

    with ExitStack() as l0:
        w1b = sb(l0, "w1b", [P, 2, KC, 64], BF16)
        a1b = sb(l0, "a1b", [P, 2, KC, 64], BF16)
        w2b = sb(l0, "w2b", [64, 2, D], BF16)
        a2b = sb(l0, "a2b", [64, 2, D], BF16)
        lnxb = sb(l0, "lnxb", [P, 2, D], F32)
        postg = sb(l0, "postg", [P, D], F32)
        b_lc = Buf("l0consts")
        tmpA = sb(l0, "tmpA", [P, D], F32)
        tmpB = sb(l0, "tmpB", [P, D], F32)
        b_tmpA, b_tmpB = Buf("tmpA"), Buf("tmpB")
        T_.dma(sp, tmpA[:].rearrange("p (d k n) -> p d k n", d=2, k=KC),
               wd['rk_w1'][0].rearrange("d (k p) n -> p d k n", p=P), writes=[b_tmpA])
        op(dve, lambda e: e.tensor_copy(out=w1b[:].rearrange("p d k n -> p (d k n)"), in_=tmpA[:]),
           reads=[b_tmpA], writes=[b_lc])
        T_.dma(sp, tmpB[:].rearrange("p (d k n) -> p d k n", d=2, k=KC),
               wd['rk_a1'][0].rearrange("d (k p) n -> p d k n", p=P), writes=[b_tmpB])
        op(dve, lambda e: e.tensor_copy(out=a1b[:].rearrange("p d k n -> p (d k n)"), in_=tmpB[:]),
           reads=[b_tmpB], writes=[b_lc])
        for d_ in range(2):
            T_.dma(sp, tmpA[0:64, :], wd['rk_w2'][0, d_], writes=[b_tmpA])
            op(dve, lambda e, d_=d_: e.tensor_copy(out=w2b[:, d_, :], in_=tmpA[0:64, :]), reads=[b_tmpA], writes=[b_lc])
            T_.dma(sp, tmpB[0:64, :], wd['rk_a2'][0, d_], writes=[b_tmpB])
            op(dve, lambda e, d_=d_: e.tensor_copy(out=a2b[:, d_, :], in_=tmpB[0:64, :]), reads=[b_tmpB], writes=[b_lc])
        T_.dma(sp, lnxb[:, 0, :], wd['rk_lnx_w'][0].partition_broadcast(P), writes=[b_lc])
        T_.dma(sp, lnxb[:, 1, :], wd['rk_lnx_b'][0].partition_broadcast(P), writes=[b_lc])
        T_.dma(sp, postg[:], wd['post_norm_g'][L].partition_broadcast(P), writes=[b_lc])

        def mk(nm, sh, dt, n):
            return [(sb(l0, "%s%d" % (nm, i), sh, dt), Buf("%s%d" % (nm, i))) for i in range(n)]

        p_xin = RPool(mk("xin", [P, D], F32, 1))
        xs, b_xs = mk("xs", [P, D], BF16, 1)[0]
        stat = sb(l0, "stat", [P, 8], F32)
        b_stat = Buf("stat")
        p_slot = RPool(mk("xnT", [P, KC, P + 2], BF16, 3))
        xx, b_xx = mk("xx", [P, KC, P], BF16, 1)[0]
        ltmp, b_ltmp = mk("ltmp", [P, P], F32, 1)[0]
        p_lr = RPool(mk("lrp_r", [P, KC, P], BF16, DBUF))
        p_lk = RPool(mk("lrp_k", [P, KC, P], BF16, DBUF))
        p_lt = mk("lrp_t", [P, KC, P], BF16, 1)
        lt_i = [0]
        p_vt = RPool(mk("Vtm", [P, D], BF16, DBUF))
        p_zs = RPool(mk("zs", [P, D], BF16, 1))
        p_th = RPool(mk("th", [64, 2 * P], BF16, 2))
        Hf = sb(l0, "Hf", [P, KC, 64], F32)
        Hb = sb(l0, "Hb", [P, KC, 64], BF16)
        b_H = [Buf("H%d" % j) for j in range(KC)]
        bonF = sb(l0, "bonF", [P, NCH, 16], F32)
        b_bonF = [Buf("bonF%d" % c) for c in range(NCH)]
        p_bonB = RPool(mk("bonB", [P, 16], F32, 2))
        p_yb = RPool(mk("Yb", [P, D], F32, 1))
        yg, b_yg = mk("yg", [P, D], BF16, 1)[0]
        ygT, b_ygT = mk("ygT", [P, KC, P], BF16, 1)[0]
        gst = sb(l0, "gst", [P, 6, 16], F32)
        b_gst = Buf("gst")

        def mkset(i):
            u = NS()
            u.i = i
            def t(nm, sh, dt):
                tt = sb(l0, "u%d_%s" % (i, nm), sh, dt)
                setattr(u, nm, tt)
                setattr(u, "b_" + nm, Buf("u%d_%s" % (i, nm)))
            t("rk", [P, 2 * P], F32)
            for nm in ("sg", "al", "kk", "rs", "Ein", "Eex", "ein", "kd"):
                t(nm, [P, P], F32)
            t("sq", [P, P], BF16)
            t("arT", [P, 2 * P], BF16)
            t("btT", [P, P], BF16)
            t("ktT", [P, P], BF16)
            t("pr", [P, P], BF16)
            t("BK", [P, 2 * P], BF16)
            t("AT0", [P, 512], BF16)
            t("AT1", [P, 512], BF16)
            for h in range(2):
                for k in range(2):
                    t("C%d%d" % (h, k), [P, 3 * P], BF16)
            t("Xp", [P, P], BF16)
            t("Up", [P, P], BF16)
            t("Hs", [P, 64], F32)
            t("sc", [P, 4], F32)
            u.banks = (2 + 3 * i, 3 + 3 * i, 4 + 3 * i)
            return u

        p_uset = RPool([mkset(0), mkset(1)])

        def gen_Na(cx):
            c = cx.c
            xin, b_xin = cx.xin
            T_.dma(sp, xin[:], g.x_in[si, c * P:(c + 1) * P, :], writes=[b_xin])
            op(act, lambda e: e.activation(out=xs[:], in_=xin[:], func=AF.Square, accum_out=stat[:, 0:1]),
               reads=[b_xin], writes=[b_xs, b_stat])
            op(dve, lambda e: e.tensor_scalar(out=stat[:, 1:2], in0=stat[:, 0:1], scalar1=1.0 / D, scalar2=RMS_EPS,
                                              op0=ALU.mult, op1=ALU.add), reads=[b_stat], writes=[b_stat])
            op(act, lambda e: e.activation(out=stat[:, 2:3], in_=stat[:, 1:2], func=AF.Sqrt), reads=[b_stat],
               writes=[b_stat])
            op(dve, lambda e: e.reciprocal(out=stat[:, 3:4], in_=stat[:, 2:3]), reads=[b_stat], writes=[b_stat])
            op(act, lambda e: e.activation(out=xs[:], in_=xin[:], func=AF.Copy, scale=stat[:, 3:4]),
               reads=[b_xin, b_stat], writes=[b_xs])
            yield

        def gen_Nb(cx, d, first, last, prev):
            c = cx.c
            slot, b_slot = cx.slot
            for kc in range(KC):
                op(pe, lambda e, kc=kc: e.transpose(out=banks_bf[0][:, kc * P:(kc + 1) * P],
                                                    in_=xs[:, kc * P:(kc + 1) * P], identity=g.identb[:]),
                   reads=[b_xs, g.b_identb], writes=[bbuf[0]], sig=(kc == KC - 1))
            for kc in range(KC):
                sc = pvc('pre_g', L * 8 + kc)
                if kc % 2 == 0:
                    op(dve, lambda e, kc=kc, sc=sc: e.tensor_scalar(out=slot[:, kc, 1:P + 1],
                                                                  in0=banks_bf[0][:, kc * P:(kc + 1) * P],
                                                                  scalar1=sc, scalar2=None, op0=ALU.mult),
                       reads=[bbuf[0], b_pv], writes=[b_slot])
                else:
                    op(act, lambda e, kc=kc, sc=sc: e.activation(out=slot[:, kc, 1:P + 1],
                                                               in_=banks_bf[0][:, kc * P:(kc + 1) * P],
                                                               func=AF.Copy, scale=sc),
                       reads=[bbuf[0], b_pv], writes=[b_slot])
            near, far = (0, P + 1) if d == 0 else (P + 1, 0)
            if first:
                op(pool, lambda e: e.memset(slot[:, :, near:near + 1], 0.0), writes=[b_slot])
            else:
                pslot, b_pslot = prev.slot
                src_own = 1 if d == 0 else P
                src_prev = P if d == 0 else 1
                op(pool, lambda e: e.tensor_copy(out=slot[:, :, near:near + 1], in_=pslot[:, :, src_prev:src_prev + 1]),
                   reads=[b_pslot], writes=[b_slot])
                op(pool, lambda e: e.tensor_copy(out=pslot[:, :, far:far + 1], in_=slot[:, :, src_own:src_own + 1]),
                   reads=[b_slot], writes=[b_pslot])
            if last:
                op(pool, lambda e: e.memset(slot[:, :, far:far + 1], 0.0), writes=[b_slot])
            yield

        def gen_F(cx, d):
            slot, b_slot = cx.slot
            xn = slot[:, :, 1:P + 1]
            op(pool, lambda e: e.tensor_tensor(out=xx[:], in0=slot[:, :, 0:P], in1=slot[:, :, 2:P + 2], op=ALU.add),
               reads=[b_slot], writes=[b_xx])
            op(dve, lambda e: e.scalar_tensor_tensor(out=xx[:], in0=xx[:], scalar=0.5, in1=xn, op0=ALU.mult,
                                                     op1=ALU.subtract), reads=[b_xx, b_slot], writes=[b_xx])
            yield

            def lerp(m, dst, b_dst):
                for kc in range(KC):
                    sc = pvc('mu', m * 8 + kc)
                    op(dve, lambda e, kc=kc, sc=sc: e.scalar_tensor_tensor(
                        out=dst[:, kc, :], in0=xx[:, kc, :], scalar=sc, in1=slot[:, kc, 1:P + 1],
                        op0=ALU.mult, op1=ALU.add), reads=[b_xx, b_slot, b_pv], writes=[b_dst])

            def next_lt():
                r = p_lt[0]
                lt_i[0] += 1
                return r

            lr, b_lr = cx.lr
            lk, b_lk = cx.lk
            vt, b_vt = cx.vt
            lv, b_lv = next_lt()
            lerp(2, lv, b_lv)
            yield
            lerp(0, lr, b_lr)
            yield
            for _ in range(FY):
                yield
            for half in range(2):
                for q2 in range(NQ2):
                    w, b_w = g.loadA(L, 0, half * NQ2 + q2)
                    for kc in range(KC):
                        op(pe, lambda e, kc=kc, w=w, q2=q2: e.matmul(banks[1][:, q2 * W2:(q2 + 1) * W2],
                                                                    lhsT=lv[:, kc, :], rhs=w[:, kc, :],
                                                                    start=(kc == 0), stop=(kc == KC - 1)),
                           reads=[b_lv, b_w], writes=[bbuf[1]], sig=(kc == KC - 1 and q2 == NQ2 - 1))
                op(act, lambda e, half=half: e.activation(out=vt[:, half * 512:(half + 1) * 512], in_=banks[1][:, :],
                                                          func=AF.Copy), reads=[bbuf[1]], writes=[b_vt])
                yield
            if d == 1:
                zs, b_zs = cx.zs
                for half in range(2):
                    for q2 in range(NQ2):
                        w, b_w = g.loadA(L, 1, half * NQ2 + q2)
                        for kc in range(KC):
                            op(pe, lambda e, kc=kc, w=w, q2=q2: e.matmul(banks[1][:, q2 * W2:(q2 + 1) * W2],
                                                                        lhsT=slot[:, kc, 1:P + 1], rhs=w[:, kc, :],
                                                                        start=(kc == 0), stop=(kc == KC - 1)),
                               reads=[b_slot, b_w], writes=[bbuf[1]], sig=(kc == KC - 1 and q2 == NQ2 - 1))
                    op(act, lambda e, half=half: e.activation(out=zs[:, half * 512:(half + 1) * 512],
                                                              in_=banks[1][:, :], func=AF.Silu),
                       reads=[bbuf[1]], writes=[b_zs])
                    yield
            lerp(1, lk, b_lk)
            yield
            th, b_th = cx.th
            lw, b_lw = next_lt()
            lerp(3 + d, lw, b_lw)
            yield
            for _ in range(FY):
                yield
            for kc in range(KC):
                op(pe, lambda e, kc=kc: e.matmul(banks[1][0:64, 0:P], lhsT=w1b[:, d, kc, :], rhs=lw[:, kc, :],
                                                 start=(kc == 0), stop=(kc == KC - 1)),
                   reads=[b_lw, b_lc], writes=[bbuf[1]], sig=(kc == KC - 1))
            la, b_la = next_lt()
            lerp(5 + d, la, b_la)
            yield
            for _ in range(FY):
                yield
            for kc in range(KC):
                op(pe, lambda e, kc=kc: e.matmul(banks[1][0:64, P:2 * P], lhsT=a1b[:, d, kc, :], rhs=la[:, kc, :],
                                                 start=(kc == 0), stop=(kc == KC - 1)),
                   reads=[b_la, b_lc], writes=[bbuf[1]], sig=(kc == KC - 1))
            op(act, lambda e: e.activation(out=th[:, 0:P], in_=banks[1][0:64, 0:P], func=AF.Tanh),
               reads=[bbuf[1]], writes=[b_th])
            op(act, lambda e: e.activation(out=th[:, P:2 * P], in_=banks[1][0:64, P:2 * P], func=AF.Copy),
               reads=[bbuf[1]], writes=[b_th])
            yield

        def gen_U(cx, d, j, u):
            c = cx.c
            Ba, Bb, Bc = u.banks
            lr, b_lr = cx.lr
            lk, b_lk = cx.lk
            vt, b_vt = cx.vt
            th, b_th = cx.th
            hq = [slice(0, 64), slice(64, 128)]
            mq = g.maskq[d]
            mA = g.maskq[1 - d][:, 0:P]
            for _ in range(STAGGER if (j % 2 == 1) else 0):
                yield
            wr, b_wr = g.loadB(L, 0, j)
            wk, b_wk = g.loadB(L, 1, j)
            for kc in range(KC):
                op(pe, lambda e, kc=kc: e.matmul(banks[Ba][:, 0:P], lhsT=wr[:, kc, :], rhs=lr[:, kc, :],
                                                 start=(kc == 0), stop=(kc == KC - 1)),
                   reads=[b_wr, b_lr], writes=[bbuf[Ba]], sig=False)
            for kc in range(KC):
                op(pe, lambda e, kc=kc: e.matmul(banks[Ba][:, P:2 * P], lhsT=wk[:, kc, :], rhs=lk[:, kc, :],
                                                 start=(kc == 0), stop=(kc == KC - 1)),
                   reads=[b_wk, b_lk], writes=[bbuf[Ba]], sig=False)
            op(pe, lambda e: e.matmul(banks[Ba][:, 2 * P:3 * P], lhsT=w2b[:, d, j * P:(j + 1) * P], rhs=th[:, 0:P],
                                      start=True, stop=True), reads=[b_lc, b_th], writes=[bbuf[Ba]], sig=False)
            op(pe, lambda e: e.matmul(banks[Ba][:, 3 * P:4 * P], lhsT=a2b[:, d, j * P:(j + 1) * P], rhs=th[:, P:2 * P],
                                      start=True, stop=True), reads=[b_lc, b_th], writes=[bbuf[Ba]])
            op(act, lambda e: e.activation(out=u.rk[:], in_=banks[Ba][:, 0:2 * P], func=AF.Copy),
               reads=[bbuf[Ba]], writes=[u.b_rk])
            op(act, lambda e: e.activation(out=u.sg[:], in_=banks[Ba][:, 2 * P:3 * P], func=AF.Sigmoid,
                                           bias=pvc('w0', d * 8 + j)), reads=[bbuf[Ba], b_pv], writes=[u.b_sg])
            op(act, lambda e: e.activation(out=u.sq[:], in_=banks[Ba][:, P:2 * P], func=AF.Square,
                                           scale=pvc('k_k', j)), reads=[bbuf[Ba], b_pv], writes=[u.b_sq])
            op(act, lambda e: e.activation(out=u.al[:], in_=banks[Ba][:, 3 * P:4 * P], func=AF.Sigmoid,
                                           bias=pvc('a0', d * 8 + j)), reads=[bbuf[Ba], b_pv], writes=[u.b_al])
            yield
            rT = u.rk[:, 0:P]
            kT = u.rk[:, P:2 * P]
            op(dve, lambda e: e.tensor_scalar(out=u.kk[:], in0=kT, scalar1=pvc('k_k', j), scalar2=None, op0=ALU.mult),
               reads=[u.b_rk, b_pv], writes=[u.b_kk])
            op(pe, lambda e: e.matmul(banks[Ba][:, 0:P], lhsT=g.bdones[:], rhs=u.sq[:], start=True, stop=True),
               reads=[b_cst, u.b_sq], writes=[bbuf[Ba]])
            op(act, lambda e: e.activation(out=u.rs[:], in_=banks[Ba][:, 0:P], func=AF.Ln),
               reads=[bbuf[Ba]], writes=[u.b_rs])
            op(act, lambda e: e.activation(out=u.rs[:], in_=u.rs[:], func=AF.Exp, scale=-0.5),
               reads=[u.b_rs], writes=[u.b_rs])
            op(pool, lambda e: e.tensor_tensor(out=u.kk[:], in0=u.kk[:], in1=u.rs[:], op=ALU.mult),
               reads=[u.b_kk, u.b_rs], writes=[u.b_kk])
            op(dve, lambda e: e.tensor_tensor_scan(out=u.Ein[:], data0=g.ones_f[:], data1=u.sg[:], initial=0.0,
                                                   op0=ALU.mult, op1=ALU.add),
               reads=[b_cst, u.b_sg], writes=[u.b_Ein])
            tot = u.Ein[:, P - 1:P]
            op(act, lambda e: e.activation(out=u.sc[:, 0:1], in_=tot, func=AF.Exp, scale=-LAM),
               reads=[u.b_Ein], writes=[u.b_sc])
            if d == 0:
                op(pool, lambda e: e.tensor_tensor(out=u.Eex[:], in0=u.Ein[:], in1=u.sg[:], op=ALU.subtract),
                   reads=[u.b_Ein, u.b_sg], writes=[u.b_Eex])
            else:
                op(dve, lambda e: e.tensor_copy(out=u.sc[:, 1:2], in_=tot), reads=[u.b_Ein], writes=[u.b_sc])
                op(dve, lambda e: e.tensor_scalar(out=u.Eex[:], in0=u.Ein[:], scalar1=u.sc[:, 1:2], scalar2=-1.0,
                                                  op0=ALU.subtract, op1=ALU.mult),
                   reads=[u.b_Ein, u.b_sc], writes=[u.b_Eex])
                op(pool, lambda e: e.tensor_tensor(out=u.Ein[:], in0=u.Eex[:], in1=u.sg[:], op=ALU.add),
                   reads=[u.b_Eex, u.b_sg], writes=[u.b_Ein])
            yield
            op(act, lambda e: e.activation(out=u.ein[:], in_=u.Ein[:], func=AF.Exp, scale=-LAM),
               reads=[u.b_Ein], writes=[u.b_ein])
            op(act, lambda e: e.activation(out=u.Eex[:], in_=u.Eex[:], func=AF.Exp, scale=-LAM),
               reads=[u.b_Eex], writes=[u.b_Eex])
            op(act, lambda e: e.activation(out=u.Ein[:], in_=u.Ein[:], func=AF.Exp, scale=LAM),
               reads=[u.b_Ein], writes=[u.b_Ein])
            eng_ = u.Ein
            eex_ = u.Eex
            op(dve, lambda e: e.scalar_tensor_tensor(out=u.arT[:, 0:P], in0=u.kk[:], scalar=-1.0, in1=eex_[:],
                                                     op0=ALU.mult, op1=ALU.mult),
               reads=[u.b_kk, u.b_Eex], writes=[u.b_arT])
            op(pool, lambda e: e.tensor_tensor(out=u.arT[:, P:2 * P], in0=rT, in1=u.ein[:], op=ALU.mult),
               reads=[u.b_rk, u.b_ein], writes=[u.b_arT])
            op(dve, lambda e: e.tensor_scalar(out=u.kd[:], in0=u.al[:], scalar1=pvc('k_a', j),
                                               scalar2=pvc('omk_a', j), op0=ALU.mult, op1=ALU.add),
               reads=[u.b_al, b_pv], writes=[u.b_kd])
            op(pool, lambda e: e.tensor_tensor(out=u.kd[:], in0=u.kd[:], in1=kT, op=ALU.mult),
               reads=[u.b_kd, u.b_rk], writes=[u.b_kd])
            op(dve, lambda e: e.tensor_tensor(out=u.ktT[:], in0=u.kd[:], in1=eng_[:], op=ALU.mult),
               reads=[u.b_kd, u.b_Ein], writes=[u.b_ktT])
            op(pool, lambda e: e.tensor_tensor(out=u.al[:], in0=u.al[:], in1=u.kk[:], op=ALU.mult),
               reads=[u.b_al, u.b_kk], writes=[u.b_al])
            op(dve, lambda e: e.tensor_tensor(out=u.btT[:], in0=u.al[:], in1=eng_[:], op=ALU.mult),
               reads=[u.b_al, u.b_Ein], writes=[u.b_btT])
            op(dve, lambda e: e.scalar_tensor_tensor(out=u.pr[:], in0=rT, scalar=pvc('r_k', j), in1=u.kd[:],
                                                     op0=ALU.mult, op1=ALU.mult),
               reads=[u.b_rk, u.b_kd, b_pv], writes=[u.b_pr])
            yield
            for _ in range(UY):
                yield
            op(pe, lambda e: e.transpose(out=banks_bf[Ba][:, 0:P], in_=u.btT[:], identity=g.identb[:]),
               reads=[u.b_btT, g.b_identb], writes=[bbuf[Ba]], sig=False)
            op(pe, lambda e: e.transpose(out=banks_bf[Ba][:, P:2 * P], in_=u.ktT[:], identity=g.identb[:]),
               reads=[u.b_ktT, g.b_identb], writes=[bbuf[Ba]], sig=False)
            op(pe, lambda e: e.matmul(banks[Ba][:, 2 * P:2 * P + 2], lhsT=u.pr[:], rhs=g.hsel[:], start=True, stop=True),
               reads=[u.b_pr, b_cst], writes=[bbuf[Ba]])
            op(act, lambda e: e.activation(out=u.BK[:], in_=banks_bf[Ba][:, 0:2 * P], func=AF.Copy),
               reads=[bbuf[Ba]], writes=[u.b_BK])
            if d == 0:
                bdst, b_bdst = bonF[:, c, 2 * j:2 * j + 2], b_bonF[c]
            else:
                bt_, b_bdst = cx.bonB
                bdst = bt_[:, 2 * j:2 * j + 2]
            op(act, lambda e: e.activation(out=bdst, in_=banks[Ba][:, 2 * P:2 * P + 2], func=AF.Copy),
               reads=[bbuf[Ba]], writes=[b_bdst])
            yield
            for _ in range(UY):
                yield
            AT = [u.AT0, u.AT1]
            b_AT = [u.b_AT0, u.b_AT1]
            Cc = [[u.C00, u.C01], [u.C10, u.C11]]
            b_C = [[u.b_C00, u.b_C01], [u.b_C10, u.b_C11]]
            hb = [Bb, Bc]
            for h in range(2):
                B_ = hb[h]
                op(pe, lambda e, h=h, B_=B_: e.matmul(banks[B_][:, 0:2 * P], lhsT=u.btT[hq[h], :], rhs=u.arT[hq[h], :],
                                                      start=True, stop=True),
                   reads=[u.b_btT, u.b_arT], writes=[bbuf[B_]], sig=False)
                op(pe, lambda e, h=h, B_=B_: e.matmul(banks[B_][:, 2 * P:4 * P], lhsT=u.ktT[hq[h], :],
                                                      rhs=u.arT[hq[h], :], start=True, stop=True),
                   reads=[u.b_ktT, u.b_arT], writes=[bbuf[B_]])
                op(dve, lambda e, h=h, B_=B_: e.tensor_tensor(out=AT[h][:], in0=banks[B_][:, :], in1=mq[:],
                                                              op=ALU.mult),
                   reads=[bbuf[B_], b_cst], writes=[b_AT[h]])
                op(pe, lambda e, h=h, B_=B_: e.matmul(banks[B_][:, 0:P], lhsT=u.arT[hq[h], 0:P], rhs=u.btT[hq[h], :],
                                                      start=True, stop=True),
                   reads=[u.b_btT, u.b_arT], writes=[bbuf[B_]])
                op(dve, lambda e, h=h, B_=B_: e.tensor_tensor(out=Cc[h][0][:, 0:P], in0=banks[B_][:, 0:P], in1=mA,
                                                              op=ALU.mult),
                   reads=[bbuf[B_], b_cst], writes=[b_C[h][0]])
                op(pool, lambda e, h=h: e.tensor_tensor(out=Cc[h][1][:, 2 * P:3 * P], in0=AT[h][:, 0:P],
                                                        in1=g.identb[:], op=ALU.add),
                   reads=[b_AT[h], g.b_identb], writes=[b_C[h][1]])
            yield
            for h in range(2):
                B_ = hb[h]
                A0 = Cc[h][0][:, 0:P]
                B0 = AT[h][:, 0:P]
                op(pe, lambda e, B_=B_, A0=A0, B0=B0: e.matmul(banks[B_][:, 0:P], lhsT=B0, rhs=A0, start=True, stop=True),
                   reads=[b_AT[h], b_C[h][0]], writes=[bbuf[B_]], sig=False)
                op(pe, lambda e, B_=B_, A0=A0, B0=B0: e.matmul(banks[B_][:, P:2 * P], lhsT=A0, rhs=B0, start=True,
                                                               stop=True),
                   reads=[b_AT[h], b_C[h][0]], writes=[bbuf[B_]])
                ev = dve if h == 0 else act
                if ev is dve:
                    op(dve, lambda e, h=h, B_=B_: e.tensor_copy(out=Cc[h][1][:, 0:2 * P], in_=banks[B_][:, 0:2 * P]),
                       reads=[bbuf[B_]], writes=[b_C[h][1]])
                else:
                    op(act, lambda e, h=h, B_=B_: e.activation(out=Cc[h][1][:, 0:2 * P], in_=banks[B_][:, 0:2 * P],
                                                               func=AF.Copy),
                       reads=[bbuf[B_]], writes=[b_C[h][1]])
            yield
            for lev in range(1, 7):
                src_i = lev % 2
                dst_i = 1 - src_i
                for h in range(2):
                    B_ = hb[h]
                    S = Cc[h][src_i]
                    Dd = Cc[h][dst_i]
                    bS, bD = b_C[h][src_i], b_C[h][dst_i]
                    Ak, Bk, Mk = S[:, 0:P], S[:, P:2 * P], S[:, 2 * P:3 * P]
                    lo = 0
                    if lev <= 5:
                        op(pe, lambda e, B_=B_, Ak=Ak, Bk=Bk: e.matmul(banks[B_][:, 0:P], lhsT=Bk, rhs=Ak, start=True,
                                                                       stop=True),
                           reads=[bS], writes=[bbuf[B_]], sig=False)
                    else:
                        lo = 2 * P
                    r0 = P if lev <= 4 else 2 * P
                    op(pe, lambda e, B_=B_, Ak=Ak, S=S, r0=r0: e.matmul(banks[B_][:, r0:3 * P], lhsT=Ak, rhs=S[:, r0:3 * P],
                                                                        start=True, stop=True),
                       reads=[bS], writes=[bbuf[B_]], sig=(h == 0))
                    if h == 1:
                        op(pe, lambda e, B_=B_, Mk=Mk: e.matmul(banks[B_][:, 2 * P:3 * P], lhsT=g.identb[:], rhs=Mk,
                                                                start=False, stop=True),
                           reads=[bS, g.b_identb], writes=[bbuf[B_]])
                        op(act, lambda e, B_=B_, Dd=Dd, lo=lo: e.activation(out=Dd[:, lo:3 * P],
                                                                            in_=banks[B_][:, lo:3 * P], func=AF.Copy),
                           reads=[bbuf[B_]], writes=[bD])
                    else:
                        if lev <= 5:
                            hi_ = 2 * P if lev <= 4 else P
                            op(dve, lambda e, B_=B_, Dd=Dd, hi_=hi_: e.tensor_copy(out=Dd[:, 0:hi_],
                                                                                   in_=banks[B_][:, 0:hi_]),
                               reads=[bbuf[B_]], writes=[bD])
                        op(dve, lambda e, B_=B_, Dd=Dd, Mk=Mk: e.tensor_tensor(out=Dd[:, 2 * P:3 * P],
                                                                               in0=banks[B_][:, 2 * P:3 * P], in1=Mk,
                                                                               op=ALU.add),
                           reads=[bbuf[B_], bS], writes=[bD])
                yield
            Mfin = [Cc[h][1][:, 2 * P:3 * P] for h in range(2)]
            b_Mfin = [b_C[h][1] for h in range(2)]
            b_Hj = g_bH[j]
            for h in range(2):
                op(pe, lambda e, h=h: e.matmul(banks[Ba][:, h * 64:(h + 1) * 64], lhsT=u.arT[hq[h], 0:P],
                                               rhs=Hb[hq[h], j, :], start=True, stop=False),
                   reads=[u.b_arT, b_Hj], writes=[bbuf[Ba]], sig=False)
                op(pe, lambda e, h=h: e.matmul(banks[Ba][:, h * 64:(h + 1) * 64], lhsT=AT[h][:, 2 * P:3 * P],
                                               rhs=vt[:, (2 * j + h) * 64:(2 * j + h + 1) * 64], start=False, stop=True),
                   reads=[b_AT[h], b_vt], writes=[bbuf[Ba]], sig=(h == 1))
            op(act, lambda e: e.activation(out=u.Xp[:], in_=banks[Ba][:, 0:P], func=AF.Copy),
               reads=[bbuf[Ba]], writes=[u.b_Xp])
            for h in range(2):
                op(pe, lambda e, h=h: e.matmul(banks[Ba][:, P + h * 64:P + (h + 1) * 64], lhsT=Mfin[h],
                                               rhs=u.Xp[:, h * 64:(h + 1) * 64], start=True, stop=True),
                   reads=[b_Mfin[h], u.b_Xp], writes=[bbuf[Ba]], sig=(h == 1))
            op(dve, lambda e: e.tensor_copy(out=u.Up[:], in_=banks[Ba][:, P:2 * P]), reads=[bbuf[Ba]], writes=[u.b_Up])
            op(dve, lambda e: e.tensor_scalar(out=u.Hs[:], in0=Hf[:, j, :], scalar1=u.sc[:, 0:1], scalar2=None,
                                               op0=ALU.mult), reads=[b_Hj, u.b_sc], writes=[u.b_Hs])
            yield
            for h in range(2):
                yo = 2 * P + h * 64
                op(pe, lambda e, h=h, yo=yo: e.matmul(banks[Ba][:, yo:yo + 64], lhsT=u.arT[hq[h], P:2 * P],
                                                      rhs=Hb[hq[h], j, :], start=True, stop=False),
                   reads=[u.b_arT, b_Hj], writes=[bbuf[Ba]], sig=False)
                op(pe, lambda e, h=h, yo=yo: e.matmul(banks[Ba][:, yo:yo + 64], lhsT=AT[h][:, P:2 * P],
                                                      rhs=u.Up[:, h * 64:(h + 1) * 64], start=False, stop=False),
                   reads=[b_AT[h], u.b_Up], writes=[bbuf[Ba]], sig=False)
                op(pe, lambda e, h=h, yo=yo: e.matmul(banks[Ba][:, yo:yo + 64], lhsT=AT[h][:, 3 * P:4 * P],
                                                      rhs=vt[:, (2 * j + h) * 64:(2 * j + h + 1) * 64],
                                                      start=False, stop=True),
                   reads=[b_AT[h], b_vt], writes=[bbuf[Ba]], sig=False)
            op(pe, lambda e: e.matmul(banks[Ba][:, 3 * P:4 * P], lhsT=u.BK[:, 0:P], rhs=u.Up[:], start=True, stop=False),
               reads=[u.b_BK, u.b_Up], writes=[bbuf[Ba]], sig=False)
            op(pe, lambda e: e.matmul(banks[Ba][:, 3 * P:4 * P], lhsT=u.BK[:, P:2 * P], rhs=vt[:, j * P:(j + 1) * P],
                                      start=False, stop=True),
               reads=[u.b_BK, b_vt], writes=[bbuf[Ba]])
            if d == 0:
                ydst = x1[:, c, :].bitcast(BF16)[:, j * P:(j + 1) * P]
                b_yd = bx1[c]
            else:
                yb_, b_yd = cx.yb
                ydst = yb_[:, j * P:(j + 1) * P]
            op(act, lambda e: e.activation(out=ydst, in_=banks[Ba][:, 2 * P:3 * P], func=AF.Copy),
               reads=[bbuf[Ba]], writes=[b_yd])
            for h in range(2):
                op(dve, lambda e, h=h: e.scalar_tensor_tensor(out=Hf[hq[h], j, :],
                                                              in0=banks[Ba][hq[h], 3 * P + h * 64:3 * P + (h + 1) * 64],
                                                              scalar=u.sc[hq[h], 0:1], in1=u.Hs[hq[h], :],
                                                              op0=ALU.mult, op1=ALU.add),
                   reads=[bbuf[Ba], u.b_sc, u.b_Hs], writes=[b_Hj])
            op(pool, lambda e: e.tensor_copy(out=Hb[:, j, :], in_=Hf[:, j, :]), reads=[b_Hj], writes=[b_Hj])
            yield

        g_bH = b_H

        def gen_Ba(cx):
            c = cx.c
            yb_, b_yb = cx.yb
            zs, b_zs = cx.zs
            vt, b_vt = cx.vt
            bonB, b_bonB = cx.bonB
            yf = x1[:, c, :].bitcast(BF16)[:, 0:D]
            op(dve, lambda e: e.tensor_tensor(out=yb_[:], in0=yb_[:], in1=yf, op=ALU.add),
               reads=[b_yb, bx1[c]], writes=[b_yb])
            y3 = yb_[:].rearrange("p (h n) -> p h n", h=16)
            op(dve, lambda e: e.tensor_reduce(out=gst[:, 0, :], in_=y3, axis=AX.X, op=ALU.add),
               reads=[b_yb], writes=[b_gst])
            op(act, lambda e: e.activation(out=tmpA[:], in_=yb_[:], func=AF.Square), reads=[b_yb], writes=[b_tmpA])
            op(dve, lambda e: e.tensor_reduce(out=gst[:, 1, :], in_=tmpA[:].rearrange("p (h n) -> p h n", h=16),
                                              axis=AX.X, op=ALU.add), reads=[b_tmpA], writes=[b_gst])
            yield
            op(dve, lambda e: e.tensor_scalar(out=gst[:, 2, :], in0=gst[:, 0, :], scalar1=1.0 / 64, scalar2=None,
                                              op0=ALU.mult), reads=[b_gst], writes=[b_gst])
            op(dve, lambda e: e.tensor_tensor(out=gst[:, 3, :], in0=gst[:, 2, :], in1=gst[:, 2, :], op=ALU.mult),
               reads=[b_gst], writes=[b_gst])
            op(dve, lambda e: e.scalar_tensor_tensor(out=gst[:, 3, :], in0=gst[:, 1, :], scalar=1.0 / 64,
                                                     in1=gst[:, 3, :], op0=ALU.mult, op1=ALU.subtract),
               reads=[b_gst], writes=[b_gst])
            op(dve, lambda e: e.tensor_scalar(out=gst[:, 3, :], in0=gst[:, 3, :], scalar1=LNX_EPS, scalar2=None,
                                              op0=ALU.add), reads=[b_gst], writes=[b_gst])
            op(act, lambda e: e.activation(out=gst[:, 3, :], in_=gst[:, 3, :], func=AF.Sqrt), reads=[b_gst],
               writes=[b_gst])
            op(dve, lambda e: e.reciprocal(out=gst[:, 4, :], in_=gst[:, 3, :]), reads=[b_gst], writes=[b_gst])
            op(dve, lambda e: e.tensor_tensor(out=gst[:, 5, :], in0=bonF[:, c, :], in1=bonB[:], op=ALU.add),
               reads=[b_bonF[c], b_bonB], writes=[b_gst])
            mean_b = gst[:, 2, :].unsqueeze(2).broadcast_to([P, 16, 64])
            rstd_b = gst[:, 4, :].unsqueeze(2).broadcast_to([P, 16, 64])
            bon_b = gst[:, 5, :].unsqueeze(2).broadcast_to([P, 16, 64])
            op(dve, lambda e: e.tensor_tensor(out=y3, in0=y3, in1=mean_b, op=ALU.subtract),
               reads=[b_yb, b_gst], writes=[b_yb])
            op(pool, lambda e: e.tensor_tensor(out=y3, in0=y3, in1=rstd_b, op=ALU.mult),
               reads=[b_yb, b_gst], writes=[b_yb])
            yield
            op(dve, lambda e: e.tensor_tensor(out=yb_[:], in0=yb_[:], in1=lnxb[:, 0, :], op=ALU.mult),
               reads=[b_yb, b_lc], writes=[b_yb])
            op(pool, lambda e: e.tensor_tensor(out=yb_[:], in0=yb_[:], in1=lnxb[:, 1, :], op=ALU.add),
               reads=[b_yb, b_lc], writes=[b_yb])
            t3 = tmpA[:].rearrange("p (h n) -> p h n", h=16)
            op(dve, lambda e: e.tensor_tensor(out=t3, in0=vt[:].rearrange("p (h n) -> p h n", h=16), in1=bon_b,
                                              op=ALU.mult), reads=[b_vt, b_gst], writes=[b_tmpA])
            op(pool, lambda e: e.tensor_tensor(out=yb_[:], in0=yb_[:], in1=tmpA[:], op=ALU.add),
               reads=[b_yb, b_tmpA], writes=[b_yb])
            op(dve, lambda e: e.tensor_tensor(out=yg[:], in0=yb_[:], in1=zs[:], op=ALU.mult),
               reads=[b_yb, b_zs], writes=[b_yg])
            yield

        def gen_Bb(cx):
            c = cx.c
            xin, b_xin = tmpA, b_tmpA
            T_.dma(sp, xin[:], g.x_in[si, c * P:(c + 1) * P, :], writes=[b_xin])
            for kc in range(KC):
                op(pe, lambda e, kc=kc: e.transpose(out=banks_bf[0][:, kc * P:(kc + 1) * P],
                                                    in_=yg[:, kc * P:(kc + 1) * P], identity=g.identb[:]),
                   reads=[b_yg, g.b_identb], writes=[bbuf[0]], sig=(kc == KC - 1))
            op(act, lambda e: e.activation(out=ygT[:].rearrange("p k n -> p (k n)"), in_=banks_bf[0][:, :],
                                           func=AF.Copy), reads=[bbuf[0]], writes=[b_ygT])
            yield
            for half in range(2):
                for q2 in range(NQ2):
                    w, b_w = g.loadA(L, 2, half * NQ2 + q2)
                    for kc in range(KC):
                        op(pe, lambda e, kc=kc, w=w, q2=q2: e.matmul(banks[0][:, q2 * W2:(q2 + 1) * W2],
                                                                    lhsT=ygT[:, kc, :], rhs=w[:, kc, :],
                                                                    start=(kc == 0), stop=(kc == KC - 1)),
                           reads=[b_ygT, b_w], writes=[bbuf[0]], sig=(kc == KC - 1 and q2 == NQ2 - 1))
                op(act, lambda e, half=half: e.activation(out=tmpB[:, half * 512:(half + 1) * 512], in_=banks[0][:, :],
                                                          func=AF.Copy), reads=[bbuf[0]], writes=[b_tmpB])
                yield
            op(act, lambda e: e.activation(out=yg[:], in_=tmpB[:], func=AF.Square, accum_out=stat[:, 4:5]),
               reads=[b_tmpB], writes=[b_yg, b_stat])
            op(dve, lambda e: e.tensor_scalar(out=stat[:, 5:6], in0=stat[:, 4:5], scalar1=1.0 / D, scalar2=RMS_EPS,
                                              op0=ALU.mult, op1=ALU.add), reads=[b_stat], writes=[b_stat])
            op(act, lambda e: e.activation(out=stat[:, 6:7], in_=stat[:, 5:6], func=AF.Sqrt), reads=[b_stat],
               writes=[b_stat])
            op(dve, lambda e: e.reciprocal(out=stat[:, 7:8], in_=stat[:, 6:7]), reads=[b_stat], writes=[b_stat])
            op(dve, lambda e: e.scalar_tensor_tensor(out=tmpB[:], in0=tmpB[:], scalar=stat[:, 7:8], in1=postg[:],
                                                     op0=ALU.mult, op1=ALU.mult),
               reads=[b_tmpB, b_stat, b_lc], writes=[b_tmpB])
            op(pool, lambda e: e.tensor_tensor(out=x1[:, c, :], in0=tmpB[:], in1=xin[:], op=ALU.add),
               reads=[b_tmpB, b_xin], writes=[bx1[c]])
            yield

        def gen_Z():
            op(dve, lambda e: e.memset(Hf[:], 0.0), writes=b_H)
            op(pool, lambda e: e.memset(Hb[:], 0.0), writes=b_H)
            yield

        tasks = []
        lastB = None
        lastF = None
        for d in range(2):
            order = list(range(NCH)) if d == 0 else list(range(NCH - 1, -1, -1))
            tz = Task("Z%d" % d)
            tz.gen = gen_Z()
            tz.deps += [t for t in tasks if t.name.startswith("U")]
            tasks.append(tz)
            cxs = [None] * NCH

            def make_Na(i):
                cx = NS()
                cx.c = order[i]
                tn = Task("N%d_%d" % (d, cx.c))
                _, cx.xin = p_xin.acquire(tn)
                if i > 0:
                    tn.deps.append(cxs[i - 1].tnb)
                tn.gen = gen_Na(cx)
                cx.tna = tn
                cxs[i] = cx
                tasks.append(tn)

            def make_Nb(i):
                cx = cxs[i]
                tn = Task("N%d_%db" % (d, cx.c))
                cx.ks, cx.slot = p_slot.acquire(tn)
                tn.deps.append(cx.tna)
                prev = cxs[i - 1] if i > 0 else None
                if prev is not None:
                    p_slot.share(tn, prev.ks)
                tn.gen = gen_Nb(cx, d, i == 0, i == NCH - 1, prev)
                cx.tnb = tn
                cx.tn = tn
                tasks.append(tn)

            make_Na(0)
            make_Nb(0)
            if NCH > 1:
                make_Na(1)
                make_Nb(1)
            pendBb = None
            for i in range(NCH):
                cx = cxs[i]
                c = cx.c
                tf = Task("F%d_%d" % (d, c))
                tf.deps.append(cx.tn)
                if i + 1 < NCH:
                    tf.deps.append(cxs[i + 1].tn)
                if lastF is not None:
                    tf.deps.append(lastF)
                lastF = tf
                p_slot.share(tf, cx.ks)
                klr, cx.lr = p_lr.acquire(tf)
                klk, cx.lk = p_lk.acquire(tf)
                kvt, cx.vt = p_vt.acquire(tf)
                kth, cx.th = p_th.acquire(tf)
                if d == 1:
                    kzs, cx.zs = p_zs.acquire(tf)
                tf.gen = gen_F(cx, d)
                tasks.append(tf)
                tba = None
                if d == 1:
                    tba = Task("B%d" % c)
                    _, cx.yb = p_yb.acquire(tba)
                    _, cx.bonB = p_bonB.acquire(tba)
                tus = []
                for j in range(KC):
                    tu = Task("U%d_%d_%d" % (d, c, j))
                    tu.deps += [tf, tz]
                    if d == 1:
                        tu.deps += list(tba.deps)
                    _, uset = p_uset.acquire(tu)
                    p_lr.share(tu, klr)
                    p_lk.share(tu, klk)
                    p_vt.share(tu, kvt)
                    p_th.share(tu, kth)
                    tu.gen = gen_U(cx, d, j, uset)
                    if i > 0:
                        tu.deps.append(cxs[i - 1].tus[j])
                    tus.append(tu)
                    tasks.append(tu)
                    if j == 1 and i + 2 < NCH:
                        make_Na(i + 2)
                    if j == 3 and pendBb is not None:
                        tasks.append(pendBb)
                        pendBb = None
                cx.tus = tus
                if d == 1:
                    tba.deps += tus + [tf]
                    if lastB is not None:
                        tba.deps.append(lastB)
                    p_vt.share(tba, kvt)
                    p_zs.share(tba, kzs)
                    tba.gen = gen_Ba(cx)
                    tasks.append(tba)
                    tbb = Task("B%db" % c)
                    tbb.deps.append(tba)
                    tbb.gen = gen_Bb(cx)
                    lastB = tbb
                    if i == NCH - 1:
                        tasks.append(tbb)
                    else:
                        pendBb = tbb
                if i + 2 < NCH:
                    make_Nb(i + 2)
        import os
        allow = os.environ.get("L0_TASKS", "ZNFUB")
        tasks = [t for t in tasks if t.name[0] in allow]
        for t in tasks:
            t.deps = [d_ for d_ in t.deps if d_.name[0] in allow]
        run_tasks(tasks, window=WINDOW)


GRID_W = 64
N_ROWS = T_SEQ // GRID_W
NEG = -30000.0


def _rs(r):
    return min(max(r - 4, 0), N_ROWS - 8)


def l1_geometry():
    geo = []
    pats = []
    for i in range(N_ROWS // 2):
        rows = [2 * i, 2 * i + 1]
        lo = min(_rs(r) for r in rows)
        hi = max(_rs(r) + 7 for r in rows)
        ent = []
        for kt in range(lo // 2, hi // 2 + 1):
            pat = tuple(tuple(1 if _rs(2 * i + rl) <= 2 * kt + krl < _rs(2 * i + rl) + 8 else 0 for krl in range(2))
                        for rl in range(2))
            if pat not in pats:
                pats.append(pat)
            ent.append((kt, (2 * (kt - i) + 6) // 2, pats.index(pat)))
        geo.append(ent)
    return geo, pats


def l1_prologue(g):
    nc, T_, op, sb = g.nc, g.T_, g.op, g.sb
    pe, act, dve, pool, sp = g.pe, g.act, g.dve, g.pool, g.sp
    geo, pats = l1_geometry()
    g.l1_geo, g.l1_pats = geo, pats
    NMK = len(pats)
    g.padD = nc.dram_tensor("padD", [240, P], F32, kind="Internal").ap()
    g.rbD = nc.dram_tensor("rbD", [P, 16 * 7 * P], BF16, kind="Internal").ap()
    g.mkD = nc.dram_tensor("mkD", [P, (NMK + 1) * P], BF16, kind="Internal").ap()
    x1 = g.x1
    with ExitStack() as p1:
        b_t = Buf("l1pro")
        padt = sb(p1, "padt", [120, 2, P], F32)
        op(dve, lambda e: e.memset(padt[:], 0.0), writes=[b_t])
        rp = g.wd['na_rpb'][0].rearrange("h r m -> (h r) m")
        for gi in range(2):
            T_.dma(sp, padt[:, gi, 48:79], rp[gi * 120:(gi + 1) * 120, :], writes=[b_t])
        for gi in range(2):
            T_.dma(sp, g.padD[gi * 120:(gi + 1) * 120, :], padt[:, gi, :], reads=[b_t])
        T_.barrier()
        T_.finish()
        Hs = x1[:].rearrange("p c d -> p (c d)")[:, 0:240 * 64].rearrange("p (b k) -> p b k", k=64)
        b_hs = Buf("Hs")
        for h in range(16):
            for r2 in range(2):
                src = bass.AP(tensor=g.padD.tensor, offset=h * 15 * P, ap=[[1, 64], [P, 15], [1, 64]])
                T_.dma(sp, Hs[64 * r2:64 * r2 + 64, h * 15:(h + 1) * 15, :], src, writes=[b_hs])
        RBs = sb(p1, "RBs", [P, 16, 7, P], BF16)
        b_rb = Buf("RBs")
        op(pool, lambda e: e.memset(RBs[:].rearrange("p h d k -> p (h d k)"), 0.0), writes=[b_rb])
        engs = [dve, pool, act]
        n = 0
        for h in range(16):
            for rl in range(2):
                for krl in range(2):
                    dis = [di for di in range(7) if 0 <= 2 * di + 1 + krl - rl <= 14]
                    d0, nd = dis[0], len(dis)
                    ri0 = 2 * d0 + 1 + krl - rl
                    srcv = Hs[64 * rl:64 * rl + 64, h * 15 + ri0:h * 15 + ri0 + 2 * (nd - 1) + 1:2, :]
                    dstv = RBs[64 * rl:64 * rl + 64, h, d0:d0 + nd, 64 * krl:64 * krl + 64]
                    e_ = engs[n % 3]
                    n += 1
                    if e_ is act:
                        op(act, lambda e, s=srcv, d=dstv: e.activation(out=d, in_=s, func=AF.Copy),
                           reads=[b_hs], writes=[b_rb])
                    else:
                        op(e_, lambda e, s=srcv, d=dstv: e.tensor_copy(out=d, in_=s), reads=[b_hs], writes=[b_rb])
        T_.dma(sp, g.rbD[:, :], RBs[:].rearrange("p h d k -> p (h d k)"), reads=[b_rb])
        ior = sb(p1, "ior1", [P, P], F32)
        ioc = sb(p1, "ioc1", [P, P], F32)
        t1 = sb(p1, "mt1", [P, P], F32)
        t2 = sb(p1, "mt2", [P, P], F32)
        cm = sb(p1, "cm", [P, P], BF16)
        neg = sb(p1, "negt", [P, P], BF16)
        MKs = sb(p1, "MKs", [P, NMK + 1, P], BF16)
        b_m = Buf("mk")
        op(pool, lambda e: e.iota(ior[:], pattern=[[0, P]], base=0, channel_multiplier=1,
                                  allow_small_or_imprecise_dtypes=True), writes=[b_m])
        op(pool, lambda e: e.iota(ioc[:], pattern=[[1, P]], base=0, channel_multiplier=0,
                                  allow_small_or_imprecise_dtypes=True), writes=[b_m])
        op(dve, lambda e: e.tensor_tensor(out=t1[:], in0=ior[:], in1=ioc[:], op=ALU.add), reads=[b_m], writes=[b_m])
        op(dve, lambda e: e.tensor_scalar(out=t2[:], in0=t1[:], scalar1=63.0, scalar2=None, op0=ALU.is_equal),
           reads=[b_m], writes=[b_m])
        op(dve, lambda e: e.tensor_scalar(out=t1[:], in0=t1[:], scalar1=191.0, scalar2=None, op0=ALU.is_equal),
           reads=[b_m], writes=[b_m])
        op(dve, lambda e: e.tensor_tensor(out=g.Jrev[:], in0=t1[:], in1=t2[:], op=ALU.add), reads=[b_m],
           writes=[g.b_cst])
        op(dve, lambda e: e.tensor_scalar(out=t1[:], in0=ior[:], scalar1=63.5, scalar2=-64.0, op0=ALU.is_gt,
                                          op1=ALU.mult), reads=[b_m], writes=[b_m])
        op(dve, lambda e: e.tensor_tensor(out=t1[:], in0=t1[:], in1=ior[:], op=ALU.add), reads=[b_m], writes=[b_m])
        op(dve, lambda e: e.tensor_scalar(out=t1[:], in0=t1[:], scalar1=-1.0, scalar2=55.0, op0=ALU.mult,
                                          op1=ALU.add), reads=[b_m], writes=[b_m])
        op(dve, lambda e: e.tensor_scalar(out=t1[:], in0=t1[:], scalar1=0.0, scalar2=48.0, op0=ALU.max, op1=ALU.min),
           reads=[b_m], writes=[b_m])
        op(dve, lambda e: e.tensor_scalar(out=t2[:], in0=ioc[:], scalar1=63.5, scalar2=-64.0, op0=ALU.is_gt,
                                          op1=ALU.mult), reads=[b_m], writes=[b_m])
        op(dve, lambda e: e.tensor_tensor(out=t2[:], in0=t2[:], in1=ioc[:], op=ALU.add), reads=[b_m], writes=[b_m])
        op(dve, lambda e: e.tensor_tensor(out=t2[:], in0=t2[:], in1=t1[:], op=ALU.subtract), reads=[b_m],
           writes=[b_m])
        op(dve, lambda e: e.tensor_scalar(out=t1[:], in0=t2[:], scalar1=-0.5, scalar2=None, op0=ALU.is_gt),
           reads=[b_m], writes=[b_m])
        op(dve, lambda e: e.tensor_scalar(out=t2[:], in0=t2[:], scalar1=15.5, scalar2=None, op0=ALU.is_lt),
           reads=[b_m], writes=[b_m])
        op(dve, lambda e: e.tensor_tensor(out=t1[:], in0=t1[:], in1=t2[:], op=ALU.mult), reads=[b_m], writes=[b_m])
        op(dve, lambda e: e.tensor_scalar(out=cm[:], in0=t1[:], scalar1=-1.0, scalar2=-NEG, op0=ALU.add, op1=ALU.mult),
           reads=[b_m], writes=[b_m])
        op(dve, lambda e: e.memset(neg[:], NEG), writes=[b_m])
        for pi, pat in enumerate(pats):
            for rl in range(2):
                for krl in range(2):
                    src_t = cm if pat[rl][krl] else neg
                    op(pool, lambda e, pi=pi, rl=rl, krl=krl, s=src_t: e.tensor_copy(
                        out=MKs[64 * rl:64 * rl + 64, pi, 64 * krl:64 * krl + 64],
                        in_=s[64 * rl:64 * rl + 64, 64 * krl:64 * krl + 64]), reads=[b_m], writes=[b_m])
        op(pool, lambda e: e.tensor_copy(out=MKs[:, NMK, :], in_=neg[:]), reads=[b_m], writes=[b_m])
        T_.dma(sp, g.mkD[:, :], MKs[:].rearrange("p n k -> p (n k)"), reads=[b_m])
        T_.barrier()
        T_.finish()


def layer1(g, si):
    nc, T_, op = g.nc, g.T_, g.op
    pe, act, dve, pool, sp = g.pe, g.act, g.dve, g.pool, g.sp
    banks, bbuf, banks_bf = g.banks, g.bbuf, g.banks_bf
    NCH, x1, bx1, pv, b_pv, pvc = g.NCH, g.x1, g.bx1, g.pv, g.b_pv, g.pvc
    wd, sb = g.wd, g.sb
    b_cst = g.b_cst
    L = 1
    geo, pats = g.l1_geo, g.l1_pats
    NMK = len(pats)
    hq = [slice(0, 64), slice(64, 128)]

    with ExitStack() as l1:
        postg = sb(l1, "postg1", [P, D], F32)
        RB = sb(l1, "RB", [P, 16, 7, P], BF16)
        MK = sb(l1, "MK", [P, NMK + 1, P], BF16)
        b_lc = Buf("l1consts")
        T_.dma(sp, postg[:], wd['post_norm_g'][L].partition_broadcast(P), writes=[b_lc])
        T_.dma(sp, RB[:].rearrange("p h d k -> p (h d k)"), g.rbD[:, :], writes=[b_lc])
        T_.dma(sp, MK[:].rearrange("p n k -> p (n k)"), g.mkD[:, :], writes=[b_lc])

        def mk(nm, sh, dt, n):
            return [(sb(l1, "%s%d" % (nm, i), sh, dt), Buf("%s%d" % (nm, i))) for i in range(n)]

        xs, b_xs = mk("xs1", [P, D], BF16, 1)[0]
        stat = sb(l1, "stat1", [P, 8], F32)
        b_stat = Buf("stat1")
        p_xn = RPool(mk("xnT1", [P, KC, P], BF16, 4))
        p_kT = RPool(mk("kT", [P, KC, P], BF16, 7))
        p_va = RPool(mk("Vaug", [P, 16, 65], BF16, 7))
        zs, b_zs = mk("zs1", [P, D], BF16, 1)[0]
        og, b_og = mk("og", [P, D], F32, 1)[0]
        yg, b_yg = mk("yg1", [P, D], BF16, 1)[0]
        ygT, b_ygT = mk("ygT1", [P, KC, P], BF16, 1)[0]
        tmpA, b_tmpA = mk("tmpA1", [P, D], F32, 1)[0]
        tmpB, b_tmpB = mk("tmpB1", [P, D], F32, 1)[0]

        def mkset(i):
            u = NS()
            def t(nm, sh, dt):
                setattr(u, nm, sb(l1, "a%d_%s" % (i, nm), sh, dt))
                setattr(u, "b_" + nm, Buf("a%d_%s" % (i, nm)))
            t("qT", [P, P], BF16)
            t("PT", [P, 5, P], BF16)
            t("rc", [P, 2], F32)
            u.banks = (2 + 3 * i, 3 + 3 * i, 4 + 3 * i)
            return u

        p_uset = RPool([mkset(0), mkset(1)])
        for (va, b_va) in p_va.items:
            op(pool, lambda e, va=va: e.memset(va[:, :, 64:65], 1.0), writes=[b_va])

        def gen_KV(cx):
            t = cx.t
            xn, b_xn = cx.xn
            kT, b_kT = cx.kT
            va, b_va = cx.va
            xsrc = x1[:, t, :]
            op(act, lambda e: e.activation(out=xs[:], in_=xsrc, func=AF.Square, accum_out=stat[:, 0:1]),
               reads=[bx1[t]], writes=[b_xs, b_stat])
            op(dve, lambda e: e.tensor_scalar(out=stat[:, 1:2], in0=stat[:, 0:1], scalar1=1.0 / D, scalar2=RMS_EPS,
                                              op0=ALU.mult, op1=ALU.add), reads=[b_stat], writes=[b_stat])
            op(act, lambda e: e.activation(out=stat[:, 2:3], in_=stat[:, 1:2], func=AF.Sqrt), reads=[b_stat],
               writes=[b_stat])
            op(dve, lambda e: e.reciprocal(out=stat[:, 3:4], in_=stat[:, 2:3]), reads=[b_stat], writes=[b_stat])
            op(act, lambda e: e.activation(out=xs[:], in_=xsrc, func=AF.Copy, scale=stat[:, 3:4]),
               reads=[bx1[t], b_stat], writes=[b_xs])
            yield
            for kc in range(KC):
                op(pe, lambda e, kc=kc: e.transpose(out=banks_bf[0][:, kc * P:(kc + 1) * P],
                                                    in_=xs[:, kc * P:(kc + 1) * P], identity=g.identb[:]),
                   reads=[b_xs, g.b_identb], writes=[bbuf[0]], sig=(kc == KC - 1))
            for kc in range(KC):
                sc = pvc('pre_g', L * 8 + kc)
                if kc % 2 == 0:
                    op(dve, lambda e, kc=kc, sc=sc: e.tensor_scalar(out=xn[:, kc, :],
                                                                  in0=banks_bf[0][:, kc * P:(kc + 1) * P],
                                                                  scalar1=sc, scalar2=None, op0=ALU.mult),
                       reads=[bbuf[0], b_pv], writes=[b_xn])
                else:
                    op(act, lambda e, kc=kc, sc=sc: e.activation(out=xn[:, kc, :],
                                                               in_=banks_bf[0][:, kc * P:(kc + 1) * P],
                                                               func=AF.Copy, scale=sc),
                       reads=[bbuf[0], b_pv], writes=[b_xn])
            yield
            for j0 in range(0, KC, 4):
                for j in range(j0, j0 + 4):
                    w, b_w = g.loadB(L, 1, j)
                    for kc in range(KC):
                        op(pe, lambda e, kc=kc, j=j, w=w: e.matmul(banks[1][:, (j - j0) * P:(j - j0 + 1) * P],
                                                                  lhsT=w[:, kc, :], rhs=xn[:, kc, :],
                                                                  start=(kc == 0), stop=(kc == KC - 1)),
                           reads=[b_w, b_xn], writes=[bbuf[1]], sig=(kc == KC - 1 and j == j0 + 3))
                for j in range(j0, j0 + 4):
                    op(act, lambda e, j=j: e.activation(out=kT[:, j, :], in_=banks[1][:, (j - j0) * P:(j - j0 + 1) * P],
                                                        func=AF.Identity, bias=pvc('bk', j)),
                       reads=[bbuf[1], b_pv], writes=[b_kT])
                yield
            for half in range(2):
                for q2 in range(NQ2):
                    w, b_w = g.loadA(L, 0, half * NQ2 + q2)
                    cs = slice(q2 * W2, (q2 + 1) * W2)
                    for kc in range(KC):
                        op(pe, lambda e, kc=kc, w=w, cs=cs: e.matmul(banks[1][:, cs], lhsT=xn[:, kc, :], rhs=w[:, kc, :],
                                                                    start=(kc == 0), stop=False),
                           reads=[b_xn, b_w], writes=[bbuf[1]], sig=False)
                    bo = 2 * D + half * 512 + q2 * W2
                    op(pe, lambda e, bo=bo, cs=cs: e.matmul(banks[1][:, cs], lhsT=g.onesrow[:],
                                                           rhs=g.brow_hi[0:1, bo:bo + W2], start=False, stop=False),
                       reads=[b_cst], writes=[bbuf[1]], sig=False)
                    op(pe, lambda e, bo=bo, cs=cs: e.matmul(banks[1][:, cs], lhsT=g.onesrow[:],
                                                           rhs=g.brow_lo[0:1, bo:bo + W2], start=False, stop=True),
                       reads=[b_cst], writes=[bbuf[1]], sig=(q2 == NQ2 - 1))
                op(act, lambda e, half=half: e.activation(out=va[:, half * 8:(half + 1) * 8, 0:64],
                                                          in_=banks[1][:, :].rearrange("p (h n) -> p h n", h=8),
                                                          func=AF.Copy), reads=[bbuf[1]], writes=[b_va])
                yield

        def gen_Q(cx):
            xn, b_xn = cx.xn
            for half in range(2):
                for q2 in range(NQ2):
                    w, b_w = g.loadA(L, 1, half * NQ2 + q2)
                    cs = slice(q2 * W2, (q2 + 1) * W2)
                    for kc in range(KC):
                        op(pe, lambda e, kc=kc, w=w, cs=cs: e.matmul(banks[1][:, cs], lhsT=xn[:, kc, :], rhs=w[:, kc, :],
                                                                    start=(kc == 0), stop=False),
                           reads=[b_xn, b_w], writes=[bbuf[1]], sig=False)
                    bo = 3 * D + half * 512 + q2 * W2
                    op(pe, lambda e, bo=bo, cs=cs: e.matmul(banks[1][:, cs], lhsT=g.onesrow[:],
                                                           rhs=g.brow_hi[0:1, bo:bo + W2], start=False, stop=False),
                       reads=[b_cst], writes=[bbuf[1]], sig=False)
                    op(pe, lambda e, bo=bo, cs=cs: e.matmul(banks[1][:, cs], lhsT=g.onesrow[:],
                                                           rhs=g.brow_lo[0:1, bo:bo + W2], start=False, stop=True),
                       reads=[b_cst], writes=[bbuf[1]], sig=(q2 == NQ2 - 1))
                op(act, lambda e, half=half: e.activation(out=zs[:, half * 512:(half + 1) * 512], in_=banks[1][:, :],
                                                          func=AF.Silu), reads=[bbuf[1]], writes=[b_zs])
                yield

        def gen_A(cx, j, u, kvs):
            i = cx.t
            xn, b_xn = cx.xn
            Ba, Bb, Bc = u.banks
            w, b_w = g.loadB(L, 0, j)
            for kc in range(KC):
                op(pe, lambda e, kc=kc: e.matmul(banks[Ba][:, 0:P], lhsT=w[:, kc, :], rhs=xn[:, kc, :],
                                                 start=(kc == 0), stop=(kc == KC - 1)),
                   reads=[b_w, b_xn], writes=[bbuf[Ba]], sig=(kc == KC - 1))
            op(act, lambda e: e.activation(out=u.qT[:], in_=banks[Ba][:, 0:P], func=AF.Identity, scale=0.125,
                                           bias=pvc('bq8', j)), reads=[bbuf[Ba], b_pv], writes=[u.b_qT])
            yield
            ent = geo[i]
            for h in range(2):
                hg = 2 * j + h
                for n_, (kt, di, pi) in enumerate(ent):
                    kT, b_kT = kvs[kt].kT
                    bk_ = Bb if n_ < 4 else Bc
                    co = (n_ % 4) * P
                    op(pe, lambda e, kT=kT, bk_=bk_, co=co, h=h: e.matmul(banks[bk_][:, co:co + P],
                                                                         lhsT=kT[hq[h], j, :], rhs=u.qT[hq[h], :],
                                                                         start=True, stop=False),
                       reads=[b_kT, u.b_qT], writes=[bbuf[bk_]], sig=False)
                    op(pe, lambda e, bk_=bk_, co=co, hg=hg, di=di: e.matmul(banks[bk_][:, co:co + P],
                                                                           lhsT=RB[:, hg, di, :], rhs=g.Jrev[:],
                                                                           start=False, stop=False),
                       reads=[b_lc, b_cst], writes=[bbuf[bk_]], sig=False)
                    last = (n_ == len(ent) - 1) or (n_ == 3)
                    op(pe, lambda e, bk_=bk_, co=co, pi=pi: e.matmul(banks[bk_][:, co:co + P], lhsT=MK[:, pi, :],
                                                                    rhs=g.Jrev[:], start=False, stop=True),
                       reads=[b_lc, b_cst], writes=[bbuf[bk_]], sig=last)
                n4 = min(4, len(ent))
                op(act, lambda e, n4=n4: e.activation(out=u.PT[:, 0:n4, :].rearrange("p n k -> p (n k)"),
                                                      in_=banks[Bb][:, 0:n4 * P], func=AF.Exp),
                   reads=[bbuf[Bb]], writes=[u.b_PT])
                if len(ent) > 4:
                    op(act, lambda e: e.activation(out=u.PT[:, 4, :], in_=banks[Bc][:, 0:P], func=AF.Exp),
                       reads=[bbuf[Bc]], writes=[u.b_PT])
                for n_, (kt, di, pi) in enumerate(ent):
                    va, b_va = kvs[kt].va
                    op(pe, lambda e, n_=n_, va=va, hg=hg, h=h: e.matmul(banks[Ba][:, 2 * P + h * 65:2 * P + h * 65 + 65],
                                                                       lhsT=u.PT[:, n_, :], rhs=va[:, hg, :],
                                                                       start=(n_ == 0), stop=(n_ == len(ent) - 1)),
                       reads=[u.b_PT, b_va], writes=[bbuf[Ba]], sig=(n_ == len(ent) - 1))
                yield
            for h in range(2):
                o0 = 2 * P + h * 65
                op(dve, lambda e, o0=o0, h=h: e.reciprocal(out=u.rc[:, h:h + 1], in_=banks[Ba][:, o0 + 64:o0 + 65]),
                   reads=[bbuf[Ba]], writes=[u.b_rc])
                op(dve, lambda e, o0=o0, h=h: e.tensor_scalar(out=og[:, (2 * j + h) * 64:(2 * j + h + 1) * 64],
                                                              in0=banks[Ba][:, o0:o0 + 64], scalar1=u.rc[:, h:h + 1],
                                                              scalar2=None, op0=ALU.mult),
                   reads=[bbuf[Ba], u.b_rc], writes=[b_og])
            yield

        def gen_O(cx):
            i = cx.t
            op(dve, lambda e: e.tensor_tensor(out=yg[:], in0=og[:], in1=zs[:], op=ALU.mult),
               reads=[b_og, b_zs], writes=[b_yg])
            for kc in range(KC):
                op(pe, lambda e, kc=kc: e.transpose(out=banks_bf[0][:, kc * P:(kc + 1) * P],
                                                    in_=yg[:, kc * P:(kc + 1) * P], identity=g.identb[:]),
                   reads=[b_yg, g.b_identb], writes=[bbuf[0]], sig=(kc == KC - 1))
            op(act, lambda e: e.activation(out=ygT[:].rearrange("p k n -> p (k n)"), in_=banks_bf[0][:, :],
                                           func=AF.Copy), reads=[bbuf[0]], writes=[b_ygT])
            yield
            for half in range(2):
                for q2 in range(NQ2):
                    w, b_w = g.loadA(L, 2, half * NQ2 + q2)
                    cs = slice(q2 * W2, (q2 + 1) * W2)
                    for kc in range(KC):
                        op(pe, lambda e, kc=kc, w=w, cs=cs: e.matmul(banks[0][:, cs], lhsT=ygT[:, kc, :], rhs=w[:, kc, :],
                                                                    start=(kc == 0), stop=False),
                           reads=[b_ygT, b_w], writes=[bbuf[0]], sig=False)
                    bo = 4 * D + half * 512 + q2 * W2
                    op(pe, lambda e, bo=bo, cs=cs: e.matmul(banks[0][:, cs], lhsT=g.onesrow[:],
                                                           rhs=g.brow_hi[0:1, bo:bo + W2], start=False, stop=False),
                       reads=[b_cst], writes=[bbuf[0]], sig=False)
                    op(pe, lambda e, bo=bo, cs=cs: e.matmul(banks[0][:, cs], lhsT=g.onesrow[:],
                                                           rhs=g.brow_lo[0:1, bo:bo + W2], start=False, stop=True),
                       reads=[b_cst], writes=[bbuf[0]], sig=(q2 == NQ2 - 1))
                op(act, lambda e, half=half: e.activation(out=tmpB[:, half * 512:(half + 1) * 512], in_=banks[0][:, :],
                                                          func=AF.Copy), reads=[bbuf[0]], writes=[b_tmpB])
                yield
            op(act, lambda e: e.activation(out=tmpA[:], in_=tmpB[:], func=AF.Square, accum_out=stat[:, 4:5]),
               reads=[b_tmpB], writes=[b_tmpA, b_stat])
            op(dve, lambda e: e.tensor_scalar(out=stat[:, 5:6], in0=stat[:, 4:5], scalar1=1.0 / D, scalar2=RMS_EPS,
                                              op0=ALU.mult, op1=ALU.add), reads=[b_stat], writes=[b_stat])
            op(act, lambda e: e.activation(out=stat[:, 6:7], in_=stat[:, 5:6], func=AF.Sqrt), reads=[b_stat],
               writes=[b_stat])
            op(dve, lambda e: e.reciprocal(out=stat[:, 7:8], in_=stat[:, 6:7]), reads=[b_stat], writes=[b_stat])
            op(dve, lambda e: e.scalar_tensor_tensor(out=tmpB[:], in0=tmpB[:], scalar=stat[:, 7:8], in1=postg[:],
                                                     op0=ALU.mult, op1=ALU.mult),
               reads=[b_tmpB, b_stat, b_lc], writes=[b_tmpB])
            op(pool, lambda e: e.tensor_tensor(out=tmpA[:], in0=tmpB[:], in1=x1[:, i, :], op=ALU.add),
               reads=[b_tmpB, bx1[i]], writes=[b_tmpA])
            T_.dma(pool, g.y_out[si, i * P:(i + 1) * P, :], tmpA[:], reads=[b_tmpA])
            yield

        tasks = []
        kvs = [None] * NCH
        lastO = None
        lastKV = None

        def make_KV(t):
            cx = NS()
            cx.t = t
            tk = Task("K%d" % t)
            cx.kxn, cx.xn = p_xn.acquire(tk)
            cx.kkT, cx.kT = p_kT.acquire(tk)
            cx.kva, cx.va = p_va.acquire(tk)
            if lastKV[0] is not None:
                tk.deps.append(lastKV[0])
            tk.gen = gen_KV(cx)
            cx.tk = tk
            kvs[t] = cx
            tasks.append(tk)
            lastKV[0] = tk

        lastKV = [None]
        for s in range(NCH + 3):
            if s < NCH:
                make_KV(s)
            i = s - 3
            if i < 0:
                continue
            cx = kvs[i]
            need = [kvs[kt].tk for (kt, _, _) in geo[i]]
            tq = Task("Q%d" % i)
            tq.deps += [cx.tk]
            if lastO is not None:
                tq.deps.append(lastO)
            p_xn.share(tq, cx.kxn)
            tq.gen = gen_Q(cx)
            tasks.append(tq)
            tas = []
            for j in range(KC):
                ta = Task("A%d_%d" % (i, j))
                ta.deps += need + [cx.tk]
                if lastO is not None:
                    ta.deps.append(lastO)
                _, uset = p_uset.acquire(ta)
                p_xn.share(ta, cx.kxn)
                for (kt, _, _) in geo[i]:
                    p_kT.share(ta, kvs[kt].kkT)
                    p_va.share(ta, kvs[kt].kva)
                ta.gen = gen_A(cx, j, uset, kvs)
                tas.append(ta)
                tasks.append(ta)
            to = Task("O%d" % i)
            to.deps += tas + [tq]
            to.gen = gen_O(cx)
            tasks.append(to)
            lastO = to
        run_tasks(tasks, window=WINDOW)


def kernel(**inputs):
    xp = np.asarray(inputs['x_prompt'], dtype=np.float32)
    xs_ = np.asarray(inputs['x_sample'], dtype=np.float32)
    xall = np.concatenate([xp, xs_], axis=0)
    nseq = xall.shape[0] // N_CORES
    nc = build(nseq)
    in_maps = []
    for ci in range(N_CORES):
        m = {"x": np.ascontiguousarray(xall[ci * nseq:(ci + 1) * nseq])}
        for nm in W_NAMES:
            m[nm] = np.ascontiguousarray(np.asarray(inputs[nm], dtype=np.float32))
        in_maps.append(m)
    res = run_bass_kernel_spmd(nc, in_maps, core_ids=list(range(N_CORES)))
    yall = np.concatenate([r["y"] for r in res.results], axis=0)
    nb = xp.shape[0]
    return (np.ascontiguousarray(yall[:nb]), np.ascontiguousarray(yall[nb:]))
```

```python
import numpy as np
from contextlib import ExitStack
import concourse.bass as bass
import concourse.mybir as mybir
from concourse.bass_utils import run_bass_kernel_spmd
from concourse.alu_op_type import AluOpType as ALU

F32 = mybir.dt.float32
BF16 = mybir.dt.bfloat16
AF = mybir.ActivationFunctionType
AX = mybir.AxisListType

N_CORES = 8
D = 1024
KC = 8
P = 128
T_SEQ = 2048
LAM = float(np.exp(-0.5))
LNX_EPS = 64e-5
RMS_EPS = 1e-6

import os
SAME_ENGINE_SYNC = os.environ.get('K_SES', '1') == '1'
WINDOW = int(os.environ.get('K_WIN', '4'))
FY = int(os.environ.get('K_FY', '0'))
UY = int(os.environ.get('K_UY', '1'))
STAGGER = int(os.environ.get('K_STAG', '0'))
NQ2 = int(os.environ.get('K_NQ2', '1'))
W2 = 512 // NQ2
DBUF = int(os.environ.get('K_DBUF', '1'))


class _Sem:
    def __init__(self, sem, name):
        self.sem = sem
        self.name = name
        self.n = 0


class Eng:
    def __init__(self, h, sem, name, is_pe=False):
        self.h = h
        self.s = _Sem(sem, name)
        self.name = name
        self.is_pe = is_pe
        self.waited = {}


class Buf:
    __slots__ = ("name", "w", "r", "excl")

    def __init__(self, name="", excl=False):
        self.name = name
        self.w = None
        self.r = {}
        self.excl = excl


class Trk:
    def __init__(self, nc, es, n_slots=16):
        self.nc = nc
        mk = lambda nm: es.enter_context(nc.semaphore(nm))
        self.pe = Eng(nc.tensor, mk("s_pe"), "pe", is_pe=True)
        self.act = Eng(nc.scalar, mk("s_act"), "act")
        self.dve = Eng(nc.vector, mk("s_dve"), "dve")
        self.pool = Eng(nc.gpsimd, mk("s_pool"), "pool")
        self.sp = Eng(nc.sync, mk("s_sp"), "sp")
        self.engs = [self.pe, self.act, self.dve, self.pool, self.sp]
        self.slots = [_Sem(mk("s_dma%d" % i), "dma%d" % i) for i in range(n_slots)]
        self.dma_i = 0
        self.slots_sw = [_Sem(mk("s_swdma%d" % i), "swdma%d" % i) for i in range(4)]
        self.dma_sw_i = 0
        self.n_inst = 0
        self.cnt = {}

    def _deps(self, reads, writes):
        deps = {}
        for b in reads:
            if b.w is not None:
                s, v = b.w
                if deps.get(s, 0) < v:
                    deps[s] = v
        for b in writes:
            if b.w is not None:
                s, v = b.w
                if deps.get(s, 0) < v:
                    deps[s] = v
            for s, v in b.r.items():
                if deps.get(s, 0) < v:
                    deps[s] = v
        return deps

    def _wait(self, eng, deps):
        for s, v in deps.items():
            if eng.waited.get(s, 0) >= v:
                continue
            if s is eng.s:
                if eng.is_pe or not SAME_ENGINE_SYNC:
                    continue
            eng.h.wait_ge(s.sem, v)
            eng.waited[s] = v

    def _mark(self, tok, reads, writes):
        s, v = tok
        for b in reads:
            if b.r.get(s, 0) < v:
                b.r[s] = v
        for b in writes:
            b.w = tok
            b.r = {}

    def op(self, eng, fn, reads=(), writes=(), sig=True):
        if any(b.excl for b in reads):
            writes = list(writes) + [b for b in reads if b.excl]
            reads = [b for b in reads if not b.excl]
        self._wait(eng, self._deps(reads, writes))
        inst = fn(eng.h)
        tok = (eng.s, eng.s.n + 1)
        if sig:
            inst.then_inc(eng.s.sem, 1)
            eng.s.n += 1
        self._mark(tok, reads, writes)
        self.n_inst += 1
        self.cnt[eng.name] = self.cnt.get(eng.name, 0) + 1
        return inst

    def dma(self, q, out, in_, reads=(), writes=(), **kw):
        if q is self.pool:
            slot = self.slots_sw[self.dma_sw_i % len(self.slots_sw)]
            self.dma_sw_i += 1
        else:
            slot = self.slots[self.dma_i % len(self.slots)]
            self.dma_i += 1
        deps = self._deps(reads, writes)
        if slot.n > 0 and deps.get(slot, 0) < slot.n:
            deps[slot] = slot.n
        self._wait(q, deps)
        inst = q.h.dma_start(out=out, in_=in_, **kw)
        inst.then_inc(slot.sem, 16)
        slot.n += 16
        self._mark((slot, slot.n), reads, writes)
        self.n_inst += 1
        self.cnt["dma"] = self.cnt.get("dma", 0) + 1
        return inst

    def barrier(self):
        allsems = [e.s for e in self.engs] + self.slots + self.slots_sw
        for e in self.engs:
            deps = {s: s.n for s in allsems if s.n > 0 and s is not e.s}
            self._wait(e, deps)

    def finish(self):
        deps = {s: s.n for s in self.slots + self.slots_sw if s.n > 0}
        self._wait(self.sp, deps)


class Task:
    def __init__(self, name):
        self.name = name
        self.deps = []
        self.done = False
        self.gen = None


def run_tasks(tasks, window=3):
    pending = list(tasks)
    active = []
    while pending or active:
        while pending and len(active) < window and all(d.done for d in pending[0].deps):
            active.append(pending.pop(0))
        if not active:
            raise RuntimeError("scheduler deadlock at %s" % pending[0].name)
        for t in list(active):
            try:
                next(t.gen)
            except StopIteration:
                t.done = True
                active.remove(t)


class RPool:
    def __init__(self, items):
        self.items = items
        self.i = 0
        self.users = [[] for _ in items]

    def acquire(self, task):
        k = self.i % len(self.items)
        self.i += 1
        task.deps += self.users[k]
        self.users[k] = [task]
        return k, self.items[k]

    def share(self, task, k):
        self.users[k].append(task)


class NS:
    pass


W_NAMES = ['pre_norm_g', 'post_norm_g', 'rk_mu', 'rk_w_r', 'rk_w_k', 'rk_w_v', 'rk_w_z', 'rk_w0', 'rk_w1', 'rk_w2',
           'rk_a0', 'rk_a1', 'rk_a2', 'rk_k_k', 'rk_k_a', 'rk_r_k', 'rk_lnx_w', 'rk_lnx_b', 'rk_w_o', 'na_w_in',
           'na_b_in', 'na_rpb', 'na_w_o', 'na_b_o']
W_SHAPES = {
    'pre_norm_g': [2, D], 'post_norm_g': [2, D], 'rk_mu': [1, 7, D], 'rk_w_r': [1, D, D], 'rk_w_k': [1, D, D],
    'rk_w_v': [1, D, D], 'rk_w_z': [1, D, D], 'rk_w0': [1, 2, D], 'rk_w1': [1, 2, D, 64], 'rk_w2': [1, 2, 64, D],
    'rk_a0': [1, 2, D], 'rk_a1': [1, 2, D, 64], 'rk_a2': [1, 2, 64, D], 'rk_k_k': [1, D], 'rk_k_a': [1, D],
    'rk_r_k': [1, 16, 64], 'rk_lnx_w': [1, D], 'rk_lnx_b': [1, D], 'rk_w_o': [1, D, D], 'na_w_in': [1, D, 4 * D],
    'na_b_in': [1, 4 * D], 'na_rpb': [1, 16, 15, 31], 'na_w_o': [1, D, D], 'na_b_o': [1, D],
}

PV = {}
_c = 0
for _nm, _n in [('mu', 7), ('w0', 2), ('a0', 2), ('k_k', 1), ('k_a', 1), ('r_k', 1), ('pre_g', 2), ('bq', 1),
                ('bk', 1), ('omk_a', 1), ('bq8', 1)]:
    PV[_nm] = _c
    _c += _n * 8
PV_COLS = _c
PV_ROWS = PV['omk_a']


def build(nseq, T=T_SEQ, stage="full"):
    NCH = T // P
    nc = bass.Bass("TRN2", target_bir_lowering=False)
    x_in = nc.dram_tensor("x", [nseq, T, D], F32, kind="ExternalInput").ap()
    y_out = nc.dram_tensor("y", [nseq, T, D], F32, kind="ExternalOutput").ap()
    wd = {nm: nc.dram_tensor(nm, W_SHAPES[nm], F32, kind="ExternalInput").ap() for nm in W_NAMES}
    wbJ = [nc.dram_tensor("wbJ%d" % l, [2, KC, P, KC, P], BF16, kind="Internal").ap() for l in range(2)]
    wbH = [nc.dram_tensor("wbH%d" % l, [3, 2 * NQ2, P, KC, W2], BF16, kind="Internal").ap() for l in range(2)]

    es = ExitStack()
    with es:
        T_ = Trk(nc, es)
        pe, act, dve, pool, sp = T_.pe, T_.act, T_.dve, T_.pool, T_.sp
        op = T_.op

        uid = [0]

        def sb(es_, nm, sh, dt):
            uid[0] += 1
            return es_.enter_context(nc.sbuf_tensor("%s_%d" % (nm, uid[0]), sh, dt))

        banks = [es.enter_context(nc.psum_tensor("pb%d" % i, [P, 512], F32)) for i in range(8)]
        bbuf = [Buf("pb%d" % i, excl=True) for i in range(8)]
        banks_bf = [b[:].bitcast(BF16) for b in banks]

        x1 = sb(es, "x1", [P, NCH, D], F32)
        bx1 = [Buf("x1_%d" % c) for c in range(NCH)]
        ringA = [sb(es, "ringA%d" % i, [P, KC, W2], BF16) for i in range(2)]
        b_ringA = [Buf("ringA%d" % i) for i in range(2)]
        ringB = [sb(es, "ringB%d" % i, [P, KC, P], BF16) for i in range(4)]
        b_ringB = [Buf("ringB%d" % i) for i in range(4)]
        rA_i = [0]
        rB_i = [0]
        identb = sb(es, "identb", [P, P], BF16)
        b_identb = Buf("identb")
        pv = sb(es, "pv", [P, PV_COLS], F32)
        b_pv = Buf("pv")
        onesrow = sb(es, "onesrow", [1, P], BF16)
        brow_hi = sb(es, "brow_hi", [1, 5 * D], BF16)
        brow_lo = sb(es, "brow_lo", [1, 5 * D], BF16)
        b_cst = Buf("consts")
        maskq = [sb(es, "maskq%d" % d, [P, 512], BF16) for d in range(2)]
        bdones = sb(es, "bdones", [P, P], BF16)
        hsel = sb(es, "hsel", [P, 2], BF16)
        ones_f = sb(es, "ones_f", [P, P], F32)
        Jrev = sb(es, "Jrev", [P, P], BF16)
        pes0 = ExitStack()
        io_r = sb(pes0, "io_r", [P, P], F32)
        io_c = sb(pes0, "io_c", [P, P], F32)

        def pvc(nm, idx):
            c0 = PV[nm] + idx
            return pv[:, c0:c0 + 1]

        def loadA(layer, m, qt):
            k = rA_i[0] % 2
            rA_i[0] += 1
            T_.dma(sp, ringA[k][:], wbH[layer][m, qt], writes=[b_ringA[k]])
            return ringA[k], b_ringA[k]

        def loadB(layer, m, j):
            k = rB_i[0] % 4
            rB_i[0] += 1
            T_.dma(sp, ringB[k][:], wbJ[layer][m, j], writes=[b_ringB[k]])
            return ringB[k], b_ringB[k]

        op(pool, lambda e: e.iota(io_r[:], pattern=[[0, P]], base=0, channel_multiplier=1,
                                  allow_small_or_imprecise_dtypes=True), writes=[b_cst])
        op(pool, lambda e: e.iota(io_c[:], pattern=[[1, P]], base=0, channel_multiplier=0,
                                  allow_small_or_imprecise_dtypes=True), writes=[b_cst])
        op(dve, lambda e: e.tensor_tensor(out=identb[:], in0=io_r[:], in1=io_c[:], op=ALU.is_equal),
           reads=[b_cst], writes=[b_identb])
        op(dve, lambda e: e.memset(onesrow[:], 1.0), writes=[b_cst])
        op(dve, lambda e: e.memset(ones_f[:], 1.0), writes=[b_cst])
        for d_, (o_s, o_i) in enumerate([(ALU.is_lt, ALU.is_le), (ALU.is_gt, ALU.is_ge)]):
            for q4 in range(4):
                o_ = o_s if q4 % 2 == 0 else o_i
                op(dve, lambda e, d_=d_, q4=q4, o_=o_: e.tensor_tensor(out=maskq[d_][:, q4 * P:(q4 + 1) * P],
                                                                      in0=io_r[:], in1=io_c[:], op=o_),
                   reads=[b_cst], writes=[b_cst])

        with ExitStack() as pes:
            identf = sb(pes, "identf", [P, P], F32)
            rb_ = sb(pes, "rb_", [P, P], F32)
            cb_ = sb(pes, "cb_", [P, P], F32)
            op(dve, lambda e: e.tensor_tensor(out=identf[:], in0=io_r[:], in1=io_c[:], op=ALU.is_equal),
               reads=[b_cst], writes=[b_cst])
            op(dve, lambda e: e.tensor_scalar(out=rb_[:], in0=io_r[:], scalar1=63.5, scalar2=None, op0=ALU.is_gt),
               reads=[b_cst], writes=[b_cst])
            op(dve, lambda e: e.tensor_scalar(out=cb_[:], in0=io_c[:], scalar1=63.5, scalar2=None, op0=ALU.is_gt),
               reads=[b_cst], writes=[b_cst])
            op(dve, lambda e: e.tensor_tensor(out=bdones[:], in0=rb_[:], in1=cb_[:], op=ALU.is_equal),
               reads=[b_cst], writes=[b_cst])
            op(dve, lambda e: e.tensor_copy(out=hsel[:, 1:2], in_=rb_[:, 0:1]), reads=[b_cst], writes=[b_cst])
            op(dve, lambda e: e.tensor_scalar(out=hsel[:, 0:1], in0=rb_[:, 0:1], scalar1=-1.0, scalar2=1.0,
                                              op0=ALU.mult, op1=ALU.add), reads=[b_cst], writes=[b_cst])

            rows = sb(pes, "pvrows", [P, 2, P], F32)
            b_rows = Buf("pvrows")
            op(dve, lambda e: e.memset(rows[:], 0.0), writes=[b_rows])

            def load_rows(r0, src):
                n = src.shape[0]
                g, o = divmod(r0, P)
                assert o + n <= P, (r0, n)
                T_.dma(sp, rows[o:o + n, g, :], src, writes=[b_rows])

            load_rows(PV['mu'], wd['rk_mu'][0].rearrange("m (j q) -> (m j) q", q=P))
            load_rows(PV['w0'], wd['rk_w0'][0].rearrange("m (j q) -> (m j) q", q=P))
            load_rows(PV['a0'], wd['rk_a0'][0].rearrange("m (j q) -> (m j) q", q=P))
            load_rows(PV['k_k'], wd['rk_k_k'][0].rearrange("(j q) -> j q", q=P))
            load_rows(PV['k_a'], wd['rk_k_a'][0].rearrange("(j q) -> j q", q=P))
            load_rows(PV['r_k'], wd['rk_r_k'][0].rearrange("(j h) c -> j (h c)", h=2))
            load_rows(PV['pre_g'], wd['pre_norm_g'].rearrange("m (j q) -> (m j) q", q=P))
            load_rows(PV['bq'], wd['na_b_in'][0, 0:D].rearrange("(j q) -> j q", q=P))
            load_rows(PV['bk'], wd['na_b_in'][0, D:2 * D].rearrange("(j q) -> j q", q=P))
            for g in range(2):
                op(pe, lambda e, g=g: e.transpose(out=banks[0][:, g * P:(g + 1) * P], in_=rows[:, g, :],
                                                  identity=identf[:]),
                   reads=[b_rows, b_cst], writes=[bbuf[0]])
            op(dve, lambda e: e.tensor_copy(out=pv[:, 0:PV_ROWS], in_=banks[0][:, 0:PV_ROWS]),
               reads=[bbuf[0]], writes=[b_pv])
            op(dve, lambda e: e.tensor_scalar(out=pv[:, PV['omk_a']:PV['omk_a'] + 8],
                                              in0=pv[:, PV['k_a']:PV['k_a'] + 8], scalar1=-1.0, scalar2=1.0,
                                              op0=ALU.mult, op1=ALU.add), reads=[b_pv], writes=[b_pv])
            op(dve, lambda e: e.tensor_scalar(out=pv[:, PV['bq8']:PV['bq8'] + 8], in0=pv[:, PV['bq']:PV['bq'] + 8],
                                              scalar1=0.125, scalar2=None, op0=ALU.mult), reads=[b_pv], writes=[b_pv])

            brow_f = sb(pes, "brow_f", [1, 5 * D], F32)
            brow_t = sb(pes, "brow_t", [1, 5 * D], F32)
            b_bf = Buf("brow_f")
            T_.dma(sp, brow_f[0:1, 0:4 * D], wd['na_b_in'][0:1, :], writes=[b_bf])
            T_.dma(sp, brow_f[0:1, 4 * D:5 * D], wd['na_b_o'][0:1, :], writes=[b_bf])
            op(act, lambda e: e.activation(out=brow_hi[:], in_=brow_f[:], func=AF.Copy), reads=[b_bf], writes=[b_cst])
            op(dve, lambda e: e.tensor_tensor(out=brow_t[:], in0=brow_f[:], in1=brow_hi[:], op=ALU.subtract),
               reads=[b_bf, b_cst], writes=[b_bf])
            op(act, lambda e: e.activation(out=brow_lo[:], in_=brow_t[:], func=AF.Copy), reads=[b_bf], writes=[b_cst])

            stg = [sb(pes, "stg%d" % i, [P, D], F32) for i in range(3)]
            stb = [sb(pes, "stb%d" % i, [P, D], BF16) for i in range(3)]
            b_stg = [Buf() for _ in range(3)]
            b_stb = [Buf() for _ in range(3)]
            srcs = []
            for kc in range(KC):
                rs_ = slice(kc * P, (kc + 1) * P)
                for m, nm in enumerate(['rk_w_r', 'rk_w_k']):
                    srcs.append((wd[nm][0, rs_, :], wbJ[0][m, :, :, kc, :].rearrange("j p n -> p j n"), 'J'))
                for m, nm in enumerate(['rk_w_v', 'rk_w_z', 'rk_w_o']):
                    srcs.append((wd[nm][0, rs_, :], wbH[0][m, :, :, kc, :].rearrange("h p n -> p h n"), 'H'))
                for m in range(2):
                    srcs.append((wd['na_w_in'][0, rs_, m * D:(m + 1) * D],
                                 wbJ[1][m, :, :, kc, :].rearrange("j p n -> p j n"), 'J'))
                for m in range(2):
                    srcs.append((wd['na_w_in'][0, rs_, (m + 2) * D:(m + 3) * D],
                                 wbH[1][m, :, :, kc, :].rearrange("h p n -> p h n"), 'H'))
                srcs.append((wd['na_w_o'][0, rs_, :], wbH[1][2, :, :, kc, :].rearrange("h p n -> p h n"), 'H'))
            cast_engs = [act, dve, pool]
            for i, (src, dst, kind) in enumerate(srcs):
                k = i % 3
                T_.dma(sp, stg[k][:], src, writes=[b_stg[k]])
                if cast_engs[k] is act:
                    op(act, lambda e, k=k: e.activation(out=stb[k][:], in_=stg[k][:], func=AF.Copy),
                       reads=[b_stg[k]], writes=[b_stb[k]])
                else:
                    op(cast_engs[k], lambda e, k=k: e.tensor_copy(out=stb[k][:], in_=stg[k][:]),
                       reads=[b_stg[k]], writes=[b_stb[k]])
                if kind == 'J':
                    srcv = stb[k][:].rearrange("p (j n) -> p j n", j=KC)
                else:
                    srcv = stb[k][:].rearrange("p (h n) -> p h n", h=2 * NQ2)
                T_.dma(sp, dst, srcv, reads=[b_stb[k]])
            T_.barrier()
            T_.finish()
        pes0.close()

        ctx = NS()
        ctx.__dict__.update(locals())
        if stage != "l0":
            l1_prologue(ctx)
        for si in range(nseq):
            with nc.named_scope('L0_%d' % si):
                layer0(ctx, si)
            T_.barrier()
            T_.finish()
            if stage == "l0":
                for c in range(NCH):
                    T_.dma(sp, y_out[si, c * P:(c + 1) * P, :], x1[:, c, :], reads=[bx1[c]])
            else:
                with nc.named_scope('L1_%d' % si):
                    layer1(ctx, si)
            T_.barrier()
            T_.finish()
        print("instructions:", T_.n_inst, T_.cnt)
    return nc


def layer0(g, si):
    nc, T_, op = g.nc, g.T_, g.op
    pe, act, dve, pool, sp = g.pe, g.act, g.dve, g.pool, g.sp
    banks, bbuf, banks_bf = g.banks, g.bbuf, g.banks_bf
    NCH, x1, bx1, pv, b_pv, pvc = g.NCH, g.x1, g.bx1, g.pv, g.b_pv, g.pvc
    wd, sb = g.wd, g.sb
    b_cst = g.b_cst
    L = 0

    with ExitStack() as l0:
        w1b = sb(l0, "w1b", [P, 2, KC, 64], BF16)
        a1b = sb(l0, "a1b", [P, 2, KC, 64], BF16)
        w2b = sb(l0, "w2b", [64, 2, D], BF16)
        a2b = sb(l0, "a2b", [64, 2, D], BF16)
        lnxb = sb(l0, "lnxb", [P, 2, D], F32)
        postg = sb(l0, "postg", [P, D], F32)
        b_lc = Buf("l0consts")
        tmpA = sb(l0, "tmpA", [P, D], F32)
        tmpB = sb(l0, "tmpB", [P, D], F32)
        b_tmpA, b_tmpB = Buf("tmpA"), Buf("tmpB")
        T_.dma(sp, tmpA[:].rearrange("p (d k n) -> p d k n", d=2, k=KC),
               wd['rk_w1'][0].rearrange("d (k p) n -> p d k n", p=P), writes=[b_tmpA])
        op(dve, lambda e: e.tensor_copy(out=w1b[:].rearrange("p d k n -> p (d k n)"), in_=tmpA[:]),
           reads=[b_tmpA], writes=[b_lc])
        T_.dma(sp, tmpB[:].rearrange("p (d k n) -> p d k n", d=2, k=KC),
               wd['rk_a1'][0].rearrange("d (k p) n -> p d k n", p=P), writes=[b_tmpB])
        op(dve, lambda e: e.tensor_copy(out=a1b[:].rearrange("p d k n -> p (d k n)"), in_=tmpB[:]),
           reads=[b_tmpB], writes=[b_lc])
        for d_ in range(2):
            T_.dma(sp, tmpA[0:64, :], wd['rk_w2'][0, d_], writes=[b_tmpA])
            op(dve, lambda e, d_=d_: e.tensor_copy(out=w2b[:, d_, :], in_=tmpA[0:64, :]), reads=[b_tmpA], writes=[b_lc])
            T_.dma(sp, tmpB[0:64, :], wd['rk_a2'][0, d_], writes=[b_tmpB])
            op(dve, lambda e, d_=d_: e.tensor_copy(out=a2b[:, d_, :], in_=tmpB[0:64, :]), reads=[b_tmpB], writes=[b_lc])
        T_.dma(sp, lnxb[:, 0, :], wd['rk_lnx_w'][0].partition_broadcast(P), writes=[b_lc])
        T_.dma(sp, lnxb[:, 1, :], wd['rk_lnx_b'][0].partition_broadcast(P), writes=[b_lc])
        T_.dma(sp, postg[:], wd['post_norm_g'][L].partition_broadcast(P), writes=[b_lc])

        def mk(nm, sh, dt, n):
            return [(sb(l0, "%s%d" % (nm, i), sh, dt), Buf("%s%d" % (nm, i))) for i in range(n)]

        p_xin = RPool(mk("xin", [P, D], F32, 1))
        xs, b_xs = mk("xs", [P, D], BF16, 1)[0]
        stat = sb(l0, "stat", [P, 8], F32)
        b_stat = Buf("stat")
        p_slot = RPool(mk("xnT", [P, KC, P + 2], BF16, 3))
        xx, b_xx = mk("xx", [P, KC, P], BF16, 1)[0]
        ltmp, b_ltmp = mk("ltmp", [P, P], F32, 1)[0]
        p_lr = RPool(mk("lrp_r", [P, KC, P], BF16, DBUF))
        p_lk = RPool(mk("lrp_k", [P, KC, P], BF16, DBUF))
        p_lt = mk("lrp_t", [P, KC, P], BF16, 1)
        lt_i = [0]
        p_vt = RPool(mk("Vtm", [P, D], BF16, DBUF))
        p_zs = RPool(mk("zs", [P, D], BF16, 1))
        p_th = RPool(mk("th", [64, 2 * P], BF16, 2))
        Hf = sb(l0, "Hf", [P, KC, 64], F32)
        Hb = sb(l0, "Hb", [P, KC, 64], BF16)
        b_H = [Buf("H%d" % j) for j in range(KC)]
        bonF = sb(l0, "bonF", [P, NCH, 16], F32)
        b_bonF = [Buf("bonF%d" % c) for c in range(NCH)]
        p_bonB = RPool(mk("bonB", [P, 16], F32, 2))
        p_yb = RPool(mk("Yb", [P, D], F32, 1))
        yg, b_yg = mk("yg", [P, D], BF16, 1)[0]
        ygT, b_ygT = mk("ygT", [P, KC, P], BF16, 1)[0]
        gst = sb(l0, "gst", [P, 6, 16], F32)
        b_gst = Buf("gst")

        def mkset(i):
            u = NS()
            u.i = i
            def t(nm, sh, dt):
                tt = sb(l0, "u%d_%s" % (i, nm), sh, dt)
                setattr(u, nm, tt)
                setattr(u, "b_" + nm, Buf("u%d_%s" % (i, nm)))
            t("rk", [P, 2 * P], F32)
            for nm in ("sg", "al", "kk", "rs", "Ein", "Eex", "ein", "kd"):
                t(nm, [P, P], F32)
            t("sq", [P, P], BF16)
            t("arT", [P, 2 * P], BF16)
            t("btT", [P, P], BF16)
            t("ktT", [P, P], BF16)
            t("pr", [P, P], BF16)
            t("BK", [P, 2 * P], BF16)
            t("AT0", [P, 512], BF16)
            t("AT1", [P, 512], BF16)
            for h in range(2):
                for k in range(2):
                    t("C%d%d" % (h, k), [P, 3 * P], BF16)
            t("Xp", [P, P], BF16)
            t("Up", [P, P], BF16)
            t("Hs", [P, 64], F32)
            t("sc", [P, 4], F32)
            u.banks = (2 + 3 * i, 3 + 3 * i, 4 + 3 * i)
            return u

        p_uset = RPool([mkset(0), mkset(1)])

        def gen_Na(cx):
            c = cx.c
            xin, b_xin = cx.xin
            T_.dma(sp, xin[:], g.x_in[si, c * P:(c + 1) * P, :], writes=[b_xin])
            op(act, lambda e: e.activation(out=xs[:], in_=xin[:], func=AF.Square, accum_out=stat[:, 0:1]),
               reads=[b_xin], writes=[b_xs, b_stat])
            op(dve, lambda e: e.tensor_scalar(out=stat[:, 1:2], in0=stat[:, 0:1], scalar1=1.0 / D, scalar2=RMS_EPS,
                                              op0=ALU.mult, op1=ALU.add), reads=[b_stat], writes=[b_stat])
            op(act, lambda e: e.activation(out=stat[:, 2:3], in_=stat[:, 1:2], func=AF.Sqrt), reads=[b_stat],
               writes=[b_stat])
            op(dve, lambda e: e.reciprocal(out=stat[:, 3:4], in_=stat[:, 2:3]), reads=[b_stat], writes=[b_stat])
            op(act, lambda e: e.activation(out=xs[:], in_=xin[:], func=AF.Copy, scale=stat[:, 3:4]),
               reads=[b_xin, b_stat], writes=[b_xs])
            yield

        def gen_Nb(cx, d, first, last, prev):
            c = cx.c
            slot, b_slot = cx.slot
            for kc in range(KC):
                op(pe, lambda e, kc=kc: e.transpose(out=banks_bf[0][:, kc * P:(kc + 1) * P],
                                                    in_=xs[:, kc * P:(kc + 1) * P], identity=g.identb[:]),
                   reads=[b_xs, g.b_identb], writes=[bbuf[0]], sig=(kc == KC - 1))
            for kc in range(KC):
                sc = pvc('pre_g', L * 8 + kc)
                if kc % 2 == 0:
                    op(dve, lambda e, kc=kc, sc=sc: e.tensor_scalar(out=slot[:, kc, 1:P + 1],
                                                                  in0=banks_bf[0][:, kc * P:(kc + 1) * P],
                                                                  scalar1=sc, scalar2=None, op0=ALU.mult),
                       reads=[bbuf[0], b_pv], writes=[b_slot])
                else:
                    op(act, lambda e, kc=kc, sc=sc: e.activation(out=slot[:, kc, 1:P + 1],
                                                               in_=banks_bf[0][:, kc * P:(kc + 1) * P],
                                                               func=AF.Copy, scale=sc),
                       reads=[bbuf[0], b_pv], writes=[b_slot])
            near, far = (0, P + 1) if d == 0 else (P + 1, 0)
            if first:
                op(pool, lambda e: e.memset(slot[:, :, near:near + 1], 0.0), writes=[b_slot])
            else:
                pslot, b_pslot = prev.slot
                src_own = 1 if d == 0 else P
                src_prev = P if d == 0 else 1
                op(pool, lambda e: e.tensor_copy(out=slot[:, :, near:near + 1], in_=pslot[:, :, src_prev:src_prev + 1]),
                   reads=[b_pslot], writes=[b_slot])
                op(pool, lambda e: e.tensor_copy(out=pslot[:, :, far:far + 1], in_=slot[:, :, src_own:src_own + 1]),
                   reads=[b_slot], writes=[b_pslot])
            if last:
                op(pool, lambda e: e.memset(slot[:, :, far:far + 1], 0.0), writes=[b_slot])
            yield

        def gen_F(cx, d):
            slot, b_slot = cx.slot
            xn = slot[:, :, 1:P + 1]
            op(pool, lambda e: e.tensor_tensor(out=xx[:], in0=slot[:, :, 0:P], in1=slot[:, :, 2:P + 2], op=ALU.add),
               reads=[b_slot], writes=[b_xx])
            op(dve, lambda e: e.scalar_tensor_tensor(out=xx[:], in0=xx[:], scalar=0.5, in1=xn, op0=ALU.mult,
                                                     op1=ALU.subtract), reads=[b_xx, b_slot], writes=[b_xx])
            yield

            def lerp(m, dst, b_dst):
                for kc in range(KC):
                    sc = pvc('mu', m * 8 + kc)
                    op(dve, lambda e, kc=kc, sc=sc: e.scalar_tensor_tensor(
                        out=dst[:, kc, :], in0=xx[:, kc, :], scalar=sc, in1=slot[:, kc, 1:P + 1],
                        op0=ALU.mult, op1=ALU.add), reads=[b_xx, b_slot, b_pv], writes=[b_dst])

            def next_lt():
                r = p_lt[0]
                lt_i[0] += 1
                return r

            lr, b_lr = cx.lr
            lk, b_lk = cx.lk
            vt, b_vt = cx.vt
            lv, b_lv = next_lt()
            lerp(2, lv, b_lv)
            yield
            lerp(0, lr, b_lr)
            yield
            for _ in range(FY):
                yield
            for half in range(2):
                for q2 in range(NQ2):
                    w, b_w = g.loadA(L, 0, half * NQ2 + q2)
                    for kc in range(KC):
                        op(pe, lambda e, kc=kc, w=w, q2=q2: e.matmul(banks[1][:, q2 * W2:(q2 + 1) * W2],
                                                                    lhsT=lv[:, kc, :], rhs=w[:, kc, :],
                                                                    start=(kc == 0), stop=(kc == KC - 1)),
                           reads=[b_lv, b_w], writes=[bbuf[1]], sig=(kc == KC - 1 and q2 == NQ2 - 1))
                op(act, lambda e, half=half: e.activation(out=vt[:, half * 512:(half + 1) * 512], in_=banks[1][:, :],
                                                          func=AF.Copy), reads=[bbuf[1]], writes=[b_vt])
                yield
            if d == 1:
                zs, b_zs = cx.zs
                for half in range(2):
                    for q2 in range(NQ2):
                        w, b_w = g.loadA(L, 1, half * NQ2 + q2)
                        for kc in range(KC):
                            op(pe, lambda e, kc=kc, w=w, q2=q2: e.matmul(banks[1][:, q2 * W2:(q2 + 1) * W2],
                                                                        lhsT=slot[:, kc, 1:P + 1], rhs=w[:, kc, :],
                                                                        start=(kc == 0), stop=(kc == KC - 1)),
                               reads=[b_slot, b_w], writes=[bbuf[1]], sig=(kc == KC - 1 and q2 == NQ2 - 1))
                    op(act, lambda e, half=half: e.activation(out=zs[:, half * 512:(half + 1) * 512],
                                                              in_=banks[1][:, :], func=AF.Silu),
                       reads=[bbuf[1]], writes=[b_zs])
                    yield
            lerp(1, lk, b_lk)
            yield
            th, b_th = cx.th
            lw, b_lw = next_lt()
            lerp(3 + d, lw, b_lw)
            yield
            for _ in range(FY):
                yield
            for kc in range(KC):
                op(pe, lambda e, kc=kc: e.matmul(banks[1][0:64, 0:P], lhsT=w1b[:, d, kc, :], rhs=lw[:, kc, :],
                                                 start=(kc == 0), stop=(kc == KC - 1)),
                   reads=[b_lw, b_lc], writes=[bbuf[1]], sig=(kc == KC - 1))
            la, b_la = next_lt()
            lerp(5 + d, la, b_la)
            yield
            for _ in range(FY):
                yield
            for kc in range(KC):
                op(pe, lambda e, kc=kc: e.matmul(banks[1][0:64, P:2 * P], lhsT=a1b[:, d, kc, :], rhs=la[:, kc, :],
                                                 start=(kc == 0), stop=(kc == KC - 1)),
                   reads=[b_la, b_lc], writes=[bbuf[1]], sig=(kc == KC - 1))
            op(act, lambda e: e.activation(out=th[:, 0:P], in_=banks[1][0:64, 0:P], func=AF.Tanh),
               reads=[bbuf[1]], writes=[b_th])
            op(act, lambda e: e.activation(out=th[:, P:2 * P], in_=banks[1][0:64, P:2 * P], func=AF.Copy),
               reads=[bbuf[1]], writes=[b_th])
            yield

        def gen_U(cx, d, j, u):
            c = cx.c
            Ba, Bb, Bc = u.banks
            lr, b_lr = cx.lr
            lk, b_lk = cx.lk
            vt, b_vt = cx.vt
            th, b_th = cx.th
            hq = [slice(0, 64), slice(64, 128)]
            mq = g.maskq[d]
            mA = g.maskq[1 - d][:, 0:P]
            for _ in range(STAGGER if (j % 2 == 1) else 0):
                yield
            wr, b_wr = g.loadB(L, 0, j)
            wk, b_wk = g.loadB(L, 1, j)
            for kc in range(KC):
                op(pe, lambda e, kc=kc: e.matmul(banks[Ba][:, 0:P], lhsT=wr[:, kc, :], rhs=lr[:, kc, :],
                                                 start=(kc == 0), stop=(kc == KC - 1)),
                   reads=[b_wr, b_lr], writes=[bbuf[Ba]], sig=False)
            for kc in range(KC):
                op(pe, lambda e, kc=kc: e.matmul(banks[Ba][:, P:2 * P], lhsT=wk[:, kc, :], rhs=lk[:, kc, :],
                                                 start=(kc == 0), stop=(kc == KC - 1)),
                   reads=[b_wk, b_lk], writes=[bbuf[Ba]], sig=False)
            op(pe, lambda e: e.matmul(banks[Ba][:, 2 * P:3 * P], lhsT=w2b[:, d, j * P:(j + 1) * P], rhs=th[:, 0:P],
                                      start=True, stop=True), reads=[b_lc, b_th], writes=[bbuf[Ba]], sig=False)
            op(pe, lambda e: e.matmul(banks[Ba][:, 3 * P:4 * P], lhsT=a2b[:, d, j * P:(j + 1) * P], rhs=th[:, P:2 * P],
                                      start=True, stop=True), reads=[b_lc, b_th], writes=[bbuf[Ba]])
            op(act, lambda e: e.activation(out=u.rk[:], in_=banks[Ba][:, 0:2 * P], func=AF.Copy),
               reads=[bbuf[Ba]], writes=[u.b_rk])
            op(act, lambda e: e.activation(out=u.sg[:], in_=banks[Ba][:, 2 * P:3 * P], func=AF.Sigmoid,
                                           bias=pvc('w0', d * 8 + j)), reads=[bbuf[Ba], b_pv], writes=[u.b_sg])
            op(act, lambda e: e.activation(out=u.sq[:], in_=banks[Ba][:, P:2 * P], func=AF.Square,
                                           scale=pvc('k_k', j)), reads=[bbuf[Ba], b_pv], writes=[u.b_sq])
            op(act, lambda e: e.activation(out=u.al[:], in_=banks[Ba][:, 3 * P:4 * P], func=AF.Sigmoid,
                                           bias=pvc('a0', d * 8 + j)), reads=[bbuf[Ba], b_pv], writes=[u.b_al])
            yield
            rT = u.rk[:, 0:P]
            kT = u.rk[:, P:2 * P]
            op(dve, lambda e: e.tensor_scalar(out=u.kk[:], in0=kT, scalar1=pvc('k_k', j), scalar2=None, op0=ALU.mult),
               reads=[u.b_rk, b_pv], writes=[u.b_kk])
            op(pe, lambda e: e.matmul(banks[Ba][:, 0:P], lhsT=g.bdones[:], rhs=u.sq[:], start=True, stop=True),
               reads=[b_cst, u.b_sq], writes=[bbuf[Ba]])
            op(act, lambda e: e.activation(out=u.rs[:], in_=banks[Ba][:, 0:P], func=AF.Ln),
               reads=[bbuf[Ba]], writes=[u.b_rs])
            op(act, lambda e: e.activation(out=u.rs[:], in_=u.rs[:], func=AF.Exp, scale=-0.5),
               reads=[u.b_rs], writes=[u.b_rs])
            op(pool, lambda e: e.tensor_tensor(out=u.kk[:], in0=u.kk[:], in1=u.rs[:], op=ALU.mult),
               reads=[u.b_kk, u.b_rs], writes=[u.b_kk])
            op(dve, lambda e: e.tensor_tensor_scan(out=u.Ein[:], data0=g.ones_f[:], data1=u.sg[:], initial=0.0,
                                                   op0=ALU.mult, op1=ALU.add),
               reads=[b_cst, u.b_sg], writes=[u.b_Ein])
            tot = u.Ein[:, P - 1:P]
            op(act, lambda e: e.activation(out=u.sc[:, 0:1], in_=tot, func=AF.Exp, scale=-LAM),
               reads=[u.b_Ein], writes=[u.b_sc])
            if d == 0:
                op(pool, lambda e: e.tensor_tensor(out=u.Eex[:], in0=u.Ein[:], in1=u.sg[:], op=ALU.subtract),
                   reads=[u.b_Ein, u.b_sg], writes=[u.b_Eex])
            else:
                op(dve, lambda e: e.tensor_copy(out=u.sc[:, 1:2], in_=tot), reads=[u.b_Ein], writes=[u.b_sc])
                op(dve, lambda e: e.tensor_scalar(out=u.Eex[:], in0=u.Ein[:], scalar1=u.sc[:, 1:2], scalar2=-1.0,
                                                  op0=ALU.subtract, op1=ALU.mult),
                   reads=[u.b_Ein, u.b_sc], writes=[u.b_Eex])
                op(pool, lambda e: e.tensor_tensor(out=u.Ein[:], in0=u.Eex[:], in1=u.sg[:], op=ALU.add),
                   reads=[u.b_Eex, u.b_sg], writes=[u.b_Ein])
            yield
            op(act, lambda e: e.activation(out=u.ein[:], in_=u.Ein[:], func=AF.Exp, scale=-LAM),
               reads=[u.b_Ein], writes=[u.b_ein])
            op(act, lambda e: e.activation(out=u.Eex[:], in_=u.Eex[:], func=AF.Exp, scale=-LAM),
               reads=[u.b_Eex], writes=[u.b_Eex])
            op(act, lambda e: e.activation(out=u.Ein[:], in_=u.Ein[:], func=AF.Exp, scale=LAM),
               reads=[u.b_Ein], writes=[u.b_Ein])
            eng_ = u.Ein
            eex_ = u.Eex
            op(dve, lambda e: e.scalar_tensor_tensor(out=u.arT[:, 0:P], in0=u.kk[:], scalar=-1.0, in1=eex_[:],
                                                     op0=ALU.mult, op1=ALU.mult),
               reads=[u.b_kk, u.b_Eex], writes=[u.b_arT])
            op(pool, lambda e: e.tensor_tensor(out=u.arT[:, P:2 * P], in0=rT, in1=u.ein[:], op=ALU.mult),
               reads=[u.b_rk, u.b_ein], writes=[u.b_arT])
            op(dve, lambda e: e.tensor_scalar(out=u.kd[:], in0=u.al[:], scalar1=pvc('k_a', j),
                                               scalar2=pvc('omk_a', j), op0=ALU.mult, op1=ALU.add),
               reads=[u.b_al, b_pv], writes=[u.b_kd])
            op(pool, lambda e: e.tensor_tensor(out=u.kd[:], in0=u.kd[:], in1=kT, op=ALU.mult),
               reads=[u.b_kd, u.b_rk], writes=[u.b_kd])
            op(dve, lambda e: e.tensor_tensor(out=u.ktT[:], in0=u.kd[:], in1=eng_[:], op=ALU.mult),
               reads=[u.b_kd, u.b_Ein], writes=[u.b_ktT])
            op(pool, lambda e: e.tensor_tensor(out=u.al[:], in0=u.al[:], in1=u.kk[:], op=ALU.mult),
               reads=[u.b_al, u.b_kk], writes=[u.b_al])
            op(dve, lambda e: e.tensor_tensor(out=u.btT[:], in0=u.al[:], in1=eng_[:], op=ALU.mult),
               reads=[u.b_al, u.b_Ein], writes=[u.b_btT])
            op(dve, lambda e: e.scalar_tensor_tensor(out=u.pr[:], in0=rT, scalar=pvc('r_k', j), in1=u.kd[:],
                                                     op0=ALU.mult, op1=ALU.mult),
               reads=[u.b_rk, u.b_kd, b_pv], writes=[u.b_pr])
            yield
            for _ in range(UY):
                yield
            op(pe, lambda e: e.transpose(out=banks_bf[Ba][:, 0:P], in_=u.btT[:], identity=g.identb[:]),
               reads=[u.b_btT, g.b_identb], writes=[bbuf[Ba]], sig=False)
            op(pe, lambda e: e.transpose(out=banks_bf[Ba][:, P:2 * P], in_=u.ktT[:], identity=g.identb[:]),
               reads=[u.b_ktT, g.b_identb], writes=[bbuf[Ba]], sig=False)
            op(pe, lambda e: e.matmul(banks[Ba][:, 2 * P:2 * P + 2], lhsT=u.pr[:], rhs=g.hsel[:], start=True, stop=True),
               reads=[u.b_pr, b_cst], writes=[bbuf[Ba]])
            op(act, lambda e: e.activation(out=u.BK[:], in_=banks_bf[Ba][:, 0:2 * P], func=AF.Copy),
               reads=[bbuf[Ba]], writes=[u.b_BK])
            if d == 0:
                bdst, b_bdst = bonF[:, c, 2 * j:2 * j + 2], b_bonF[c]
            else:
                bt_, b_bdst = cx.bonB
                bdst = bt_[:, 2 * j:2 * j + 2]
            op(act, lambda e: e.activation(out=bdst, in_=banks[Ba][:, 2 * P:2 * P + 2], func=AF.Copy),
               reads=[bbuf[Ba]], writes=[b_bdst])
            yield
            for _ in range(UY):
                yield
            AT = [u.AT0, u.AT1]
            b_AT = [u.b_AT0, u.b_AT1]
            Cc = [[u.C00, u.C01], [u.C10, u.C11]]
            b_C = [[u.b_C00, u.b_C01], [u.b_C10, u.b_C11]]
            hb = [Bb, Bc]
            for h in range(2):
                B_ = hb[h]
                op(pe, lambda e, h=h, B_=B_: e.matmul(banks[B_][:, 0:2 * P], lhsT=u.btT[hq[h], :], rhs=u.arT[hq[h], :],
                                                      start=True, stop=True),
                   reads=[u.b_btT, u.b_arT], writes=[bbuf[B_]], sig=False)
                op(pe, lambda e, h=h, B_=B_: e.matmul(banks[B_][:, 2 * P:4 * P], lhsT=u.ktT[hq[h], :],
                                                      rhs=u.arT[hq[h], :], start=True, stop=True),
                   reads=[u.b_ktT, u.b_arT], writes=[bbuf[B_]])
                op(pe, lambda e, h=h: e.matmul(banks[Ba][:, h * P:(h + 1) * P], lhsT=u.arT[hq[h], 0:P],
                                               rhs=u.btT[hq[h], :], start=True, stop=True),
                   reads=[u.b_btT, u.b_arT], writes=[bbuf[Ba]])
            for h in range(2):
                B_ = hb[h]
                op(dve, lambda e, h=h, B_=B_: e.tensor_tensor(out=AT[h][:], in0=banks[B_][:, :], in1=mq[:],
                                                              op=ALU.mult),
                   reads=[bbuf[B_], b_cst], writes=[b_AT[h]])
                op(pool, lambda e, h=h: e.tensor_tensor(out=Cc[h][1][:, 2 * P:3 * P], in0=AT[h][:, 0:P],
                                                        in1=g.identb[:], op=ALU.add),
                   reads=[b_AT[h], g.b_identb], writes=[b_C[h][1]])
            for h in range(2):
                op(dve, lambda e, h=h: e.tensor_tensor(out=Cc[h][0][:, 0:P], in0=banks[Ba][:, h * P:(h + 1) * P],
                                                       in1=mA, op=ALU.mult),
                   reads=[bbuf[Ba], b_cst], writes=[b_C[h][0]])
            yield
            for h in range(2):
                B_ = hb[h]
                A0 = Cc[h][0][:, 0:P]
                B0 = AT[h][:, 0:P]
                op(pe, lambda e, B_=B_, A0=A0, B0=B0: e.matmul(banks[B_][:, 0:P], lhsT=B0, rhs=A0, start=True, stop=True),
                   reads=[b_AT[h], b_C[h][0]], writes=[bbuf[B_]], sig=False)
                op(pe, lambda e, B_=B_, A0=A0, B0=B0: e.matmul(banks[B_][:, P:2 * P], lhsT=A0, rhs=B0, start=True,
                                                               stop=True),
                   reads=[b_AT[h], b_C[h][0]], writes=[bbuf[B_]])
                ev = dve if h == 0 else act
                if ev is dve:
                    op(dve, lambda e, h=h, B_=B_: e.tensor_copy(out=Cc[h][1][:, 0:2 * P], in_=banks[B_][:, 0:2 * P]),
                       reads=[bbuf[B_]], writes=[b_C[h][1]])
                else:
                    op(act, lambda e, h=h, B_=B_: e.activation(out=Cc[h][1][:, 0:2 * P], in_=banks[B_][:, 0:2 * P],
                                                               func=AF.Copy),
                       reads=[bbuf[B_]], writes=[b_C[h][1]])
            yield
            for lev in range(1, 7):
                src_i = lev % 2
                dst_i = 1 - src_i
                for h in range(2):
                    B_ = hb[h]
                    S = Cc[h][src_i]
                    Dd = Cc[h][dst_i]
                    bS, bD = b_C[h][src_i], b_C[h][dst_i]
                    Ak, Bk, Mk = S[:, 0:P], S[:, P:2 * P], S[:, 2 * P:3 * P]
                    lo = 0
                    if lev <= 5:
                        op(pe, lambda e, B_=B_, Ak=Ak, Bk=Bk: e.matmul(banks[B_][:, 0:P], lhsT=Bk, rhs=Ak, start=True,
                                                                       stop=True),
                           reads=[bS], writes=[bbuf[B_]], sig=False)
                    else:
                        lo = 2 * P
                    r0 = P if lev <= 4 else 2 * P
                    op(pe, lambda e, B_=B_, Ak=Ak, S=S, r0=r0: e.matmul(banks[B_][:, r0:3 * P], lhsT=Ak, rhs=S[:, r0:3 * P],
                                                                        start=True, stop=True),
                       reads=[bS], writes=[bbuf[B_]], sig=(h == 0))
                    if h == 1:
                        op(pe, lambda e, B_=B_, Mk=Mk: e.matmul(banks[B_][:, 2 * P:3 * P], lhsT=g.identb[:], rhs=Mk,
                                                                start=False, stop=True),
                           reads=[bS, g.b_identb], writes=[bbuf[B_]])
                        op(act, lambda e, B_=B_, Dd=Dd, lo=lo: e.activation(out=Dd[:, lo:3 * P],
                                                                            in_=banks[B_][:, lo:3 * P], func=AF.Copy),
                           reads=[bbuf[B_]], writes=[bD])
                    else:
                        if lev <= 5:
                            hi_ = 2 * P if lev <= 4 else P
                            op(dve, lambda e, B_=B_, Dd=Dd, hi_=hi_: e.tensor_copy(out=Dd[:, 0:hi_],
                                                                                   in_=banks[B_][:, 0:hi_]),
                               reads=[bbuf[B_]], writes=[bD])
                        op(dve, lambda e, B_=B_, Dd=Dd, Mk=Mk: e.tensor_tensor(out=Dd[:, 2 * P:3 * P],
                                                                               in0=banks[B_][:, 2 * P:3 * P], in1=Mk,
                                                                               op=ALU.add),
                           reads=[bbuf[B_], bS], writes=[bD])
                yield
            Mfin = [Cc[h][1][:, 2 * P:3 * P] for h in range(2)]
            b_Mfin = [b_C[h][1] for h in range(2)]
            b_Hj = g_bH[j]
            for h in range(2):
                op(pe, lambda e, h=h: e.matmul(banks[Ba][:, h * 64:(h + 1) * 64], lhsT=u.arT[hq[h], 0:P],
                                               rhs=Hb[hq[h], j, :], start=True, stop=False),
                   reads=[u.b_arT, b_Hj], writes=[bbuf[Ba]], sig=False)
                op(pe, lambda e, h=h: e.matmul(banks[Ba][:, h * 64:(h + 1) * 64], lhsT=AT[h][:, 2 * P:3 * P],
                                               rhs=vt[:, (2 * j + h) * 64:(2 * j + h + 1) * 64], start=False, stop=True),
                   reads=[b_AT[h], b_vt], writes=[bbuf[Ba]], sig=(h == 1))
            op(act, lambda e: e.activation(out=u.Xp[:], in_=banks[Ba][:, 0:P], func=AF.Copy),
               reads=[bbuf[Ba]], writes=[u.b_Xp])
            for h in range(2):
                op(pe, lambda e, h=h: e.matmul(banks[Ba][:, P + h * 64:P + (h + 1) * 64], lhsT=Mfin[h],
                                               rhs=u.Xp[:, h * 64:(h + 1) * 64], start=True, stop=True),
                   reads=[b_Mfin[h], u.b_Xp], writes=[bbuf[Ba]], sig=(h == 1))
            op(dve, lambda e: e.tensor_copy(out=u.Up[:], in_=banks[Ba][:, P:2 * P]), reads=[bbuf[Ba]], writes=[u.b_Up])
            op(dve, lambda e: e.tensor_scalar(out=u.Hs[:], in0=Hf[:, j, :], scalar1=u.sc[:, 0:1], scalar2=None,
                                               op0=ALU.mult), reads=[b_Hj, u.b_sc], writes=[u.b_Hs])
            yield
            for h in range(2):
                yo = 2 * P + h * 64
                op(pe, lambda e, h=h, yo=yo: e.matmul(banks[Ba][:, yo:yo + 64], lhsT=u.arT[hq[h], P:2 * P],
                                                      rhs=Hb[hq[h], j, :], start=True, stop=False),
                   reads=[u.b_arT, b_Hj], writes=[bbuf[Ba]], sig=False)
                op(pe, lambda e, h=h, yo=yo: e.matmul(banks[Ba][:, yo:yo + 64], lhsT=AT[h][:, P:2 * P],
                                                      rhs=u.Up[:, h * 64:(h + 1) * 64], start=False, stop=False),
                   reads=[b_AT[h], u.b_Up], writes=[bbuf[Ba]], sig=False)
                op(pe, lambda e, h=h, yo=yo: e.matmul(banks[Ba][:, yo:yo + 64], lhsT=AT[h][:, 3 * P:4 * P],
                                                      rhs=vt[:, (2 * j + h) * 64:(2 * j + h + 1) * 64],
                                                      start=False, stop=True),
                   reads=[b_AT[h], b_vt], writes=[bbuf[Ba]], sig=False)
            op(pe, lambda e: e.matmul(banks[Ba][:, 3 * P:4 * P], lhsT=u.BK[:, 0:P], rhs=u.Up[:], start=True, stop=False),
               reads=[u.b_BK, u.b_Up], writes=[bbuf[Ba]], sig=False)
            op(pe, lambda e: e.matmul(banks[Ba][:, 3 * P:4 * P], lhsT=u.BK[:, P:2 * P], rhs=vt[:, j * P:(j + 1) * P],
                                      start=False, stop=True),
               reads=[u.b_BK, b_vt], writes=[bbuf[Ba]])
            if d == 0:
                ydst = x1[:, c, :].bitcast(BF16)[:, j * P:(j + 1) * P]
                b_yd = bx1[c]
            else:
                yb_, b_yd = cx.yb
                ydst = yb_[:, j * P:(j + 1) * P]
            op(act, lambda e: e.activation(out=ydst, in_=banks[Ba][:, 2 * P:3 * P], func=AF.Copy),
               reads=[bbuf[Ba]], writes=[b_yd])
            for h in range(2):
                op(dve, lambda e, h=h: e.scalar_tensor_tensor(out=Hf[hq[h], j, :],
                                                              in0=banks[Ba][hq[h], 3 * P + h * 64:3 * P + (h + 1) * 64],
                                                              scalar=u.sc[hq[h], 0:1], in1=u.Hs[hq[h], :],
                                                              op0=ALU.mult, op1=ALU.add),
                   reads=[bbuf[Ba], u.b_sc, u.b_Hs], writes=[b_Hj])
            op(pool, lambda e: e.tensor_copy(out=Hb[:, j, :], in_=Hf[:, j, :]), reads=[b_Hj], writes=[b_Hj])
            yield

        g_bH = b_H

        def gen_Ba(cx):
            c = cx.c
            yb_, b_yb = cx.yb
            zs, b_zs = cx.zs
            vt, b_vt = cx.vt
            bonB, b_bonB = cx.bonB
            yf = x1[:, c, :].bitcast(BF16)[:, 0:D]
            op(dve, lambda e: e.tensor_tensor(out=yb_[:], in0=yb_[:], in1=yf, op=ALU.add),
               reads=[b_yb, bx1[c]], writes=[b_yb])
            y3 = yb_[:].rearrange("p (h n) -> p h n", h=16)
            op(dve, lambda e: e.tensor_reduce(out=gst[:, 0, :], in_=y3, axis=AX.X, op=ALU.add),
               reads=[b_yb], writes=[b_gst])
            op(act, lambda e: e.activation(out=tmpA[:], in_=yb_[:], func=AF.Square), reads=[b_yb], writes=[b_tmpA])
            op(dve, lambda e: e.tensor_reduce(out=gst[:, 1, :], in_=tmpA[:].rearrange("p (h n) -> p h n", h=16),
                                              axis=AX.X, op=ALU.add), reads=[b_tmpA], writes=[b_gst])
            yield
            op(dve, lambda e: e.tensor_scalar(out=gst[:, 2, :], in0=gst[:, 0, :], scalar1=1.0 / 64, scalar2=None,
                                              op0=ALU.mult), reads=[b_gst], writes=[b_gst])
            op(dve, lambda e: e.tensor_tensor(out=gst[:, 3, :], in0=gst[:, 2, :], in1=gst[:, 2, :], op=ALU.mult),
               reads=[b_gst], writes=[b_gst])
            op(dve, lambda e: e.scalar_tensor_tensor(out=gst[:, 3, :], in0=gst[:, 1, :], scalar=1.0 / 64,
                                                     in1=gst[:, 3, :], op0=ALU.mult, op1=ALU.subtract),
               reads=[b_gst], writes=[b_gst])
            op(dve, lambda e: e.tensor_scalar(out=gst[:, 3, :], in0=gst[:, 3, :], scalar1=LNX_EPS, scalar2=None,
                                              op0=ALU.add), reads=[b_gst], writes=[b_gst])
            op(act, lambda e: e.activation(out=gst[:, 3, :], in_=gst[:, 3, :], func=AF.Sqrt), reads=[b_gst],
               writes=[b_gst])
            op(dve, lambda e: e.reciprocal(out=gst[:, 4, :], in_=gst[:, 3, :]), reads=[b_gst], writes=[b_gst])
            op(dve, lambda e: e.tensor_tensor(out=gst[:, 5, :], in0=bonF[:, c, :], in1=bonB[:], op=ALU.add),
               reads=[b_bonF[c], b_bonB], writes=[b_gst])
            mean_b = gst[:, 2, :].unsqueeze(2).broadcast_to([P, 16, 64])
            rstd_b = gst[:, 4, :].unsqueeze(2).broadcast_to([P, 16, 64])
            bon_b = gst[:, 5, :].unsqueeze(2).broadcast_to([P, 16, 64])
            op(dve, lambda e: e.tensor_tensor(out=y3, in0=y3, in1=mean_b, op=ALU.subtract),
               reads=[b_yb, b_gst], writes=[b_yb])
            op(pool, lambda e: e.tensor_tensor(out=y3, in0=y3, in1=rstd_b, op=ALU.mult),
               reads=[b_yb, b_gst], writes=[b_yb])
            yield
            op(dve, lambda e: e.tensor_tensor(out=yb_[:], in0=yb_[:], in1=lnxb[:, 0, :], op=ALU.mult),
               reads=[b_yb, b_lc], writes=[b_yb])
            op(pool, lambda e: e.tensor_tensor(out=yb_[:], in0=yb_[:], in1=lnxb[:, 1, :], op=ALU.add),
               reads=[b_yb, b_lc], writes=[b_yb])
            t3 = tmpA[:].rearrange("p (h n) -> p h n", h=16)
            op(dve, lambda e: e.tensor_tensor(out=t3, in0=vt[:].rearrange("p (h n) -> p h n", h=16), in1=bon_b,
                                              op=ALU.mult), reads=[b_vt, b_gst], writes=[b_tmpA])
            op(pool, lambda e: e.tensor_tensor(out=yb_[:], in0=yb_[:], in1=tmpA[:], op=ALU.add),
               reads=[b_yb, b_tmpA], writes=[b_yb])
            op(dve, lambda e: e.tensor_tensor(out=yg[:], in0=yb_[:], in1=zs[:], op=ALU.mult),
               reads=[b_yb, b_zs], writes=[b_yg])
            yield

        def gen_Bb(cx):
            c = cx.c
            xin, b_xin = tmpA, b_tmpA
            T_.dma(sp, xin[:], g.x_in[si, c * P:(c + 1) * P, :], writes=[b_xin])
            for kc in range(KC):
                op(pe, lambda e, kc=kc: e.transpose(out=banks_bf[0][:, kc * P:(kc + 1) * P],
                                                    in_=yg[:, kc * P:(kc + 1) * P], identity=g.identb[:]),
                   reads=[b_yg, g.b_identb], writes=[bbuf[0]], sig=(kc == KC - 1))
            op(act, lambda e: e.activation(out=ygT[:].rearrange("p k n -> p (k n)"), in_=banks_bf[0][:, :],
                                           func=AF.Copy), reads=[bbuf[0]], writes=[b_ygT])
            yield
            for half in range(2):
                for q2 in range(NQ2):
                    w, b_w = g.loadA(L, 2, half * NQ2 + q2)
                    for kc in range(KC):
                        op(pe, lambda e, kc=kc, w=w, q2=q2: e.matmul(banks[0][:, q2 * W2:(q2 + 1) * W2],
                                                                    lhsT=ygT[:, kc, :], rhs=w[:, kc, :],
                                                                    start=(kc == 0), stop=(kc == KC - 1)),
                           reads=[b_ygT, b_w], writes=[bbuf[0]], sig=(kc == KC - 1 and q2 == NQ2 - 1))
                op(act, lambda e, half=half: e.activation(out=tmpB[:, half * 512:(half + 1) * 512], in_=banks[0][:, :],
                                                          func=AF.Copy), reads=[bbuf[0]], writes=[b_tmpB])
                yield
            op(act, lambda e: e.activation(out=yg[:], in_=tmpB[:], func=AF.Square, accum_out=stat[:, 4:5]),
               reads=[b_tmpB], writes=[b_yg, b_stat])
            op(dve, lambda e: e.tensor_scalar(out=stat[:, 5:6], in0=stat[:, 4:5], scalar1=1.0 / D, scalar2=RMS_EPS,
                                              op0=ALU.mult, op1=ALU.add), reads=[b_stat], writes=[b_stat])
            op(act, lambda e: e.activation(out=stat[:, 6:7], in_=stat[:, 5:6], func=AF.Sqrt), reads=[b_stat],
               writes=[b_stat])
            op(dve, lambda e: e.reciprocal(out=stat[:, 7:8], in_=stat[:, 6:7]), reads=[b_stat], writes=[b_stat])
            op(dve, lambda e: e.scalar_tensor_tensor(out=tmpB[:], in0=tmpB[:], scalar=stat[:, 7:8], in1=postg[:],
                                                     op0=ALU.mult, op1=ALU.mult),
               reads=[b_tmpB, b_stat, b_lc], writes=[b_tmpB])
            op(pool, lambda e: e.tensor_tensor(out=x1[:, c, :], in0=tmpB[:], in1=xin[:], op=ALU.add),
               reads=[b_tmpB, b_xin], writes=[bx1[c]])
            yield

        def gen_Z():
            op(dve, lambda e: e.memset(Hf[:], 0.0), writes=b_H)
            op(pool, lambda e: e.memset(Hb[:], 0.0), writes=b_H)
            yield

        tasks = []
        lastB = None
        lastF = None
        for d in range(2):
            order = list(range(NCH)) if d == 0 else list(range(NCH - 1, -1, -1))
            tz = Task("Z%d" % d)
            tz.gen = gen_Z()
            tz.deps += [t for t in tasks if t.name.startswith("U")]
            tasks.append(tz)
            cxs = [None] * NCH

            def make_Na(i):
                cx = NS()
                cx.c = order[i]
                tn = Task("N%d_%d" % (d, cx.c))
                _, cx.xin = p_xin.acquire(tn)
                if i > 0:
                    tn.deps.append(cxs[i - 1].tnb)
                tn.gen = gen_Na(cx)
                cx.tna = tn
                cxs[i] = cx
                tasks.append(tn)

            def make_Nb(i):
                cx = cxs[i]
                tn = Task("N%d_%db" % (d, cx.c))
                cx.ks, cx.slot = p_slot.acquire(tn)
                tn.deps.append(cx.tna)
                prev = cxs[i - 1] if i > 0 else None
                if prev is not None:
                    p_slot.share(tn, prev.ks)
                tn.gen = gen_Nb(cx, d, i == 0, i == NCH - 1, prev)
                cx.tnb = tn
                cx.tn = tn
                tasks.append(tn)

            make_Na(0)
            make_Nb(0)
            if NCH > 1:
                make_Na(1)
                make_Nb(1)
            pendBb = None
            for i in range(NCH):
                cx = cxs[i]
                c = cx.c
                tf = Task("F%d_%d" % (d, c))
                tf.deps.append(cx.tn)
                if i + 1 < NCH:
                    tf.deps.append(cxs[i + 1].tn)
                if lastF is not None:
                    tf.deps.append(lastF)
                lastF = tf
                p_slot.share(tf, cx.ks)
                klr, cx.lr = p_lr.acquire(tf)
                klk, cx.lk = p_lk.acquire(tf)
                kvt, cx.vt = p_vt.acquire(tf)
                kth, cx.th = p_th.acquire(tf)
                if d == 1:
                    kzs, cx.zs = p_zs.acquire(tf)
                tf.gen = gen_F(cx, d)
                tasks.append(tf)
                tba = None
                if d == 1:
                    tba = Task("B%d" % c)
                    _, cx.yb = p_yb.acquire(tba)
                    _, cx.bonB = p_bonB.acquire(tba)
                tus = []
                for j in range(KC):
                    tu = Task("U%d_%d_%d" % (d, c, j))
                    tu.deps += [tf, tz]
                    if d == 1:
                        tu.deps += list(tba.deps)
                    _, uset = p_uset.acquire(tu)
                    p_lr.share(tu, klr)
                    p_lk.share(tu, klk)
                    p_vt.share(tu, kvt)
                    p_th.share(tu, kth)
                    tu.gen = gen_U(cx, d, j, uset)
                    if i > 0:
                        tu.deps.append(cxs[i - 1].tus[j])
                    tus.append(tu)
                    tasks.append(tu)
                    if j == 1 and i + 2 < NCH:
                        make_Na(i + 2)
                    if j == 3 and pendBb is not None:
                        tasks.append(pendBb)
                        pendBb = None
                cx.tus = tus
                if d == 1:
                    tba.deps += tus + [tf]
                    if lastB is not None:
                        tba.deps.append(lastB)
                    p_vt.share(tba, kvt)
                    p_zs.share(tba, kzs)
                    tba.gen = gen_Ba(cx)
                    tasks.append(tba)
                    tbb = Task("B%db" % c)
                    tbb.deps.append(tba)
                    tbb.gen = gen_Bb(cx)
                    lastB = tbb
                    if i == NCH - 1:
                        tasks.append(tbb)
                    else:
                        pendBb = tbb
                if i + 2 < NCH:
                    make_Nb(i + 2)
        import os
        allow = os.environ.get("L0_TASKS", "ZNFUB")
        tasks = [t for t in tasks if t.name[0] in allow]
        for t in tasks:
            t.deps = [d_ for d_ in t.deps if d_.name[0] in allow]
        run_tasks(tasks, window=WINDOW)


GRID_W = 64
N_ROWS = T_SEQ // GRID_W
NEG = -30000.0


def _rs(r):
    return min(max(r - 4, 0), N_ROWS - 8)


def l1_geometry():
    geo = []
    pats = []
    for i in range(N_ROWS // 2):
        rows = [2 * i, 2 * i + 1]
        lo = min(_rs(r) for r in rows)
        hi = max(_rs(r) + 7 for r in rows)
        ent = []
        for kt in range(lo // 2, hi // 2 + 1):
            pat = tuple(tuple(1 if _rs(2 * i + rl) <= 2 * kt + krl < _rs(2 * i + rl) + 8 else 0 for krl in range(2))
                        for rl in range(2))
            if pat not in pats:
                pats.append(pat)
            ent.append((kt, (2 * (kt - i) + 6) // 2, pats.index(pat)))
        geo.append(ent)
    return geo, pats


def l1_prologue(g):
    nc, T_, op, sb = g.nc, g.T_, g.op, g.sb
    pe, act, dve, pool, sp = g.pe, g.act, g.dve, g.pool, g.sp
    geo, pats = l1_geometry()
    g.l1_geo, g.l1_pats = geo, pats
    NMK = len(pats)
    g.padD = nc.dram_tensor("padD", [240, P], F32, kind="Internal").ap()
    g.rbD = nc.dram_tensor("rbD", [P, 16 * 7 * P], BF16, kind="Internal").ap()
    g.mkD = nc.dram_tensor("mkD", [P, (NMK + 1) * P], BF16, kind="Internal").ap()
    x1 = g.x1
    with ExitStack() as p1:
        b_t = Buf("l1pro")
        padt = sb(p1, "padt", [120, 2, P], F32)
        op(dve, lambda e: e.memset(padt[:], 0.0), writes=[b_t])
        rp = g.wd['na_rpb'][0].rearrange("h r m -> (h r) m")
        for gi in range(2):
            T_.dma(sp, padt[:, gi, 48:79], rp[gi * 120:(gi + 1) * 120, :], writes=[b_t])
        for gi in range(2):
            T_.dma(sp, g.padD[gi * 120:(gi + 1) * 120, :], padt[:, gi, :], reads=[b_t])
        T_.barrier()
        T_.finish()
        Hs = x1[:].rearrange("p c d -> p (c d)")[:, 0:240 * 64].rearrange("p (b k) -> p b k", k=64)
        b_hs = Buf("Hs")
        for h in range(16):
            for r2 in range(2):
                src = bass.AP(tensor=g.padD.tensor, offset=h * 15 * P, ap=[[1, 64], [P, 15], [1, 64]])
                T_.dma(sp, Hs[64 * r2:64 * r2 + 64, h * 15:(h + 1) * 15, :], src, writes=[b_hs])
        RBs = sb(p1, "RBs", [P, 16, 7, P], BF16)
        b_rb = Buf("RBs")
        op(pool, lambda e: e.memset(RBs[:].rearrange("p h d k -> p (h d k)"), 0.0), writes=[b_rb])
        engs = [dve, pool, act]
        n = 0
        for h in range(16):
            for rl in range(2):
                for krl in range(2):
                    dis = [di for di in range(7) if 0 <= 2 * di + 1 + krl - rl <= 14]
                    d0, nd = dis[0], len(dis)
                    ri0 = 2 * d0 + 1 + krl - rl
                    srcv = Hs[64 * rl:64 * rl + 64, h * 15 + ri0:h * 15 + ri0 + 2 * (nd - 1) + 1:2, :]
                    dstv = RBs[64 * rl:64 * rl + 64, h, d0:d0 + nd, 64 * krl:64 * krl + 64]
                    e_ = engs[n % 3]
                    n += 1
                    if e_ is act:
                        op(act, lambda e, s=srcv, d=dstv: e.activation(out=d, in_=s, func=AF.Copy),
                           reads=[b_hs], writes=[b_rb])
                    else:
                        op(e_, lambda e, s=srcv, d=dstv: e.tensor_copy(out=d, in_=s), reads=[b_hs], writes=[b_rb])
        T_.dma(sp, g.rbD[:, :], RBs[:].rearrange("p h d k -> p (h d k)"), reads=[b_rb])
        ior = sb(p1, "ior1", [P, P], F32)
        ioc = sb(p1, "ioc1", [P, P], F32)
        t1 = sb(p1, "mt1", [P, P], F32)
        t2 = sb(p1, "mt2", [P, P], F32)
        cm = sb(p1, "cm", [P, P], BF16)
        neg = sb(p1, "negt", [P, P], BF16)
        MKs = sb(p1, "MKs", [P, NMK + 1, P], BF16)
        b_m = Buf("mk")
        op(pool, lambda e: e.iota(ior[:], pattern=[[0, P]], base=0, channel_multiplier=1,
                                  allow_small_or_imprecise_dtypes=True), writes=[b_m])
        op(pool, lambda e: e.iota(ioc[:], pattern=[[1, P]], base=0, channel_multiplier=0,
                                  allow_small_or_imprecise_dtypes=True), writes=[b_m])
        op(dve, lambda e: e.tensor_tensor(out=t1[:], in0=ior[:], in1=ioc[:], op=ALU.add), reads=[b_m], writes=[b_m])
        op(dve, lambda e: e.tensor_scalar(out=t2[:], in0=t1[:], scalar1=63.0, scalar2=None, op0=ALU.is_equal),
           reads=[b_m], writes=[b_m])
        op(dve, lambda e: e.tensor_scalar(out=t1[:], in0=t1[:], scalar1=191.0, scalar2=None, op0=ALU.is_equal),
           reads=[b_m], writes=[b_m])
        op(dve, lambda e: e.tensor_tensor(out=g.Jrev[:], in0=t1[:], in1=t2[:], op=ALU.add), reads=[b_m],
           writes=[g.b_cst])
        op(dve, lambda e: e.tensor_scalar(out=t1[:], in0=ior[:], scalar1=63.5, scalar2=-64.0, op0=ALU.is_gt,
                                          op1=ALU.mult), reads=[b_m], writes=[b_m])
        op(dve, lambda e: e.tensor_tensor(out=t1[:], in0=t1[:], in1=ior[:], op=ALU.add), reads=[b_m], writes=[b_m])
        op(dve, lambda e: e.tensor_scalar(out=t1[:], in0=t1[:], scalar1=-1.0, scalar2=55.0, op0=ALU.mult,
                                          op1=ALU.add), reads=[b_m], writes=[b_m])
        op(dve, lambda e: e.tensor_scalar(out=t1[:], in0=t1[:], scalar1=0.0, scalar2=48.0, op0=ALU.max, op1=ALU.min),
           reads=[b_m], writes=[b_m])
        op(dve, lambda e: e.tensor_scalar(out=t2[:], in0=ioc[:], scalar1=63.5, scalar2=-64.0, op0=ALU.is_gt,
                                          op1=ALU.mult), reads=[b_m], writes=[b_m])
        op(dve, lambda e: e.tensor_tensor(out=t2[:], in0=t2[:], in1=ioc[:], op=ALU.add), reads=[b_m], writes=[b_m])
        op(dve, lambda e: e.tensor_tensor(out=t2[:], in0=t2[:], in1=t1[:], op=ALU.subtract), reads=[b_m],
           writes=[b_m])
        op(dve, lambda e: e.tensor_scalar(out=t1[:], in0=t2[:], scalar1=-0.5, scalar2=None, op0=ALU.is_gt),
           reads=[b_m], writes=[b_m])
        op(dve, lambda e: e.tensor_scalar(out=t2[:], in0=t2[:], scalar1=15.5, scalar2=None, op0=ALU.is_lt),
           reads=[b_m], writes=[b_m])
        op(dve, lambda e: e.tensor_tensor(out=t1[:], in0=t1[:], in1=t2[:], op=ALU.mult), reads=[b_m], writes=[b_m])
        op(dve, lambda e: e.tensor_scalar(out=cm[:], in0=t1[:], scalar1=-1.0, scalar2=-NEG, op0=ALU.add, op1=ALU.mult),
           reads=[b_m], writes=[b_m])
        op(dve, lambda e: e.memset(neg[:], NEG), writes=[b_m])
        for pi, pat in enumerate(pats):
            for rl in range(2):
                for krl in range(2):
                    src_t = cm if pat[rl][krl] else neg
                    op(pool, lambda e, pi=pi, rl=rl, krl=krl, s=src_t: e.tensor_copy(
                        out=MKs[64 * rl:64 * rl + 64, pi, 64 * krl:64 * krl + 64],
                        in_=s[64 * rl:64 * rl + 64, 64 * krl:64 * krl + 64]), reads=[b_m], writes=[b_m])
        op(pool, lambda e: e.tensor_copy(out=MKs[:, NMK, :], in_=neg[:]), reads=[b_m], writes=[b_m])
        T_.dma(sp, g.mkD[:, :], MKs[:].rearrange("p n k -> p (n k)"), reads=[b_m])
        T_.barrier()
        T_.finish()


def layer1(g, si):
    nc, T_, op = g.nc, g.T_, g.op
    pe, act, dve, pool, sp = g.pe, g.act, g.dve, g.pool, g.sp
    banks, bbuf, banks_bf = g.banks, g.bbuf, g.banks_bf
    NCH, x1, bx1, pv, b_pv, pvc = g.NCH, g.x1, g.bx1, g.pv, g.b_pv, g.pvc
    wd, sb = g.wd, g.sb
    b_cst = g.b_cst
    L = 1
    geo, pats = g.l1_geo, g.l1_pats
    NMK = len(pats)
    hq = [slice(0, 64), slice(64, 128)]

    with ExitStack() as l1:
        postg = sb(l1, "postg1", [P, D], F32)
        RB = sb(l1, "RB", [P, 16, 7, P], BF16)
        MK = sb(l1, "MK", [P, NMK + 1, P], BF16)
        b_lc = Buf("l1consts")
        T_.dma(sp, postg[:], wd['post_norm_g'][L].partition_broadcast(P), writes=[b_lc])
        T_.dma(sp, RB[:].rearrange("p h d k -> p (h d k)"), g.rbD[:, :], writes=[b_lc])
        T_.dma(sp, MK[:].rearrange("p n k -> p (n k)"), g.mkD[:, :], writes=[b_lc])

        def mk(nm, sh, dt, n):
            return [(sb(l1, "%s%d" % (nm, i), sh, dt), Buf("%s%d" % (nm, i))) for i in range(n)]

        xs, b_xs = mk("xs1", [P, D], BF16, 1)[0]
        stat = sb(l1, "stat1", [P, 8], F32)
        b_stat = Buf("stat1")
        p_xn = RPool(mk("xnT1", [P, KC, P], BF16, 4))
        p_kT = RPool(mk("kT", [P, KC, P], BF16, 7))
        p_va = RPool(mk("Vaug", [P, 16, 65], BF16, 7))
        zs, b_zs = mk("zs1", [P, D], BF16, 1)[0]
        og, b_og = mk("og", [P, D], F32, 1)[0]
        yg, b_yg = mk("yg1", [P, D], BF16, 1)[0]
        ygT, b_ygT = mk("ygT1", [P, KC, P], BF16, 1)[0]
        tmpA, b_tmpA = mk("tmpA1", [P, D], F32, 1)[0]
        tmpB, b_tmpB = mk("tmpB1", [P, D], F32, 1)[0]

        def mkset(i):
            u = NS()
            def t(nm, sh, dt):
                setattr(u, nm, sb(l1, "a%d_%s" % (i, nm), sh, dt))
                setattr(u, "b_" + nm, Buf("a%d_%s" % (i, nm)))
            t("qT", [P, P], BF16)
            t("PT", [P, 5, P], BF16)
            t("rc", [P, 2], F32)
            u.banks = (2 + 3 * i, 3 + 3 * i, 4 + 3 * i)
            return u

        p_uset = RPool([mkset(0), mkset(1)])
        for (va, b_va) in p_va.items:
            op(pool, lambda e, va=va: e.memset(va[:, :, 64:65], 1.0), writes=[b_va])

        def gen_KV(cx):
            t = cx.t
            xn, b_xn = cx.xn
            kT, b_kT = cx.kT
            va, b_va = cx.va
            xsrc = x1[:, t, :]
            op(act, lambda e: e.activation(out=xs[:], in_=xsrc, func=AF.Square, accum_out=stat[:, 0:1]),
               reads=[bx1[t]], writes=[b_xs, b_stat])
            op(dve, lambda e: e.tensor_scalar(out=stat[:, 1:2], in0=stat[:, 0:1], scalar1=1.0 / D, scalar2=RMS_EPS,
                                              op0=ALU.mult, op1=ALU.add), reads=[b_stat], writes=[b_stat])
            op(act, lambda e: e.activation(out=stat[:, 2:3], in_=stat[:, 1:2], func=AF.Sqrt), reads=[b_stat],
               writes=[b_stat])
            op(dve, lambda e: e.reciprocal(out=stat[:, 3:4], in_=stat[:, 2:3]), reads=[b_stat], writes=[b_stat])
            op(act, lambda e: e.activation(out=xs[:], in_=xsrc, func=AF.Copy, scale=stat[:, 3:4]),
               reads=[bx1[t], b_stat], writes=[b_xs])
            yield
            for kc in range(KC):
                op(pe, lambda e, kc=kc: e.transpose(out=banks_bf[0][:, kc * P:(kc + 1) * P],
                                                    in_=xs[:, kc * P:(kc + 1) * P], identity=g.identb[:]),
                   reads=[b_xs, g.b_identb], writes=[bbuf[0]], sig=(kc == KC - 1))
            for kc in range(KC):
                sc = pvc('pre_g', L * 8 + kc)
                if kc % 2 == 0:
                    op(dve, lambda e, kc=kc, sc=sc: e.tensor_scalar(out=xn[:, kc, :],
                                                                  in0=banks_bf[0][:, kc * P:(kc + 1) * P],
                                                                  scalar1=sc, scalar2=None, op0=ALU.mult),
                       reads=[bbuf[0], b_pv], writes=[b_xn])
                else:
                    op(act, lambda e, kc=kc, sc=sc: e.activation(out=xn[:, kc, :],
                                                               in_=banks_bf[0][:, kc * P:(kc + 1) * P],
                                                               func=AF.Copy, scale=sc),
                       reads=[bbuf[0], b_pv], writes=[b_xn])
            yield
            for j0 in range(0, KC, 4):
                for j in range(j0, j0 + 4):
                    w, b_w = g.loadB(L, 1, j)
                    for kc in range(KC):
                        op(pe, lambda e, kc=kc, j=j, w=w: e.matmul(banks[1][:, (j - j0) * P:(j - j0 + 1) * P],
                                                                  lhsT=w[:, kc, :], rhs=xn[:, kc, :],
                                                                  start=(kc == 0), stop=(kc == KC - 1)),
                           reads=[b_w, b_xn], writes=[bbuf[1]], sig=(kc == KC - 1 and j == j0 + 3))
                for j in range(j0, j0 + 4):
                    op(act, lambda e, j=j: e.activation(out=kT[:, j, :], in_=banks[1][:, (j - j0) * P:(j - j0 + 1) * P],
                                                        func=AF.Identity, bias=pvc('bk', j)),
                       reads=[bbuf[1], b_pv], writes=[b_kT])
                yield
            for half in range(2):
                for q2 in range(NQ2):
                    w, b_w = g.loadA(L, 0, half * NQ2 + q2)
                    cs = slice(q2 * W2, (q2 + 1) * W2)
                    for kc in range(KC):
                        op(pe, lambda e, kc=kc, w=w, cs=cs: e.matmul(banks[1][:, cs], lhsT=xn[:, kc, :], rhs=w[:, kc, :],
                                                                    start=(kc == 0), stop=False),
                           reads=[b_xn, b_w], writes=[bbuf[1]], sig=False)
                    bo = 2 * D + half * 512 + q2 * W2
                    op(pe, lambda e, bo=bo, cs=cs: e.matmul(banks[1][:, cs], lhsT=g.onesrow[:],
                                                           rhs=g.brow_hi[0:1, bo:bo + W2], start=False, stop=False),
                       reads=[b_cst], writes=[bbuf[1]], sig=False)
                    op(pe, lambda e, bo=bo, cs=cs: e.matmul(banks[1][:, cs], lhsT=g.onesrow[:],
                                                           rhs=g.brow_lo[0:1, bo:bo + W2], start=False, stop=True),
                       reads=[b_cst], writes=[bbuf[1]], sig=(q2 == NQ2 - 1))
                op(act, lambda e, half=half: e.activation(out=va[:, half * 8:(half + 1) * 8, 0:64],
                                                          in_=banks[1][:, :].rearrange("p (h n) -> p h n", h=8),
                                                          func=AF.Copy), reads=[bbuf[1]], writes=[b_va])
                yield

        def gen_Q(cx):
            xn, b_xn = cx.xn
            for half in range(2):
                for q2 in range(NQ2):
                    w, b_w = g.loadA(L, 1, half * NQ2 + q2)
                    cs = slice(q2 * W2, (q2 + 1) * W2)
                    for kc in range(KC):
                        op(pe, lambda e, kc=kc, w=w, cs=cs: e.matmul(banks[1][:, cs], lhsT=xn[:, kc, :], rhs=w[:, kc, :],
                                                                    start=(kc == 0), stop=False),
                           reads=[b_xn, b_w], writes=[bbuf[1]], sig=False)
                    bo = 3 * D + half * 512 + q2 * W2
                    op(pe, lambda e, bo=bo, cs=cs: e.matmul(banks[1][:, cs], lhsT=g.onesrow[:],
                                                           rhs=g.brow_hi[0:1, bo:bo + W2], start=False, stop=False),
                       reads=[b_cst], writes=[bbuf[1]], sig=False)
                    op(pe, lambda e, bo=bo, cs=cs: e.matmul(banks[1][:, cs], lhsT=g.onesrow[:],
                                                           rhs=g.brow_lo[0:1, bo:bo + W2], start=False, stop=True),
                       reads=[b_cst], writes=[bbuf[1]], sig=(q2 == NQ2 - 1))
                op(act, lambda e, half=half: e.activation(out=zs[:, half * 512:(half + 1) * 512], in_=banks[1][:, :],
                                                          func=AF.Silu), reads=[bbuf[1]], writes=[b_zs])
                yield

        def gen_A(cx, j, u, kvs):
            i = cx.t
            xn, b_xn = cx.xn
            Ba, Bb, Bc = u.banks
            w, b_w = g.loadB(L, 0, j)
            for kc in range(KC):
                op(pe, lambda e, kc=kc: e.matmul(banks[Ba][:, 0:P], lhsT=w[:, kc, :], rhs=xn[:, kc, :],
                                                 start=(kc == 0), stop=(kc == KC - 1)),
                   reads=[b_w, b_xn], writes=[bbuf[Ba]], sig=(kc == KC - 1))
            op(act, lambda e: e.activation(out=u.qT[:], in_=banks[Ba][:, 0:P], func=AF.Identity, scale=0.125,
                                           bias=pvc('bq8', j)), reads=[bbuf[Ba], b_pv], writes=[u.b_qT])
            yield
            ent = geo[i]
            for h in range(2):
                hg = 2 * j + h
                for n_, (kt, di, pi) in enumerate(ent):
                    kT, b_kT = kvs[kt].kT
                    bk_ = Bb if n_ < 4 else Bc
                    co = (n_ % 4) * P
                    op(pe, lambda e, kT=kT, bk_=bk_, co=co, h=h: e.matmul(banks[bk_][:, co:co + P],
                                                                         lhsT=kT[hq[h], j, :], rhs=u.qT[hq[h], :],
                                                                         start=True, stop=False),
                       reads=[b_kT, u.b_qT], writes=[bbuf[bk_]], sig=False)
                    op(pe, lambda e, bk_=bk_, co=co, hg=hg, di=di: e.matmul(banks[bk_][:, co:co + P],
                                                                           lhsT=RB[:, hg, di, :], rhs=g.Jrev[:],
                                                                           start=False, stop=False),
                       reads=[b_lc, b_cst], writes=[bbuf[bk_]], sig=False)
                    last = (n_ == len(ent) - 1) or (n_ == 3)
                    op(pe, lambda e, bk_=bk_, co=co, pi=pi: e.matmul(banks[bk_][:, co:co + P], lhsT=MK[:, pi, :],
                                                                    rhs=g.Jrev[:], start=False, stop=True),
                       reads=[b_lc, b_cst], writes=[bbuf[bk_]], sig=last)
                n4 = min(4, len(ent))
                op(act, lambda e, n4=n4: e.activation(out=u.PT[:, 0:n4, :].rearrange("p n k -> p (n k)"),
                                                      in_=banks[Bb][:, 0:n4 * P], func=AF.Exp),
                   reads=[bbuf[Bb]], writes=[u.b_PT])
                if len(ent) > 4:
                    op(act, lambda e: e.activation(out=u.PT[:, 4, :], in_=banks[Bc][:, 0:P], func=AF.Exp),
                       reads=[bbuf[Bc]], writes=[u.b_PT])
                for n_, (kt, di, pi) in enumerate(ent):
                    va, b_va = kvs[kt].va
                    op(pe, lambda e, n_=n_, va=va, hg=hg, h=h: e.matmul(banks[Ba][:, 2 * P + h * 65:2 * P + h * 65 + 65],
                                                                       lhsT=u.PT[:, n_, :], rhs=va[:, hg, :],
                                                                       start=(n_ == 0), stop=(n_ == len(ent) - 1)),
                       reads=[u.b_PT, b_va], writes=[bbuf[Ba]], sig=(n_ == len(ent) - 1))
                yield
            for h in range(2):
                o0 = 2 * P + h * 65
                op(dve, lambda e, o0=o0, h=h: e.reciprocal(out=u.rc[:, h:h + 1], in_=banks[Ba][:, o0 + 64:o0 + 65]),
                   reads=[bbuf[Ba]], writes=[u.b_rc])
                op(dve, lambda e, o0=o0, h=h: e.tensor_scalar(out=og[:, (2 * j + h) * 64:(2 * j + h + 1) * 64],
                                                              in0=banks[Ba][:, o0:o0 + 64], scalar1=u.rc[:, h:h + 1],
                                                              scalar2=None, op0=ALU.mult),
                   reads=[bbuf[Ba], u.b_rc], writes=[b_og])
            yield

        def gen_O(cx):
            i = cx.t
            op(dve, lambda e: e.tensor_tensor(out=yg[:], in0=og[:], in1=zs[:], op=ALU.mult),
               reads=[b_og, b_zs], writes=[b_yg])
            for kc in range(KC):
                op(pe, lambda e, kc=kc: e.transpose(out=banks_bf[0][:, kc * P:(kc + 1) * P],
                                                    in_=yg[:, kc * P:(kc + 1) * P], identity=g.identb[:]),
                   reads=[b_yg, g.b_identb], writes=[bbuf[0]], sig=(kc == KC - 1))
            op(act, lambda e: e.activation(out=ygT[:].rearrange("p k n -> p (k n)"), in_=banks_bf[0][:, :],
                                           func=AF.Copy), reads=[bbuf[0]], writes=[b_ygT])
            yield
            for half in range(2):
                for q2 in range(NQ2):
                    w, b_w = g.loadA(L, 2, half * NQ2 + q2)
                    cs = slice(q2 * W2, (q2 + 1) * W2)
                    for kc in range(KC):
                        op(pe, lambda e, kc=kc, w=w, cs=cs: e.matmul(banks[0][:, cs], lhsT=ygT[:, kc, :], rhs=w[:, kc, :],
                                                                    start=(kc == 0), stop=False),
                           reads=[b_ygT, b_w], writes=[bbuf[0]], sig=False)
                    bo = 4 * D + half * 512 + q2 * W2
                    op(pe, lambda e, bo=bo, cs=cs: e.matmul(banks[0][:, cs], lhsT=g.onesrow[:],
                                                           rhs=g.brow_hi[0:1, bo:bo + W2], start=False, stop=False),
                       reads=[b_cst], writes=[bbuf[0]], sig=False)
                    op(pe, lambda e, bo=bo, cs=cs: e.matmul(banks[0][:, cs], lhsT=g.onesrow[:],
                                                           rhs=g.brow_lo[0:1, bo:bo + W2], start=False, stop=True),
                       reads=[b_cst], writes=[bbuf[0]], sig=(q2 == NQ2 - 1))
                op(act, lambda e, half=half: e.activation(out=tmpB[:, half * 512:(half + 1) * 512], in_=banks[0][:, :],
                                                          func=AF.Copy), reads=[bbuf[0]], writes=[b_tmpB])
                yield
            op(act, lambda e: e.activation(out=tmpA[:], in_=tmpB[:], func=AF.Square, accum_out=stat[:, 4:5]),
               reads=[b_tmpB], writes=[b_tmpA, b_stat])
            op(dve, lambda e: e.tensor_scalar(out=stat[:, 5:6], in0=stat[:, 4:5], scalar1=1.0 / D, scalar2=RMS_EPS,
                                              op0=ALU.mult, op1=ALU.add), reads=[b_stat], writes=[b_stat])
            op(act, lambda e: e.activation(out=stat[:, 6:7], in_=stat[:, 5:6], func=AF.Sqrt), reads=[b_stat],
               writes=[b_stat])
            op(dve, lambda e: e.reciprocal(out=stat[:, 7:8], in_=stat[:, 6:7]), reads=[b_stat], writes=[b_stat])
            op(dve, lambda e: e.scalar_tensor_tensor(out=tmpB[:], in0=tmpB[:], scalar=stat[:, 7:8], in1=postg[:],
                                                     op0=ALU.mult, op1=ALU.mult),
               reads=[b_tmpB, b_stat, b_lc], writes=[b_tmpB])
            op(pool, lambda e: e.tensor_tensor(out=tmpA[:], in0=tmpB[:], in1=x1[:, i, :], op=ALU.add),
               reads=[b_tmpB, bx1[i]], writes=[b_tmpA])
            T_.dma(pool, g.y_out[si, i * P:(i + 1) * P, :], tmpA[:], reads=[b_tmpA])
            yield

        tasks = []
        kvs = [None] * NCH
        lastO = None
        lastKV = None

        def make_KV(t):
            cx = NS()
            cx.t = t
            tk = Task("K%d" % t)
            cx.kxn, cx.xn = p_xn.acquire(tk)
            cx.kkT, cx.kT = p_kT.acquire(tk)
            cx.kva, cx.va = p_va.acquire(tk)
            if lastKV[0] is not None:
                tk.deps.append(lastKV[0])
            tk.gen = gen_KV(cx)
            cx.tk = tk
            kvs[t] = cx
            tasks.append(tk)
            lastKV[0] = tk

        lastKV = [None]
        for s in range(NCH + 3):
            if s < NCH:
                make_KV(s)
            i = s - 3
            if i < 0:
                continue
            cx = kvs[i]
            need = [kvs[kt].tk for (kt, _, _) in geo[i]]
            tq = Task("Q%d" % i)
            tq.deps += [cx.tk]
            if lastO is not None:
                tq.deps.append(lastO)
            p_xn.share(tq, cx.kxn)
            tq.gen = gen_Q(cx)
            tasks.append(tq)
            tas = []
            for j in range(KC):
                ta = Task("A%d_%d" % (i, j))
                ta.deps += need + [cx.tk]
                if lastO is not None:
                    ta.deps.append(lastO)
                _, uset = p_uset.acquire(ta)
                p_xn.share(ta, cx.kxn)
                for (kt, _, _) in geo[i]:
                    p_kT.share(ta, kvs[kt].kkT)
                    p_va.share(ta, kvs[kt].kva)
                ta.gen = gen_A(cx, j, uset, kvs)
                tas.append(ta)
                tasks.append(ta)
            to = Task("O%d" % i)
            to.deps += tas + [tq]
            to.gen = gen_O(cx)
            tasks.append(to)
            lastO = to
        run_tasks(tasks, window=WINDOW)


def kernel(**inputs):
    xp = np.asarray(inputs['x_prompt'], dtype=np.float32)
    xs_ = np.asarray(inputs['x_sample'], dtype=np.float32)
    xall = np.concatenate([xp, xs_], axis=0)
    nseq = xall.shape[0] // N_CORES
    nc = build(nseq)
    in_maps = []
    for ci in range(N_CORES):
        m = {"x": np.ascontiguousarray(xall[ci * nseq:(ci + 1) * nseq])}
        for nm in W_NAMES:
            m[nm] = np.ascontiguousarray(np.asarray(inputs[nm], dtype=np.float32))
        in_maps.append(m)
    res = run_bass_kernel_spmd(nc, in_maps, core_ids=list(range(N_CORES)))
    yall = np.concatenate([r["y"] for r in res.results], axis=0)
    nb = xp.shape[0]
    return (np.ascontiguousarray(yall[:nb]), np.ascontiguousarray(yall[nb:]))
```

```python
import numpy as np
from contextlib import ExitStack
import concourse.bass as bass
import concourse.mybir as mybir
from concourse.bass_utils import run_bass_kernel_spmd
from concourse.alu_op_type import AluOpType as ALU

F32 = mybir.dt.float32
BF16 = mybir.dt.bfloat16
AF = mybir.ActivationFunctionType
AX = mybir.AxisListType

N_CORES = 8
D = 1024
KC = 8
P = 128
T_SEQ = 2048
LAM = float(np.exp(-0.5))
LNX_EPS = 64e-5
RMS_EPS = 1e-6

import os
SAME_ENGINE_SYNC = os.environ.get('K_SES', '1') == '1'
WINDOW = int(os.environ.get('K_WIN', '4'))
FY = int(os.environ.get('K_FY', '0'))
UY = int(os.environ.get('K_UY', '1'))
STAGGER = int(os.environ.get('K_STAG', '0'))
NQ2 = int(os.environ.get('K_NQ2', '1'))
W2 = 512 // NQ2
DBUF = int(os.environ.get('K_DBUF', '1'))


class _Sem:
    def __init__(self, sem, name):
        self.sem = sem
        self.name = name
        self.n = 0


class Eng:
    def __init__(self, h, sem, name, is_pe=False):
        self.h = h
        self.s = _Sem(sem, name)
        self.name = name
        self.is_pe = is_pe
        self.waited = {}


class Buf:
    __slots__ = ("name", "w", "r", "excl")

    def __init__(self, name="", excl=False):
        self.name = name
        self.w = None
        self.r = {}
        self.excl = excl


class Trk:
    def __init__(self, nc, es, n_slots=16):
        self.nc = nc
        mk = lambda nm: es.enter_context(nc.semaphore(nm))
        self.pe = Eng(nc.tensor, mk("s_pe"), "pe", is_pe=True)
        self.act = Eng(nc.scalar, mk("s_act"), "act")
        self.dve = Eng(nc.vector, mk("s_dve"), "dve")
        self.pool = Eng(nc.gpsimd, mk("s_pool"), "pool")
        self.sp = Eng(nc.sync, mk("s_sp"), "sp")
        self.engs = [self.pe, self.act, self.dve, self.pool, self.sp]
        self.slots = [_Sem(mk("s_dma%d" % i), "dma%d" % i) for i in range(n_slots)]
        self.dma_i = 0
        self.slots_sw = [_Sem(mk("s_swdma%d" % i), "swdma%d" % i) for i in range(4)]
        self.dma_sw_i = 0
        self.n_inst = 0
        self.cnt = {}

    def _deps(self, reads, writes):
        deps = {}
        for b in reads:
            if b.w is not None:
                s, v = b.w
                if deps.get(s, 0) < v:
                    deps[s] = v
        for b in writes:
            if b.w is not None:
                s, v = b.w
                if deps.get(s, 0) < v:
                    deps[s] = v
            for s, v in b.r.items():
                if deps.get(s, 0) < v:
                    deps[s] = v
        return deps

    def _wait(self, eng, deps):
        for s, v in deps.items():
            if eng.waited.get(s, 0) >= v:
                continue
            if s is eng.s:
                if eng.is_pe or not SAME_ENGINE_SYNC:
                    continue
            eng.h.wait_ge(s.sem, v)
            eng.waited[s] = v

    def _mark(self, tok, reads, writes):
        s, v = tok
        for b in reads:
            if b.r.get(s, 0) < v:
                b.r[s] = v
        for b in writes:
            b.w = tok
            b.r = {}

    def op(self, eng, fn, reads=(), writes=(), sig=True):
        if any(b.excl for b in reads):
            writes = list(writes) + [b for b in reads if b.excl]
            reads = [b for b in reads if not b.excl]
        self._wait(eng, self._deps(reads, writes))
        inst = fn(eng.h)
        tok = (eng.s, eng.s.n + 1)
        if sig:
            inst.then_inc(eng.s.sem, 1)
            eng.s.n += 1
        self._mark(tok, reads, writes)
        self.n_inst += 1
        self.cnt[eng.name] = self.cnt.get(eng.name, 0) + 1
        return inst

    def dma(self, q, out, in_, reads=(), writes=(), **kw):
        if q is self.pool:
            slot = self.slots_sw[self.dma_sw_i % len(self.slots_sw)]
            self.dma_sw_i += 1
        else:
            slot = self.slots[self.dma_i % len(self.slots)]
            self.dma_i += 1
        deps = self._deps(reads, writes)
        if slot.n > 0 and deps.get(slot, 0) < slot.n:
            deps[slot] = slot.n
        self._wait(q, deps)
        inst = q.h.dma_start(out=out, in_=in_, **kw)
        inst.then_inc(slot.sem, 16)
        slot.n += 16
        self._mark((slot, slot.n), reads, writes)
        self.n_inst += 1
        self.cnt["dma"] = self.cnt.get("dma", 0) + 1
        return inst

    def barrier(self):
        allsems = [e.s for e in self.engs] + self.slots + self.slots_sw
        for e in self.engs:
            deps = {s: s.n for s in allsems if s.n > 0 and s is not e.s}
            self._wait(e, deps)

    def finish(self):
        deps = {s: s.n for s in self.slots + self.slots_sw if s.n > 0}
        self._wait(self.sp, deps)


class Task:
    def __init__(self, name):
        self.name = name
        self.deps = []
        self.done = False
        self.gen = None


def run_tasks(tasks, window=3):
    pending = list(tasks)
    active = []
    while pending or active:
        while pending and len(active) < window and all(d.done for d in pending[0].deps):
            active.append(pending.pop(0))
        if not active:
            raise RuntimeError("scheduler deadlock at %s" % pending[0].name)
        for t in list(active):
            try:
                next(t.gen)
            except StopIteration:
                t.done = True
                active.remove(t)


class RPool:
    def __init__(self, items):
        self.items = items
        self.i = 0
        self.users = [[] for _ in items]

    def acquire(self, task):
        k = self.i % len(self.items)
        self.i += 1
        task.deps += self.users[k]
        self.users[k] = [task]
        return k, self.items[k]

    def share(self, task, k):
        self.users[k].append(task)


class NS:
    pass


W_NAMES = ['pre_norm_g', 'post_norm_g', 'rk_mu', 'rk_w_r', 'rk_w_k', 'rk_w_v', 'rk_w_z', 'rk_w0', 'rk_w1', 'rk_w2',
           'rk_a0', 'rk_a1', 'rk_a2', 'rk_k_k', 'rk_k_a', 'rk_r_k', 'rk_lnx_w', 'rk_lnx_b', 'rk_w_o', 'na_w_in',
           'na_b_in', 'na_rpb', 'na_w_o', 'na_b_o']
W_SHAPES = {
    'pre_norm_g': [2, D], 'post_norm_g': [2, D], 'rk_mu': [1, 7, D], 'rk_w_r': [1, D, D], 'rk_w_k': [1, D, D],
    'rk_w_v': [1, D, D], 'rk_w_z': [1, D, D], 'rk_w0': [1, 2, D], 'rk_w1': [1, 2, D, 64], 'rk_w2': [1, 2, 64, D],
    'rk_a0': [1, 2, D], 'rk_a1': [1, 2, D, 64], 'rk_a2': [1, 2, 64, D], 'rk_k_k': [1, D], 'rk_k_a': [1, D],
    'rk_r_k': [1, 16, 64], 'rk_lnx_w': [1, D], 'rk_lnx_b': [1, D], 'rk_w_o': [1, D, D], 'na_w_in': [1, D, 4 * D],
    'na_b_in': [1, 4 * D], 'na_rpb': [1, 16, 15, 31], 'na_w_o': [1, D, D], 'na_b_o': [1, D],
}

PV = {}
_c = 0
for _nm, _n in [('mu', 7), ('w0', 2), ('a0', 2), ('k_k', 1), ('k_a', 1), ('r_k', 1), ('pre_g', 2), ('bq', 1),
                ('bk', 1), ('omk_a', 1), ('bq8', 1)]:
    PV[_nm] = _c
    _c += _n * 8
PV_COLS = _c
PV_ROWS = PV['omk_a']


def build(nseq, T=T_SEQ, stage="full"):
    NCH = T // P
    nc = bass.Bass("TRN2", target_bir_lowering=False)
    x_in = nc.dram_tensor("x", [nseq, T, D], F32, kind="ExternalInput").ap()
    y_out = nc.dram_tensor("y", [nseq, T, D], F32, kind="ExternalOutput").ap()
    wd = {nm: nc.dram_tensor(nm, W_SHAPES[nm], F32, kind="ExternalInput").ap() for nm in W_NAMES}
    wbJ = [nc.dram_tensor("wbJ%d" % l, [2, KC, P, KC, P], BF16, kind="Internal").ap() for l in range(2)]
    wbH = [nc.dram_tensor("wbH%d" % l, [3, 2 * NQ2, P, KC, W2], BF16, kind="Internal").ap() for l in range(2)]

    es = ExitStack()
    with es:
        T_ = Trk(nc, es)
        pe, act, dve, pool, sp = T_.pe, T_.act, T_.dve, T_.pool, T_.sp
        op = T_.op

        uid = [0]

        def sb(es_, nm, sh, dt):
            uid[0] += 1
            return es_.enter_context(nc.sbuf_tensor("%s_%d" % (nm, uid[0]), sh, dt))

        banks = [es.enter_context(nc.psum_tensor("pb%d" % i, [P, 512], F32)) for i in range(8)]
        bbuf = [Buf("pb%d" % i, excl=True) for i in range(8)]
        banks_bf = [b[:].bitcast(BF16) for b in banks]

        x1 = sb(es, "x1", [P, NCH, D], F32)
        bx1 = [Buf("x1_%d" % c) for c in range(NCH)]
        ringA = [sb(es, "ringA%d" % i, [P, KC, W2], BF16) for i in range(2)]
        b_ringA = [Buf("ringA%d" % i) for i in range(2)]
        ringB = [sb(es, "ringB%d" % i, [P, KC, P], BF16) for i in range(4)]
        b_ringB = [Buf("ringB%d" % i) for i in range(4)]
        rA_i = [0]
        rB_i = [0]
        identb = sb(es, "identb", [P, P], BF16)
        b_identb = Buf("identb")
        pv = sb(es, "pv", [P, PV_COLS], F32)
        b_pv = Buf("pv")
        onesrow = sb(es, "onesrow", [1, P], BF16)
        brow_hi = sb(es, "brow_hi", [1, 5 * D], BF16)
        brow_lo = sb(es, "brow_lo", [1, 5 * D], BF16)
        b_cst = Buf("consts")
        maskq = [sb(es, "maskq%d" % d, [P, 512], BF16) for d in range(2)]
        bdones = sb(es, "bdones", [P, P], BF16)
        hsel = sb(es, "hsel", [P, 2], BF16)
        ones_f = sb(es, "ones_f", [P, P], F32)
        Jrev = sb(es, "Jrev", [P, P], BF16)
        pes0 = ExitStack()
        io_r = sb(pes0, "io_r", [P, P], F32)
        io_c = sb(pes0, "io_c", [P, P], F32)

        def pvc(nm, idx):
            c0 = PV[nm] + idx
            return pv[:, c0:c0 + 1]

        def loadA(layer, m, qt):
            k = rA_i[0] % 2
            rA_i[0] += 1
            T_.dma(sp, ringA[k][:], wbH[layer][m, qt], writes=[b_ringA[k]])
            return ringA[k], b_ringA[k]

        def loadB(layer, m, j):
            k = rB_i[0] % 4
            rB_i[0] += 1
            T_.dma(sp, ringB[k][:], wbJ[layer][m, j], writes=[b_ringB[k]])
            return ringB[k], b_ringB[k]

        op(pool, lambda e: e.iota(io_r[:], pattern=[[0, P]], base=0, channel_multiplier=1,
                                  allow_small_or_imprecise_dtypes=True), writes=[b_cst])
        op(pool, lambda e: e.iota(io_c[:], pattern=[[1, P]], base=0, channel_multiplier=0,
                                  allow_small_or_imprecise_dtypes=True), writes=[b_cst])
        op(dve, lambda e: e.tensor_tensor(out=identb[:], in0=io_r[:], in1=io_c[:], op=ALU.is_equal),
           reads=[b_cst], writes=[b_identb])
        op(dve, lambda e: e.memset(onesrow[:], 1.0), writes=[b_cst])
        op(dve, lambda e: e.memset(ones_f[:], 1.0), writes=[b_cst])
        for d_, (o_s, o_i) in enumerate([(ALU.is_lt, ALU.is_le), (ALU.is_gt, ALU.is_ge)]):
            for q4 in range(4):
                o_ = o_s if q4 % 2 == 0 else o_i
                op(dve, lambda e, d_=d_, q4=q4, o_=o_: e.tensor_tensor(out=maskq[d_][:, q4 * P:(q4 + 1) * P],
                                                                      in0=io_r[:], in1=io_c[:], op=o_),
                   reads=[b_cst], writes=[b_cst])

        with ExitStack() as pes:
            identf = sb(pes, "identf", [P, P], F32)
            rb_ = sb(pes, "rb_", [P, P], F32)
            cb_ = sb(pes, "cb_", [P, P], F32)
            op(dve, lambda e: e.tensor_tensor(out=identf[:], in0=io_r[:], in1=io_c[:], op=ALU.is_equal),
               reads=[b_cst], writes=[b_cst])
            op(dve, lambda e: e.tensor_scalar(out=rb_[:], in0=io_r[:], scalar1=63.5, scalar2=None, op0=ALU.is_gt),
               reads=[b_cst], writes=[b_cst])
            op(dve, lambda e: e.tensor_scalar(out=cb_[:], in0=io_c[:], scalar1=63.5, scalar2=None, op0=ALU.is_gt),
               reads=[b_cst], writes=[b_cst])
            op(dve, lambda e: e.tensor_tensor(out=bdones[:], in0=rb_[:], in1=cb_[:], op=ALU.is_equal),
               reads=[b_cst], writes=[b_cst])
            op(dve, lambda e: e.tensor_copy(out=hsel[:, 1:2], in_=rb_[:, 0:1]), reads=[b_cst], writes=[b_cst])
            op(dve, lambda e: e.tensor_scalar(out=hsel[:, 0:1], in0=rb_[:, 0:1], scalar1=-1.0, scalar2=1.0,
                                              op0=ALU.mult, op1=ALU.add), reads=[b_cst], writes=[b_cst])

            rows = sb(pes, "pvrows", [P, 2, P], F32)
            b_rows = Buf("pvrows")
            op(dve, lambda e: e.memset(rows[:], 0.0), writes=[b_rows])

            def load_rows(r0, src):
                n = src.shape[0]
                g, o = divmod(r0, P)
                assert o + n <= P, (r0, n)
                T_.dma(sp, rows[o:o + n, g, :], src, writes=[b_rows])

            load_rows(PV['mu'], wd['rk_mu'][0].rearrange("m (j q) -> (m j) q", q=P))
            load_rows(PV['w0'], wd['rk_w0'][0].rearrange("m (j q) -> (m j) q", q=P))
            load_rows(PV['a0'], wd['rk_a0'][0].rearrange("m (j q) -> (m j) q", q=P))
            load_rows(PV['k_k'], wd['rk_k_k'][0].rearrange("(j q) -> j q", q=P))
            load_rows(PV['k_a'], wd['rk_k_a'][0].rearrange("(j q) -> j q", q=P))
            load_rows(PV['r_k'], wd['rk_r_k'][0].rearrange("(j h) c -> j (h c)", h=2))
            load_rows(PV['pre_g'], wd['pre_norm_g'].rearrange("m (j q) -> (m j) q", q=P))
            load_rows(PV['bq'], wd['na_b_in'][0, 0:D].rearrange("(j q) -> j q", q=P))
            load_rows(PV['bk'], wd['na_b_in'][0, D:2 * D].rearrange("(j q) -> j q", q=P))
            for g in range(2):
                op(pe, lambda e, g=g: e.transpose(out=banks[0][:, g * P:(g + 1) * P], in_=rows[:, g, :],
                                                  identity=identf[:]),
                   reads=[b_rows, b_cst], writes=[bbuf[0]])
            op(dve, lambda e: e.tensor_copy(out=pv[:, 0:PV_ROWS], in_=banks[0][:, 0:PV_ROWS]),
               reads=[bbuf[0]], writes=[b_pv])
            op(dve, lambda e: e.tensor_scalar(out=pv[:, PV['omk_a']:PV['omk_a'] + 8],
                                              in0=pv[:, PV['k_a']:PV['k_a'] + 8], scalar1=-1.0, scalar2=1.0,
                                              op0=ALU.mult, op1=ALU.add), reads=[b_pv], writes=[b_pv])
            op(dve, lambda e: e.tensor_scalar(out=pv[:, PV['bq8']:PV['bq8'] + 8], in0=pv[:, PV['bq']:PV['bq'] + 8],
                                              scalar1=0.125, scalar2=None, op0=ALU.mult), reads=[b_pv], writes=[b_pv])

            brow_f = sb(pes, "brow_f", [1, 5 * D], F32)
            brow_t = sb(pes, "brow_t", [1, 5 * D], F32)
            b_bf = Buf("brow_f")
            T_.dma(sp, brow_f[0:1, 0:4 * D], wd['na_b_in'][0:1, :], writes=[b_bf])
            T_.dma(sp, brow_f[0:1, 4 * D:5 * D], wd['na_b_o'][0:1, :], writes=[b_bf])
            op(act, lambda e: e.activation(out=brow_hi[:], in_=brow_f[:], func=AF.Copy), reads=[b_bf], writes=[b_cst])
            op(dve, lambda e: e.tensor_tensor(out=brow_t[:], in0=brow_f[:], in1=brow_hi[:], op=ALU.subtract),
               reads=[b_bf, b_cst], writes=[b_bf])
            op(act, lambda e: e.activation(out=brow_lo[:], in_=brow_t[:], func=AF.Copy), reads=[b_bf], writes=[b_cst])

            stg = [sb(pes, "stg%d" % i, [P, D], F32) for i in range(3)]
            stb = [sb(pes, "stb%d" % i, [P, D], BF16) for i in range(3)]
            b_stg = [Buf() for _ in range(3)]
            b_stb = [Buf() for _ in range(3)]
            srcs = []
            for kc in range(KC):
                rs_ = slice(kc * P, (kc + 1) * P)
                for m, nm in enumerate(['rk_w_r', 'rk_w_k']):
                    srcs.append((wd[nm][0, rs_, :], wbJ[0][m, :, :, kc, :].rearrange("j p n -> p j n"), 'J'))
                for m, nm in enumerate(['rk_w_v', 'rk_w_z', 'rk_w_o']):
                    srcs.append((wd[nm][0, rs_, :], wbH[0][m, :, :, kc, :].rearrange("h p n -> p h n"), 'H'))
                for m in range(2):
                    srcs.append((wd['na_w_in'][0, rs_, m * D:(m + 1) * D],
                                 wbJ[1][m, :, :, kc, :].rearrange("j p n -> p j n"), 'J'))
                for m in range(2):
                    srcs.append((wd['na_w_in'][0, rs_, (m + 2) * D:(m + 3) * D],
                                 wbH[1][m, :, :, kc, :].rearrange("h p n -> p h n"), 'H'))
                srcs.append((wd['na_w_o'][0, rs_, :], wbH[1][2, :, :, kc, :].rearrange("h p n -> p h n"), 'H'))
            cast_engs = [act, dve, pool]
            for i, (src, dst, kind) in enumerate(srcs):
                k = i % 3
                T_.dma(sp, stg[k][:], src, writes=[b_stg[k]])
                if cast_engs[k] is act:
                    op(act, lambda e, k=k: e.activation(out=stb[k][:], in_=stg[k][:], func=AF.Copy),
                       reads=[b_stg[k]], writes=[b_stb[k]])
                else:
                    op(cast_engs[k], lambda e, k=k: e.tensor_copy(out=stb[k][:], in_=stg[k][:]),
                       reads=[b_stg[k]], writes=[b_stb[k]])
                if kind == 'J':
                    srcv = stb[k][:].rearrange("p (j n) -> p j n", j=KC)
                else:
                    srcv = stb[k][:].rearrange("p (h n) -> p h n", h=2 * NQ2)
                T_.dma(sp, dst, srcv, reads=[b_stb[k]])
            T_.barrier()
            T_.finish()
        pes0.close()

        ctx = NS()
        ctx.__dict__.update(locals())
        if stage != "l0":
            l1_prologue(ctx)
        for si in range(nseq):
            with nc.named_scope('L0_%d' % si):
                layer0(ctx, si)
            T_.barrier()
            T_.finish()
            if stage == "l0":
                for c in range(NCH):
                    T_.dma(sp, y_out[si, c * P:(c + 1) * P, :], x1[:, c, :], reads=[bx1[c]])
            else:
                with nc.named_scope('L1_%d' % si):
                    layer1(ctx, si)
            T_.barrier()
            T_.finish()
        print("instructions:", T_.n_inst, T_.cnt)
    return nc


def layer0(g, si):
    nc, T_, op = g.nc, g.T_, g.op
    pe, act, dve, pool, sp = g.pe, g.act, g.dve, g.pool, g.sp
    banks, bbuf, banks_bf = g.banks, g.bbuf, g.banks_bf
    NCH, x1, bx1, pv, b_pv, pvc = g.NCH, g.x1, g.bx1, g.pv, g.b_pv, g.pvc
    wd, sb = g.wd, g.sb
    b_cst = g.b_cst
    L = 0

    with ExitStack() as l0:
        w1b = sb(l0, "w1b", [P, 2, KC, 64], BF16)
        a1b = sb(l0, "a1b", [P, 2, KC, 64], BF16)
        w2b = sb(l0, "w2b", [64, 2, D], BF16)
        a2b = sb(l0, "a2b", [64, 2, D], BF16)
        lnxb = sb(l0, "lnxb", [P, 2, D], F32)
        postg = sb(l0, "postg", [P, D], F32)
        b_lc = Buf("l0consts")
        tmpA = sb(l0, "tmpA", [P, D], F32)
        tmpB = sb(l0, "tmpB", [P, D], F32)
        b_tmpA, b_tmpB = Buf("tmpA"), Buf("tmpB")
        T_.dma(sp, tmpA[:].rearrange("p (d k n) -> p d k n", d=2, k=KC),
               wd['rk_w1'][0].rearrange("d (k p) n -> p d k n", p=P), writes=[b_tmpA])
        op(dve, lambda e: e.tensor_copy(out=w1b[:].rearrange("p d k n -> p (d k n)"), in_=tmpA[:]),
           reads=[b_tmpA], writes=[b_lc])
        T_.dma(sp, tmpB[:].rearrange("p (d k n) -> p d k n", d=2, k=KC),
               wd['rk_a1'][0].rearrange("d (k p) n -> p d k n", p=P), writes=[b_tmpB])
        op(dve, lambda e: e.tensor_copy(out=a1b[:].rearrange("p d k n -> p (d k n)"), in_=tmpB[:]),
           reads=[b_tmpB], writes=[b_lc])
        for d_ in range(2):
            T_.dma(sp, tmpA[0:64, :], wd['rk_w2'][0, d_], writes=[b_tmpA])
            op(dve, lambda e, d_=d_: e.tensor_copy(out=w2b[:, d_, :], in_=tmpA[0:64, :]), reads=[b_tmpA], writes=[b_lc])
            T_.dma(sp, tmpB[0:64, :], wd['rk_a2'][0, d_], writes=[b_tmpB])
            op(dve, lambda e, d_=d_: e.tensor_copy(out=a2b[:, d_, :], in_=tmpB[0:64, :]), reads=[b_tmpB], writes=[b_lc])
        T_.dma(sp, lnxb[:, 0, :], wd['rk_lnx_w'][0].partition_broadcast(P), writes=[b_lc])
        T_.dma(sp, lnxb[:, 1, :], wd['rk_lnx_b'][0].partition_broadcast(P), writes=[b_lc])
        T_.dma(sp, postg[:], wd['post_norm_g'][L].partition_broadcast(P), writes=[b_lc])

        def mk(nm, sh, dt, n):
            return [(sb(l0, "%s%d" % (nm, i), sh, dt), Buf("%s%d" % (nm, i))) for i in range(n)]

        p_xin = RPool(mk("xin", [P, D], F32, 1))
        xs, b_xs = mk("xs", [P, D], BF16, 1)[0]
        stat = sb(l0, "stat", [P, 8], F32)
        b_stat = Buf("stat")
        p_slot = RPool(mk("xnT", [P, KC, P + 2], BF16, 3))
        xx, b_xx = mk("xx", [P, KC, P], BF16, 1)[0]
        ltmp, b_ltmp = mk("ltmp", [P, P], F32, 1)[0]
        p_lr = RPool(mk("lrp_r", [P, KC, P], BF16, DBUF))
        p_lk = RPool(mk("lrp_k", [P, KC, P], BF16, DBUF))
        p_lt = mk("lrp_t", [P, KC, P], BF16, 1)
        lt_i = [0]
        p_vt = RPool(mk("Vtm", [P, D], BF16, DBUF))
        p_zs = RPool(mk("zs", [P, D], BF16, 1))
        p_th = RPool(mk("th", [64, 2 * P], BF16, 2))
        Hf = sb(l0, "Hf", [P, KC, 64], F32)
        Hb = sb(l0, "Hb", [P, KC, 64], BF16)
        b_H = [Buf("H%d" % j) for j in range(KC)]
        bonF = sb(l0, "bonF", [P, NCH, 16], F32)
        b_bonF = [Buf("bonF%d" % c) for c in range(NCH)]
        p_bonB = RPool(mk("bonB", [P, 16], F32, 2))
        p_yb = RPool(mk("Yb", [P, D], F32, 1))
        yg, b_yg = mk("yg", [P, D], BF16, 1)[0]
        ygT, b_ygT = mk("ygT", [P, KC, P], BF16, 1)[0]
        gst = sb(l0, "gst", [P, 6, 16], F32)
        b_gst = Buf("gst")

        def mkset(i):
            u = NS()
            u.i = i
            def t(nm, sh, dt):
                tt = sb(l0, "u%d_%s" % (i, nm), sh, dt)
                setattr(u, nm, tt)
                setattr(u, "b_" + nm, Buf("u%d_%s" % (i, nm)))
            t("rk", [P, 2 * P], F32)
            for nm in ("sg", "al", "kk", "rs", "Ein", "Eex", "ein", "kd"):
                t(nm, [P, P], F32)
            t("sq", [P, P], BF16)
            t("arT", [P, 2 * P], BF16)
            t("btT", [P, P], BF16)
            t("ktT", [P, P], BF16)
            t("pr", [P, P], BF16)
            t("BK", [P, 2 * P], BF16)
            t("AT0", [P, 512], BF16)
            t("AT1", [P, 512], BF16)
            for h in range(2):
                for k in range(2):
                    t("C%d%d" % (h, k), [P, 3 * P], BF16)
            t("Xp", [P, P], BF16)
            t("Up", [P, P], BF16)
            t("Hs", [P, 64], F32)
            t("sc", [P, 4], F32)
            u.banks = (2 + 3 * i, 3 + 3 * i, 4 + 3 * i)
            return u

        p_uset = RPool([mkset(0), mkset(1)])

        def gen_Na(cx):
            c = cx.c
            xin, b_xin = cx.xin
            T_.dma(sp, xin[:], g.x_in[si, c * P:(c + 1) * P, :], writes=[b_xin])
            op(act, lambda e: e.activation(out=xs[:], in_=xin[:], func=AF.Square, accum_out=stat[:, 0:1]),
               reads=[b_xin], writes=[b_xs, b_stat])
            op(dve, lambda e: e.tensor_scalar(out=stat[:, 1:2], in0=stat[:, 0:1], scalar1=1.0 / D, scalar2=RMS_EPS,
                                              op0=ALU.mult, op1=ALU.add), reads=[b_stat], writes=[b_stat])
            op(act, lambda e: e.activation(out=stat[:, 2:3], in_=stat[:, 1:2], func=AF.Sqrt), reads=[b_stat],
               writes=[b_stat])
            op(dve, lambda e: e.reciprocal(out=stat[:, 3:4], in_=stat[:, 2:3]), reads=[b_stat], writes=[b_stat])
            op(act, lambda e: e.activation(out=xs[:], in_=xin[:], func=AF.Copy, scale=stat[:, 3:4]),
               reads=[b_xin, b_stat], writes=[b_xs])
            yield

        def gen_Nb(cx, d, first, last, prev):
            c = cx.c
            slot, b_slot = cx.slot
            for kc in range(KC):
                op(pe, lambda e, kc=kc: e.transpose(out=banks_bf[0][:, kc * P:(kc + 1) * P],
                                                    in_=xs[:, kc * P:(kc + 1) * P], identity=g.identb[:]),
                   reads=[b_xs, g.b_identb], writes=[bbuf[0]], sig=(kc == KC - 1))
            for kc in range(KC):
                sc = pvc('pre_g', L * 8 + kc)
                if kc % 2 == 0:
                    op(dve, lambda e, kc=kc, sc=sc: e.tensor_scalar(out=slot[:, kc, 1:P + 1],
                                                                  in0=banks_bf[0][:, kc * P:(kc + 1) * P],
                                                                  scalar1=sc, scalar2=None, op0=ALU.mult),
                       reads=[bbuf[0], b_pv], writes=[b_slot])
                else:
                    op(act, lambda e, kc=kc, sc=sc: e.activation(out=slot[:, kc, 1:P + 1],
                                                               in_=banks_bf[0][:, kc * P:(kc + 1) * P],
                                                               func=AF.Copy, scale=sc),
                       reads=[bbuf[0], b_pv], writes=[b_slot])
            near, far = (0, P + 1) if d == 0 else (P + 1, 0)
            if first:
                op(pool, lambda e: e.memset(slot[:, :, near:near + 1], 0.0), writes=[b_slot])
            else:
                pslot, b_pslot = prev.slot
                src_own = 1 if d == 0 else P
                src_prev = P if d == 0 else 1
                op(pool, lambda e: e.tensor_copy(out=slot[:, :, near:near + 1], in_=pslot[:, :, src_prev:src_prev + 1]),
                   reads=[b_pslot], writes=[b_slot])
                op(pool, lambda e: e.tensor_copy(out=pslot[:, :, far:far + 1], in_=slot[:, :, src_own:src_own + 1]),
                   reads=[b_slot], writes=[b_pslot])
            if last:
                op(pool, lambda e: e.memset(slot[:, :, far:far + 1], 0.0), writes=[b_slot])
            yield

        def gen_F(cx, d):
            slot, b_slot = cx.slot
            xn = slot[:, :, 1:P + 1]
            op(pool, lambda e: e.tensor_tensor(out=xx[:], in0=slot[:, :, 0:P], in1=slot[:, :, 2:P + 2], op=ALU.add),
               reads=[b_slot], writes=[b_xx])
            op(dve, lambda e: e.scalar_tensor_tensor(out=xx[:], in0=xx[:], scalar=0.5, in1=xn, op0=ALU.mult,
                                                     op1=ALU.subtract), reads=[b_xx, b_slot], writes=[b_xx])
            yield

            def lerp(m, dst, b_dst):
                for kc in range(KC):
                    sc = pvc('mu', m * 8 + kc)
                    op(dve, lambda e, kc=kc, sc=sc: e.scalar_tensor_tensor(
                        out=dst[:, kc, :], in0=xx[:, kc, :], scalar=sc, in1=slot[:, kc, 1:P + 1],
                        op0=ALU.mult, op1=ALU.add), reads=[b_xx, b_slot, b_pv], writes=[b_dst])

            def next_lt():
                r = p_lt[0]
                lt_i[0] += 1
                return r

            lr, b_lr = cx.lr
            lk, b_lk = cx.lk
            vt, b_vt = cx.vt
            lv, b_lv = next_lt()
            lerp(2, lv, b_lv)
            yield
            lerp(0, lr, b_lr)
            yield
            for _ in range(FY):
                yield
            for half in range(2):
                for q2 in range(NQ2):
                    w, b_w = g.loadA(L, 0, half * NQ2 + q2)
                    for kc in range(KC):
                        op(pe, lambda e, kc=kc, w=w, q2=q2: e.matmul(banks[1][:, q2 * W2:(q2 + 1) * W2],
                                                                    lhsT=lv[:, kc, :], rhs=w[:, kc, :],
                                                                    start=(kc == 0), stop=(kc == KC - 1)),
                           reads=[b_lv, b_w], writes=[bbuf[1]], sig=(kc == KC - 1 and q2 == NQ2 - 1))
                op(act, lambda e, half=half: e.activation(out=vt[:, half * 512:(half + 1) * 512], in_=banks[1][:, :],
                                                          func=AF.Copy), reads=[bbuf[1]], writes=[b_vt])
                yield
            if d == 1:
                zs, b_zs = cx.zs
                for half in range(2):
                    for q2 in range(NQ2):
                        w, b_w = g.loadA(L, 1, half * NQ2 + q2)
                        for kc in range(KC):
                            op(pe, lambda e, kc=kc, w=w, q2=q2: e.matmul(banks[1][:, q2 * W2:(q2 + 1) * W2],
                                                                        lhsT=slot[:, kc, 1:P + 1], rhs=w[:, kc, :],
                                                                        start=(kc == 0), stop=(kc == KC - 1)),
                               reads=[b_slot, b_w], writes=[bbuf[1]], sig=(kc == KC - 1 and q2 == NQ2 - 1))
                    op(act, lambda e, half=half: e.activation(out=zs[:, half * 512:(half + 1) * 512],
                                                              in_=banks[1][:, :], func=AF.Silu),
                       reads=[bbuf[1]], writes=[b_zs])
                    yield
            lerp(1, lk, b_lk)
            yield
            th, b_th = cx.th
            lw, b_lw = next_lt()
            lerp(3 + d, lw, b_lw)
            yield
            for _ in range(FY):
                yield
            for kc in range(KC):
                op(pe, lambda e, kc=kc: e.matmul(banks[1][0:64, 0:P], lhsT=w1b[:, d, kc, :], rhs=lw[:, kc, :],
                                                 start=(kc == 0), stop=(kc == KC - 1)),
                   reads=[b_lw, b_lc], writes=[bbuf[1]], sig=(kc == KC - 1))
            la, b_la = next_lt()
            lerp(5 + d, la, b_la)
            yield
            for _ in range(FY):
                yield
            for kc in range(KC):
                op(pe, lambda e, kc=kc: e.matmul(banks[1][0:64, P:2 * P], lhsT=a1b[:, d, kc, :], rhs=la[:, kc, :],
                                                 start=(kc == 0), stop=(kc == KC - 1)),
                   reads=[b_la, b_lc], writes=[bbuf[1]], sig=(kc == KC - 1))
            op(act, lambda e: e.activation(out=th[:, 0:P], in_=banks[1][0:64, 0:P], func=AF.Tanh),
               reads=[bbuf[1]], writes=[b_th])
            op(act, lambda e: e.activation(out=th[:, P:2 * P], in_=banks[1][0:64, P:2 * P], func=AF.Copy),
               reads=[bbuf[1]], writes=[b_th])
            yield

        def gen_U(cx, d, j, u):
            c = cx.c
            Ba, Bb, Bc = u.banks
            lr, b_lr = cx.lr
            lk, b_lk = cx.lk
            vt, b_vt = cx.vt
            th, b_th = cx.th
            hq = [slice(0, 64), slice(64, 128)]
            mq = g.maskq[d]
            mA = g.maskq[1 - d][:, 0:P]
            for _ in range(STAGGER if (j % 2 == 1) else 0):
                yield
            wr, b_wr = g.loadB(L, 0, j)
            wk, b_wk = g.loadB(L, 1, j)
            for kc in range(KC):
                op(pe, lambda e, kc=kc: e.matmul(banks[Ba][:, 0:P], lhsT=wr[:, kc, :], rhs=lr[:, kc, :],
                                                 start=(kc == 0), stop=(kc == KC - 1)),
                   reads=[b_wr, b_lr], writes=[bbuf[Ba]], sig=False)
            for kc in range(KC):
                op(pe, lambda e, kc=kc: e.matmul(banks[Ba][:, P:2 * P], lhsT=wk[:, kc, :], rhs=lk[:, kc, :],
                                                 start=(kc == 0), stop=(kc == KC - 1)),
                   reads=[b_wk, b_lk], writes=[bbuf[Ba]], sig=False)
            op(pe, lambda e: e.matmul(banks[Ba][:, 2 * P:3 * P], lhsT=w2b[:, d, j * P:(j + 1) * P], rhs=th[:, 0:P],
                                      start=True, stop=True), reads=[b_lc, b_th], writes=[bbuf[Ba]], sig=False)
            op(pe, lambda e: e.matmul(banks[Ba][:, 3 * P:4 * P], lhsT=a2b[:, d, j * P:(j + 1) * P], rhs=th[:, P:2 * P],
                                      start=True, stop=True), reads=[b_lc, b_th], writes=[bbuf[Ba]])
            op(act, lambda e: e.activation(out=u.rk[:], in_=banks[Ba][:, 0:2 * P], func=AF.Copy),
               reads=[bbuf[Ba]], writes=[u.b_rk])
            op(act, lambda e: e.activation(out=u.sg[:], in_=banks[Ba][:, 2 * P:3 * P], func=AF.Sigmoid,
                                           bias=pvc('w0', d * 8 + j)), reads=[bbuf[Ba], b_pv], writes=[u.b_sg])
            op(act, lambda e: e.activation(out=u.sq[:], in_=banks[Ba][:, P:2 * P], func=AF.Square,
                                           scale=pvc('k_k', j)), reads=[bbuf[Ba], b_pv], writes=[u.b_sq])
            op(act, lambda e: e.activation(out=u.al[:], in_=banks[Ba][:, 3 * P:4 * P], func=AF.Sigmoid,
                                           bias=pvc('a0', d * 8 + j)), reads=[bbuf[Ba], b_pv], writes=[u.b_al])
            yield
            rT = u.rk[:, 0:P]
            kT = u.rk[:, P:2 * P]
            op(dve, lambda e: e.tensor_scalar(out=u.kk[:], in0=kT, scalar1=pvc('k_k', j), scalar2=None, op0=ALU.mult),
               reads=[u.b_rk, b_pv], writes=[u.b_kk])
            op(pe, lambda e: e.matmul(banks[Ba][:, 0:P], lhsT=g.bdones[:], rhs=u.sq[:], start=True, stop=True),
               reads=[b_cst, u.b_sq], writes=[bbuf[Ba]])
            op(act, lambda e: e.activation(out=u.rs[:], in_=banks[Ba][:, 0:P], func=AF.Ln),
               reads=[bbuf[Ba]], writes=[u.b_rs])
            op(act, lambda e: e.activation(out=u.rs[:], in_=u.rs[:], func=AF.Exp, scale=-0.5),
               reads=[u.b_rs], writes=[u.b_rs])
            op(pool, lambda e: e.tensor_tensor(out=u.kk[:], in0=u.kk[:], in1=u.rs[:], op=ALU.mult),
               reads=[u.b_kk, u.b_rs], writes=[u.b_kk])
            op(dve, lambda e: e.tensor_tensor_scan(out=u.Ein[:], data0=g.ones_f[:], data1=u.sg[:], initial=0.0,
                                                   op0=ALU.mult, op1=ALU.add),
               reads=[b_cst, u.b_sg], writes=[u.b_Ein])
            tot = u.Ein[:, P - 1:P]
            op(act, lambda e: e.activation(out=u.sc[:, 0:1], in_=tot, func=AF.Exp, scale=-LAM),
               reads=[u.b_Ein], writes=[u.b_sc])
            if d == 0:
                op(pool, lambda e: e.tensor_tensor(out=u.Eex[:], in0=u.Ein[:], in1=u.sg[:], op=ALU.subtract),
                   reads=[u.b_Ein, u.b_sg], writes=[u.b_Eex])
            else:
                op(dve, lambda e: e.tensor_copy(out=u.sc[:, 1:2], in_=tot), reads=[u.b_Ein], writes=[u.b_sc])
                op(dve, lambda e: e.tensor_scalar(out=u.Eex[:], in0=u.Ein[:], scalar1=u.sc[:, 1:2], scalar2=-1.0,
                                                  op0=ALU.subtract, op1=ALU.mult),
                   reads=[u.b_Ein, u.b_sc], writes=[u.b_Eex])
                op(pool, lambda e: e.tensor_tensor(out=u.Ein[:], in0=u.Eex[:], in1=u.sg[:], op=ALU.add),
                   reads=[u.b_Eex, u.b_sg], writes=[u.b_Ein])
            yield
            op(act, lambda e: e.activation(out=u.ein[:], in_=u.Ein[:], func=AF.Exp, scale=-LAM),
               reads=[u.b_Ein], writes=[u.b_ein])
            op(act, lambda e: e.activation(out=u.Eex[:], in_=u.Eex[:], func=AF.Exp, scale=-LAM),
               reads=[u.b_Eex], writes=[u.b_Eex])
            op(act, lambda e: e.activation(out=u.Ein[:], in_=u.Ein[:], func=AF.Exp, scale=LAM),
               reads=[u.b_Ein], writes=[u.b_Ein])
            eng_ = u.Ein
            eex_ = u.Eex
            op(dve, lambda e: e.scalar_tensor_tensor(out=u.arT[:, 0:P], in0=u.kk[:], scalar=-1.0, in1=eex_[:],
                                                     op0=ALU.mult, op1=ALU.mult),
               reads=[u.b_kk, u.b_Eex], writes=[u.b_arT])
            op(pool, lambda e: e.tensor_tensor(out=u.arT[:, P:2 * P], in0=rT, in1=u.ein[:], op=ALU.mult),
               reads=[u.b_rk, u.b_ein], writes=[u.b_arT])
            op(dve, lambda e: e.tensor_scalar(out=u.kd[:], in0=u.al[:], scalar1=pvc('k_a', j),
                                               scalar2=pvc('omk_a', j), op0=ALU.mult, op1=ALU.add),
               reads=[u.b_al, b_pv], writes=[u.b_kd])
            op(pool, lambda e: e.tensor_tensor(out=u.kd[:], in0=u.kd[:], in1=kT, op=ALU.mult),
               reads=[u.b_kd, u.b_rk], writes=[u.b_kd])
            op(dve, lambda e: e.tensor_tensor(out=u.ktT[:], in0=u.kd[:], in1=eng_[:], op=ALU.mult),
               reads=[u.b_kd, u.b_Ein], writes=[u.b_ktT])
            op(pool, lambda e: e.tensor_tensor(out=u.al[:], in0=u.al[:], in1=u.kk[:], op=ALU.mult),
               reads=[u.b_al, u.b_kk], writes=[u.b_al])
            op(dve, lambda e: e.tensor_tensor(out=u.btT[:], in0=u.al[:], in1=eng_[:], op=ALU.mult),
               reads=[u.b_al, u.b_Ein], writes=[u.b_btT])
            op(dve, lambda e: e.scalar_tensor_tensor(out=u.pr[:], in0=rT, scalar=pvc('r_k', j), in1=u.kd[:],
                                                     op0=ALU.mult, op1=ALU.mult),
               reads=[u.b_rk, u.b_kd, b_pv], writes=[u.b_pr])
            yield
            for _ in range(UY):
                yield
            op(pe, lambda e: e.transpose(out=banks_bf[Ba][:, 0:P], in_=u.btT[:], identity=g.identb[:]),
               reads=[u.b_btT, g.b_identb], writes=[bbuf[Ba]], sig=False)
            op(pe, lambda e: e.transpose(out=banks_bf[Ba][:, P:2 * P], in_=u.ktT[:], identity=g.identb[:]),
               reads=[u.b_ktT, g.b_identb], writes=[bbuf[Ba]], sig=False)
            op(pe, lambda e: e.matmul(banks[Ba][:, 2 * P:2 * P + 2], lhsT=u.pr[:], rhs=g.hsel[:], start=True, stop=True),
               reads=[u.b_pr, b_cst], writes=[bbuf[Ba]])
            op(act, lambda e: e.activation(out=u.BK[:], in_=banks_bf[Ba][:, 0:2 * P], func=AF.Copy),
               reads=[bbuf[Ba]], writes=[u.b_BK])
            if d == 0:
                bdst, b_bdst = bonF[:, c, 2 * j:2 * j + 2], b_bonF[c]
            else:
                bt_, b_bdst = cx.bonB
                bdst = bt_[:, 2 * j:2 * j + 2]
            op(act, lambda e: e.activation(out=bdst, in_=banks[Ba][:, 2 * P:2 * P + 2], func=AF.Copy),
               reads=[bbuf[Ba]], writes=[b_bdst])
            yield
            for _ in range(UY):
                yield
            AT = [u.AT0, u.AT1]
            b_AT = [u.b_AT0, u.b_AT1]
            Cc = [[u.C00, u.C01], [u.C10, u.C11]]
            b_C = [[u.b_C00, u.b_C01], [u.b_C10, u.b_C11]]
            hb = [Bb, Bc]
            for h in range(2):
                B_ = hb[h]
                op(pe, lambda e, h=h, B_=B_: e.matmul(banks[B_][:, 0:2 * P], lhsT=u.btT[hq[h], :], rhs=u.arT[hq[h], :],
                                                      start=True, stop=True),
                   reads=[u.b_btT, u.b_arT], writes=[bbuf[B_]], sig=False)
                op(pe, lambda e, h=h, B_=B_: e.matmul(banks[B_][:, 2 * P:4 * P], lhsT=u.ktT[hq[h], :],
                                                      rhs=u.arT[hq[h], :], start=True, stop=True),
                   reads=[u.b_ktT, u.b_arT], writes=[bbuf[B_]])
                op(pe, lambda e, h=h: e.matmul(banks[Ba][:, h * P:(h + 1) * P], lhsT=u.arT[hq[h], 0:P],
                                               rhs=u.btT[hq[h], :], start=True, stop=True),
                   reads=[u.b_btT, u.b_arT], writes=[bbuf[Ba]])
            for h in range(2):
                B_ = hb[h]
                op(dve, lambda e, h=h, B_=B_: e.tensor_tensor(out=AT[h][:], in0=banks[B_][:, :], in1=mq[:],
                                                              op=ALU.mult),
                   reads=[bbuf[B_], b_cst], writes=[b_AT[h]])
                op(dve, lambda e, h=h: e.tensor_tensor(out=Cc[h][0][:, 0:P], in0=banks[Ba][:, h * P:(h + 1) * P],
                                                       in1=mA, op=ALU.mult),
                   reads=[bbuf[Ba], b_cst], writes=[b_C[h][0]])
                op(pool, lambda e, h=h: e.tensor_tensor(out=Cc[h][1][:, 2 * P:3 * P], in0=AT[h][:, 0:P],
                                                        in1=g.identb[:], op=ALU.add),
                   reads=[b_AT[h], g.b_identb], writes=[b_C[h][1]])
            yield
            for h in range(2):
                B_ = hb[h]
                A0 = Cc[h][0][:, 0:P]
                B0 = AT[h][:, 0:P]
                op(pe, lambda e, B_=B_, A0=A0, B0=B0: e.matmul(banks[B_][:, 0:P], lhsT=B0, rhs=A0, start=True, stop=True),
                   reads=[b_AT[h], b_C[h][0]], writes=[bbuf[B_]], sig=False)
                op(pe, lambda e, B_=B_, A0=A0, B0=B0: e.matmul(banks[B_][:, P:2 * P], lhsT=A0, rhs=B0, start=True,
                                                               stop=True),
                   reads=[b_AT[h], b_C[h][0]], writes=[bbuf[B_]])
                ev = dve if h == 0 else act
                if ev is dve:
                    op(dve, lambda e, h=h, B_=B_: e.tensor_copy(out=Cc[h][1][:, 0:2 * P], in_=banks[B_][:, 0:2 * P]),
                       reads=[bbuf[B_]], writes=[b_C[h][1]])
                else:
                    op(act, lambda e, h=h, B_=B_: e.activation(out=Cc[h][1][:, 0:2 * P], in_=banks[B_][:, 0:2 * P],
                                                               func=AF.Copy),
                       reads=[bbuf[B_]], writes=[b_C[h][1]])
            yield
            for lev in range(1, 7):
                src_i = lev % 2
                dst_i = 1 - src_i
                for h in range(2):
                    B_ = hb[h]
                    S = Cc[h][src_i]
                    Dd = Cc[h][dst_i]
                    bS, bD = b_C[h][src_i], b_C[h][dst_i]
                    Ak, Bk, Mk = S[:, 0:P], S[:, P:2 * P], S[:, 2 * P:3 * P]
                    lo = 0
                    if lev <= 5:
                        op(pe, lambda e, B_=B_, Ak=Ak, Bk=Bk: e.matmul(banks[B_][:, 0:P], lhsT=Bk, rhs=Ak, start=True,
                                                                       stop=True),
                           reads=[bS], writes=[bbuf[B_]], sig=False)
                    else:
                        lo = 2 * P
                    r0 = P if lev <= 4 else 2 * P
                    op(pe, lambda e, B_=B_, Ak=Ak, S=S, r0=r0: e.matmul(banks[B_][:, r0:3 * P], lhsT=Ak, rhs=S[:, r0:3 * P],
                                                                        start=True, stop=True),
                       reads=[bS], writes=[bbuf[B_]], sig=(h == 0))
                    if h == 1:
                        op(pe, lambda e, B_=B_, Mk=Mk: e.matmul(banks[B_][:, 2 * P:3 * P], lhsT=g.identb[:], rhs=Mk,
                                                                start=False, stop=True),
                           reads=[bS, g.b_identb], writes=[bbuf[B_]])
                        op(act, lambda e, B_=B_, Dd=Dd, lo=lo: e.activation(out=Dd[:, lo:3 * P],
                                                                            in_=banks[B_][:, lo:3 * P], func=AF.Copy),
                           reads=[bbuf[B_]], writes=[bD])
                    else:
                        if lev <= 5:
                            hi_ = 2 * P if lev <= 4 else P
                            op(dve, lambda e, B_=B_, Dd=Dd, hi_=hi_: e.tensor_copy(out=Dd[:, 0:hi_],
                                                                                   in_=banks[B_][:, 0:hi_]),
                               reads=[bbuf[B_]], writes=[bD])
                        op(dve, lambda e, B_=B_, Dd=Dd, Mk=Mk: e.tensor_tensor(out=Dd[:, 2 * P:3 * P],
                                                                               in0=banks[B_][:, 2 * P:3 * P], in1=Mk,
                                                                               op=ALU.add),
                           reads=[bbuf[B_], bS], writes=[bD])
                yield
            Mfin = [Cc[h][1][:, 2 * P:3 * P] for h in range(2)]
            b_Mfin = [b_C[h][1] for h in range(2)]
            b_Hj = g_bH[j]
            for h in range(2):
                op(pe, lambda e, h=h: e.matmul(banks[Ba][:, h * 64:(h + 1) * 64], lhsT=u.arT[hq[h], 0:P],
                                               rhs=Hb[hq[h], j, :], start=True, stop=False),
                   reads=[u.b_arT, b_Hj], writes=[bbuf[Ba]], sig=False)
                op(pe, lambda e, h=h: e.matmul(banks[Ba][:, h * 64:(h + 1) * 64], lhsT=AT[h][:, 2 * P:3 * P],
                                               rhs=vt[:, (2 * j + h) * 64:(2 * j + h + 1) * 64], start=False, stop=True),
                   reads=[b_AT[h], b_vt], writes=[bbuf[Ba]], sig=(h == 1))
            op(act, lambda e: e.activation(out=u.Xp[:], in_=banks[Ba][:, 0:P], func=AF.Copy),
               reads=[bbuf[Ba]], writes=[u.b_Xp])
            for h in range(2):
                op(pe, lambda e, h=h: e.matmul(banks[Ba][:, P + h * 64:P + (h + 1) * 64], lhsT=Mfin[h],
                                               rhs=u.Xp[:, h * 64:(h + 1) * 64], start=True, stop=True),
                   reads=[b_Mfin[h], u.b_Xp], writes=[bbuf[Ba]], sig=(h == 1))
            op(dve, lambda e: e.tensor_copy(out=u.Up[:], in_=banks[Ba][:, P:2 * P]), reads=[bbuf[Ba]], writes=[u.b_Up])
            op(dve, lambda e: e.tensor_scalar(out=u.Hs[:], in0=Hf[:, j, :], scalar1=u.sc[:, 0:1], scalar2=None,
                                               op0=ALU.mult), reads=[b_Hj, u.b_sc], writes=[u.b_Hs])
            yield
            for h in range(2):
                yo = 2 * P + h * 64
                op(pe, lambda e, h=h, yo=yo: e.matmul(banks[Ba][:, yo:yo + 64], lhsT=u.arT[hq[h], P:2 * P],
                                                      rhs=Hb[hq[h], j, :], start=True, stop=False),
                   reads=[u.b_arT, b_Hj], writes=[bbuf[Ba]], sig=False)
                op(pe, lambda e, h=h, yo=yo: e.matmul(banks[Ba][:, yo:yo + 64], lhsT=AT[h][:, P:2 * P],
                                                      rhs=u.Up[:, h * 64:(h + 1) * 64], start=False, stop=False),
                   reads=[b_AT[h], u.b_Up], writes=[bbuf[Ba]], sig=False)
                op(pe, lambda e, h=h, yo=yo: e.matmul(banks[Ba][:, yo:yo + 64], lhsT=AT[h][:, 3 * P:4 * P],
                                                      rhs=vt[:, (2 * j + h) * 64:(2 * j + h + 1) * 64],
                                                      start=False, stop=True),
                   reads=[b_AT[h], b_vt], writes=[bbuf[Ba]], sig=False)
            op(pe, lambda e: e.matmul(banks[Ba][:, 3 * P:4 * P], lhsT=u.BK[:, 0:P], rhs=u.Up[:], start=True, stop=False),
               reads=[u.b_BK, u.b_Up], writes=[bbuf[Ba]], sig=False)
            op(pe, lambda e: e.matmul(banks[Ba][:, 3 * P:4 * P], lhsT=u.BK[:, P:2 * P], rhs=vt[:, j * P:(j + 1) * P],
                                      start=False, stop=True),
               reads=[u.b_BK, b_vt], writes=[bbuf[Ba]])
            if d == 0:
                ydst = x1[:, c, :].bitcast(BF16)[:, j * P:(j + 1) * P]
                b_yd = bx1[c]
            else:
                yb_, b_yd = cx.yb
                ydst = yb_[:, j * P:(j + 1) * P]
            op(act, lambda e: e.activation(out=ydst, in_=banks[Ba][:, 2 * P:3 * P], func=AF.Copy),
               reads=[bbuf[Ba]], writes=[b_yd])
            for h in range(2):
                op(dve, lambda e, h=h: e.scalar_tensor_tensor(out=Hf[hq[h], j, :],
                                                              in0=banks[Ba][hq[h], 3 * P + h * 64:3 * P + (h + 1) * 64],
                                                              scalar=u.sc[hq[h], 0:1], in1=u.Hs[hq[h], :],
                                                              op0=ALU.mult, op1=ALU.add),
                   reads=[bbuf[Ba], u.b_sc, u.b_Hs], writes=[b_Hj])
            op(pool, lambda e: e.tensor_copy(out=Hb[:, j, :], in_=Hf[:, j, :]), reads=[b_Hj], writes=[b_Hj])
            yield

        g_bH = b_H

        def gen_Ba(cx):
            c = cx.c
            yb_, b_yb = cx.yb
            zs, b_zs = cx.zs
            vt, b_vt = cx.vt
            bonB, b_bonB = cx.bonB
            yf = x1[:, c, :].bitcast(BF16)[:, 0:D]
            op(dve, lambda e: e.tensor_tensor(out=yb_[:], in0=yb_[:], in1=yf, op=ALU.add),
               reads=[b_yb, bx1[c]], writes=[b_yb])
            y3 = yb_[:].rearrange("p (h n) -> p h n", h=16)
            op(dve, lambda e: e.tensor_reduce(out=gst[:, 0, :], in_=y3, axis=AX.X, op=ALU.add),
               reads=[b_yb], writes=[b_gst])
            op(act, lambda e: e.activation(out=tmpA[:], in_=yb_[:], func=AF.Square), reads=[b_yb], writes=[b_tmpA])
            op(dve, lambda e: e.tensor_reduce(out=gst[:, 1, :], in_=tmpA[:].rearrange("p (h n) -> p h n", h=16),
                                              axis=AX.X, op=ALU.add), reads=[b_tmpA], writes=[b_gst])
            yield
            op(dve, lambda e: e.tensor_scalar(out=gst[:, 2, :], in0=gst[:, 0, :], scalar1=1.0 / 64, scalar2=None,
                                              op0=ALU.mult), reads=[b_gst], writes=[b_gst])
            op(dve, lambda e: e.tensor_tensor(out=gst[:, 3, :], in0=gst[:, 2, :], in1=gst[:, 2, :], op=ALU.mult),
               reads=[b_gst], writes=[b_gst])
            op(dve, lambda e: e.scalar_tensor_tensor(out=gst[:, 3, :], in0=gst[:, 1, :], scalar=1.0 / 64,
                                                     in1=gst[:, 3, :], op0=ALU.mult, op1=ALU.subtract),
               reads=[b_gst], writes=[b_gst])
            op(dve, lambda e: e.tensor_scalar(out=gst[:, 3, :], in0=gst[:, 3, :], scalar1=LNX_EPS, scalar2=None,
                                              op0=ALU.add), reads=[b_gst], writes=[b_gst])
            op(act, lambda e: e.activation(out=gst[:, 3, :], in_=gst[:, 3, :], func=AF.Sqrt), reads=[b_gst],
               writes=[b_gst])
            op(dve, lambda e: e.reciprocal(out=gst[:, 4, :], in_=gst[:, 3, :]), reads=[b_gst], writes=[b_gst])
            op(dve, lambda e: e.tensor_tensor(out=gst[:, 5, :], in0=bonF[:, c, :], in1=bonB[:], op=ALU.add),
               reads=[b_bonF[c], b_bonB], writes=[b_gst])
            mean_b = gst[:, 2, :].unsqueeze(2).broadcast_to([P, 16, 64])
            rstd_b = gst[:, 4, :].unsqueeze(2).broadcast_to([P, 16, 64])
            bon_b = gst[:, 5, :].unsqueeze(2).broadcast_to([P, 16, 64])
            op(dve, lambda e: e.tensor_tensor(out=y3, in0=y3, in1=mean_b, op=ALU.subtract),
               reads=[b_yb, b_gst], writes=[b_yb])
            op(pool, lambda e: e.tensor_tensor(out=y3, in0=y3, in1=rstd_b, op=ALU.mult),
               reads=[b_yb, b_gst], writes=[b_yb])
            yield
            op(dve, lambda e: e.tensor_tensor(out=yb_[:], in0=yb_[:], in1=lnxb[:, 0, :], op=ALU.mult),
               reads=[b_yb, b_lc], writes=[b_yb])
            op(pool, lambda e: e.tensor_tensor(out=yb_[:], in0=yb_[:], in1=lnxb[:, 1, :], op=ALU.add),
               reads=[b_yb, b_lc], writes=[b_yb])
            t3 = tmpA[:].rearrange("p (h n) -> p h n", h=16)
            op(dve, lambda e: e.tensor_tensor(out=t3, in0=vt[:].rearrange("p (h n) -> p h n", h=16), in1=bon_b,
                                              op=ALU.mult), reads=[b_vt, b_gst], writes=[b_tmpA])
            op(pool, lambda e: e.tensor_tensor(out=yb_[:], in0=yb_[:], in1=tmpA[:], op=ALU.add),
               reads=[b_yb, b_tmpA], writes=[b_yb])
            op(dve, lambda e: e.tensor_tensor(out=yg[:], in0=yb_[:], in1=zs[:], op=ALU.mult),
               reads=[b_yb, b_zs], writes=[b_yg])
            yield

        def gen_Bb(cx):
            c = cx.c
            xin, b_xin = tmpA, b_tmpA
            T_.dma(sp, xin[:], g.x_in[si, c * P:(c + 1) * P, :], writes=[b_xin])
            for kc in range(KC):
                op(pe, lambda e, kc=kc: e.transpose(out=banks_bf[0][:, kc * P:(kc + 1) * P],
                                                    in_=yg[:, kc * P:(kc + 1) * P], identity=g.identb[:]),
                   reads=[b_yg, g.b_identb], writes=[bbuf[0]], sig=(kc == KC - 1))
            op(act, lambda e: e.activation(out=ygT[:].rearrange("p k n -> p (k n)"), in_=banks_bf[0][:, :],
                                           func=AF.Copy), reads=[bbuf[0]], writes=[b_ygT])
            yield
            for half in range(2):
                for q2 in range(NQ2):
                    w, b_w = g.loadA(L, 2, half * NQ2 + q2)
                    for kc in range(KC):
                        op(pe, lambda e, kc=kc, w=w, q2=q2: e.matmul(banks[0][:, q2 * W2:(q2 + 1) * W2],
                                                                    lhsT=ygT[:, kc, :], rhs=w[:, kc, :],
                                                                    start=(kc == 0), stop=(kc == KC - 1)),
                           reads=[b_ygT, b_w], writes=[bbuf[0]], sig=(kc == KC - 1 and q2 == NQ2 - 1))
                op(act, lambda e, half=half: e.activation(out=tmpB[:, half * 512:(half + 1) * 512], in_=banks[0][:, :],
                                                          func=AF.Copy), reads=[bbuf[0]], writes=[b_tmpB])
                yield
            op(act, lambda e: e.activation(out=yg[:], in_=tmpB[:], func=AF.Square, accum_out=stat[:, 4:5]),
               reads=[b_tmpB], writes=[b_yg, b_stat])
            op(dve, lambda e: e.tensor_scalar(out=stat[:, 5:6], in0=stat[:, 4:5], scalar1=1.0 / D, scalar2=RMS_EPS,
                                              op0=ALU.mult, op1=ALU.add), reads=[b_stat], writes=[b_stat])
            op(act, lambda e: e.activation(out=stat[:, 6:7], in_=stat[:, 5:6], func=AF.Sqrt), reads=[b_stat],
               writes=[b_stat])
            op(dve, lambda e: e.reciprocal(out=stat[:, 7:8], in_=stat[:, 6:7]), reads=[b_stat], writes=[b_stat])
            op(dve, lambda e: e.scalar_tensor_tensor(out=tmpB[:], in0=tmpB[:], scalar=stat[:, 7:8], in1=postg[:],
                                                     op0=ALU.mult, op1=ALU.mult),
               reads=[b_tmpB, b_stat, b_lc], writes=[b_tmpB])
            op(pool, lambda e: e.tensor_tensor(out=x1[:, c, :], in0=tmpB[:], in1=xin[:], op=ALU.add),
               reads=[b_tmpB, b_xin], writes=[bx1[c]])
            yield

        def gen_Z():
            op(dve, lambda e: e.memset(Hf[:], 0.0), writes=b_H)
            op(pool, lambda e: e.memset(Hb[:], 0.0), writes=b_H)
            yield

        tasks = []
        lastB = None
        lastF = None
        for d in range(2):
            order = list(range(NCH)) if d == 0 else list(range(NCH - 1, -1, -1))
            tz = Task("Z%d" % d)
            tz.gen = gen_Z()
            tz.deps += [t for t in tasks if t.name.startswith("U")]
            tasks.append(tz)
            cxs = [None] * NCH

            def make_Na(i):
                cx = NS()
                cx.c = order[i]
                tn = Task("N%d_%d" % (d, cx.c))
                _, cx.xin = p_xin.acquire(tn)
                if i > 0:
                    tn.deps.append(cxs[i - 1].tnb)
                tn.gen = gen_Na(cx)
                cx.tna = tn
                cxs[i] = cx
                tasks.append(tn)

            def make_Nb(i):
                cx = cxs[i]
                tn = Task("N%d_%db" % (d, cx.c))
                cx.ks, cx.slot = p_slot.acquire(tn)
                tn.deps.append(cx.tna)
                prev = cxs[i - 1] if i > 0 else None
                if prev is not None:
                    p_slot.share(tn, prev.ks)
                tn.gen = gen_Nb(cx, d, i == 0, i == NCH - 1, prev)
                cx.tnb = tn
                cx.tn = tn
                tasks.append(tn)

            make_Na(0)
            make_Nb(0)
            if NCH > 1:
                make_Na(1)
                make_Nb(1)
            pendBb = None
            for i in range(NCH):
                cx = cxs[i]
                c = cx.c
                tf = Task("F%d_%d" % (d, c))
                tf.deps.append(cx.tn)
                if i + 1 < NCH:
                    tf.deps.append(cxs[i + 1].tn)
                if lastF is not None:
                    tf.deps.append(lastF)
                lastF = tf
                p_slot.share(tf, cx.ks)
                klr, cx.lr = p_lr.acquire(tf)
                klk, cx.lk = p_lk.acquire(tf)
                kvt, cx.vt = p_vt.acquire(tf)
                kth, cx.th = p_th.acquire(tf)
                if d == 1:
                    kzs, cx.zs = p_zs.acquire(tf)
                tf.gen = gen_F(cx, d)
                tasks.append(tf)
                tba = None
                if d == 1:
                    tba = Task("B%d" % c)
                    _, cx.yb = p_yb.acquire(tba)
                    _, cx.bonB = p_bonB.acquire(tba)
                tus = []
                for j in range(KC):
                    tu = Task("U%d_%d_%d" % (d, c, j))
                    tu.deps += [tf, tz]
                    if d == 1:
                        tu.deps += list(tba.deps)
                    _, uset = p_uset.acquire(tu)
                    p_lr.share(tu, klr)
                    p_lk.share(tu, klk)
                    p_vt.share(tu, kvt)
                    p_th.share(tu, kth)
                    tu.gen = gen_U(cx, d, j, uset)
                    if i > 0:
                        tu.deps.append(cxs[i - 1].tus[j])
                    tus.append(tu)
                    tasks.append(tu)
                    if j == 1 and i + 2 < NCH:
                        make_Na(i + 2)
                    if j == 3 and pendBb is not None:
                        tasks.append(pendBb)
                        pendBb = None
                cx.tus = tus
                if d == 1:
                    tba.deps += tus + [tf]
                    if lastB is not None:
                        tba.deps.append(lastB)
                    p_vt.share(tba, kvt)
                    p_zs.share(tba, kzs)
                    tba.gen = gen_Ba(cx)
                    tasks.append(tba)
                    tbb = Task("B%db" % c)
                    tbb.deps.append(tba)
                    tbb.gen = gen_Bb(cx)
                    lastB = tbb
                    if i == NCH - 1:
                        tasks.append(tbb)
                    else:
                        pendBb = tbb
                if i + 2 < NCH:
                    make_Nb(i + 2)
        import os
        allow = os.environ.get("L0_TASKS", "ZNFUB")
        tasks = [t for t in tasks if t.name[0] in allow]
        for t in tasks:
            t.deps = [d_ for d_ in t.deps if d_.name[0] in allow]
        run_tasks(tasks, window=WINDOW)


GRID_W = 64
N_ROWS = T_SEQ // GRID_W
NEG = -30000.0


def _rs(r):
    return min(max(r - 4, 0), N_ROWS - 8)


def l1_geometry():
    geo = []
    pats = []
    for i in range(N_ROWS // 2):
        rows = [2 * i, 2 * i + 1]
        lo = min(_rs(r) for r in rows)
        hi = max(_rs(r) + 7 for r in rows)
        ent = []
        for kt in range(lo // 2, hi // 2 + 1):
            pat = tuple(tuple(1 if _rs(2 * i + rl) <= 2 * kt + krl < _rs(2 * i + rl) + 8 else 0 for krl in range(2))
                        for rl in range(2))
            if pat not in pats:
                pats.append(pat)
            ent.append((kt, (2 * (kt - i) + 6) // 2, pats.index(pat)))
        geo.append(ent)
    return geo, pats


def l1_prologue(g):
    nc, T_, op, sb = g.nc, g.T_, g.op, g.sb
    pe, act, dve, pool, sp = g.pe, g.act, g.dve, g.pool, g.sp
    geo, pats = l1_geometry()
    g.l1_geo, g.l1_pats = geo, pats
    NMK = len(pats)
    g.padD = nc.dram_tensor("padD", [240, P], F32, kind="Internal").ap()
    g.rbD = nc.dram_tensor("rbD", [P, 16 * 7 * P], BF16, kind="Internal").ap()
    g.mkD = nc.dram_tensor("mkD", [P, (NMK + 1) * P], BF16, kind="Internal").ap()
    x1 = g.x1
    with ExitStack() as p1:
        b_t = Buf("l1pro")
        padt = sb(p1, "padt", [120, 2, P], F32)
        op(dve, lambda e: e.memset(padt[:], 0.0), writes=[b_t])
        rp = g.wd['na_rpb'][0].rearrange("h r m -> (h r) m")
        for gi in range(2):
            T_.dma(sp, padt[:, gi, 48:79], rp[gi * 120:(gi + 1) * 120, :], writes=[b_t])
        for gi in range(2):
            T_.dma(sp, g.padD[gi * 120:(gi + 1) * 120, :], padt[:, gi, :], reads=[b_t])
        T_.barrier()
        T_.finish()
        Hs = x1[:].rearrange("p c d -> p (c d)")[:, 0:240 * 64].rearrange("p (b k) -> p b k", k=64)
        b_hs = Buf("Hs")
        for h in range(16):
            for r2 in range(2):
                src = bass.AP(tensor=g.padD.tensor, offset=h * 15 * P, ap=[[1, 64], [P, 15], [1, 64]])
                T_.dma(sp, Hs[64 * r2:64 * r2 + 64, h * 15:(h + 1) * 15, :], src, writes=[b_hs])
        RBs = sb(p1, "RBs", [P, 16, 7, P], BF16)
        b_rb = Buf("RBs")
        op(pool, lambda e: e.memset(RBs[:].rearrange("p h d k -> p (h d k)"), 0.0), writes=[b_rb])
        engs = [dve, pool, act]
        n = 0
        for h in range(16):
            for rl in range(2):
                for krl in range(2):
                    dis = [di for di in range(7) if 0 <= 2 * di + 1 + krl - rl <= 14]
                    d0, nd = dis[0], len(dis)
                    ri0 = 2 * d0 + 1 + krl - rl
                    srcv = Hs[64 * rl:64 * rl + 64, h * 15 + ri0:h * 15 + ri0 + 2 * (nd - 1) + 1:2, :]
                    dstv = RBs[64 * rl:64 * rl + 64, h, d0:d0 + nd, 64 * krl:64 * krl + 64]
                    e_ = engs[n % 3]
                    n += 1
                    if e_ is act:
                        op(act, lambda e, s=srcv, d=dstv: e.activation(out=d, in_=s, func=AF.Copy),
                           reads=[b_hs], writes=[b_rb])
                    else:
                        op(e_, lambda e, s=srcv, d=dstv: e.tensor_copy(out=d, in_=s), reads=[b_hs], writes=[b_rb])
        T_.dma(sp, g.rbD[:, :], RBs[:].rearrange("p h d k -> p (h d k)"), reads=[b_rb])
        ior = sb(p1, "ior1", [P, P], F32)
        ioc = sb(p1, "ioc1", [P, P], F32)
        t1 = sb(p1, "mt1", [P, P], F32)
        t2 = sb(p1, "mt2", [P, P], F32)
        cm = sb(p1, "cm", [P, P], BF16)
        neg = sb(p1, "negt", [P, P], BF16)
        MKs = sb(p1, "MKs", [P, NMK + 1, P], BF16)
        b_m = Buf("mk")
        op(pool, lambda e: e.iota(ior[:], pattern=[[0, P]], base=0, channel_multiplier=1,
                                  allow_small_or_imprecise_dtypes=True), writes=[b_m])
        op(pool, lambda e: e.iota(ioc[:], pattern=[[1, P]], base=0, channel_multiplier=0,
                                  allow_small_or_imprecise_dtypes=True), writes=[b_m])
        op(dve, lambda e: e.tensor_tensor(out=t1[:], in0=ior[:], in1=ioc[:], op=ALU.add), reads=[b_m], writes=[b_m])
        op(dve, lambda e: e.tensor_scalar(out=t2[:], in0=t1[:], scalar1=63.0, scalar2=None, op0=ALU.is_equal),
           reads=[b_m], writes=[b_m])
        op(dve, lambda e: e.tensor_scalar(out=t1[:], in0=t1[:], scalar1=191.0, scalar2=None, op0=ALU.is_equal),
           reads=[b_m], writes=[b_m])
        op(dve, lambda e: e.tensor_tensor(out=g.Jrev[:], in0=t1[:], in1=t2[:], op=ALU.add), reads=[b_m],
           writes=[g.b_cst])
        op(dve, lambda e: e.tensor_scalar(out=t1[:], in0=ior[:], scalar1=63.5, scalar2=-64.0, op0=ALU.is_gt,
                                          op1=ALU.mult), reads=[b_m], writes=[b_m])
        op(dve, lambda e: e.tensor_tensor(out=t1[:], in0=t1[:], in1=ior[:], op=ALU.add), reads=[b_m], writes=[b_m])
        op(dve, lambda e: e.tensor_scalar(out=t1[:], in0=t1[:], scalar1=-1.0, scalar2=55.0, op0=ALU.mult,
                                          op1=ALU.add), reads=[b_m], writes=[b_m])
        op(dve, lambda e: e.tensor_scalar(out=t1[:], in0=t1[:], scalar1=0.0, scalar2=48.0, op0=ALU.max, op1=ALU.min),
           reads=[b_m], writes=[b_m])
        op(dve, lambda e: e.tensor_scalar(out=t2[:], in0=ioc[:], scalar1=63.5, scalar2=-64.0, op0=ALU.is_gt,
                                          op1=ALU.mult), reads=[b_m], writes=[b_m])
        op(dve, lambda e: e.tensor_tensor(out=t2[:], in0=t2[:], in1=ioc[:], op=ALU.add), reads=[b_m], writes=[b_m])
        op(dve, lambda e: e.tensor_tensor(out=t2[:], in0=t2[:], in1=t1[:], op=ALU.subtract), reads=[b_m],
           writes=[b_m])
        op(dve, lambda e: e.tensor_scalar(out=t1[:], in0=t2[:], scalar1=-0.5, scalar2=None, op0=ALU.is_gt),
           reads=[b_m], writes=[b_m])
        op(dve, lambda e: e.tensor_scalar(out=t2[:], in0=t2[:], scalar1=15.5, scalar2=None, op0=ALU.is_lt),
           reads=[b_m], writes=[b_m])
        op(dve, lambda e: e.tensor_tensor(out=t1[:], in0=t1[:], in1=t2[:], op=ALU.mult), reads=[b_m], writes=[b_m])
        op(dve, lambda e: e.tensor_scalar(out=cm[:], in0=t1[:], scalar1=-1.0, scalar2=-NEG, op0=ALU.add, op1=ALU.mult),
           reads=[b_m], writes=[b_m])
        op(dve, lambda e: e.memset(neg[:], NEG), writes=[b_m])
        for pi, pat in enumerate(pats):
            for rl in range(2):
                for krl in range(2):
                    src_t = cm if pat[rl][krl] else neg
                    op(pool, lambda e, pi=pi, rl=rl, krl=krl, s=src_t: e.tensor_copy(
                        out=MKs[64 * rl:64 * rl + 64, pi, 64 * krl:64 * krl + 64],
                        in_=s[64 * rl:64 * rl + 64, 64 * krl:64 * krl + 64]), reads=[b_m], writes=[b_m])
        op(pool, lambda e: e.tensor_copy(out=MKs[:, NMK, :], in_=neg[:]), reads=[b_m], writes=[b_m])
        T_.dma(sp, g.mkD[:, :], MKs[:].rearrange("p n k -> p (n k)"), reads=[b_m])
        T_.barrier()
        T_.finish()


def layer1(g, si):
    nc, T_, op = g.nc, g.T_, g.op
    pe, act, dve, pool, sp = g.pe, g.act, g.dve, g.pool, g.sp
    banks, bbuf, banks_bf = g.banks, g.bbuf, g.banks_bf
    NCH, x1, bx1, pv, b_pv, pvc = g.NCH, g.x1, g.bx1, g.pv, g.b_pv, g.pvc
    wd, sb = g.wd, g.sb
    b_cst = g.b_cst
    L = 1
    geo, pats = g.l1_geo, g.l1_pats
    NMK = len(pats)
    hq = [slice(0, 64), slice(64, 128)]

    with ExitStack() as l1:
        postg = sb(l1, "postg1", [P, D], F32)
        RB = sb(l1, "RB", [P, 16, 7, P], BF16)
        MK = sb(l1, "MK", [P, NMK + 1, P], BF16)
        b_lc = Buf("l1consts")
        T_.dma(sp, postg[:], wd['post_norm_g'][L].partition_broadcast(P), writes=[b_lc])
        T_.dma(sp, RB[:].rearrange("p h d k -> p (h d k)"), g.rbD[:, :], writes=[b_lc])
        T_.dma(sp, MK[:].rearrange("p n k -> p (n k)"), g.mkD[:, :], writes=[b_lc])

        def mk(nm, sh, dt, n):
            return [(sb(l1, "%s%d" % (nm, i), sh, dt), Buf("%s%d" % (nm, i))) for i in range(n)]

        xs, b_xs = mk("xs1", [P, D], BF16, 1)[0]
        stat = sb(l1, "stat1", [P, 8], F32)
        b_stat = Buf("stat1")
        p_xn = RPool(mk("xnT1", [P, KC, P], BF16, 4))
        p_kT = RPool(mk("kT", [P, KC, P], BF16, 7))
        p_va = RPool(mk("Vaug", [P, 16, 65], BF16, 7))
        zs, b_zs = mk("zs1", [P, D], BF16, 1)[0]
        og, b_og = mk("og", [P, D], F32, 1)[0]
        yg, b_yg = mk("yg1", [P, D], BF16, 1)[0]
        ygT, b_ygT = mk("ygT1", [P, KC, P], BF16, 1)[0]
        tmpA, b_tmpA = mk("tmpA1", [P, D], F32, 1)[0]
        tmpB, b_tmpB = mk("tmpB1", [P, D], F32, 1)[0]

        def mkset(i):
            u = NS()
            def t(nm, sh, dt):
                setattr(u, nm, sb(l1, "a%d_%s" % (i, nm), sh, dt))
                setattr(u, "b_" + nm, Buf("a%d_%s" % (i, nm)))
            t("qT", [P, P], BF16)
            t("PT", [P, 5, P], BF16)
            t("rc", [P, 2], F32)
            u.banks = (2 + 3 * i, 3 + 3 * i, 4 + 3 * i)
            return u

        p_uset = RPool([mkset(0), mkset(1)])
        for (va, b_va) in p_va.items:
            op(pool, lambda e, va=va: e.memset(va[:, :, 64:65], 1.0), writes=[b_va])

        def gen_KV(cx):
            t = cx.t
            xn, b_xn = cx.xn
            kT, b_kT = cx.kT
            va, b_va = cx.va
            xsrc = x1[:, t, :]
            op(act, lambda e: e.activation(out=xs[:], in_=xsrc, func=AF.Square, accum_out=stat[:, 0:1]),
               reads=[bx1[t]], writes=[b_xs, b_stat])
            op(dve, lambda e: e.tensor_scalar(out=stat[:, 1:2], in0=stat[:, 0:1], scalar1=1.0 / D, scalar2=RMS_EPS,
                                              op0=ALU.mult, op1=ALU.add), reads=[b_stat], writes=[b_stat])
            op(act, lambda e: e.activation(out=stat[:, 2:3], in_=stat[:, 1:2], func=AF.Sqrt), reads=[b_stat],
               writes=[b_stat])
            op(dve, lambda e: e.reciprocal(out=stat[:, 3:4], in_=stat[:, 2:3]), reads=[b_stat], writes=[b_stat])
            op(act, lambda e: e.activation(out=xs[:], in_=xsrc, func=AF.Copy, scale=stat[:, 3:4]),
               reads=[bx1[t], b_stat], writes=[b_xs])
            yield
            for kc in range(KC):
                op(pe, lambda e, kc=kc: e.transpose(out=banks_bf[0][:, kc * P:(kc + 1) * P],
                                                    in_=xs[:, kc * P:(kc + 1) * P], identity=g.identb[:]),
                   reads=[b_xs, g.b_identb], writes=[bbuf[0]], sig=(kc == KC - 1))
            for kc in range(KC):
                sc = pvc('pre_g', L * 8 + kc)
                if kc % 2 == 0:
                    op(dve, lambda e, kc=kc, sc=sc: e.tensor_scalar(out=xn[:, kc, :],
                                                                  in0=banks_bf[0][:, kc * P:(kc + 1) * P],
                                                                  scalar1=sc, scalar2=None, op0=ALU.mult),
                       reads=[bbuf[0], b_pv], writes=[b_xn])
                else:
                    op(act, lambda e, kc=kc, sc=sc: e.activation(out=xn[:, kc, :],
                                                               in_=banks_bf[0][:, kc * P:(kc + 1) * P],
                                                               func=AF.Copy, scale=sc),
                       reads=[bbuf[0], b_pv], writes=[b_xn])
            yield
            for j0 in range(0, KC, 4):
                for j in range(j0, j0 + 4):
                    w, b_w = g.loadB(L, 1, j)
                    for kc in range(KC):
                        op(pe, lambda e, kc=kc, j=j, w=w: e.matmul(banks[1][:, (j - j0) * P:(j - j0 + 1) * P],
                                                                  lhsT=w[:, kc, :], rhs=xn[:, kc, :],
                                                                  start=(kc == 0), stop=(kc == KC - 1)),
                           reads=[b_w, b_xn], writes=[bbuf[1]], sig=(kc == KC - 1 and j == j0 + 3))
                for j in range(j0, j0 + 4):
                    op(act, lambda e, j=j: e.activation(out=kT[:, j, :], in_=banks[1][:, (j - j0) * P:(j - j0 + 1) * P],
                                                        func=AF.Identity, bias=pvc('bk', j)),
                       reads=[bbuf[1], b_pv], writes=[b_kT])
                yield
            for half in range(2):
                for q2 in range(NQ2):
                    w, b_w = g.loadA(L, 0, half * NQ2 + q2)
                    cs = slice(q2 * W2, (q2 + 1) * W2)
                    for kc in range(KC):
                        op(pe, lambda e, kc=kc, w=w, cs=cs: e.matmul(banks[1][:, cs], lhsT=xn[:, kc, :], rhs=w[:, kc, :],
                                                                    start=(kc == 0), stop=False),
                           reads=[b_xn, b_w], writes=[bbuf[1]], sig=False)
                    bo = 2 * D + half * 512 + q2 * W2
                    op(pe, lambda e, bo=bo, cs=cs: e.matmul(banks[1][:, cs], lhsT=g.onesrow[:],
                                                           rhs=g.brow_hi[0:1, bo:bo + W2], start=False, stop=False),
                       reads=[b_cst], writes=[bbuf[1]], sig=False)
                    op(pe, lambda e, bo=bo, cs=cs: e.matmul(banks[1][:, cs], lhsT=g.onesrow[:],
                                                           rhs=g.brow_lo[0:1, bo:bo + W2], start=False, stop=True),
                       reads=[b_cst], writes=[bbuf[1]], sig=(q2 == NQ2 - 1))
                op(act, lambda e, half=half: e.activation(out=va[:, half * 8:(half + 1) * 8, 0:64],
                                                          in_=banks[1][:, :].rearrange("p (h n) -> p h n", h=8),
                                                          func=AF.Copy), reads=[bbuf[1]], writes=[b_va])
                yield

        def gen_Q(cx):
            xn, b_xn = cx.xn
            for half in range(2):
                for q2 in range(NQ2):
                    w, b_w = g.loadA(L, 1, half * NQ2 + q2)
                    cs = slice(q2 * W2, (q2 + 1) * W2)
                    for kc in range(KC):
                        op(pe, lambda e, kc=kc, w=w, cs=cs: e.matmul(banks[1][:, cs], lhsT=xn[:, kc, :], rhs=w[:, kc, :],
                                                                    start=(kc == 0), stop=False),
                           reads=[b_xn, b_w], writes=[bbuf[1]], sig=False)
                    bo = 3 * D + half * 512 + q2 * W2
                    op(pe, lambda e, bo=bo, cs=cs: e.matmul(banks[1][:, cs], lhsT=g.onesrow[:],
                                                           rhs=g.brow_hi[0:1, bo:bo + W2], start=False, stop=False),
                       reads=[b_cst], writes=[bbuf[1]], sig=False)
                    op(pe, lambda e, bo=bo, cs=cs: e.matmul(banks[1][:, cs], lhsT=g.onesrow[:],
                                                           rhs=g.brow_lo[0:1, bo:bo + W2], start=False, stop=True),
                       reads=[b_cst], writes=[bbuf[1]], sig=(q2 == NQ2 - 1))
                op(act, lambda e, half=half: e.activation(out=zs[:, half * 512:(half + 1) * 512], in_=banks[1][:, :],
                                                          func=AF.Silu), reads=[bbuf[1]], writes=[b_zs])
                yield

        def gen_A(cx, j, u, kvs):
            i = cx.t
            xn, b_xn = cx.xn
            Ba, Bb, Bc = u.banks
            w, b_w = g.loadB(L, 0, j)
            for kc in range(KC):
                op(pe, lambda e, kc=kc: e.matmul(banks[Ba][:, 0:P], lhsT=w[:, kc, :], rhs=xn[:, kc, :],
                                                 start=(kc == 0), stop=(kc == KC - 1)),
                   reads=[b_w, b_xn], writes=[bbuf[Ba]], sig=(kc == KC - 1))
            op(act, lambda e: e.activation(out=u.qT[:], in_=banks[Ba][:, 0:P], func=AF.Identity, scale=0.125,
                                           bias=pvc('bq8', j)), reads=[bbuf[Ba], b_pv], writes=[u.b_qT])
            yield
            ent = geo[i]
            for h in range(2):
                hg = 2 * j + h
                for n_, (kt, di, pi) in enumerate(ent):
                    kT, b_kT = kvs[kt].kT
                    bk_ = Bb if n_ < 4 else Bc
                    co = (n_ % 4) * P
                    op(pe, lambda e, kT=kT, bk_=bk_, co=co, h=h: e.matmul(banks[bk_][:, co:co + P],
                                                                         lhsT=kT[hq[h], j, :], rhs=u.qT[hq[h], :],
                                                                         start=True, stop=False),
                       reads=[b_kT, u.b_qT], writes=[bbuf[bk_]], sig=False)
                    op(pe, lambda e, bk_=bk_, co=co, hg=hg, di=di: e.matmul(banks[bk_][:, co:co + P],
                                                                           lhsT=RB[:, hg, di, :], rhs=g.Jrev[:],
                                                                           start=False, stop=False),
                       reads=[b_lc, b_cst], writes=[bbuf[bk_]], sig=False)
                    last = (n_ == len(ent) - 1) or (n_ == 3)
                    op(pe, lambda e, bk_=bk_, co=co, pi=pi: e.matmul(banks[bk_][:, co:co + P], lhsT=MK[:, pi, :],
                                                                    rhs=g.Jrev[:], start=False, stop=True),
                       reads=[b_lc, b_cst], writes=[bbuf[bk_]], sig=last)
                n4 = min(4, len(ent))
                op(act, lambda e, n4=n4: e.activation(out=u.PT[:, 0:n4, :].rearrange("p n k -> p (n k)"),
                                                      in_=banks[Bb][:, 0:n4 * P], func=AF.Exp),
                   reads=[bbuf[Bb]], writes=[u.b_PT])
                if len(ent) > 4:
                    op(act, lambda e: e.activation(out=u.PT[:, 4, :], in_=banks[Bc][:, 0:P], func=AF.Exp),
                       reads=[bbuf[Bc]], writes=[u.b_PT])
                for n_, (kt, di, pi) in enumerate(ent):
                    va, b_va = kvs[kt].va
                    op(pe, lambda e, n_=n_, va=va, hg=hg, h=h: e.matmul(banks[Ba][:, 2 * P + h * 65:2 * P + h * 65 + 65],
                                                                       lhsT=u.PT[:, n_, :], rhs=va[:, hg, :],
                                                                       start=(n_ == 0), stop=(n_ == len(ent) - 1)),
                       reads=[u.b_PT, b_va], writes=[bbuf[Ba]], sig=(n_ == len(ent) - 1))
                yield
            for h in range(2):
                o0 = 2 * P + h * 65
                op(dve, lambda e, o0=o0, h=h: e.reciprocal(out=u.rc[:, h:h + 1], in_=banks[Ba][:, o0 + 64:o0 + 65]),
                   reads=[bbuf[Ba]], writes=[u.b_rc])
                op(dve, lambda e, o0=o0, h=h: e.tensor_scalar(out=og[:, (2 * j + h) * 64:(2 * j + h + 1) * 64],
                                                              in0=banks[Ba][:, o0:o0 + 64], scalar1=u.rc[:, h:h + 1],
                                                              scalar2=None, op0=ALU.mult),
                   reads=[bbuf[Ba], u.b_rc], writes=[b_og])
            yield

        def gen_O(cx):
            i = cx.t
            op(dve, lambda e: e.tensor_tensor(out=yg[:], in0=og[:], in1=zs[:], op=ALU.mult),
               reads=[b_og, b_zs], writes=[b_yg])
            for kc in range(KC):
                op(pe, lambda e, kc=kc: e.transpose(out=banks_bf[0][:, kc * P:(kc + 1) * P],
                                                    in_=yg[:, kc * P:(kc + 1) * P], identity=g.identb[:]),
                   reads=[b_yg, g.b_identb], writes=[bbuf[0]], sig=(kc == KC - 1))
            op(act, lambda e: e.activation(out=ygT[:].rearrange("p k n -> p (k n)"), in_=banks_bf[0][:, :],
                                           func=AF.Copy), reads=[bbuf[0]], writes=[b_ygT])
            yield
            for half in range(2):
                for q2 in range(NQ2):
                    w, b_w = g.loadA(L, 2, half * NQ2 + q2)
                    cs = slice(q2 * W2, (q2 + 1) * W2)
                    for kc in range(KC):
                        op(pe, lambda e, kc=kc, w=w, cs=cs: e.matmul(banks[0][:, cs], lhsT=ygT[:, kc, :], rhs=w[:, kc, :],
                                                                    start=(kc == 0), stop=False),
                           reads=[b_ygT, b_w], writes=[bbuf[0]], sig=False)
                    bo = 4 * D + half * 512 + q2 * W2
                    op(pe, lambda e, bo=bo, cs=cs: e.matmul(banks[0][:, cs], lhsT=g.onesrow[:],
                                                           rhs=g.brow_hi[0:1, bo:bo + W2], start=False, stop=False),
                       reads=[b_cst], writes=[bbuf[0]], sig=False)
                    op(pe, lambda e, bo=bo, cs=cs: e.matmul(banks[0][:, cs], lhsT=g.onesrow[:],
                                                           rhs=g.brow_lo[0:1, bo:bo + W2], start=False, stop=True),
                       reads=[b_cst], writes=[bbuf[0]], sig=(q2 == NQ2 - 1))
                op(act, lambda e, half=half: e.activation(out=tmpB[:, half * 512:(half + 1) * 512], in_=banks[0][:, :],
                                                          func=AF.Copy), reads=[bbuf[0]], writes=[b_tmpB])
                yield
            op(act, lambda e: e.activation(out=tmpA[:], in_=tmpB[:], func=AF.Square, accum_out=stat[:, 4:5]),
               reads=[b_tmpB], writes=[b_tmpA, b_stat])
            op(dve, lambda e: e.tensor_scalar(out=stat[:, 5:6], in0=stat[:, 4:5], scalar1=1.0 / D, scalar2=RMS_EPS,
                                              op0=ALU.mult, op1=ALU.add), reads=[b_stat], writes=[b_stat])
            op(act, lambda e: e.activation(out=stat[:, 6:7], in_=stat[:, 5:6], func=AF.Sqrt), reads=[b_stat],
               writes=[b_stat])
            op(dve, lambda e: e.reciprocal(out=stat[:, 7:8], in_=stat[:, 6:7]), reads=[b_stat], writes=[b_stat])
            op(dve, lambda e: e.scalar_tensor_tensor(out=tmpB[:], in0=tmpB[:], scalar=stat[:, 7:8], in1=postg[:],
                                                     op0=ALU.mult, op1=ALU.mult),
               reads=[b_tmpB, b_stat, b_lc], writes=[b_tmpB])
            op(pool, lambda e: e.tensor_tensor(out=tmpA[:], in0=tmpB[:], in1=x1[:, i, :], op=ALU.add),
               reads=[b_tmpB, bx1[i]], writes=[b_tmpA])
            T_.dma(pool, g.y_out[si, i * P:(i + 1) * P, :], tmpA[:], reads=[b_tmpA])
            yield

        tasks = []
        kvs = [None] * NCH
        lastO = None
        lastKV = None

        def make_KV(t):
            cx = NS()
            cx.t = t
            tk = Task("K%d" % t)
            cx.kxn, cx.xn = p_xn.acquire(tk)
            cx.kkT, cx.kT = p_kT.acquire(tk)
            cx.kva, cx.va = p_va.acquire(tk)
            if lastKV[0] is not None:
                tk.deps.append(lastKV[0])
            tk.gen = gen_KV(cx)
            cx.tk = tk
            kvs[t] = cx
            tasks.append(tk)
            lastKV[0] = tk

        lastKV = [None]
        for s in range(NCH + 3):
            if s < NCH:
                make_KV(s)
            i = s - 3
            if i < 0:
                continue
            cx = kvs[i]
            need = [kvs[kt].tk for (kt, _, _) in geo[i]]
            tq = Task("Q%d" % i)
            tq.deps += [cx.tk]
            if lastO is not None:
                tq.deps.append(lastO)
            p_xn.share(tq, cx.kxn)
            tq.gen = gen_Q(cx)
            tasks.append(tq)
            tas = []
            for j in range(KC):
                ta = Task("A%d_%d" % (i, j))
                ta.deps += need + [cx.tk]
                if lastO is not None:
                    ta.deps.append(lastO)
                _, uset = p_uset.acquire(ta)
                p_xn.share(ta, cx.kxn)
                for (kt, _, _) in geo[i]:
                    p_kT.share(ta, kvs[kt].kkT)
                    p_va.share(ta, kvs[kt].kva)
                ta.gen = gen_A(cx, j, uset, kvs)
                tas.append(ta)
                tasks.append(ta)
            to = Task("O%d" % i)
            to.deps += tas + [tq]
            to.gen = gen_O(cx)
            tasks.append(to)
            lastO = to
        run_tasks(tasks, window=WINDOW)


def kernel(**inputs):
    xp = np.asarray(inputs['x_prompt'], dtype=np.float32)
    xs_ = np.asarray(inputs['x_sample'], dtype=np.float32)
    xall = np.concatenate([xp, xs_], axis=0)
    nseq = xall.shape[0] // N_CORES
    nc = build(nseq)
    in_maps = []
    for ci in range(N_CORES):
        m = {"x": np.ascontiguousarray(xall[ci * nseq:(ci + 1) * nseq])}
        for nm in W_NAMES:
            m[nm] = np.ascontiguousarray(np.asarray(inputs[nm], dtype=np.float32))
        in_maps.append(m)
    res = run_bass_kernel_spmd(nc, in_maps, core_ids=list(range(N_CORES)))
    yall = np.concatenate([r["y"] for r in res.results], axis=0)
    nb = xp.shape[0]
    return (np.ascontiguousarray(yall[:nb]), np.ascontiguousarray(yall[nb:]))
```
